# Optimizing a Trainium2 kernel written in Bass

```python
import jax, jax.numpy as jnp
from jax import lax
import numpy as np

D_MODEL = 1024
BATCH = 8
SEQ = 4096
DEPTH = 2

CHUNK = 64
N_A_LAYERS = DEPTH // 2
N_B_LAYERS = DEPTH - N_A_LAYERS
MIX_WIDTH = D_MODEL
MEM_TOKENS = 256
MEM_HEADS = 4
MEM_WIDTH = D_MODEL // 4
MEM_HEAD_DIM = MEM_WIDTH // MEM_HEADS
A_WIDTH = MIX_WIDTH - MEM_WIDTH
A_HEADS = 4
A_HEAD_DIM = A_WIDTH // A_HEADS
A_CONV = 4
B_HEAD_DIM = 64
B_HEADS = A_WIDTH // B_HEAD_DIM
B_WIDTH = B_HEADS * B_HEAD_DIM
BAND_CHUNKS = 9
BAND = BAND_CHUNKS * CHUNK
KV_PAD = BAND - CHUNK
MAX_REL = 128
REL_SIZE = MAX_REL + CHUNK
D_FF = ((8 * D_MODEL // 3 + 127) // 128) * 128
FFN_CONV = 3
A_IN = 4 * A_WIDTH + 2 * A_HEADS + MEM_WIDTH
B_IN = B_WIDTH + MEM_WIDTH
EPS = 1e-6

kernel_name = "yoco_mlstm_chunkband_memxattn_convffn"


def rmsnorm(x, g):
    x32 = x.astype(jnp.float32)
    y = x32 * lax.rsqrt(jnp.mean(x32 * x32, axis=-1, keepdims=True) + EPS)
    return (y * g.astype(jnp.float32)).astype(x.dtype)


def causal_dwconv(x, w, b):
    k = w.shape[0]
    s = x.shape[1]
    xp = jnp.pad(x, ((0, 0), (k - 1, 0), (0, 0)))
    y = xp[:, 0:s] * w[0]
    for j in range(1, k):
        y = y + xp[:, j:j + s] * w[j]
    return y + b


def mlstm_chunkwise(q, k, v, ig, logf):
    bsz, seq, nh, dh = q.shape
    nc = seq // CHUNK
    f32 = jnp.float32

    def to_chunks(t):
        t = t.astype(f32).reshape((bsz, nc, CHUNK) + t.shape[2:])
        perm = (1, 0, 3, 2) + tuple(range(4, t.ndim))
        return t.transpose(perm)

    qc, kc, vc = to_chunks(q), to_chunks(k * (dh ** -0.5)), to_chunks(v)
    igc, lfc = to_chunks(ig), to_chunks(logf)
    tril = jnp.tril(jnp.ones((CHUNK, CHUNK), dtype=bool))

    def step(carry, inp):
        c_st, n_st, m_st = carry
        qq, kk, vv, ii, lf = inp
        b = jnp.cumsum(lf, axis=-1)
        d = b[..., :, None] - b[..., None, :] + ii[..., None, :]
        d = jnp.where(tril, d, -jnp.inf)
        m_inter = b + m_st[..., None]
        m_t = jnp.maximum(m_inter, jnp.max(d, axis=-1))
        inter = jnp.exp(m_inter - m_t)
        s = jnp.einsum('bhtd,bhsd->bhts', qq, kk) * jnp.exp(d - m_t[..., None])
        num = jnp.einsum('bhts,bhse->bhte', s, vv) + inter[..., None] * jnp.einsum('bhtd,bhde->bhte', qq, c_st)
        den = jnp.sum(s, axis=-1) + inter * jnp.einsum('bhtd,bhd->bht', qq, n_st)
        h = num / jnp.maximum(jnp.abs(den), jnp.exp(-m_t))[..., None]
        b_last = b[..., -1]
        w = b_last[..., None] - b + ii
        m_new = jnp.maximum(b_last + m_st, jnp.max(w, axis=-1))
        w_exp = jnp.exp(w - m_new[..., None])
        decay = jnp.exp(b_last + m_st - m_new)
        c_new = decay[..., None, None] * c_st + jnp.einsum('bhs,bhsd,bhse->bhde', w_exp, kk, vv)
        n_new = decay[..., None] * n_st + jnp.einsum('bhs,bhsd->bhd', w_exp, kk)
        return (c_new, n_new, m_new), h

    init = (jnp.zeros((bsz, nh, dh, dh), f32), jnp.zeros((bsz, nh, dh), f32), jnp.zeros((bsz, nh), f32))
    _, h = lax.scan(step, init, (qc, kc, vc, igc, lfc))
    return h.transpose(1, 0, 3, 2, 4).reshape(bsz, seq, nh, dh)


def band_attention(q, k_pad, v_pad, rel_table):
    bsz, seq, _ = q.shape
    nc = seq // CHUNK
    q = q.reshape(bsz, seq, B_HEADS, B_HEAD_DIM)
    qi = jnp.arange(CHUNK)
    kj = jnp.arange(BAND)
    rel = KV_PAD + qi[:, None] - kj[None, :]
    idx = jnp.clip(rel, -(CHUNK - 1), MAX_REL) + (CHUNK - 1)
    bias = rel_table.astype(jnp.float32)[:, idx]
    scale = B_HEAD_DIM ** -0.5

    def one_chunk(c):
        start = c * CHUNK
        qc = lax.dynamic_slice_in_dim(q, start, CHUNK, axis=1)
        kc = lax.dynamic_slice_in_dim(k_pad, start, BAND, axis=1)
        vc = lax.dynamic_slice_in_dim(v_pad, start, BAND, axis=1)
        s = jnp.einsum('bqhd,bkhd->bhqk', qc, kc).astype(jnp.float32) * scale + bias
        valid = (start - KV_PAD + kj) >= 0
        s = jnp.where(valid, s, -jnp.inf)
        p = jax.nn.softmax(s, axis=-1).astype(vc.dtype)
        return jnp.einsum('bhqk,bkhd->bqhd', p, vc)

    out = lax.map(one_chunk, jnp.arange(nc))
    return out.transpose(1, 0, 2, 3, 4).reshape(bsz, seq, B_WIDTH)


def memory_attention(q, mem, w_kv):
    bsz, seq, _ = q.shape
    q = q.reshape(bsz, seq, MEM_HEADS, MEM_HEAD_DIM)
    mk, mv = jnp.split(mem @ w_kv, 2, axis=-1)
    mk = mk.reshape(bsz, -1, MEM_HEADS, MEM_HEAD_DIM)
    mv = mv.reshape(bsz, -1, MEM_HEADS, MEM_HEAD_DIM)
    s = jnp.einsum('bshd,bmhd->bhsm', q, mk).astype(jnp.float32) * (MEM_HEAD_DIM ** -0.5)
    p = jax.nn.softmax(s, axis=-1).astype(mv.dtype)
    return jnp.einsum('bhsm,bmhd->bshd', p, mv).reshape(bsz, seq, MEM_WIDTH)


def conv_ffn(h, w_up, conv_w, conv_b, w_down):
    u, g = jnp.split(h @ w_up, 2, axis=-1)
    g = causal_dwconv(g, conv_w, conv_b)
    return (jax.nn.silu(g) * u) @ w_down


def setup_inputs(seed: int = 0) -> dict:
    key = jax.random.key(seed)
    ks = jax.random.split(key, 24)
    f32 = jnp.float32

    def nrm(k, shape, scale):
        return jax.random.normal(k, shape, f32) * scale

    gate_b = jnp.concatenate([
        nrm(ks[5], (N_A_LAYERS, A_HEADS), 0.1),
        jnp.linspace(3.0, 6.0, A_HEADS, dtype=f32)[None, :] + nrm(ks[6], (N_A_LAYERS, A_HEADS), 0.1),
    ], axis=-1)
    return {
        "x": nrm(ks[0], (BATCH, SEQ, D_MODEL), 1.0),
        "mem": nrm(ks[1], (BATCH, MEM_TOKENS, D_MODEL), 1.0),
        "norm_mix_g": 1.0 + nrm(ks[2], (DEPTH, D_MODEL), 0.1),
        "norm_ffn_g": 1.0 + nrm(ks[3], (DEPTH, D_MODEL), 0.1),
        "a_w_in": nrm(ks[4], (N_A_LAYERS, D_MODEL, A_IN), D_MODEL ** -0.5),
        "a_gate_b": gate_b,
        "a_conv_w": nrm(ks[7], (N_A_LAYERS, A_CONV, 2 * A_WIDTH), A_CONV ** -0.5),
        "a_conv_b": nrm(ks[8], (N_A_LAYERS, 2 * A_WIDTH), 0.02),
        "a_head_g": 1.0 + nrm(ks[9], (N_A_LAYERS, A_WIDTH), 0.1),
        "a_w_out": nrm(ks[10], (N_A_LAYERS, MIX_WIDTH, D_MODEL), MIX_WIDTH ** -0.5),
        "kv_norm_g": 1.0 + nrm(ks[11], (D_MODEL,), 0.1),
        "w_kv": nrm(ks[12], (D_MODEL, 2 * B_WIDTH), D_MODEL ** -0.5),
        "b_w_in": nrm(ks[13], (N_B_LAYERS, D_MODEL, B_IN), D_MODEL ** -0.5),
        "b_rel_bias": nrm(ks[14], (N_B_LAYERS, B_HEADS, REL_SIZE), 0.5),
        "b_w_out": nrm(ks[15], (N_B_LAYERS, MIX_WIDTH, D_MODEL), MIX_WIDTH ** -0.5),
        "mem_w_kv": nrm(ks[16], (DEPTH, D_MODEL, 2 * MEM_WIDTH), D_MODEL ** -0.5),
        "ffn_w_up": nrm(ks[17], (DEPTH, D_MODEL, 2 * D_FF), D_MODEL ** -0.5),
        "ffn_conv_w": nrm(ks[18], (DEPTH, FFN_CONV, D_FF), FFN_CONV ** -0.5),
        "ffn_conv_b": nrm(ks[19], (DEPTH, D_FF), 0.02),
        "ffn_w_down": nrm(ks[20], (DEPTH, D_FF, D_MODEL), D_FF ** -0.5),
        "final_g": 1.0 + nrm(ks[21], (D_MODEL,), 0.1),
    }


def reference(x, mem, norm_mix_g, norm_ffn_g, a_w_in, a_gate_b, a_conv_w, a_conv_b, a_head_g, a_w_out,
              kv_norm_g, w_kv, b_w_in, b_rel_bias, b_w_out, mem_w_kv, ffn_w_up, ffn_conv_w, ffn_conv_b,
              ffn_w_down, final_g):
    bsz, seq, _ = x.shape
    k_pad = None
    v_pad = None
    for l in range(DEPTH):
        h = rmsnorm(x, norm_mix_g[l])
        if l < N_A_LAYERS:
            a = l
            proj = h @ a_w_in[a]
            qk, v, o, gates, q_mem = jnp.split(
                proj, [2 * A_WIDTH, 3 * A_WIDTH, 4 * A_WIDTH, 4 * A_WIDTH + 2 * A_HEADS], axis=-1)
            qk = jax.nn.silu(causal_dwconv(qk, a_conv_w[a], a_conv_b[a]))
            q, k = jnp.split(qk, 2, axis=-1)
            gates = gates.astype(jnp.float32) + a_gate_b[a].astype(jnp.float32)
            ig, fg = jnp.split(gates, 2, axis=-1)
            hm = mlstm_chunkwise(q.reshape(bsz, seq, A_HEADS, A_HEAD_DIM),
                                 k.reshape(bsz, seq, A_HEADS, A_HEAD_DIM),
                                 v.reshape(bsz, seq, A_HEADS, A_HEAD_DIM),
                                 ig, jax.nn.log_sigmoid(fg))
            hm = hm * lax.rsqrt(jnp.mean(hm * hm, axis=-1, keepdims=True) + EPS)
            hm = hm.reshape(bsz, seq, A_WIDTH) * a_head_g[a].astype(jnp.float32)
            mix_out = (hm * jax.nn.sigmoid(o.astype(jnp.float32))).astype(x.dtype)
            w_out = a_w_out[a]
        else:
            bl = l - N_A_LAYERS
            proj = h @ b_w_in[bl]
            q, q_mem = jnp.split(proj, [B_WIDTH], axis=-1)
            mix_out = band_attention(q, k_pad, v_pad, b_rel_bias[bl])
            w_out = b_w_out[bl]
        mem_out = memory_attention(q_mem, mem, mem_w_kv[l])
        x = x + jnp.concatenate([mix_out, mem_out], axis=-1) @ w_out
        x = x + conv_ffn(rmsnorm(x, norm_ffn_g[l]), ffn_w_up[l], ffn_conv_w[l], ffn_conv_b[l], ffn_w_down[l])
        if l == N_A_LAYERS - 1:
            ks_, vs_ = jnp.split(rmsnorm(x, kv_norm_g) @ w_kv, 2, axis=-1)
            pad = ((0, 0), (KV_PAD, 0), (0, 0), (0, 0))
            k_pad = jnp.pad(ks_.reshape(bsz, seq, B_HEADS, B_HEAD_DIM), pad)
            v_pad = jnp.pad(vs_.reshape(bsz, seq, B_HEADS, B_HEAD_DIM), pad)
    return rmsnorm(x, final_g)
```

```python
import numpy as np
from contextlib import ExitStack
import concourse.bass as bass
import concourse.mybir as mybir
from concourse.bass_types import AP
from concourse.bass_utils import run_bass_kernel_spmd

F32 = mybir.dt.float32
BF16 = mybir.dt.bfloat16
ALU = mybir.AluOpType
AF = mybir.ActivationFunctionType
AX = mybir.AxisListType

PE, ACT, DVE, POOL, SP = "tensor", "scalar", "vector", "gpsimd", "sync"
ENGS = (PE, ACT, DVE, POOL, SP)

D = 1024
KC = 8
SEQ = 4096
T = 512
import os
NT = int(os.environ.get("K_NT", SEQ // T))
SUB = 128
NSUB = T // SUB
DFF = 2816
NFT = DFF // 128
A_IN = 3336
AW = 768
DH = 192
MEMT = 256
EPS = 1e-6
NEG = -30000.0
XSLOTS = 6


class Region:
    __slots__ = ("name", "writer", "readers", "strict")

    def __init__(self, name):
        self.name = name
        self.writer = None
        self.readers = []
        self.strict = False


class _Op:
    __slots__ = ("eng", "idx", "fn", "waits", "needs_inc", "dma_key", "snap")

    def __init__(self, eng, idx, fn):
        self.eng = eng
        self.idx = idx
        self.fn = fn
        self.waits = []
        self.needs_inc = False
        self.dma_key = None
        self.snap = None


class Sched:
    def __init__(self, nc):
        self.nc = nc
        self.ops = {e: [] for e in ENGS}
        self.clock = {e: {x: -1 for x in ENGS} for e in ENGS}
        self.dclock = {e: {} for e in ENGS}
        self.dma_count = {}
        self.all_regions = []

    def region(self, name=None):
        r = Region(name or f"r{len(self.all_regions)}")
        self.all_regions.append(r)
        return r

    def regions(self, n, name="r"):
        return [self.region(f"{name}{i}") for i in range(n)]

    def _add(self, eng, fn, reads, writes, dma_key=None):
        o = _Op(eng, len(self.ops[eng]), fn)
        o.dma_key = dma_key
        deps = []
        for r in reads:
            if r.writer is not None:
                deps.append((r.writer, True))
        for w in writes:
            if w.writer is not None:
                deps.append((w.writer, w.strict))
            for rd in w.readers:
                deps.append((rd, w.strict))
        clk = self.clock[eng]
        dclk = self.dclock[eng]
        for tok, is_raw in deps:
            if tok[0] == "c":
                _, e2, n = tok
                if e2 == eng and (not is_raw or eng == PE):
                    continue
                if clk[e2] >= n:
                    continue
                o.waits.append(tok)
                self.ops[e2][n].needs_inc = True
                clk[e2] = n
                sn = self.ops[e2][n].snap
                for k, v in sn.items():
                    if k != eng and clk[k] < v:
                        clk[k] = v
            else:
                _, key, val = tok
                if dclk.get(key, 0) >= val:
                    continue
                cur = self.dma_count[key]
                o.waits.append(("d", key, cur))
                dclk[key] = cur
        o.snap = dict(clk)
        self.ops[eng].append(o)
        if dma_key is not None:
            self.dma_count[dma_key] = self.dma_count.get(dma_key, 0) + 16
            tok = ("d", dma_key, self.dma_count[dma_key])
        else:
            tok = ("c", eng, o.idx)
        for r in reads:
            r.readers.append(tok)
        for w in writes:
            w.writer = tok
            w.readers = []
        return o

    def op(self, eng, fn, reads=(), writes=()):
        return self._add(eng, fn, reads, writes)

    def dma(self, eng, fn, reads=(), writes=(), key=None):
        return self._add(eng, fn, reads, writes, dma_key=key.name + "@" + eng)

    def emit(self, G):
        nc = self.nc
        self._add(SP, None, list(self.all_regions), list(self.all_regions))
        keys = list(self.dma_count)
        assert len(keys) <= len(G.dsem), len(keys)
        dsem = {k: G.dsem[i] for i, k in enumerate(keys)}
        dbase = {k: G.dbase[i] for i, k in enumerate(keys)}
        esem, ebase = G.esem, dict(G.ebase)
        G.phase += 1
        barv = G.phase
        cnt = {}
        for e in ENGS:
            c = 0
            arr = []
            for o in self.ops[e]:
                if o.needs_inc:
                    c += 1
                arr.append(c)
            cnt[e] = arr
            G.ebase[e] += c
        for i, k in enumerate(keys):
            G.dbase[i] += self.dma_count[k]
        with nc.Block() as block:
            def make(e):
                def body(engh):
                    for o in self.ops[e]:
                        for w in o.waits:
                            if w[0] == "c":
                                engh.wait_ge(esem[w[1]], ebase[w[1]] + cnt[w[1]][w[2]])
                            else:
                                engh.wait_ge(dsem[w[1]], dbase[w[1]] + w[2])
                        if o.fn is None:
                            continue
                        ins = o.fn(engh)
                        if o.dma_key is not None:
                            ins.then_inc(dsem[o.dma_key], 16)
                        elif o.needs_inc:
                            ins.then_inc(esem[e], 1)
                    if e == SP:
                        engh.sem_inc(G.bar, 1)
                    else:
                        engh.wait_ge(G.bar, barv)
                return body

            for e in ENGS:
                getattr(block, e)(make(e))
        return {e: len(self.ops[e]) for e in ENGS}


class SemPool:
    NDMA = 56

    def __init__(self, nc, st):
        self.esem = {e: st.enter_context(nc.semaphore(f"s_{e}")) for e in ENGS}
        self.dsem = [st.enter_context(nc.semaphore(f"d_{i}")) for i in range(self.NDMA)]
        self.bar = st.enter_context(nc.semaphore("bar"))
        self.ebase = {e: 0 for e in ENGS}
        self.dbase = [0] * self.NDMA
        self.phase = 0
        allsem = list(self.esem.values()) + self.dsem + [self.bar]
        with nc.Block() as block:
            @block.gpsimd
            def _(g):
                for s in allsem:
                    g.sem_clear(s)
        nc.all_engine_barrier()


class Prog:
    def __init__(self, phases, final_norm=True):
        self.nc = nc = bass.Bass("TRN2", target_bir_lowering=False)
        self.phases = phases
        self.final_norm = final_norm
        din = lambda n, s: nc.dram_tensor(n, list(s), F32, kind="ExternalInput").ap()
        self.x = din("x", (SEQ, D))
        self.mem = din("mem", (MEMT, D))
        self.a_w_in = din("a_w_in", (D, A_IN))
        self.a_w_out = din("a_w_out", (D, D))
        self.w_kv = din("w_kv", (D, 1536))
        self.b_w_in = din("b_w_in", (D, D))
        self.b_w_out = din("b_w_out", (D, D))
        self.mem_w_kv = din("mem_w_kv", (2, D, 512))
        self.ffn_w_up = din("ffn_w_up", (2, D, 2 * DFF))
        self.ffn_w_down = din("ffn_w_down", (2, DFF, D))
        self.gT_all = din("gT_all", (128, 5, KC))
        self.final_g = din("final_g", (1, D))
        self.gate_b = din("gate_b", (1, 8))
        self.cwA = din("cwA", (96, 16, 4))
        self.cbA = din("cbA", (96, 16))
        self.head_g = din("head_g", (1, AW))
        self.cwF = din("cwF", (128, 2, NFT, 3))
        self.cbF = din("cbF", (128, 2, NFT))
        self.relbias = din("relbias", (128, 5, 12, 128))
        self.c_ident = din("c_ident", (128, 128))
        self.c_negU = din("c_negU", (128, 128))
        self.c_maskc = din("c_maskc", (128, 128))
        self.xa = nc.dram_tensor("xa", [SEQ, D], F32, kind="Internal").ap()
        self.xb = nc.dram_tensor("xb", [SEQ, D], F32, kind="Internal").ap()
        self.out = nc.dram_tensor("out", [SEQ, D], F32, kind="ExternalOutput").ap()
        self.stats = {}

    def mm(self, S, out, lhsT, rhs, start, stop, reads, writes, skip=False):
        S.op(PE, lambda e: e.matmul(out, lhsT=lhsT, rhs=rhs, start=start, stop=stop,
                                    skip_group_check=skip), reads, writes)

    def tr(self, S, out, in_, ident, reads, writes):
        S.op(PE, lambda e: e.transpose(out=out, in_=in_, identity=ident), reads, writes)

    def act(self, S, out, in_, func, reads, writes, **kw):
        S.op(ACT, lambda e: e.activation(out=out, in_=in_, func=func, **kw), reads, writes)

    def tt(self, S, eng, out, in0, in1, op, reads, writes):
        S.op(eng, lambda e: e.tensor_tensor(out=out, in0=in0, in1=in1, op=op), reads, writes)

    def ts(self, S, eng, out, in0, s1, s2, op0, op1, reads, writes):
        if op1 is None:
            S.op(eng, lambda e: e.tensor_scalar(out=out, in0=in0, scalar1=s1, scalar2=None, op0=op0),
                 reads, writes)
        else:
            S.op(eng, lambda e: e.tensor_scalar(out=out, in0=in0, scalar1=s1, scalar2=s2, op0=op0, op1=op1),
                 reads, writes)

    def stt(self, S, out, in0, scalar, in1, op0, op1, reads, writes):
        S.op(DVE, lambda e: e.scalar_tensor_tensor(out=out, in0=in0, scalar=scalar, in1=in1, op0=op0, op1=op1),
             reads, writes)

    def cp(self, S, eng, out, in_, reads, writes):
        if eng == ACT:
            S.op(ACT, lambda e: e.copy(out=out, in_=in_), reads, writes)
        else:
            S.op(eng, lambda e: e.tensor_copy(out=out, in_=in_), reads, writes)

    def ms(self, S, eng, ap, val, writes):
        S.op(eng, lambda e: e.memset(ap, val), (), writes)

    def ld(self, S, eng, out, in_, writes, key, reads=(), **kw):
        S.dma(eng, lambda e: e.dma_start(out=out, in_=in_, **kw), reads, writes, key=key)

    def load_w(self, S, dst, reg, src2d, c0, c1, kc0=0, kc1=None):
        kcn = src2d.shape[0] // 128
        kc1 = kcn if kc1 is None else kc1
        src = src2d.rearrange("(kc p) n -> p kc n", p=128)
        step = 2048
        for a in range(c0, c1, step):
            b = min(c1, a + step)
            self.ld(S, POOL, dst[:, kc0:kc1, a:b], src[:, kc0:kc1, a:b], [reg], reg)

    def norm_T(self, S, C, xt_ap, xr, outs):
        i = C["nrm_i"]
        C["nrm_i"] += 1
        k = i % 2
        ss = C["ss"][:, k, 0:1]
        ms_ = C["ss"][:, k, 1:2]
        rstd = C["ss"][:, k, 2:3]
        ssr = C["ss_r"][k]
        xh = C["xh"][:, k, :]
        xhr = C["xh_r"][k]
        self.act(S, xh, xt_ap, AF.Square, [xr], [xhr, ssr], accum_out=ss)
        self.ts(S, DVE, ms_, ss, 1.0 / D, EPS, ALU.mult, ALU.add, [ssr], [ssr])
        self.tt(S, POOL, rstd, ms_, C["neghalf"][:, 0:1], ALU.pow, [ssr, C["const_r"]], [ssr])
        self.act(S, xh, xt_ap, AF.Copy, [xr, ssr], [xhr], scale=rstd)
        pT = C["pT"]
        for kc in range(KC):
            self.tr(S, pT[:, kc * 128:(kc + 1) * 128], xh[:, kc * 128:(kc + 1) * 128], C["identb"][:, :],
                    [xhr, C["const_r"]], [C["pT_r"]])
        pT3 = pT[:, :].rearrange("p (a b) -> p a b", a=KC)
        for gT, dst, dr in outs:
            self.tt(S, DVE, dst, pT3, gT.unsqueeze(2).to_broadcast([128, KC, 128]), ALU.mult,
                    [C["pT_r"], C["const_r"]], [dr])
        return rstd, ssr

    def phase_consts(self, S, st, which_gains):
        nc = self.nc
        C = {"nrm_i": 0}
        self.phase_i = getattr(self, "phase_i", 0) + 1
        pfx = f"p{self.phase_i}_"
        sb = lambda n, s, d: st.enter_context(nc.sbuf_tensor(pfx + n, list(s), d))
        ps = lambda n, s, d: st.enter_context(nc.psum_tensor(pfx + n, list(s), d))
        C["sb"] = sb
        C["ps"] = ps
        C["const_r"] = cr = S.region("const")
        C["identb"] = sb("identb", (128, 128), BF16)
        C["neghalf"] = sb("neghalf", (128, 1), F32)
        C["gT"] = sb("gT", (128, 5, KC), F32)
        self.ld(S, POOL, C["identb"][:, :], self.c_ident[:, :], [cr], cr)
        self.ld(S, SP, C["gT"][:, :, :], self.gT_all[:, :, :], [cr], cr)
        self.ms(S, DVE, C["neghalf"][:, :], -0.5, [cr])
        C["ss"] = sb("ss", (128, 2, 4), F32)
        C["ss_r"] = S.regions(2, "ss")
        C["xh"] = sb("xh", (128, 2, D), BF16)
        C["xh_r"] = S.regions(2, "xh")
        C["pT"] = ps("pT", (128, D), BF16)
        C["pT_r"] = S.region("pT")
        C["xt"] = sb("xt", (128, XSLOTS, D), F32)
        C["xt_r"] = S.regions(XSLOTS, "xt")
        return C

    def load_x(self, S, C, src, gi):
        sl = gi % XSLOTS
        self.ld(S, SP, C["xt"][:, sl, :], src[gi * 128:(gi + 1) * 128, :], [C["xt_r"][sl]], C["xt_r"][sl])

    def phase_ffn(self, l, src, dst, final):
        nc = self.nc
        S = Sched(nc)
        with ExitStack() as st:
            C = self.phase_consts(S, st, None)
            sb, ps = C["sb"], C["ps"]
            gT = C["gT"][:, 1 if l == 0 else 4, :]
            wup = sb("wup", (128, KC, 2 * DFF), BF16)
            wup_r = S.regions(4, "wup")
            wdn = sb("wdn", (128, NFT, D), BF16)
            wdn_r = S.regions(2, "wdn")
            hT = sb("hT", (128, KC, T), BF16)
            hT_r = S.regions(NSUB, "hT")
            aT = sb("aT", (128, NFT, T), BF16)
            aT_r = S.regions(NFT, "aT")
            graw = sb("graw", (128, 2, T + 2), F32)
            graw_r = S.regions(2, "graw")
            acc = sb("acc", (128, 2, T), F32)
            acc_r = S.regions(2, "acc")
            tnh = sb("tnh", (128, 2, T), F32)
            tnh_r = S.regions(2, "tnh")
            halo = sb("halo", (128, NFT, 2), F32)
            halo_r = S.regions(NFT, "halo")
            cw = sb("cw", (128, NFT, 3), F32)
            cb = sb("cb", (128, NFT), F32)
            cr = C["const_r"]
            pb = [ps(f"pb{i}", (128, 512), F32) for i in range(7)]
            pb_r = S.regions(7, "pb")
            if final:
                fg = sb("fg", (128, D), F32)
                self.ld(S, SP, fg[:, :], self.final_g.partition_broadcast(128), [cr], cr)
                fss = sb("fss", (128, 2, 4), F32)
                fss_r = S.regions(2, "fss")
                for r_ in C["xh_r"]:
                    r_.strict = True
            for gi in range(min(XSLOTS, NSUB + 2)):
                self.load_x(S, C, src, gi)
            loaded = min(XSLOTS, NSUB + 2)
            self.ld(S, SP, cw[:, :, :], self.cwF[:, l, :, :], [cr], cr)
            self.ld(S, SP, cb[:, :], self.cbF[:, l, :], [cr], cr)
            self.ts(S, POOL, cw[:, :, :], cw[:, :, :], 0.5, None, ALU.mult, None, [cr], [cr])
            self.ts(S, POOL, cb[:, :], cb[:, :], 0.5, None, ALU.mult, None, [cr], [cr])
            self.ms(S, POOL, halo[:, :, :], 0.0, halo_r)
            wu = self.ffn_w_up[l]
            for c in range(4):
                self.load_w(S, wup, wup_r[c], wu, c * 1408, (c + 1) * 1408)
            wd = self.ffn_w_down[l]
            self.load_w(S, wdn, wdn_r[0], wd, 0, D, 0, 11)
            self.load_w(S, wdn, wdn_r[1], wd, 0, D, 11, 22)

            pbi = 0
            for ti in range(NT):
                for s in range(NSUB):
                    gi = ti * NSUB + s
                    sl = gi % XSLOTS
                    self.norm_T(S, C, C["xt"][:, sl, :], C["xt_r"][sl],
                                [(gT, hT[:, :, s * 128:(s + 1) * 128], hT_r[s])])
                for j in range(NFT):
                    pu, pur = pb[pbi % 6], pb_r[pbi % 6]
                    pg, pgr = pb[(pbi + 1) % 6], pb_r[(pbi + 1) % 6]
                    pbi += 2
                    for kc in range(KC):
                        self.mm(S, pg[:, :], wup[:, kc, DFF + j * 128:DFF + (j + 1) * 128], hT[:, kc, :],
                                kc == 0, kc == KC - 1, hT_r + [wup_r[2 + j // 11]], [pgr])
                    for kc in range(KC):
                        self.mm(S, pu[:, :], wup[:, kc, j * 128:(j + 1) * 128], hT[:, kc, :],
                                kc == 0, kc == KC - 1, hT_r + [wup_r[j // 11]], [pur])
                    r = j % 2
                    gr, grr = graw[:, r, :], graw_r[r]
                    ac, acr = acc[:, r, :], acc_r[r]
                    tn, tnr = tnh[:, r, :], tnh_r[r]
                    self.cp(S, POOL, gr[:, 0:2], halo[:, j, :], [halo_r[j]], [grr])
                    self.cp(S, ACT, gr[:, 2:T + 2], pg[:, :], [pgr], [grr])
                    self.cp(S, POOL, halo[:, j, :], gr[:, T:T + 2], [grr], [halo_r[j]])
                    self.ts(S, DVE, ac, gr[:, 2:T + 2], cw[:, j, 2:3], cb[:, j:j + 1], ALU.mult, ALU.add,
                            [grr, cr], [acr])
                    self.stt(S, ac, gr[:, 1:T + 1], cw[:, j, 1:2], ac, ALU.mult, ALU.add, [grr, cr, acr], [acr])
                    self.stt(S, ac, gr[:, 0:T], cw[:, j, 0:1], ac, ALU.mult, ALU.add, [grr, cr, acr], [acr])
                    self.act(S, tn, ac, AF.Tanh, [acr], [tnr])
                    self.stt(S, tn, tn, 1.0, ac, ALU.add, ALU.mult, [tnr, acr], [tnr])
                    self.tt(S, DVE, aT[:, j, :], tn, pu[:, :], ALU.mult, [tnr, pur], [aT_r[j]])
                for s in range(NSUB):
                    gi = ti * NSUB + s
                    sl = gi % XSLOTS
                    xt = C["xt"][:, sl, :]
                    xr = C["xt_r"][sl]
                    for hf in range(2):
                        po, por = pb[6], pb_r[6]
                        if hf == 1:
                            po, por = pb[pbi % 6], pb_r[pbi % 6]
                            pbi += 1
                        for kc in range(NFT):
                            self.mm(S, po[:, :], aT[:, kc, s * 128:(s + 1) * 128], wdn[:, kc, hf * 512:(hf + 1) * 512],
                                    kc == 0, kc == NFT - 1, [aT_r[kc], wdn_r[kc // 11]], [por])
                        self.tt(S, DVE, xt[:, hf * 512:(hf + 1) * 512], po[:, :], xt[:, hf * 512:(hf + 1) * 512],
                                ALU.add, [por, xr], [xr])
                    if final:
                        k = gi % 2
                        ss = fss[:, k, 0:1]
                        ms_ = fss[:, k, 1:2]
                        rstd = fss[:, k, 2:3]
                        self.act(S, C["xh"][:, k, :], xt, AF.Square, [xr], [C["xh_r"][k], fss_r[k]], accum_out=ss)
                        self.ts(S, DVE, ms_, ss, 1.0 / D, EPS, ALU.mult, ALU.add, [fss_r[k]], [fss_r[k]])
                        self.tt(S, POOL, rstd, ms_, C["neghalf"][:, 0:1], ALU.pow, [fss_r[k], cr], [fss_r[k]])
                        self.stt(S, xt, xt, rstd, fg[:, :], ALU.mult, ALU.mult, [xr, fss_r[k], cr], [xr])
                    self.ld(S, SP, dst[gi * 128:(gi + 1) * 128, :], xt, [], xr, reads=[xr])
                    if loaded < NT * NSUB and loaded % XSLOTS == sl:
                        self.load_x(S, C, src, loaded)
                        loaded += 1
                while loaded < min(NT * NSUB, (ti + 1) * NSUB + XSLOTS):
                    self.load_x(S, C, src, loaded)
                    loaded += 1
            self.stats[f"ffn{l}"] = S.emit(self.G)

    def build(self):
        chain = {"A_mix": self.phase_amix, "A_ffn": lambda s, d, f: self.phase_ffn(0, s, d, False),
                 "B_mix": self.phase_bmix, "B_ffn": lambda s, d, f: self.phase_ffn(1, s, d, f)}
        src = self.x
        scr = [self.xa, self.xb]
        with ExitStack() as gst:
            self.G = SemPool(self.nc, gst)
            for i, ph in enumerate(self.phases):
                last = i == len(self.phases) - 1
                dst = self.out if last else scr[i % 2]
                chain[ph](src, dst, last and self.final_norm)
                src = dst
        return self.nc

    def phase_amix(self, src, dst, final):
        nc = self.nc
        S = Sched(nc)
        with ExitStack() as st:
            C = self.phase_consts(S, st, None)
            sb, ps = C["sb"], C["ps"]
            cr = C["const_r"]
            gT_mix = C["gT"][:, 0, :]
            wain = sb("wain", (128, KC, A_IN), BF16)
            wain_r = S.regions(4, "wain")
            waout = sb("waout", (128, KC, D), BF16)
            waout_r = S.regions(1, "waout")
            hT = sb("hT", (128, KC, T), BF16)
            hT_r = S.regions(NSUB, "hT")
            qkT = sb("qkT", (128, 16, T), BF16)
            qk_r = S.regions(16, "qk")
            qmT = sb("qmT", (128, 2, T), BF16)
            qm_r = S.regions(2, "qm")
            raw = sb("raw", (128, 2, T + 3), F32)
            raw_r = S.regions(2, "raw")
            acc = sb("acc", (128, 2, T), F32)
            acc_r = S.regions(2, "acc")
            tnh = sb("tnh", (128, 2, T), F32)
            tnh_r = S.regions(2, "tnh")
            halo = sb("halo", (128, 16, 3), F32)
            halo_r = S.regions(16, "halo")
            cw = sb("cw", (128, 16, 4), F32)
            cb = sb("cb", (128, 16), F32)
            ktok = sb("ktok", (128, NSUB, AW), BF16)
            ktok_r = S.regions(NSUB, "ktok")
            vw = sb("vw", (128, NSUB, 4, DH + 1), BF16)
            vw_r = S.regions(NSUB, "vw")
            G2 = sb("G2", (128, NSUB, AW), F32)
            G2_r = S.regions(NSUB, "G2")
            hgh = sb("hgh", (128, AW), F32)
            gb_bc = sb("gb_bc", (128, 8), F32)
            gsb = sb("gsb", (128, NSUB, 8), F32)
            gw = sb("gw", (128, 16, 16), F32)
            g_r = S.region("gates")
            EP, SPL, AA, BBL, AMX, MALL, MST, T48 = 0, 1, 2, 3, 5, 6, 7, 8
            WGF = 11
            mcar = sb("mcar", (128, 4), F32)
            am16 = sb("am16", (16, 20), F32)
            identF = sb("identF", (128, 128), F32)
            negU = sb("negU", (128, 128), F32)
            negO = sb("negO", (128, 128), F32)
            ones16 = sb("ones16", (16, 128), F32)
            maskc = sb("maskc", (128, 4, 128), F32)
            Cn = sb("Cn", (128, 4, 2, DH + 1), F32)
            Cn_r = S.regions(4, "Cn")
            Gbf = sb("Gbf", (128, 4, 2, DH + 1), BF16)
            Gbf_r = S.regions(4, "Gbf")
            sm = sb("sm", (128, 2, 4, 128), BF16)
            sm_r = S.regions(2, "sm")
            hm = sb("hm", (128, 2, 4, DH), F32)
            hm_r = S.regions(2, "hm")
            hj = sb("hj", (128, 4, DH), BF16)
            hj_r = S.regions(4, "hj")
            for r_ in hj_r:
                r_.strict = True
            hst = sb("hst", (128, 2, 16), F32)
            hst_r = S.regions(2, "hst")
            mkT = sb("mkT", (128, 2, MEMT), BF16)
            mvx = sb("mvx", (128, 2, 4, 65), BF16)
            mem_r = S.region("memkv")
            pmT = sb("pmT", (128, 8, T), BF16)
            pmT_r = S.region("pmT")
            mix = sb("mix", (128, 2, D), BF16)
            mix_r = S.regions(2, "mix")
            mixT = sb("mixT", (128, 2, KC, 128), BF16)
            mixT_r = S.regions(2, "mixT")
            rr = sb("rr", (128, 4), F32)
            rr_r = S.region("rr")
            pb = [ps(f"pb{i}", (128, 512), F32) for i in range(7)]
            pb_r = S.regions(7, "pb")

            self.mem_prologue(S, C, 0, qkT[:, 0:8, :], qk_r[0:8], hT, hT_r, mkT, mvx, mem_r, pb[0], pb_r[0])
            self.load_w(S, wain, wain_r[0], self.a_w_in, 0, 1536)
            self.load_w(S, wain, wain_r[1], self.a_w_in, 1536, 2304)
            self.load_w(S, wain, wain_r[2], self.a_w_in, 2304, 3080)
            self.load_w(S, wain, wain_r[3], self.a_w_in, 3080, A_IN)
            self.load_w(S, waout, waout_r[0], self.a_w_out, 0, D)
            self.ld(S, SP, cw[0:96, :, :], self.cwA[:, :, :], [cr], cr)
            self.ld(S, SP, cb[0:96, :], self.cbA[:, :], [cr], cr)
            self.ld(S, SP, hgh[:, :], self.head_g.partition_broadcast(128), [cr], cr)
            self.ld(S, SP, gb_bc[:, :], self.gate_b.partition_broadcast(128), [cr], cr)
            self.ld(S, SP, identF[:, :], self.c_ident[:, :], [cr], cr)
            self.ld(S, SP, negU[:, :], self.c_negU[:, :], [cr], cr)
            for h in range(4):
                self.ld(S, SP, maskc[:, h, :], self.c_maskc[:, :], [cr], cr)
            self.ts(S, POOL, cw[0:96, :, :], cw[0:96, :, :], 0.5, None, ALU.mult, None, [cr], [cr])
            self.ts(S, POOL, cb[0:96, :], cb[0:96, :], 0.5, None, ALU.mult, None, [cr], [cr])
            self.ts(S, POOL, hgh[:, :], hgh[:, :], 0.5, None, ALU.mult, None, [cr], [cr])
            self.ms(S, POOL, negO[:, :], -1.0, [cr])
            self.ms(S, POOL, ones16[:, :], 1.0, [cr])
            self.ms(S, POOL, halo[:, :, :], 0.0, halo_r)
            self.ms(S, POOL, Cn[:, :, :, :], 0.0, Cn_r)
            self.ms(S, POOL, mcar[:, :], 0.0, [g_r])
            for gi in range(XSLOTS):
                self.load_x(S, C, src, gi)
            loaded = XSLOTS
            ia = 0
            for ti in range(NT):
                for s in range(NSUB):
                    gi = ti * NSUB + s
                    sl = gi % XSLOTS
                    self.norm_T(S, C, C["xt"][:, sl, :], C["xt_r"][sl],
                                [(gT_mix, hT[:, :, s * 128:(s + 1) * 128], hT_r[s])])
                pg, pgr = pb[6], pb_r[6]
                for s in range(NSUB):
                    for kc in range(KC):
                        self.mm(S, pg[:, s * 8:(s + 1) * 8], hT[:, kc, s * 128:(s + 1) * 128], wain[:, kc, 3072:3080],
                                kc == 0, kc == KC - 1, [hT_r[s], wain_r[2]], [pgr])
                self.tt(S, DVE, gsb[:, :, :], pg[:, 0:32].rearrange("p (s g) -> p s g", s=NSUB),
                        gb_bc[:, :].unsqueeze(1).to_broadcast([128, NSUB, 8]), ALU.add, [pgr, cr], [g_r])
                ga = lambda i, n=1: gw[:, i:i + n, :].rearrange("p a c -> p (a c)")
                g3 = lambda i: gw[:, i, :].rearrange("p (s h) -> p s h", s=NSUB)
                self.act(S, g3(EP), gsb[:, :, 4:8], AF.Exp, [g_r], [g_r], scale=-1.0)
                self.act(S, ga(SPL), ga(EP), AF.Ln, [g_r], [g_r], bias=1.0)
                self.mm(S, pg[:, 32:48], negU[:, :], ga(SPL), True, True, [g_r, cr], [pgr])
                self.mm(S, pg[:, 48:64], negO[:, :], ga(SPL), True, True, [g_r, cr], [pgr])
                self.cp(S, DVE, ga(BBL, 2), pg[:, 32:64], [pgr], [g_r])
                self.tt(S, DVE, g3(AA), gsb[:, :, 0:4], g3(BBL), ALU.subtract, [g_r], [g_r])
                self.tr(S, pg[0:16, 64:192], ga(AA), identF[:, :], [g_r, cr], [pgr])
                S.op(DVE, lambda e: e.reduce_max(out=am16[:, 0:1], in_=pg[0:16, 64:192], axis=AX.X), [pgr], [g_r])
                self.ts(S, DVE, am16[:, 4:20], identF[0:16, 0:16], am16[:, 0:1], None, ALU.mult, None, [g_r, cr], [g_r])
                self.mm(S, pg[:, 192:208], ones16[:, :], am16[:, 4:20], True, True, [g_r, cr], [pgr])
                self.cp(S, DVE, ga(AMX), pg[:, 192:208], [pgr], [g_r])
                for s in range(NSUB):
                    self.cp(S, DVE, g3(MST)[:, s, :], mcar[:, :], [g_r], [g_r])
                    self.tt(S, DVE, g3(MALL)[:, s, :], mcar[:, :], g3(AMX)[:, s, :], ALU.max, [g_r], [g_r])
                    self.tt(S, DVE, mcar[:, :], g3(BBL + 1)[:, s, :], g3(MALL)[:, s, :], ALU.add, [g_r], [g_r])
                self.tt(S, DVE, ga(T48), ga(AA), ga(MALL), ALU.subtract, [g_r], [g_r])
                self.tt(S, DVE, ga(T48 + 1), ga(MST), ga(MALL), ALU.subtract, [g_r], [g_r])
                self.stt(S, ga(T48 + 2), ga(BBL), -1.0, ga(MALL), ALU.mult, ALU.subtract, [g_r], [g_r])
                self.act(S, ga(WGF, 3), ga(T48, 3), AF.Exp, [g_r], [g_r])
                wv, gv, flv = g3(WGF), g3(WGF + 1), g3(WGF + 2)
                for i in range(16):
                    b, br = pb[ia % 2], pb_r[ia % 2]
                    ia += 1
                    for kc in range(KC):
                        self.mm(S, b[0:96, :], wain[:, kc, i * 96:(i + 1) * 96], hT[:, kc, :], kc == 0, kc == KC - 1,
                                hT_r + [wain_r[0]], [br])
                    r = i % 2
                    rw, rwr = raw[0:96, r, :], raw_r[r]
                    ac, acr = acc[0:96, r, :], acc_r[r]
                    tn, tnr = tnh[0:96, r, :], tnh_r[r]
                    self.cp(S, POOL, rw[:, 0:3], halo[0:96, i, :], [halo_r[i]], [rwr])
                    self.cp(S, ACT, rw[:, 3:T + 3], b[0:96, :], [br], [rwr])
                    self.cp(S, POOL, halo[0:96, i, :], rw[:, T:T + 3], [rwr], [halo_r[i]])
                    self.ts(S, DVE, ac, rw[:, 3:T + 3], cw[0:96, i, 3:4], cb[0:96, i:i + 1], ALU.mult, ALU.add,
                            [rwr, cr], [acr])
                    for j in (2, 1, 0):
                        self.stt(S, ac, rw[:, j:j + T], cw[0:96, i, j:j + 1], ac, ALU.mult, ALU.add,
                                 [rwr, cr, acr], [acr])
                    self.act(S, tn, ac, AF.Tanh, [acr], [tnr])
                    self.stt(S, qkT[0:96, i, :], tn, 1.0, ac, ALU.add, ALU.mult, [tnr, acr], [qk_r[i]])
                for s in range(NSUB):
                    pT = C["pT"]
                    for j in range(8):
                        self.tr(S, pT[:, j * 96:(j + 1) * 96], qkT[0:96, 8 + j, s * 128:(s + 1) * 128],
                                C["identb"][0:96, 0:96], [qk_r[8 + j], cr], [C["pT_r"]])
                    self.act(S, ktok[:, s, :], pT[:, 0:AW], AF.Copy, [C["pT_r"]], [ktok_r[s]], scale=float(DH ** -0.5))
                for s in range(NSUB):
                    for g in range(2):
                        b, br = pb[ia % 2], pb_r[ia % 2]
                        ia += 1
                        for kc in range(KC):
                            self.mm(S, b[:, 0:384], hT[:, kc, s * 128:(s + 1) * 128],
                                    wain[:, kc, 1536 + g * 384:1536 + (g + 1) * 384], kc == 0, kc == KC - 1,
                                    [hT_r[s], wain_r[1]], [br])
                        self.tt(S, DVE, vw[:, s, 2 * g:2 * g + 2, 0:DH], b[:, 0:384].rearrange("p (h e) -> p h e", h=2),
                                wv[:, s, 2 * g:2 * g + 2].unsqueeze(2).to_broadcast([128, 2, DH]), ALU.mult,
                                [br, g_r], [vw_r[s]])
                    self.cp(S, POOL, vw[:, s, :, DH], wv[:, s, :], [g_r], [vw_r[s]])
                    for g in range(2):
                        b, br = pb[ia % 2], pb_r[ia % 2]
                        ia += 1
                        for kc in range(KC):
                            self.mm(S, b[:, 0:384], hT[:, kc, s * 128:(s + 1) * 128],
                                    wain[:, kc, 2304 + g * 384:2304 + (g + 1) * 384], kc == 0, kc == KC - 1,
                                    [hT_r[s], wain_r[2]], [br])
                        g2 = G2[:, s, g * 384:(g + 1) * 384]
                        self.act(S, g2, b[:, 0:384], AF.Tanh, [br], [G2_r[s]], scale=0.5)
                        self.stt(S, g2, g2, 1.0, hgh[:, g * 384:(g + 1) * 384], ALU.add, ALU.mult, [G2_r[s], cr], [G2_r[s]])
                for j in range(2):
                    b, br = pb[ia % 2], pb_r[ia % 2]
                    ia += 1
                    for kc in range(KC):
                        self.mm(S, b[:, :], wain[:, kc, 3080 + j * 128:3080 + (j + 1) * 128], hT[:, kc, :], kc == 0,
                                kc == KC - 1, hT_r + [wain_r[3]], [br])
                    self.cp(S, ACT, qmT[:, j, :], b[:, :], [br], [qm_r[j]])
                self.mem_scores(S, qmT, qm_r, mkT, mem_r, pmT, pmT_r, [pb[0], pb[1]], [pb_r[0], pb_r[1]])
                for s in range(NSUB):
                    gi = ti * NSUB + s
                    sl = gi % XSLOTS
                    k2 = gi % 2
                    mx, mxr = mix[:, k2, :], mix_r[k2]
                    cs = slice(s * 128, (s + 1) * 128)
                    pS, pSr = pb[2], pb_r[2]
                    for h in range(4):
                        for j in range(2):
                            self.mm(S, pS[:, h * 128:(h + 1) * 128], qkT[0:96, 8 + 2 * h + j, cs], qkT[0:96, 2 * h + j, cs],
                                    j == 0, j == 1, [qk_r[8 + 2 * h + j], qk_r[2 * h + j]], [pSr])
                    smv, smr = sm[:, k2, :, :], sm_r[k2]
                    self.tt(S, DVE, smv.rearrange("p h t -> p (h t)"), pS[:, :], maskc[:, :, :].rearrange("p h t -> p (h t)"),
                            ALU.mult, [pSr, cr], [smr])
                    for h in range(4):
                        self.act(S, Gbf[0:96, h, :, :].rearrange("p j e -> p (j e)"),
                                 Cn[0:96, h, :, :].rearrange("p j e -> p (j e)"), AF.Copy, [Cn_r[h], g_r], [Gbf_r[h]],
                                 scale=gv[0:96, s, h:h + 1])
                    pN = [pb[3], pb[4]]
                    pNr = [pb_r[3], pb_r[4]]
                    for h in range(4):
                        o = pN[h // 2][:, (h % 2) * (DH + 1):(h % 2 + 1) * (DH + 1)]
                        self.mm(S, o, smv[:, h, :], vw[:, s, h, :], True, False, [smr, vw_r[s]], [pNr[h // 2]])
                        for j in range(2):
                            self.mm(S, o, qkT[0:96, 2 * h + j, cs], Gbf[0:96, h, j, :], False, j == 1,
                                    [qk_r[2 * h + j], Gbf_r[h]], [pNr[h // 2]])
                    for h in range(4):
                        pC, pCr = pb[5 + h % 2], pb_r[5 + h % 2]
                        for j in range(2):
                            self.mm(S, pC[0:96, j * (DH + 1):(j + 1) * (DH + 1)], ktok[:, s, h * DH + j * 96:h * DH + (j + 1) * 96],
                                    vw[:, s, h, :], True, True, [ktok_r[s], vw_r[s]], [pCr])
                        self.stt(S, Cn[0:96, h, :, :].rearrange("p j e -> p (j e)"),
                                 Cn[0:96, h, :, :].rearrange("p j e -> p (j e)"), gv[0:96, s, h:h + 1],
                                 pC[0:96, 0:2 * (DH + 1)], ALU.mult, ALU.add, [Cn_r[h], g_r, pCr], [Cn_r[h]])
                    hs, hsr = hst[:, k2, :], hst_r[k2]
                    hmv, hmr = hm[:, k2, :, :], hm_r[k2]
                    for hp2 in range(2):
                        pv = pN[hp2][:, 0:2 * (DH + 1)].rearrange("p (h e) -> p h e", h=2)
                        self.act(S, hs[:, 2 * hp2:2 * hp2 + 2], pv[:, :, DH], AF.Abs, [pNr[hp2]], [hsr])
                    self.tt(S, DVE, hs[:, 0:4], hs[:, 0:4], flv[:, s, :], ALU.max, [hsr, g_r], [hsr])
                    S.op(DVE, (lambda hs=hs: (lambda e: e.reciprocal(out=hs[:, 4:8], in_=hs[:, 0:4])))(), [hsr], [hsr])
                    for hp2 in range(2):
                        pv = pN[hp2][:, 0:2 * (DH + 1)].rearrange("p (h e) -> p h e", h=2)
                        self.tt(S, DVE, hmv[:, 2 * hp2:2 * hp2 + 2, :], pv[:, :, 0:DH],
                                hs[:, 4 + 2 * hp2:6 + 2 * hp2].unsqueeze(2).to_broadcast([128, 2, DH]), ALU.mult,
                                [pNr[hp2], hsr], [hmr])
                    for h in range(4):
                        self.act(S, hj[:, h, :], hmv[:, h, :], AF.Square, [hmr], [hj_r[h], hsr], accum_out=hs[:, 8 + h:9 + h])
                    self.ts(S, DVE, hs[:, 8:12], hs[:, 8:12], 1.0 / DH, EPS, ALU.mult, ALU.add, [hsr], [hsr])
                    self.tt(S, POOL, hs[:, 12:16], hs[:, 8:12], C["neghalf"][:, 0:1].to_broadcast([128, 4]), ALU.pow,
                            [hsr, cr], [hsr])
                    for h in range(4):
                        self.stt(S, mx[:, h * DH:(h + 1) * DH], hmv[:, h, :], hs[:, 12 + h:13 + h],
                                 G2[:, s, h * DH:(h + 1) * DH], ALU.mult, ALU.mult, [hmr, hsr, G2_r[s]], [mxr])
                    self.mem_pv(S, s, pmT, pmT_r, mvx, mem_r, pb[6], pb_r[6], rr, rr_r, mx, mxr)
                    self.out_proj(S, C, mx, mxr, mixT[:, k2, :, :], mixT_r[k2], waout, waout_r,
                                  C["xt"][:, sl, :], C["xt_r"][sl], [pb[0], pb[1]], [pb_r[0], pb_r[1]], dst, gi)
                    if loaded < NT * NSUB and loaded % XSLOTS == sl:
                        self.load_x(S, C, src, loaded)
                        loaded += 1
            self.stats["amix"] = S.emit(self.G)

    def mem_prologue(self, S, C, l, wmem, wmem_rs, memT, memT_rs, mkT, mvx, mem_r, pbank, pbank_r):
        cr = C["const_r"]
        xt, xt_r, xh, xh_r = C["xt"], C["xt_r"], C["xh"], C["xh_r"]
        self.load_w(S, wmem, wmem_rs[0], self.mem_w_kv[l], 0, 512)
        for mt in range(2):
            self.ld(S, SP, xt[:, mt, :], self.mem[mt * 128:(mt + 1) * 128, :], [xt_r[mt]], xt_r[mt])
            self.cp(S, DVE, xh[:, mt, :], xt[:, mt, :], [xt_r[mt]], [xh_r[mt]])
            pT = C["pT"]
            for kc in range(KC):
                self.tr(S, pT[:, kc * 128:(kc + 1) * 128], xh[:, mt, kc * 128:(kc + 1) * 128], C["identb"][:, :],
                        [xh_r[mt], cr], [C["pT_r"]])
            self.cp(S, DVE, memT[:, :, mt * 128:(mt + 1) * 128], pT[:, :].rearrange("p (a b) -> p a b", a=KC),
                    [C["pT_r"]], memT_rs)
        self.ms(S, POOL, mvx[:, :, :, 64:65], 1.0, [mem_r])
        for hp in range(2):
            for kc in range(KC):
                self.mm(S, pbank[:, 0:MEMT], wmem[:, kc, hp * 128:(hp + 1) * 128], memT[:, kc, 0:MEMT],
                        kc == 0, kc == KC - 1, memT_rs + wmem_rs, [pbank_r])
            self.cp(S, ACT, mkT[:, hp, :], pbank[:, 0:MEMT], [pbank_r], [mem_r])
        for mt in range(2):
            for kc in range(KC):
                self.mm(S, pbank[:, 0:256], memT[:, kc, mt * 128:(mt + 1) * 128], wmem[:, kc, 256:512],
                        kc == 0, kc == KC - 1, memT_rs + wmem_rs, [pbank_r])
            self.cp(S, DVE, mvx[:, mt, :, 0:64], pbank[:, 0:256].rearrange("p (h d) -> p h d", h=4),
                    [pbank_r], [mem_r])

    def mem_scores(self, S, qm, qm_rs, mkT, mem_r, pmT, pmT_r, banks, bank_rs):
        i = 0
        dbg = os.environ.get("K_DBG", "")
        for h in range(4):
            hp, hh = h // 2, h % 2
            if "h0" in dbg and hh == 1:
                continue
            for mt in range(2):
                b, br = banks[i % len(banks)], bank_rs[i % len(banks)]
                i += 1
                self.mm(S, b[:, :], mkT[hh * 64:(hh + 1) * 64, hp, mt * 128:(mt + 1) * 128],
                        qm[hh * 64:(hh + 1) * 64, hp, :], True, True, qm_rs + [mem_r], [br])
                if "noact" in dbg:
                    continue
                self.act(S, pmT[:, h * 2 + mt, :], b[:, :], AF.Exp, [br], [pmT_r], scale=0.125)

    def mem_pv(self, S, s, pmT, pmT_r, mvx, mem_r, pom, pom_r, rr, rr_r, mix, mix_r):
        first = True
        for h in range(4):
            for mt in range(2):
                self.mm(S, pom[:, h * 65:(h + 1) * 65], pmT[:, h * 2 + mt, s * 128:(s + 1) * 128], mvx[:, mt, h, :],
                        first, mt == 1, [pmT_r, mem_r], [pom_r], skip=True)
                first = False
        pv = pom[:, 0:260].rearrange("p (h e) -> p h e", h=4)
        S.op(DVE, lambda e: e.reciprocal(out=rr[:, 0:4], in_=pv[:, :, 64]), [pom_r], [rr_r])
        self.tt(S, DVE, mix[:, 768:1024].rearrange("p (h e) -> p h e", h=4), pv[:, :, 0:64],
                rr[:, 0:4].unsqueeze(2).to_broadcast([128, 4, 64]), ALU.mult, [pom_r, rr_r], [mix_r])

    def out_proj(self, S, C, mix, mix_r, mixT, mixT_r, wout, wout_rs, xt, xr, banks, bank_rs, dst, gi):
        cr = C["const_r"]
        pT = C["pT"]
        for kc in range(KC):
            self.tr(S, pT[:, kc * 128:(kc + 1) * 128], mix[:, kc * 128:(kc + 1) * 128], C["identb"][:, :],
                    [mix_r, cr], [C["pT_r"]])
        self.cp(S, ACT, mixT[:, :, :], pT[:, :].rearrange("p (a b) -> p a b", a=KC), [C["pT_r"]], [mixT_r])
        for hf in range(2):
            b, br = banks[hf], bank_rs[hf]
            for kc in range(KC):
                self.mm(S, b[:, :], mixT[:, kc, :], wout[:, kc, hf * 512:(hf + 1) * 512], kc == 0, kc == KC - 1,
                        [mixT_r] + wout_rs, [br])
            self.tt(S, DVE, xt[:, hf * 512:(hf + 1) * 512], b[:, :], xt[:, hf * 512:(hf + 1) * 512], ALU.add,
                    [br, xr], [xr])
        self.ld(S, SP, dst[gi * 128:(gi + 1) * 128, :], xt, [], xr, reads=[xr])

    def phase_bmix(self, src, dst, final):
        nc = self.nc
        S = Sched(nc)
        with ExitStack() as st:
            C = self.phase_consts(S, st, None)
            sb, ps = C["sb"], C["ps"]
            cr = C["const_r"]
            gT_kv = C["gT"][:, 2, :]
            gT_mix = C["gT"][:, 3, :]
            wkv = sb("wkv", (128, KC, 1536), BF16)
            wkv_r = S.regions(2, "wkv")
            wbin = sb("wbin", (128, KC, D), BF16)
            wbin_r = S.regions(1, "wbin")
            wbout = sb("wbout", (128, KC, D), BF16)
            wbout_r = S.regions(1, "wbout")
            biasT = sb("biasT", (128, 5, 12, 128), F32)
            bias_r = S.region("biasT")
            hT = sb("hT", (128, KC, T), BF16)
            hT_r = S.regions(NSUB, "hT")
            hTk = sb("hTk", (128, KC, T), BF16)
            hTk_r = S.regions(NSUB, "hTk")
            KTr = sb("KTr", (128, 2, 6, T), BF16)
            KT_r = [S.regions(6, f"KT{sl}_") for sl in range(2)]
            Vr = sb("Vr", (128, 2 * NSUB, 12, 65), BF16)
            V_r = S.regions(2 * NSUB, "V")
            QT = sb("QT", (128, 8, T), BF16)
            QT_r = S.regions(8, "QT")
            QA = sb("QA", (128, 6, T), BF16)
            QB = sb("QB", (128, 6, T), BF16)
            QAB_r = S.regions(6, "QAB")
            mkT = sb("mkT", (128, 2, MEMT), BF16)
            mvx = sb("mvx", (128, 2, 4, 65), BF16)
            mem_r = S.region("memkv")
            ssb = sb("ssb", (128, 2, 512), F32)
            ssb_r = S.regions(2, "ssb")
            pTs = sb("pTs", (128, 2, 4, 128), BF16)
            pTs_r = S.regions(2, "pTs")
            pmT = sb("pmT", (128, 8, T), BF16)
            pmT_r = S.region("pmT")
            mix = sb("mix", (128, 2, D), BF16)
            mix_r = S.regions(2, "mix")
            mixT = sb("mixT", (128, 2, KC, 128), BF16)
            mixT_r = S.regions(2, "mixT")
            rr = sb("rr", (128, 4, 4), F32)
            rr_r = S.regions(4, "rr")
            pb = [ps(f"pb{i}", (128, 512), F32) for i in range(7)]
            pb_r = S.regions(7, "pb")

            self.mem_prologue(S, C, 1, QT, QT_r, hT, hT_r, mkT, mvx, mem_r, pb[0], pb_r[0])
            self.load_w(S, wkv, wkv_r[0], self.w_kv, 0, 768)
            self.load_w(S, wkv, wkv_r[1], self.w_kv, 768, 1536)
            self.load_w(S, wbin, wbin_r[0], self.b_w_in, 0, D)
            self.load_w(S, wbout, wbout_r[0], self.b_w_out, 0, D)
            for kt in range(5):
                self.ld(S, SP, biasT[:, kt, :, :], self.relbias[:, kt, :, :], [bias_r], bias_r)
            self.ms(S, POOL, biasT[0:64, 0, :, 64:128], NEG, [bias_r])
            self.ms(S, POOL, biasT[64:128, 4, :, 0:64], NEG, [bias_r])
            self.ms(S, POOL, Vr[:, :, :, 64:65], 1.0, V_r)
            self.ms(S, POOL, QA[64:128, :, :], 0.0, QAB_r)
            self.ms(S, POOL, QB[0:64, :, :], 0.0, QAB_r)
            for gi in range(XSLOTS):
                self.load_x(S, C, src, gi)
            loaded = XSLOTS
            ia = 0
            isb = 0
            io = 0
            ipt = 0
            STOP = int(os.environ.get("K_STOP", 99))
            for ti in range(NT if STOP > 0 else 0):
                slot = ti % 2
                for s in range(NSUB):
                    gi = ti * NSUB + s
                    sl = gi % XSLOTS
                    self.norm_T(S, C, C["xt"][:, sl, :], C["xt_r"][sl],
                                [(gT_mix, hT[:, :, s * 128:(s + 1) * 128], hT_r[s]),
                                 (gT_kv, hTk[:, :, s * 128:(s + 1) * 128], hTk_r[s])])
                if STOP <= 1:
                    continue
                for j in range(6):
                    b, br = pb[ia % 2], pb_r[ia % 2]
                    ia += 1
                    for kc in range(KC):
                        self.mm(S, b[:, :], wkv[:, kc, j * 128:(j + 1) * 128], hTk[:, kc, :], kc == 0, kc == KC - 1,
                                hTk_r + [wkv_r[0]], [br])
                    self.cp(S, ACT, KTr[:, slot, j, :], b[:, :], [br], [KT_r[slot][j]])
                for s in range(NSUB):
                    vi = slot * NSUB + s
                    for (c0, c1, h0, h1) in ((768, 1280, 0, 8), (1280, 1536, 8, 12)):
                        b, br = pb[ia % 2], pb_r[ia % 2]
                        ia += 1
                        n = c1 - c0
                        for kc in range(KC):
                            self.mm(S, b[:, 0:n], hTk[:, kc, s * 128:(s + 1) * 128], wkv[:, kc, c0:c1], kc == 0,
                                    kc == KC - 1, [hTk_r[s], wkv_r[1]], [br])
                        self.cp(S, DVE, Vr[:, vi, h0:h1, 0:64], b[:, 0:n].rearrange("p (h d) -> p h d", d=64),
                                [br], [V_r[vi]])
                for j in range(8):
                    b, br = pb[ia % 2], pb_r[ia % 2]
                    ia += 1
                    for kc in range(KC):
                        self.mm(S, b[:, :], wbin[:, kc, j * 128:(j + 1) * 128], hT[:, kc, :], kc == 0, kc == KC - 1,
                                hT_r + wbin_r, [br])
                    if j < 6:
                        self.cp(S, ACT, QA[0:64, j, :], b[0:64, :], [br], [QAB_r[j]])
                        self.cp(S, DVE, QB[64:128, j, :], b[64:128, :], [br], [QAB_r[j]])
                    else:
                        self.cp(S, ACT, QT[:, j, :], b[:, :], [br], [QT_r[j]])
                if STOP <= 2:
                    continue
                self.mem_scores(S, QT[:, 6:8, :], QT_r[6:8], mkT, mem_r, pmT, pmT_r, [pb[0], pb[1]], [pb_r[0], pb_r[1]])
                for s in range(NSUB if STOP > 3 else 0):
                    P = ti * NSUB + s
                    gi = P
                    sl = gi % XSLOTS
                    mx, mxr = mix[:, P % 2, :], mix_r[P % 2]
                    kts = [kt for kt in range(5) if P - 4 + kt >= 0]
                    for hg in range(3 if STOP > 4 else 0):
                        po, por = pb[4 + io % 2], pb_r[4 + io % 2]
                        io += 1
                        for kt in kts:
                            kp = P - 4 + kt
                            sk = (kp // NSUB) % 2
                            subk = kp % NSUB
                            bs, bsr = pb[2 + isb % 2], pb_r[2 + isb % 2]
                            sbuf, sbr = ssb[:, isb % 2, :], ssb_r[isb % 2]
                            pt, ptr = pTs[:, isb % 2, :, :], pTs_r[isb % 2]
                            isb += 1
                            for hl in range(4):
                                h = hg * 4 + hl
                                Qh = QA if h % 2 == 0 else QB
                                self.mm(S, bs[:, hl * 128:(hl + 1) * 128],
                                        KTr[:, sk, h // 2, subk * 128:(subk + 1) * 128],
                                        Qh[:, h // 2, s * 128:(s + 1) * 128], True, True,
                                        [KT_r[sk][h // 2], QAB_r[h // 2]], [bsr])
                            self.stt(S, sbuf, bs[:, :], 0.125,
                                     biasT[:, kt, hg * 4:(hg + 1) * 4, :].rearrange("p h q -> p (h q)"),
                                     ALU.mult, ALU.add, [bsr, bias_r], [sbr])
                            self.act(S, pt.rearrange("p h q -> p (h q)"), sbuf, AF.Exp, [sbr], [ptr])
                            for hl in range(4):
                                h = hg * 4 + hl
                                self.mm(S, po[:, hl * 65:(hl + 1) * 65], pt[:, hl, :], Vr[:, sk * NSUB + subk, h, :],
                                        kt == kts[0] and hl == 0, kt == kts[-1], [ptr, V_r[sk * NSUB + subk]], [por],
                                        skip=True)
                        pv = po[:, 0:260].rearrange("p (h e) -> p h e", h=4)
                        rq, rqr = rr[:, hg, :], rr_r[hg]
                        S.op(DVE, (lambda pv=pv, rq=rq: (lambda e: e.reciprocal(out=rq, in_=pv[:, :, 64])))(), [por], [rqr])
                        self.tt(S, DVE, mx[:, hg * 256:(hg + 1) * 256].rearrange("p (h e) -> p h e", h=4),
                                pv[:, :, 0:64], rq.unsqueeze(2).to_broadcast([128, 4, 64]), ALU.mult,
                                [por, rqr], [mxr])
                    if STOP > 5:
                        self.mem_pv(S, s, pmT, pmT_r, mvx, mem_r, pb[6], pb_r[6], rr[:, 3, :], rr_r[3], mx, mxr)
                    if STOP > 6:
                      self.out_proj(S, C, mx, mxr, mixT[:, P % 2, :, :], mixT_r[P % 2], wbout, wbout_r,
                                  C["xt"][:, sl, :], C["xt_r"][sl], [pb[0], pb[1]], [pb_r[0], pb_r[1]], dst, gi)
                    if loaded < NT * NSUB and loaded % XSLOTS == sl:
                        self.load_x(S, C, src, loaded)
                        loaded += 1
            self.stats["bmix"] = S.emit(self.G)


def host_consts():
    ident = np.eye(128, dtype=np.float32)
    s = np.arange(128)[:, None]
    t = np.arange(128)[None, :]
    negU = np.where(s <= t, -1.0, 0.0).astype(np.float32)
    maskc = np.where(s <= t, np.float32(DH ** -0.5), np.float32(0.0)).astype(np.float32)
    return {"c_ident": ident, "c_negU": negU, "c_maskc": maskc}


def host_layout(inp):
    f = lambda a: np.ascontiguousarray(np.asarray(a, dtype=np.float32))
    g = np.stack([f(inp["norm_mix_g"])[0], f(inp["norm_ffn_g"])[0], f(inp["kv_norm_g"]),
                  f(inp["norm_mix_g"])[1], f(inp["norm_ffn_g"])[1]], 0)
    gT_all = np.ascontiguousarray(g.reshape(5, KC, 128).transpose(2, 0, 1))
    cwA = np.ascontiguousarray(f(inp["a_conv_w"])[0].reshape(4, 16, 96).transpose(2, 1, 0))
    cbA = np.ascontiguousarray(f(inp["a_conv_b"])[0].reshape(16, 96).T)
    cwF = np.ascontiguousarray(f(inp["ffn_conv_w"]).reshape(2, 3, NFT, 128).transpose(3, 0, 2, 1))
    cbF = np.ascontiguousarray(f(inp["ffn_conv_b"]).reshape(2, NFT, 128).transpose(2, 0, 1))
    rel = np.arange(768) - 127
    idx = np.clip(rel, -63, 128) + 63
    relext = f(inp["b_rel_bias"])[0][:, idx]
    kj = np.arange(128)[:, None, None]
    kt = np.arange(5)[None, :, None]
    qi = np.arange(128)[None, None, :]
    gidx = qi - kj + (4 - kt) * 128 + 127
    relbias = np.ascontiguousarray(relext[:, gidx].transpose(1, 2, 0, 3))
    shared = {
        "a_w_in": f(inp["a_w_in"])[0], "a_w_out": f(inp["a_w_out"])[0], "w_kv": f(inp["w_kv"]),
        "b_w_in": f(inp["b_w_in"])[0], "b_w_out": f(inp["b_w_out"])[0], "mem_w_kv": f(inp["mem_w_kv"]),
        "ffn_w_up": f(inp["ffn_w_up"]), "ffn_w_down": f(inp["ffn_w_down"]),
        "gT_all": gT_all, "final_g": f(inp["final_g"]).reshape(1, D), "gate_b": f(inp["a_gate_b"]).reshape(1, 8),
        "cwA": cwA, "cbA": cbA, "head_g": f(inp["a_head_g"]).reshape(1, AW), "cwF": cwF, "cbF": cbF,
        "relbias": relbias,
    }
    shared.update(host_consts())
    return shared


_CACHE = {}


def run(inputs, phases=("A_mix", "A_ffn", "B_mix", "B_ffn"), final_norm=True, ncores=8, trace=False):
    key = (tuple(phases), final_norm)
    if key not in _CACHE:
        _CACHE[key] = Prog(phases, final_norm).build()
    nc = _CACHE[key]
    shared = host_layout(inputs)
    x = np.asarray(inputs["x"], dtype=np.float32)
    mem = np.asarray(inputs["mem"], dtype=np.float32)
    in_maps = []
    for c in range(ncores):
        m = dict(shared)
        m["x"] = np.ascontiguousarray(x[c])
        m["mem"] = np.ascontiguousarray(mem[c])
        in_maps.append(m)
    res = run_bass_kernel_spmd(nc, in_maps, core_ids=list(range(ncores)), trace=trace)
    out = np.stack([np.asarray(r["out"]) for r in res.results], 0)
    return out, res


def kernel(**inputs):
    out, _ = run(inputs)
    return out.astype(np.float32)
```

```python
import numpy as np
from contextlib import ExitStack
import concourse.bass as bass
import concourse.mybir as mybir
from concourse.bass_types import AP
from concourse.bass_utils import run_bass_kernel_spmd

F32 = mybir.dt.float32
BF16 = mybir.dt.bfloat16
ALU = mybir.AluOpType
AF = mybir.ActivationFunctionType
AX = mybir.AxisListType

PE, ACT, DVE, POOL, SP = "tensor", "scalar", "vector", "gpsimd", "sync"
ENGS = (PE, ACT, DVE, POOL, SP)

D = 1024
KC = 8
SEQ = 4096
T = 512
import os
NT = int(os.environ.get("K_NT", SEQ // T))
SUB = 128
NSUB = T // SUB
DFF = 2816
NFT = DFF // 128
A_IN = 3336
AW = 768
DH = 192
MEMT = 256
EPS = 1e-6
NEG = -30000.0
XSLOTS = 6


class Region:
    __slots__ = ("name", "writer", "readers", "strict")

    def __init__(self, name):
        self.name = name
        self.writer = None
        self.readers = []
        self.strict = False


class _Op:
    __slots__ = ("eng", "idx", "fn", "waits", "needs_inc", "dma_key", "snap")

    def __init__(self, eng, idx, fn):
        self.eng = eng
        self.idx = idx
        self.fn = fn
        self.waits = []
        self.needs_inc = False
        self.dma_key = None
        self.snap = None


class Sched:
    def __init__(self, nc):
        self.nc = nc
        self.ops = {e: [] for e in ENGS}
        self.clock = {e: {x: -1 for x in ENGS} for e in ENGS}
        self.dclock = {e: {} for e in ENGS}
        self.dma_count = {}
        self.all_regions = []

    def region(self, name=None):
        r = Region(name or f"r{len(self.all_regions)}")
        self.all_regions.append(r)
        return r

    def regions(self, n, name="r"):
        return [self.region(f"{name}{i}") for i in range(n)]

    def _add(self, eng, fn, reads, writes, dma_key=None):
        o = _Op(eng, len(self.ops[eng]), fn)
        o.dma_key = dma_key
        deps = []
        for r in reads:
            if r.writer is not None:
                deps.append((r.writer, True))
        for w in writes:
            if w.writer is not None:
                deps.append((w.writer, w.strict))
            for rd in w.readers:
                deps.append((rd, w.strict))
        clk = self.clock[eng]
        dclk = self.dclock[eng]
        for tok, is_raw in deps:
            if tok[0] == "c":
                _, e2, n = tok
                if e2 == eng and (not is_raw or eng == PE):
                    continue
                if clk[e2] >= n:
                    continue
                o.waits.append(tok)
                self.ops[e2][n].needs_inc = True
                clk[e2] = n
                sn = self.ops[e2][n].snap
                for k, v in sn.items():
                    if k != eng and clk[k] < v:
                        clk[k] = v
            else:
                _, key, val = tok
                if dclk.get(key, 0) >= val:
                    continue
                cur = self.dma_count[key]
                o.waits.append(("d", key, cur))
                dclk[key] = cur
        o.snap = dict(clk)
        self.ops[eng].append(o)
        if dma_key is not None:
            self.dma_count[dma_key] = self.dma_count.get(dma_key, 0) + 16
            tok = ("d", dma_key, self.dma_count[dma_key])
        else:
            tok = ("c", eng, o.idx)
        for r in reads:
            r.readers.append(tok)
        for w in writes:
            w.writer = tok
            w.readers = []
        return o

    def op(self, eng, fn, reads=(), writes=()):
        return self._add(eng, fn, reads, writes)

    def dma(self, eng, fn, reads=(), writes=(), key=None):
        return self._add(eng, fn, reads, writes, dma_key=key.name + "@" + eng)

    def emit(self, G):
        nc = self.nc
        self._add(SP, None, list(self.all_regions), list(self.all_regions))
        keys = list(self.dma_count)
        slot = {}
        nsw = nhw = 0
        for k in keys:
            if k.endswith("@" + POOL):
                slot[k] = nsw
                nsw += 1
            else:
                slot[k] = G.NSW + nhw
                nhw += 1
        assert nsw <= G.NSW and nhw <= G.NDMA - G.NSW, (nsw, nhw)
        dsem = {k: G.dsem[slot[k]] for k in keys}
        dbase = {k: G.dbase[slot[k]] for k in keys}
        esem, ebase = G.esem, dict(G.ebase)
        G.phase += 1
        barv = G.phase
        cnt = {}
        for e in ENGS:
            c = 0
            arr = []
            for o in self.ops[e]:
                if o.needs_inc:
                    c += 1
                arr.append(c)
            cnt[e] = arr
            G.ebase[e] += c
        for k in keys:
            G.dbase[slot[k]] += self.dma_count[k]
        with nc.Block() as block:
            def make(e):
                def body(engh):
                    for o in self.ops[e]:
                        for w in o.waits:
                            if w[0] == "c":
                                engh.wait_ge(esem[w[1]], ebase[w[1]] + cnt[w[1]][w[2]])
                            else:
                                engh.wait_ge(dsem[w[1]], dbase[w[1]] + w[2])
                        if o.fn is None:
                            continue
                        ins = o.fn(engh)
                        if o.dma_key is not None:
                            ins.then_inc(dsem[o.dma_key], 16)
                        elif o.needs_inc:
                            ins.then_inc(esem[e], 1)
                    if e == SP:
                        engh.sem_inc(G.bar, 1)
                    else:
                        engh.wait_ge(G.bar, barv)
                return body

            for e in ENGS:
                getattr(block, e)(make(e))
        return {e: len(self.ops[e]) for e in ENGS}


class SemPool:
    NDMA = 56
    NSW = 16

    def __init__(self, nc, st):
        self.esem = {e: st.enter_context(nc.semaphore(f"s_{e}")) for e in ENGS}
        self.dsem = [st.enter_context(nc.semaphore(f"d_{i}")) for i in range(self.NDMA)]
        self.bar = st.enter_context(nc.semaphore("bar"))
        self.ebase = {e: 0 for e in ENGS}
        self.dbase = [0] * self.NDMA
        self.phase = 0
        allsem = list(self.esem.values()) + self.dsem + [self.bar]
        with nc.Block() as block:
            @block.gpsimd
            def _(g):
                for s in allsem:
                    g.sem_clear(s)
        nc.all_engine_barrier()


class Prog:
    def __init__(self, phases, final_norm=True):
        self.nc = nc = bass.Bass("TRN2", target_bir_lowering=False)
        self.phases = phases
        self.final_norm = final_norm
        din = lambda n, s: nc.dram_tensor(n, list(s), F32, kind="ExternalInput").ap()
        self.x = din("x", (SEQ, D))
        self.mem = din("mem", (MEMT, D))
        self.a_w_in = din("a_w_in", (D, A_IN))
        self.a_w_out = din("a_w_out", (D, D))
        self.w_kv = din("w_kv", (D, 1536))
        self.b_w_in = din("b_w_in", (D, D))
        self.b_w_out = din("b_w_out", (D, D))
        self.mem_w_kv = din("mem_w_kv", (2, D, 512))
        self.ffn_w_up = din("ffn_w_up", (2, D, 2 * DFF))
        self.ffn_w_down = din("ffn_w_down", (2, DFF, D))
        self.gT_all = din("gT_all", (128, 5, KC))
        self.final_g = din("final_g", (1, D))
        self.gate_b = din("gate_b", (1, 8))
        self.cwA = din("cwA", (96, 16, 4))
        self.cbA = din("cbA", (96, 16))
        self.head_g = din("head_g", (1, AW))
        self.cwF = din("cwF", (128, 2, NFT, 3))
        self.cbF = din("cbF", (128, 2, NFT))
        self.relbias = din("relbias", (128, 5, 12, 128))
        self.c_ident = din("c_ident", (128, 128))
        self.c_negU = din("c_negU", (128, 128))
        self.c_maskc = din("c_maskc", (128, 128))
        self.xa = nc.dram_tensor("xa", [SEQ, D], F32, kind="Internal").ap()
        self.xb = nc.dram_tensor("xb", [SEQ, D], F32, kind="Internal").ap()
        self.out = nc.dram_tensor("out", [SEQ, D], F32, kind="ExternalOutput").ap()
        self.stats = {}

    def mm(self, S, out, lhsT, rhs, start, stop, reads, writes, skip=False):
        S.op(PE, lambda e: e.matmul(out, lhsT=lhsT, rhs=rhs, start=start, stop=stop,
                                    skip_group_check=skip), reads, writes)

    def tr(self, S, out, in_, ident, reads, writes):
        S.op(PE, lambda e: e.transpose(out=out, in_=in_, identity=ident), reads, writes)

    def act(self, S, out, in_, func, reads, writes, **kw):
        S.op(ACT, lambda e: e.activation(out=out, in_=in_, func=func, **kw), reads, writes)

    def tt(self, S, eng, out, in0, in1, op, reads, writes):
        S.op(eng, lambda e: e.tensor_tensor(out=out, in0=in0, in1=in1, op=op), reads, writes)

    def ts(self, S, eng, out, in0, s1, s2, op0, op1, reads, writes):
        if op1 is None:
            S.op(eng, lambda e: e.tensor_scalar(out=out, in0=in0, scalar1=s1, scalar2=None, op0=op0),
                 reads, writes)
        else:
            S.op(eng, lambda e: e.tensor_scalar(out=out, in0=in0, scalar1=s1, scalar2=s2, op0=op0, op1=op1),
                 reads, writes)

    def stt(self, S, out, in0, scalar, in1, op0, op1, reads, writes):
        S.op(DVE, lambda e: e.scalar_tensor_tensor(out=out, in0=in0, scalar=scalar, in1=in1, op0=op0, op1=op1),
             reads, writes)

    def cp(self, S, eng, out, in_, reads, writes):
        if eng == ACT:
            S.op(ACT, lambda e: e.copy(out=out, in_=in_), reads, writes)
        else:
            S.op(eng, lambda e: e.tensor_copy(out=out, in_=in_), reads, writes)

    def ms(self, S, eng, ap, val, writes):
        S.op(eng, lambda e: e.memset(ap, val), (), writes)

    def ld(self, S, eng, out, in_, writes, key, reads=(), **kw):
        S.dma(eng, lambda e: e.dma_start(out=out, in_=in_, **kw), reads, writes, key=key)

    def load_w(self, S, dst, reg, src2d, c0, c1, kc0=0, kc1=None):
        kcn = src2d.shape[0] // 128
        kc1 = kcn if kc1 is None else kc1
        src = src2d.rearrange("(kc p) n -> p kc n", p=128)
        step = 2048
        for a in range(c0, c1, step):
            b = min(c1, a + step)
            self.ld(S, POOL, dst[:, kc0:kc1, a:b], src[:, kc0:kc1, a:b], [reg], reg)

    def norm_stats(self, S, C, xt_ap, xr):
        i = C["nrm_i"]
        C["nrm_i"] += 1
        k = i % 2
        ss = C["ss"][:, k, 0:1]
        ms_ = C["ss"][:, k, 1:2]
        rstd = C["ss"][:, k, 2:3]
        ssr = C["ss_r"][k]
        xh = C["xh"][:, k, :]
        xhr = C["xh_r"][k]
        self.act(S, xh, xt_ap, AF.Square, [xr], [xhr, ssr], accum_out=ss)
        self.ts(S, DVE, ms_, ss, 1.0 / D, EPS, ALU.mult, ALU.add, [ssr], [ssr])
        self.tt(S, POOL, rstd, ms_, C["neghalf"][:, 0:1], ALU.pow, [ssr, C["const_r"]], [ssr])
        self.act(S, xh, xt_ap, AF.Copy, [xr, ssr], [xhr], scale=rstd)
        return (xh, xhr)

    def norm_tr(self, S, C, hnd, outs):
        xh, xhr = hnd
        pT = C["pT"]
        for kc in range(KC):
            self.tr(S, pT[:, kc * 128:(kc + 1) * 128], xh[:, kc * 128:(kc + 1) * 128], C["identb"][:, :],
                    [xhr, C["const_r"]], [C["pT_r"]])
        pT3 = pT[:, :].rearrange("p (a b) -> p a b", a=KC)
        for gT, dst, dr in outs:
            self.tt(S, DVE, dst, pT3, gT.unsqueeze(2).to_broadcast([128, KC, 128]), ALU.mult,
                    [C["pT_r"], C["const_r"]], [dr])

    def norm_T(self, S, C, xt_ap, xr, outs):
        self.norm_tr(S, C, self.norm_stats(S, C, xt_ap, xr), outs)

    def phase_consts(self, S, st, which_gains):
        nc = self.nc
        C = {"nrm_i": 0}
        self.phase_i = getattr(self, "phase_i", 0) + 1
        pfx = f"p{self.phase_i}_"
        sb = lambda n, s, d: st.enter_context(nc.sbuf_tensor(pfx + n, list(s), d))
        ps = lambda n, s, d: st.enter_context(nc.psum_tensor(pfx + n, list(s), d))
        C["sb"] = sb
        C["ps"] = ps
        C["const_r"] = cr = S.region("const")
        C["identb"] = sb("identb", (128, 128), BF16)
        C["neghalf"] = sb("neghalf", (128, 1), F32)
        C["gT"] = sb("gT", (128, 5, KC), F32)
        self.ld(S, POOL, C["identb"][:, :], self.c_ident[:, :], [cr], cr)
        self.ld(S, SP, C["gT"][:, :, :], self.gT_all[:, :, :], [cr], cr)
        self.ms(S, DVE, C["neghalf"][:, :], -0.5, [cr])
        C["ss"] = sb("ss", (128, 2, 4), F32)
        C["ss_r"] = S.regions(2, "ss")
        C["xh"] = sb("xh", (128, 2, D), BF16)
        C["xh_r"] = S.regions(2, "xh")
        C["pT"] = ps("pT", (128, D), BF16)
        C["pT_r"] = S.region("pT")
        C["xt"] = sb("xt", (128, XSLOTS, D), F32)
        C["xt_r"] = S.regions(XSLOTS, "xt")
        return C

    def load_x(self, S, C, src, gi):
        sl = gi % XSLOTS
        self.ld(S, SP, C["xt"][:, sl, :], src[gi * 128:(gi + 1) * 128, :], [C["xt_r"][sl]], C["xt_r"][sl])

    def phase_ffn(self, l, src, dst, final):
        nc = self.nc
        S = Sched(nc)
        with ExitStack() as st:
            C = self.phase_consts(S, st, None)
            sb, ps = C["sb"], C["ps"]
            gT = C["gT"][:, 1 if l == 0 else 4, :]
            wup = sb("wup", (128, KC, 2 * DFF), BF16)
            wup_r = S.regions(4, "wup")
            wdn = sb("wdn", (128, NFT, D), BF16)
            wdn_r = S.regions(2, "wdn")
            hT = sb("hT", (128, KC, T), BF16)
            hT_r = S.regions(NSUB, "hT")
            aT = sb("aT", (128, NFT, T), BF16)
            aT_r = S.regions(NFT, "aT")
            graw = sb("graw", (128, 2, T + 2), F32)
            graw_r = S.regions(2, "graw")
            acc = sb("acc", (128, 2, T), F32)
            acc_r = S.regions(2, "acc")
            tnh = sb("tnh", (128, 2, T), F32)
            tnh_r = S.regions(2, "tnh")
            halo = sb("halo", (128, NFT, 2), F32)
            halo_r = S.regions(NFT, "halo")
            cw = sb("cw", (128, NFT, 3), F32)
            cb = sb("cb", (128, NFT), F32)
            cr = C["const_r"]
            pb = [ps(f"pb{i}", (128, 512), F32) for i in range(7)]
            pb_r = S.regions(7, "pb")
            if final:
                fg = sb("fg", (128, D), F32)
                self.ld(S, SP, fg[:, :], self.final_g.partition_broadcast(128), [cr], cr)
                fss = sb("fss", (128, 2, 4), F32)
                fss_r = S.regions(2, "fss")
                for r_ in C["xh_r"]:
                    r_.strict = True
            for gi in range(min(XSLOTS, NSUB + 2)):
                self.load_x(S, C, src, gi)
            loaded = min(XSLOTS, NSUB + 2)
            self.ld(S, SP, cw[:, :, :], self.cwF[:, l, :, :], [cr], cr)
            self.ld(S, SP, cb[:, :], self.cbF[:, l, :], [cr], cr)
            self.ts(S, POOL, cw[:, :, :], cw[:, :, :], 0.5, None, ALU.mult, None, [cr], [cr])
            self.ts(S, POOL, cb[:, :], cb[:, :], 0.5, None, ALU.mult, None, [cr], [cr])
            self.ms(S, POOL, halo[:, :, :], 0.0, halo_r)
            wu = self.ffn_w_up[l]
            for c in range(4):
                self.load_w(S, wup, wup_r[c], wu, c * 1408, (c + 1) * 1408)
            wd = self.ffn_w_down[l]
            self.load_w(S, wdn, wdn_r[0], wd, 0, D, 0, 11)
            self.load_w(S, wdn, wdn_r[1], wd, 0, D, 11, 22)

            pbi = 0

            def do_stats(ti, s):
                gi = ti * NSUB + s
                sl = gi % XSLOTS
                return self.norm_stats(S, C, C["xt"][:, sl, :], C["xt_r"][sl])

            def do_tr(hnd, s):
                self.norm_tr(S, C, hnd, [(gT, hT[:, :, s * 128:(s + 1) * 128], hT_r[s])])

            for s in range(NSUB):
                do_tr(do_stats(0, s), s)
            for ti in range(NT):
                for j in range(NFT):
                    pu, pur = pb[pbi % 6], pb_r[pbi % 6]
                    pg, pgr = pb[(pbi + 1) % 6], pb_r[(pbi + 1) % 6]
                    pbi += 2
                    for kc in range(KC):
                        self.mm(S, pg[:, :], wup[:, kc, DFF + j * 128:DFF + (j + 1) * 128], hT[:, kc, :],
                                kc == 0, kc == KC - 1, hT_r + [wup_r[2 + j // 11]], [pgr])
                    for kc in range(KC):
                        self.mm(S, pu[:, :], wup[:, kc, j * 128:(j + 1) * 128], hT[:, kc, :],
                                kc == 0, kc == KC - 1, hT_r + [wup_r[j // 11]], [pur])
                    r = j % 2
                    gr, grr = graw[:, r, :], graw_r[r]
                    ac, acr = acc[:, r, :], acc_r[r]
                    tn, tnr = tnh[:, r, :], tnh_r[r]
                    self.cp(S, POOL, gr[:, 0:2], halo[:, j, :], [halo_r[j]], [grr])
                    self.cp(S, ACT, gr[:, 2:T + 2], pg[:, :], [pgr], [grr])
                    self.cp(S, POOL, halo[:, j, :], gr[:, T:T + 2], [grr], [halo_r[j]])
                    self.act(S, ac, pg[:, :], AF.Identity, [pgr, cr], [acr], scale=cw[:, j, 2:3], bias=cb[:, j:j + 1])
                    self.stt(S, ac, gr[:, 1:T + 1], cw[:, j, 1:2], ac, ALU.mult, ALU.add, [grr, cr, acr], [acr])
                    self.stt(S, ac, gr[:, 0:T], cw[:, j, 0:1], ac, ALU.mult, ALU.add, [grr, cr, acr], [acr])
                    self.act(S, tn, ac, AF.Tanh, [acr], [tnr])
                    self.tt(S, DVE, gr[:, 2:T + 2], ac, pu[:, :], ALU.mult, [acr, pur], [grr])
                    self.stt(S, aT[:, j, :], tn, 1.0, gr[:, 2:T + 2], ALU.add, ALU.mult, [tnr, grr], [aT_r[j]])
                hnds = {}
                for s in range(NSUB):
                    gi = ti * NSUB + s
                    sl = gi % XSLOTS
                    xt = C["xt"][:, sl, :]
                    xr = C["xt_r"][sl]
                    for hf in range(2):
                        po, por = pb[6], pb_r[6]
                        if hf == 1:
                            po, por = pb[pbi % 6], pb_r[pbi % 6]
                            pbi += 1
                        for kc in range(NFT):
                            self.mm(S, po[:, :], aT[:, kc, s * 128:(s + 1) * 128], wdn[:, kc, hf * 512:(hf + 1) * 512],
                                    kc == 0, kc == NFT - 1, [aT_r[kc], wdn_r[kc // 11]], [por])
                        self.tt(S, DVE, xt[:, hf * 512:(hf + 1) * 512], po[:, :], xt[:, hf * 512:(hf + 1) * 512],
                                ALU.add, [por, xr], [xr])
                    if final:
                        k = gi % 2
                        ss = fss[:, k, 0:1]
                        ms_ = fss[:, k, 1:2]
                        rstd = fss[:, k, 2:3]
                        self.act(S, C["xh"][:, k, :], xt, AF.Square, [xr], [C["xh_r"][k], fss_r[k]], accum_out=ss)
                        self.ts(S, DVE, ms_, ss, 1.0 / D, EPS, ALU.mult, ALU.add, [fss_r[k]], [fss_r[k]])
                        self.tt(S, POOL, rstd, ms_, C["neghalf"][:, 0:1], ALU.pow, [fss_r[k], cr], [fss_r[k]])
                        self.stt(S, xt, xt, rstd, fg[:, :], ALU.mult, ALU.mult, [xr, fss_r[k], cr], [xr])
                    self.ld(S, SP, dst[gi * 128:(gi + 1) * 128, :], xt, [], xr, reads=[xr])
                    if loaded < NT * NSUB and loaded % XSLOTS == sl:
                        self.load_x(S, C, src, loaded)
                        loaded += 1
                    if ti + 1 < NT:
                        hnds[s] = do_stats(ti + 1, s)
                        if s >= 1:
                            do_tr(hnds.pop(s - 1), s - 1)
                if ti + 1 < NT:
                    do_tr(hnds.pop(NSUB - 1), NSUB - 1)
            self.stats[f"ffn{l}"] = S.emit(self.G)

    def build(self):
        chain = {"A_mix": self.phase_amix, "A_ffn": lambda s, d, f: self.phase_ffn(0, s, d, False),
                 "B_mix": self.phase_bmix, "B_ffn": lambda s, d, f: self.phase_ffn(1, s, d, f)}
        src = self.x
        scr = [self.xa, self.xb]
        with ExitStack() as gst:
            self.G = SemPool(self.nc, gst)
            for i, ph in enumerate(self.phases):
                last = i == len(self.phases) - 1
                dst = self.out if last else scr[i % 2]
                chain[ph](src, dst, last and self.final_norm)
                src = dst
        return self.nc

    def phase_amix(self, src, dst, final):
        nc = self.nc
        S = Sched(nc)
        with ExitStack() as st:
            C = self.phase_consts(S, st, None)
            sb, ps = C["sb"], C["ps"]
            cr = C["const_r"]
            gT_mix = C["gT"][:, 0, :]
            wain = sb("wain", (128, KC, A_IN), BF16)
            wain_r = S.regions(4, "wain")
            waout = sb("waout", (128, KC, D), BF16)
            waout_r = S.regions(1, "waout")
            hT = sb("hT", (128, KC, T), BF16)
            hT_r = S.regions(NSUB, "hT")
            qkT = sb("qkT", (128, 16, T), BF16)
            qk_r = S.regions(16, "qk")
            qmT = sb("qmT", (128, 2, T), BF16)
            qm_r = S.regions(2, "qm")
            raw = sb("raw", (128, 2, T + 3), F32)
            raw_r = S.regions(2, "raw")
            acc = sb("acc", (128, 2, T), F32)
            acc_r = S.regions(2, "acc")
            tnh = sb("tnh", (128, 2, T), F32)
            tnh_r = S.regions(2, "tnh")
            halo = sb("halo", (128, 16, 3), F32)
            halo_r = S.regions(16, "halo")
            cw = sb("cw", (128, 16, 4), F32)
            cb = sb("cb", (128, 16), F32)
            ktok = sb("ktok", (128, NSUB, AW), BF16)
            ktok_r = S.regions(NSUB, "ktok")
            vw = sb("vw", (128, NSUB, 4, DH + 1), BF16)
            vw_r = S.regions(NSUB, "vw")
            G2 = sb("G2", (128, NSUB, AW), F32)
            G2_r = S.regions(NSUB, "G2")
            hgh = sb("hgh", (128, AW), F32)
            gb_bc = sb("gb_bc", (128, 8), F32)
            gsb = sb("gsb", (128, NSUB, 8), F32)
            gw = sb("gw", (128, 16, 16), F32)
            g_r = S.region("gates")
            EP, SPL, AA, BBL, AMX, MALL, MST, T48 = 0, 1, 2, 3, 5, 6, 7, 8
            WGF = 11
            mcar = sb("mcar", (128, 4), F32)
            am16 = sb("am16", (16, 20), F32)
            identF = sb("identF", (128, 128), F32)
            negU = sb("negU", (128, 128), F32)
            negO = sb("negO", (128, 128), F32)
            ones16 = sb("ones16", (16, 128), F32)
            maskc = sb("maskc", (128, 4, 128), F32)
            Cn = sb("Cn", (128, 4, 2, DH + 1), F32)
            Cn_r = S.regions(4, "Cn")
            Gbf = sb("Gbf", (128, 4, 2, DH + 1), BF16)
            Gbf_r = S.regions(4, "Gbf")
            sm = sb("sm", (128, 2, 4, 128), BF16)
            sm_r = S.regions(2, "sm")
            hm = sb("hm", (128, 2, 4, DH), F32)
            hm_r = S.regions(2, "hm")
            hj = sb("hj", (128, 4, DH), BF16)
            hj_r = S.regions(4, "hj")
            for r_ in hj_r:
                r_.strict = True
            hst = sb("hst", (128, 2, 16), F32)
            hst_r = S.regions(2, "hst")
            mkT = sb("mkT", (128, 2, MEMT), BF16)
            mvx = sb("mvx", (128, 2, 4, 65), BF16)
            mem_r = S.region("memkv")
            pmT = sb("pmT", (128, 8, T), BF16)
            pmT_r = S.region("pmT")
            mix = sb("mix", (128, 2, D), BF16)
            mix_r = S.regions(2, "mix")
            mixT = sb("mixT", (128, 2, KC, 128), BF16)
            mixT_r = S.regions(2, "mixT")
            rr = sb("rr", (128, 4), F32)
            rr_r = S.region("rr")
            pb = [ps(f"pb{i}", (128, 512), F32) for i in range(7)]
            pb_r = S.regions(7, "pb")

            self.mem_prologue(S, C, 0, qkT[:, 0:8, :], qk_r[0:8], hT, hT_r, mkT, mvx, mem_r, pb[0], pb_r[0])
            self.load_w(S, wain, wain_r[0], self.a_w_in, 0, 1536)
            self.load_w(S, wain, wain_r[1], self.a_w_in, 1536, 2304)
            self.load_w(S, wain, wain_r[2], self.a_w_in, 2304, 3080)
            self.load_w(S, wain, wain_r[3], self.a_w_in, 3080, A_IN)
            self.load_w(S, waout, waout_r[0], self.a_w_out, 0, D)
            self.ld(S, SP, cw[0:96, :, :], self.cwA[:, :, :], [cr], cr)
            self.ld(S, SP, cb[0:96, :], self.cbA[:, :], [cr], cr)
            self.ld(S, SP, hgh[:, :], self.head_g.partition_broadcast(128), [cr], cr)
            self.ld(S, SP, gb_bc[:, :], self.gate_b.partition_broadcast(128), [cr], cr)
            self.ld(S, SP, identF[:, :], self.c_ident[:, :], [cr], cr)
            self.ld(S, SP, negU[:, :], self.c_negU[:, :], [cr], cr)
            for h in range(4):
                self.ld(S, SP, maskc[:, h, :], self.c_maskc[:, :], [cr], cr)
            self.ts(S, POOL, cw[0:96, :, :], cw[0:96, :, :], 0.5, None, ALU.mult, None, [cr], [cr])
            self.ts(S, POOL, cb[0:96, :], cb[0:96, :], 0.5, None, ALU.mult, None, [cr], [cr])
            self.ts(S, POOL, hgh[:, :], hgh[:, :], 0.5, None, ALU.mult, None, [cr], [cr])
            self.ms(S, POOL, negO[:, :], -1.0, [cr])
            self.ms(S, POOL, ones16[:, :], 1.0, [cr])
            self.ms(S, POOL, halo[:, :, :], 0.0, halo_r)
            self.ms(S, POOL, Cn[:, :, :, :], 0.0, Cn_r)
            self.ms(S, POOL, mcar[:, :], 0.0, [g_r])
            for gi in range(XSLOTS):
                self.load_x(S, C, src, gi)
            loaded = XSLOTS
            ia = 0
            for ti in range(NT):
                for s in range(NSUB):
                    gi = ti * NSUB + s
                    sl = gi % XSLOTS
                    self.norm_T(S, C, C["xt"][:, sl, :], C["xt_r"][sl],
                                [(gT_mix, hT[:, :, s * 128:(s + 1) * 128], hT_r[s])])
                pg, pgr = pb[6], pb_r[6]
                for s in range(NSUB):
                    for kc in range(KC):
                        self.mm(S, pg[:, s * 8:(s + 1) * 8], hT[:, kc, s * 128:(s + 1) * 128], wain[:, kc, 3072:3080],
                                kc == 0, kc == KC - 1, [hT_r[s], wain_r[2]], [pgr])
                self.tt(S, DVE, gsb[:, :, :], pg[:, 0:32].rearrange("p (s g) -> p s g", s=NSUB),
                        gb_bc[:, :].unsqueeze(1).to_broadcast([128, NSUB, 8]), ALU.add, [pgr, cr], [g_r])
                ga = lambda i, n=1: gw[:, i:i + n, :].rearrange("p a c -> p (a c)")
                g3 = lambda i: gw[:, i, :].rearrange("p (s h) -> p s h", s=NSUB)
                self.act(S, g3(EP), gsb[:, :, 4:8], AF.Exp, [g_r], [g_r], scale=-1.0)
                self.act(S, ga(SPL), ga(EP), AF.Ln, [g_r], [g_r], bias=1.0)
                self.mm(S, pg[:, 32:48], negU[:, :], ga(SPL), True, True, [g_r, cr], [pgr])
                self.mm(S, pg[:, 48:64], negO[:, :], ga(SPL), True, True, [g_r, cr], [pgr])
                self.cp(S, DVE, ga(BBL, 2), pg[:, 32:64], [pgr], [g_r])
                self.tt(S, DVE, g3(AA), gsb[:, :, 0:4], g3(BBL), ALU.subtract, [g_r], [g_r])
                self.tr(S, pg[0:16, 64:192], ga(AA), identF[:, :], [g_r, cr], [pgr])
                S.op(DVE, lambda e: e.reduce_max(out=am16[:, 0:1], in_=pg[0:16, 64:192], axis=AX.X), [pgr], [g_r])
                self.ts(S, DVE, am16[:, 4:20], identF[0:16, 0:16], am16[:, 0:1], None, ALU.mult, None, [g_r, cr], [g_r])
                self.mm(S, pg[:, 192:208], ones16[:, :], am16[:, 4:20], True, True, [g_r, cr], [pgr])
                self.cp(S, DVE, ga(AMX), pg[:, 192:208], [pgr], [g_r])
                for s in range(NSUB):
                    self.cp(S, DVE, g3(MST)[:, s, :], mcar[:, :], [g_r], [g_r])
                    self.tt(S, DVE, g3(MALL)[:, s, :], mcar[:, :], g3(AMX)[:, s, :], ALU.max, [g_r], [g_r])
                    self.tt(S, DVE, mcar[:, :], g3(BBL + 1)[:, s, :], g3(MALL)[:, s, :], ALU.add, [g_r], [g_r])
                self.tt(S, DVE, ga(T48), ga(AA), ga(MALL), ALU.subtract, [g_r], [g_r])
                self.tt(S, DVE, ga(T48 + 1), ga(MST), ga(MALL), ALU.subtract, [g_r], [g_r])
                self.stt(S, ga(T48 + 2), ga(BBL), -1.0, ga(MALL), ALU.mult, ALU.subtract, [g_r], [g_r])
                self.act(S, ga(WGF, 3), ga(T48, 3), AF.Exp, [g_r], [g_r])
                wv, gv, flv = g3(WGF), g3(WGF + 1), g3(WGF + 2)
                for i in range(16):
                    b, br = pb[ia % 2], pb_r[ia % 2]
                    ia += 1
                    for kc in range(KC):
                        self.mm(S, b[0:96, :], wain[:, kc, i * 96:(i + 1) * 96], hT[:, kc, :], kc == 0, kc == KC - 1,
                                hT_r + [wain_r[0]], [br])
                    r = i % 2
                    rw, rwr = raw[0:96, r, :], raw_r[r]
                    ac, acr = acc[0:96, r, :], acc_r[r]
                    tn, tnr = tnh[0:96, r, :], tnh_r[r]
                    self.cp(S, POOL, rw[:, 0:3], halo[0:96, i, :], [halo_r[i]], [rwr])
                    self.cp(S, ACT, rw[:, 3:T + 3], b[0:96, :], [br], [rwr])
                    self.cp(S, POOL, halo[0:96, i, :], rw[:, T:T + 3], [rwr], [halo_r[i]])
                    self.ts(S, DVE, ac, rw[:, 3:T + 3], cw[0:96, i, 3:4], cb[0:96, i:i + 1], ALU.mult, ALU.add,
                            [rwr, cr], [acr])
                    for j in (2, 1, 0):
                        self.stt(S, ac, rw[:, j:j + T], cw[0:96, i, j:j + 1], ac, ALU.mult, ALU.add,
                                 [rwr, cr, acr], [acr])
                    self.act(S, tn, ac, AF.Tanh, [acr], [tnr])
                    self.stt(S, qkT[0:96, i, :], tn, 1.0, ac, ALU.add, ALU.mult, [tnr, acr], [qk_r[i]])
                for s in range(NSUB):
                    pT = C["pT"]
                    for j in range(8):
                        self.tr(S, pT[:, j * 96:(j + 1) * 96], qkT[0:96, 8 + j, s * 128:(s + 1) * 128],
                                C["identb"][0:96, 0:96], [qk_r[8 + j], cr], [C["pT_r"]])
                    self.act(S, ktok[:, s, :], pT[:, 0:AW], AF.Copy, [C["pT_r"]], [ktok_r[s]], scale=float(DH ** -0.5))
                for s in range(NSUB):
                    for g in range(2):
                        b, br = pb[ia % 2], pb_r[ia % 2]
                        ia += 1
                        for kc in range(KC):
                            self.mm(S, b[:, 0:384], hT[:, kc, s * 128:(s + 1) * 128],
                                    wain[:, kc, 1536 + g * 384:1536 + (g + 1) * 384], kc == 0, kc == KC - 1,
                                    [hT_r[s], wain_r[1]], [br])
                        self.tt(S, DVE, vw[:, s, 2 * g:2 * g + 2, 0:DH], b[:, 0:384].rearrange("p (h e) -> p h e", h=2),
                                wv[:, s, 2 * g:2 * g + 2].unsqueeze(2).to_broadcast([128, 2, DH]), ALU.mult,
                                [br, g_r], [vw_r[s]])
                    self.cp(S, POOL, vw[:, s, :, DH], wv[:, s, :], [g_r], [vw_r[s]])
                    for g in range(2):
                        b, br = pb[ia % 2], pb_r[ia % 2]
                        ia += 1
                        for kc in range(KC):
                            self.mm(S, b[:, 0:384], hT[:, kc, s * 128:(s + 1) * 128],
                                    wain[:, kc, 2304 + g * 384:2304 + (g + 1) * 384], kc == 0, kc == KC - 1,
                                    [hT_r[s], wain_r[2]], [br])
                        g2 = G2[:, s, g * 384:(g + 1) * 384]
                        self.act(S, g2, b[:, 0:384], AF.Tanh, [br], [G2_r[s]], scale=0.5)
                        self.stt(S, g2, g2, 1.0, hgh[:, g * 384:(g + 1) * 384], ALU.add, ALU.mult, [G2_r[s], cr], [G2_r[s]])
                for j in range(2):
                    b, br = pb[ia % 2], pb_r[ia % 2]
                    ia += 1
                    for kc in range(KC):
                        self.mm(S, b[:, :], wain[:, kc, 3080 + j * 128:3080 + (j + 1) * 128], hT[:, kc, :], kc == 0,
                                kc == KC - 1, hT_r + [wain_r[3]], [br])
                    self.cp(S, ACT, qmT[:, j, :], b[:, :], [br], [qm_r[j]])
                self.mem_scores(S, qmT, qm_r, mkT, mem_r, pmT, pmT_r, [pb[0], pb[1]], [pb_r[0], pb_r[1]])
                for s in range(NSUB):
                    gi = ti * NSUB + s
                    sl = gi % XSLOTS
                    k2 = gi % 2
                    mx, mxr = mix[:, k2, :], mix_r[k2]
                    cs = slice(s * 128, (s + 1) * 128)
                    pS, pSr = pb[2], pb_r[2]
                    for h in range(4):
                        for j in range(2):
                            self.mm(S, pS[:, h * 128:(h + 1) * 128], qkT[0:96, 8 + 2 * h + j, cs], qkT[0:96, 2 * h + j, cs],
                                    j == 0, j == 1, [qk_r[8 + 2 * h + j], qk_r[2 * h + j]], [pSr])
                    smv, smr = sm[:, k2, :, :], sm_r[k2]
                    self.tt(S, DVE, smv.rearrange("p h t -> p (h t)"), pS[:, :], maskc[:, :, :].rearrange("p h t -> p (h t)"),
                            ALU.mult, [pSr, cr], [smr])
                    for h in range(4):
                        self.act(S, Gbf[0:96, h, :, :].rearrange("p j e -> p (j e)"),
                                 Cn[0:96, h, :, :].rearrange("p j e -> p (j e)"), AF.Copy, [Cn_r[h], g_r], [Gbf_r[h]],
                                 scale=gv[0:96, s, h:h + 1])
                    pN = [pb[3], pb[4]]
                    pNr = [pb_r[3], pb_r[4]]
                    for h in range(4):
                        o = pN[h // 2][:, (h % 2) * (DH + 1):(h % 2 + 1) * (DH + 1)]
                        self.mm(S, o, smv[:, h, :], vw[:, s, h, :], True, False, [smr, vw_r[s]], [pNr[h // 2]])
                        for j in range(2):
                            self.mm(S, o, qkT[0:96, 2 * h + j, cs], Gbf[0:96, h, j, :], False, j == 1,
                                    [qk_r[2 * h + j], Gbf_r[h]], [pNr[h // 2]])
                    for h in range(4):
                        pC, pCr = pb[5 + h % 2], pb_r[5 + h % 2]
                        for j in range(2):
                            self.mm(S, pC[0:96, j * (DH + 1):(j + 1) * (DH + 1)], ktok[:, s, h * DH + j * 96:h * DH + (j + 1) * 96],
                                    vw[:, s, h, :], True, True, [ktok_r[s], vw_r[s]], [pCr])
                        self.stt(S, Cn[0:96, h, :, :].rearrange("p j e -> p (j e)"),
                                 Cn[0:96, h, :, :].rearrange("p j e -> p (j e)"), gv[0:96, s, h:h + 1],
                                 pC[0:96, 0:2 * (DH + 1)], ALU.mult, ALU.add, [Cn_r[h], g_r, pCr], [Cn_r[h]])
                    hs, hsr = hst[:, k2, :], hst_r[k2]
                    hmv, hmr = hm[:, k2, :, :], hm_r[k2]
                    for hp2 in range(2):
                        pv = pN[hp2][:, 0:2 * (DH + 1)].rearrange("p (h e) -> p h e", h=2)
                        self.act(S, hs[:, 2 * hp2:2 * hp2 + 2], pv[:, :, DH], AF.Abs, [pNr[hp2]], [hsr])
                    self.tt(S, DVE, hs[:, 0:4], hs[:, 0:4], flv[:, s, :], ALU.max, [hsr, g_r], [hsr])
                    S.op(DVE, (lambda hs=hs: (lambda e: e.reciprocal(out=hs[:, 4:8], in_=hs[:, 0:4])))(), [hsr], [hsr])
                    for hp2 in range(2):
                        pv = pN[hp2][:, 0:2 * (DH + 1)].rearrange("p (h e) -> p h e", h=2)
                        self.tt(S, DVE, hmv[:, 2 * hp2:2 * hp2 + 2, :], pv[:, :, 0:DH],
                                hs[:, 4 + 2 * hp2:6 + 2 * hp2].unsqueeze(2).to_broadcast([128, 2, DH]), ALU.mult,
                                [pNr[hp2], hsr], [hmr])
                    for h in range(4):
                        self.act(S, hj[:, h, :], hmv[:, h, :], AF.Square, [hmr], [hj_r[h], hsr], accum_out=hs[:, 8 + h:9 + h])
                    self.ts(S, DVE, hs[:, 8:12], hs[:, 8:12], 1.0 / DH, EPS, ALU.mult, ALU.add, [hsr], [hsr])
                    self.tt(S, POOL, hs[:, 12:16], hs[:, 8:12], C["neghalf"][:, 0:1].to_broadcast([128, 4]), ALU.pow,
                            [hsr, cr], [hsr])
                    for h in range(4):
                        self.stt(S, mx[:, h * DH:(h + 1) * DH], hmv[:, h, :], hs[:, 12 + h:13 + h],
                                 G2[:, s, h * DH:(h + 1) * DH], ALU.mult, ALU.mult, [hmr, hsr, G2_r[s]], [mxr])
                    self.mem_pv(S, s, pmT, pmT_r, mvx, mem_r, pb[6], pb_r[6], rr, rr_r, mx, mxr)
                    self.out_proj(S, C, mx, mxr, mixT[:, k2, :, :], mixT_r[k2], waout, waout_r,
                                  C["xt"][:, sl, :], C["xt_r"][sl], [pb[0], pb[1]], [pb_r[0], pb_r[1]], dst, gi)
                    if loaded < NT * NSUB and loaded % XSLOTS == sl:
                        self.load_x(S, C, src, loaded)
                        loaded += 1
            self.stats["amix"] = S.emit(self.G)

    def mem_prologue(self, S, C, l, wmem, wmem_rs, memT, memT_rs, mkT, mvx, mem_r, pbank, pbank_r):
        cr = C["const_r"]
        xt, xt_r, xh, xh_r = C["xt"], C["xt_r"], C["xh"], C["xh_r"]
        self.load_w(S, wmem, wmem_rs[0], self.mem_w_kv[l], 0, 512)
        for mt in range(2):
            self.ld(S, SP, xt[:, mt, :], self.mem[mt * 128:(mt + 1) * 128, :], [xt_r[mt]], xt_r[mt])
            self.cp(S, DVE, xh[:, mt, :], xt[:, mt, :], [xt_r[mt]], [xh_r[mt]])
            pT = C["pT"]
            for kc in range(KC):
                self.tr(S, pT[:, kc * 128:(kc + 1) * 128], xh[:, mt, kc * 128:(kc + 1) * 128], C["identb"][:, :],
                        [xh_r[mt], cr], [C["pT_r"]])
            self.cp(S, DVE, memT[:, :, mt * 128:(mt + 1) * 128], pT[:, :].rearrange("p (a b) -> p a b", a=KC),
                    [C["pT_r"]], memT_rs)
        self.ms(S, POOL, mvx[:, :, :, 64:65], 1.0, [mem_r])
        for hp in range(2):
            for kc in range(KC):
                self.mm(S, pbank[:, 0:MEMT], wmem[:, kc, hp * 128:(hp + 1) * 128], memT[:, kc, 0:MEMT],
                        kc == 0, kc == KC - 1, memT_rs + wmem_rs, [pbank_r])
            self.cp(S, ACT, mkT[:, hp, :], pbank[:, 0:MEMT], [pbank_r], [mem_r])
        for mt in range(2):
            for kc in range(KC):
                self.mm(S, pbank[:, 0:256], memT[:, kc, mt * 128:(mt + 1) * 128], wmem[:, kc, 256:512],
                        kc == 0, kc == KC - 1, memT_rs + wmem_rs, [pbank_r])
            self.cp(S, DVE, mvx[:, mt, :, 0:64], pbank[:, 0:256].rearrange("p (h d) -> p h d", h=4),
                    [pbank_r], [mem_r])

    def mem_scores(self, S, qm, qm_rs, mkT, mem_r, pmT, pmT_r, banks, bank_rs):
        i = 0
        dbg = os.environ.get("K_DBG", "")
        for h in range(4):
            hp, hh = h // 2, h % 2
            if "h0" in dbg and hh == 1:
                continue
            for mt in range(2):
                b, br = banks[i % len(banks)], bank_rs[i % len(banks)]
                i += 1
                self.mm(S, b[:, :], mkT[hh * 64:(hh + 1) * 64, hp, mt * 128:(mt + 1) * 128],
                        qm[hh * 64:(hh + 1) * 64, hp, :], True, True, qm_rs + [mem_r], [br])
                if "noact" in dbg:
                    continue
                self.act(S, pmT[:, h * 2 + mt, :], b[:, :], AF.Exp, [br], [pmT_r], scale=0.125)

    def mem_pv(self, S, s, pmT, pmT_r, mvx, mem_r, pom, pom_r, rr, rr_r, mix, mix_r):
        first = True
        for h in range(4):
            for mt in range(2):
                self.mm(S, pom[:, h * 65:(h + 1) * 65], pmT[:, h * 2 + mt, s * 128:(s + 1) * 128], mvx[:, mt, h, :],
                        first, mt == 1, [pmT_r, mem_r], [pom_r], skip=True)
                first = False
        pv = pom[:, 0:260].rearrange("p (h e) -> p h e", h=4)
        S.op(DVE, lambda e: e.reciprocal(out=rr[:, 0:4], in_=pv[:, :, 64]), [pom_r], [rr_r])
        self.tt(S, DVE, mix[:, 768:1024].rearrange("p (h e) -> p h e", h=4), pv[:, :, 0:64],
                rr[:, 0:4].unsqueeze(2).to_broadcast([128, 4, 64]), ALU.mult, [pom_r, rr_r], [mix_r])

    def out_proj(self, S, C, mix, mix_r, mixT, mixT_r, wout, wout_rs, xt, xr, banks, bank_rs, dst, gi):
        cr = C["const_r"]
        pT = C["pT"]
        for kc in range(KC):
            self.tr(S, pT[:, kc * 128:(kc + 1) * 128], mix[:, kc * 128:(kc + 1) * 128], C["identb"][:, :],
                    [mix_r, cr], [C["pT_r"]])
        self.cp(S, ACT, mixT[:, :, :], pT[:, :].rearrange("p (a b) -> p a b", a=KC), [C["pT_r"]], [mixT_r])
        for hf in range(2):
            b, br = banks[hf], bank_rs[hf]
            for kc in range(KC):
                self.mm(S, b[:, :], mixT[:, kc, :], wout[:, kc, hf * 512:(hf + 1) * 512], kc == 0, kc == KC - 1,
                        [mixT_r] + wout_rs, [br])
            self.tt(S, DVE, xt[:, hf * 512:(hf + 1) * 512], b[:, :], xt[:, hf * 512:(hf + 1) * 512], ALU.add,
                    [br, xr], [xr])
        self.ld(S, SP, dst[gi * 128:(gi + 1) * 128, :], xt, [], xr, reads=[xr])

    def phase_bmix(self, src, dst, final):
        nc = self.nc
        S = Sched(nc)
        with ExitStack() as st:
            C = self.phase_consts(S, st, None)
            sb, ps = C["sb"], C["ps"]
            cr = C["const_r"]
            gT_kv = C["gT"][:, 2, :]
            gT_mix = C["gT"][:, 3, :]
            wkv = sb("wkv", (128, KC, 1536), BF16)
            wkv_r = S.regions(2, "wkv")
            wbin = sb("wbin", (128, KC, D), BF16)
            wbin_r = S.regions(1, "wbin")
            wbout = sb("wbout", (128, KC, D), BF16)
            wbout_r = S.regions(1, "wbout")
            biasT = sb("biasT", (128, 5, 12, 128), F32)
            bias_r = S.region("biasT")
            hT = sb("hT", (128, KC, T), BF16)
            hT_r = S.regions(NSUB, "hT")
            hTk = sb("hTk", (128, KC, T), BF16)
            hTk_r = S.regions(NSUB, "hTk")
            KTr = sb("KTr", (128, 2, 6, T), BF16)
            KT_r = [S.regions(6, f"KT{sl}_") for sl in range(2)]
            Vr = sb("Vr", (128, 2 * NSUB, 12, 65), BF16)
            V_r = S.regions(2 * NSUB, "V")
            QT = sb("QT", (128, 8, T), BF16)
            QT_r = S.regions(8, "QT")
            QA = sb("QA", (128, 6, T), BF16)
            QB = sb("QB", (128, 6, T), BF16)
            QAB_r = S.regions(6, "QAB")
            mkT = sb("mkT", (128, 2, MEMT), BF16)
            mvx = sb("mvx", (128, 2, 4, 65), BF16)
            mem_r = S.region("memkv")
            ssb = sb("ssb", (128, 2, 512), F32)
            ssb_r = S.regions(2, "ssb")
            pTs = sb("pTs", (128, 2, 4, 128), BF16)
            pTs_r = S.regions(2, "pTs")
            pmT = sb("pmT", (128, 8, T), BF16)
            pmT_r = S.region("pmT")
            mix = sb("mix", (128, 2, D), BF16)
            mix_r = S.regions(2, "mix")
            mixT = sb("mixT", (128, 2, KC, 128), BF16)
            mixT_r = S.regions(2, "mixT")
            rr = sb("rr", (128, 4, 4), F32)
            rr_r = S.regions(4, "rr")
            pb = [ps(f"pb{i}", (128, 512), F32) for i in range(7)]
            pb_r = S.regions(7, "pb")

            self.mem_prologue(S, C, 1, QT, QT_r, hT, hT_r, mkT, mvx, mem_r, pb[0], pb_r[0])
            self.load_w(S, wkv, wkv_r[0], self.w_kv, 0, 768)
            self.load_w(S, wkv, wkv_r[1], self.w_kv, 768, 1536)
            self.load_w(S, wbin, wbin_r[0], self.b_w_in, 0, D)
            self.load_w(S, wbout, wbout_r[0], self.b_w_out, 0, D)
            for kt in range(5):
                self.ld(S, SP, biasT[:, kt, :, :], self.relbias[:, kt, :, :], [bias_r], bias_r)
            self.ms(S, POOL, biasT[0:64, 0, :, 64:128], NEG, [bias_r])
            self.ms(S, POOL, biasT[64:128, 4, :, 0:64], NEG, [bias_r])
            self.ms(S, POOL, Vr[:, :, :, 64:65], 1.0, V_r)
            self.ms(S, POOL, QA[64:128, :, :], 0.0, QAB_r)
            self.ms(S, POOL, QB[0:64, :, :], 0.0, QAB_r)
            for gi in range(XSLOTS):
                self.load_x(S, C, src, gi)
            loaded = XSLOTS
            ia = 0
            isb = 0
            io = 0
            ipt = 0
            STOP = int(os.environ.get("K_STOP", 99))
            for ti in range(NT if STOP > 0 else 0):
                slot = ti % 2
                for s in range(NSUB):
                    gi = ti * NSUB + s
                    sl = gi % XSLOTS
                    self.norm_T(S, C, C["xt"][:, sl, :], C["xt_r"][sl],
                                [(gT_mix, hT[:, :, s * 128:(s + 1) * 128], hT_r[s]),
                                 (gT_kv, hTk[:, :, s * 128:(s + 1) * 128], hTk_r[s])])
                if STOP <= 1:
                    continue
                for j in range(6):
                    b, br = pb[ia % 2], pb_r[ia % 2]
                    ia += 1
                    for kc in range(KC):
                        self.mm(S, b[:, :], wkv[:, kc, j * 128:(j + 1) * 128], hTk[:, kc, :], kc == 0, kc == KC - 1,
                                hTk_r + [wkv_r[0]], [br])
                    self.cp(S, ACT, KTr[:, slot, j, :], b[:, :], [br], [KT_r[slot][j]])
                for s in range(NSUB):
                    vi = slot * NSUB + s
                    for (c0, c1, h0, h1) in ((768, 1280, 0, 8), (1280, 1536, 8, 12)):
                        b, br = pb[ia % 2], pb_r[ia % 2]
                        ia += 1
                        n = c1 - c0
                        for kc in range(KC):
                            self.mm(S, b[:, 0:n], hTk[:, kc, s * 128:(s + 1) * 128], wkv[:, kc, c0:c1], kc == 0,
                                    kc == KC - 1, [hTk_r[s], wkv_r[1]], [br])
                        self.cp(S, DVE, Vr[:, vi, h0:h1, 0:64], b[:, 0:n].rearrange("p (h d) -> p h d", d=64),
                                [br], [V_r[vi]])
                for j in range(8):
                    b, br = pb[ia % 2], pb_r[ia % 2]
                    ia += 1
                    for kc in range(KC):
                        self.mm(S, b[:, :], wbin[:, kc, j * 128:(j + 1) * 128], hT[:, kc, :], kc == 0, kc == KC - 1,
                                hT_r + wbin_r, [br])
                    if j < 6:
                        self.cp(S, ACT, QA[0:64, j, :], b[0:64, :], [br], [QAB_r[j]])
                        self.cp(S, DVE, QB[64:128, j, :], b[64:128, :], [br], [QAB_r[j]])
                    else:
                        self.cp(S, ACT, QT[:, j, :], b[:, :], [br], [QT_r[j]])
                if STOP <= 2:
                    continue
                self.mem_scores(S, QT[:, 6:8, :], QT_r[6:8], mkT, mem_r, pmT, pmT_r, [pb[0], pb[1]], [pb_r[0], pb_r[1]])
                for s in range(NSUB if STOP > 3 else 0):
                    P = ti * NSUB + s
                    gi = P
                    sl = gi % XSLOTS
                    mx, mxr = mix[:, P % 2, :], mix_r[P % 2]
                    kts = [kt for kt in range(5) if P - 4 + kt >= 0]
                    steps = [(hg, kt) for hg in range(3 if STOP > 4 else 0) for kt in kts]
                    po_of = {}
                    for hg in range(3):
                        po_of[hg] = (pb[4 + io % 2], pb_r[4 + io % 2])
                        io += 1
                    slots = {}

                    def emit_S(step):
                        nonlocal isb
                        hg, kt = step
                        kp = P - 4 + kt
                        sk = (kp // NSUB) % 2
                        subk = kp % NSUB
                        bs, bsr = pb[2 + isb % 2], pb_r[2 + isb % 2]
                        sbuf, sbr = ssb[:, isb % 2, :], ssb_r[isb % 2]
                        pt, ptr = pTs[:, isb % 2, :, :], pTs_r[isb % 2]
                        isb += 1
                        for hl in range(4):
                            h = hg * 4 + hl
                            Qh = QA if h % 2 == 0 else QB
                            self.mm(S, bs[:, hl * 128:(hl + 1) * 128],
                                    KTr[:, sk, h // 2, subk * 128:(subk + 1) * 128],
                                    Qh[:, h // 2, s * 128:(s + 1) * 128], True, True,
                                    [KT_r[sk][h // 2], QAB_r[h // 2]], [bsr])
                        self.stt(S, sbuf, bs[:, :], 0.125,
                                 biasT[:, kt, hg * 4:(hg + 1) * 4, :].rearrange("p h q -> p (h q)"),
                                 ALU.mult, ALU.add, [bsr, bias_r], [sbr])
                        self.act(S, pt.rearrange("p h q -> p (h q)"), sbuf, AF.Exp, [sbr], [ptr])
                        slots[step] = (pt, ptr, sk, subk)

                    def emit_PV(step):
                        hg, kt = step
                        pt, ptr, sk, subk = slots.pop(step)
                        po, por = po_of[hg]
                        for hl in range(4):
                            h = hg * 4 + hl
                            self.mm(S, po[:, hl * 65:(hl + 1) * 65], pt[:, hl, :], Vr[:, sk * NSUB + subk, h, :],
                                    kt == kts[0] and hl == 0, kt == kts[-1], [ptr, V_r[sk * NSUB + subk]], [por],
                                    skip=True)
                        if kt == kts[-1]:
                            pv = po[:, 0:260].rearrange("p (h e) -> p h e", h=4)
                            rq, rqr = rr[:, hg, :], rr_r[hg]
                            S.op(DVE, (lambda pv=pv, rq=rq: (lambda e: e.reciprocal(out=rq, in_=pv[:, :, 64])))(), [por], [rqr])
                            self.tt(S, DVE, mx[:, hg * 256:(hg + 1) * 256].rearrange("p (h e) -> p h e", h=4),
                                    pv[:, :, 0:64], rq.unsqueeze(2).to_broadcast([128, 4, 64]), ALU.mult,
                                    [por, rqr], [mxr])

                    if steps:
                        emit_S(steps[0])
                    for i_, step in enumerate(steps):
                        if i_ + 1 < len(steps):
                            emit_S(steps[i_ + 1])
                        emit_PV(step)
                    if STOP > 5:
                        self.mem_pv(S, s, pmT, pmT_r, mvx, mem_r, pb[6], pb_r[6], rr[:, 3, :], rr_r[3], mx, mxr)
                    if STOP > 6:
                      self.out_proj(S, C, mx, mxr, mixT[:, P % 2, :, :], mixT_r[P % 2], wbout, wbout_r,
                                  C["xt"][:, sl, :], C["xt_r"][sl], [pb[0], pb[1]], [pb_r[0], pb_r[1]], dst, gi)
                    if loaded < NT * NSUB and loaded % XSLOTS == sl:
                        self.load_x(S, C, src, loaded)
                        loaded += 1
            self.stats["bmix"] = S.emit(self.G)


def host_consts():
    ident = np.eye(128, dtype=np.float32)
    s = np.arange(128)[:, None]
    t = np.arange(128)[None, :]
    negU = np.where(s <= t, -1.0, 0.0).astype(np.float32)
    maskc = np.where(s <= t, np.float32(DH ** -0.5), np.float32(0.0)).astype(np.float32)
    return {"c_ident": ident, "c_negU": negU, "c_maskc": maskc}


def host_layout(inp):
    f = lambda a: np.ascontiguousarray(np.asarray(a, dtype=np.float32))
    g = np.stack([f(inp["norm_mix_g"])[0], f(inp["norm_ffn_g"])[0], f(inp["kv_norm_g"]),
                  f(inp["norm_mix_g"])[1], f(inp["norm_ffn_g"])[1]], 0)
    gT_all = np.ascontiguousarray(g.reshape(5, KC, 128).transpose(2, 0, 1))
    cwA = np.ascontiguousarray(f(inp["a_conv_w"])[0].reshape(4, 16, 96).transpose(2, 1, 0))
    cbA = np.ascontiguousarray(f(inp["a_conv_b"])[0].reshape(16, 96).T)
    cwF = np.ascontiguousarray(f(inp["ffn_conv_w"]).reshape(2, 3, NFT, 128).transpose(3, 0, 2, 1))
    cbF = np.ascontiguousarray(f(inp["ffn_conv_b"]).reshape(2, NFT, 128).transpose(2, 0, 1))
    rel = np.arange(768) - 127
    idx = np.clip(rel, -63, 128) + 63
    relext = f(inp["b_rel_bias"])[0][:, idx]
    kj = np.arange(128)[:, None, None]
    kt = np.arange(5)[None, :, None]
    qi = np.arange(128)[None, None, :]
    gidx = qi - kj + (4 - kt) * 128 + 127
    relbias = np.ascontiguousarray(relext[:, gidx].transpose(1, 2, 0, 3))
    shared = {
        "a_w_in": f(inp["a_w_in"])[0], "a_w_out": f(inp["a_w_out"])[0], "w_kv": f(inp["w_kv"]),
        "b_w_in": f(inp["b_w_in"])[0], "b_w_out": f(inp["b_w_out"])[0], "mem_w_kv": f(inp["mem_w_kv"]),
        "ffn_w_up": f(inp["ffn_w_up"]), "ffn_w_down": f(inp["ffn_w_down"]),
        "gT_all": gT_all, "final_g": f(inp["final_g"]).reshape(1, D), "gate_b": f(inp["a_gate_b"]).reshape(1, 8),
        "cwA": cwA, "cbA": cbA, "head_g": f(inp["a_head_g"]).reshape(1, AW), "cwF": cwF, "cbF": cbF,
        "relbias": relbias,
    }
    shared.update(host_consts())
    return shared


_CACHE = {}


def run(inputs, phases=("A_mix", "A_ffn", "B_mix", "B_ffn"), final_norm=True, ncores=8, trace=False):
    key = (tuple(phases), final_norm)
    if key not in _CACHE:
        _CACHE[key] = Prog(phases, final_norm).build()
    nc = _CACHE[key]
    shared = host_layout(inputs)
    x = np.asarray(inputs["x"], dtype=np.float32)
    mem = np.asarray(inputs["mem"], dtype=np.float32)
    in_maps = []
    for c in range(ncores):
        m = dict(shared)
        m["x"] = np.ascontiguousarray(x[c])
        m["mem"] = np.ascontiguousarray(mem[c])
        in_maps.append(m)
    res = run_bass_kernel_spmd(nc, in_maps, core_ids=list(range(ncores)), trace=trace)
    out = np.stack([np.asarray(r["out"]) for r in res.results], 0)
    return out, res


def kernel(**inputs):
    out, _ = run(inputs)
    return out.astype(np.float32)
```

```python
import numpy as np
from contextlib import ExitStack
import concourse.bass as bass
import concourse.mybir as mybir
from concourse.bass_types import AP
from concourse.bass_utils import run_bass_kernel_spmd

F32 = mybir.dt.float32
BF16 = mybir.dt.bfloat16
ALU = mybir.AluOpType
AF = mybir.ActivationFunctionType
AX = mybir.AxisListType

PE, ACT, DVE, POOL, SP = "tensor", "scalar", "vector", "gpsimd", "sync"
ENGS = (PE, ACT, DVE, POOL, SP)

D = 1024
KC = 8
SEQ = 4096
T = 512
import os
NT = int(os.environ.get("K_NT", SEQ // T))
SUB = 128
NSUB = T // SUB
DFF = 2816
NFT = DFF // 128
A_IN = 3336
AW = 768
DH = 192
MEMT = 256
EPS = 1e-6
NEG = -30000.0
XSLOTS = 6


class Region:
    __slots__ = ("name", "writer", "readers", "strict")

    def __init__(self, name):
        self.name = name
        self.writer = None
        self.readers = []
        self.strict = False


class _Op:
    __slots__ = ("eng", "idx", "fn", "waits", "needs_inc", "dma_key", "snap")

    def __init__(self, eng, idx, fn):
        self.eng = eng
        self.idx = idx
        self.fn = fn
        self.waits = []
        self.needs_inc = False
        self.dma_key = None
        self.snap = None


class Sched:
    def __init__(self, nc):
        self.nc = nc
        self.ops = {e: [] for e in ENGS}
        self.clock = {e: {x: -1 for x in ENGS} for e in ENGS}
        self.dclock = {e: {} for e in ENGS}
        self.dma_count = {}
        self.all_regions = []

    def region(self, name=None):
        r = Region(name or f"r{len(self.all_regions)}")
        self.all_regions.append(r)
        return r

    def regions(self, n, name="r"):
        return [self.region(f"{name}{i}") for i in range(n)]

    def _add(self, eng, fn, reads, writes, dma_key=None):
        o = _Op(eng, len(self.ops[eng]), fn)
        o.dma_key = dma_key
        deps = []
        for r in reads:
            if r.writer is not None:
                deps.append((r.writer, True))
        for w in writes:
            if w.writer is not None:
                deps.append((w.writer, w.strict))
            for rd in w.readers:
                deps.append((rd, w.strict))
        clk = self.clock[eng]
        dclk = self.dclock[eng]
        for tok, is_raw in deps:
            if tok[0] == "c":
                _, e2, n = tok
                if e2 == eng and (not is_raw or eng == PE):
                    continue
                if clk[e2] >= n:
                    continue
                o.waits.append(tok)
                self.ops[e2][n].needs_inc = True
                clk[e2] = n
                sn = self.ops[e2][n].snap
                for k, v in sn.items():
                    if k != eng and clk[k] < v:
                        clk[k] = v
            else:
                _, key, val = tok
                if dclk.get(key, 0) >= val:
                    continue
                cur = self.dma_count[key]
                o.waits.append(("d", key, cur))
                dclk[key] = cur
        o.snap = dict(clk)
        self.ops[eng].append(o)
        if dma_key is not None:
            self.dma_count[dma_key] = self.dma_count.get(dma_key, 0) + 16
            tok = ("d", dma_key, self.dma_count[dma_key])
        else:
            tok = ("c", eng, o.idx)
        for r in reads:
            r.readers.append(tok)
        for w in writes:
            w.writer = tok
            w.readers = []
        return o

    def op(self, eng, fn, reads=(), writes=()):
        return self._add(eng, fn, reads, writes)

    def dma(self, eng, fn, reads=(), writes=(), key=None):
        return self._add(eng, fn, reads, writes, dma_key=key.name + "@" + eng)

    def emit(self, G):
        nc = self.nc
        self._add(SP, None, list(self.all_regions), list(self.all_regions))
        keys = list(self.dma_count)
        slot = {}
        nsw = nhw = 0
        for k in keys:
            if k.endswith("@" + POOL):
                slot[k] = nsw
                nsw += 1
            else:
                slot[k] = G.NSW + nhw
                nhw += 1
        assert nsw <= G.NSW and nhw <= G.NDMA - G.NSW, (nsw, nhw)
        dsem = {k: G.dsem[slot[k]] for k in keys}
        dbase = {k: G.dbase[slot[k]] for k in keys}
        esem, ebase = G.esem, dict(G.ebase)
        G.phase += 1
        barv = G.phase
        cnt = {}
        for e in ENGS:
            c = 0
            arr = []
            for o in self.ops[e]:
                if o.needs_inc:
                    c += 1
                arr.append(c)
            cnt[e] = arr
            G.ebase[e] += c
        for k in keys:
            G.dbase[slot[k]] += self.dma_count[k]
        with nc.Block() as block:
            def make(e):
                def body(engh):
                    for o in self.ops[e]:
                        for w in o.waits:
                            if w[0] == "c":
                                engh.wait_ge(esem[w[1]], ebase[w[1]] + cnt[w[1]][w[2]])
                            else:
                                engh.wait_ge(dsem[w[1]], dbase[w[1]] + w[2])
                        if o.fn is None:
                            continue
                        ins = o.fn(engh)
                        if o.dma_key is not None:
                            ins.then_inc(dsem[o.dma_key], 16)
                        elif o.needs_inc:
                            ins.then_inc(esem[e], 1)
                    if e == SP:
                        engh.sem_inc(G.bar, 1)
                    else:
                        engh.wait_ge(G.bar, barv)
                return body

            for e in ENGS:
                getattr(block, e)(make(e))
        return {e: len(self.ops[e]) for e in ENGS}


class SemPool:
    NDMA = 56
    NSW = 16

    def __init__(self, nc, st):
        self.esem = {e: st.enter_context(nc.semaphore(f"s_{e}")) for e in ENGS}
        self.dsem = [st.enter_context(nc.semaphore(f"d_{i}")) for i in range(self.NDMA)]
        self.bar = st.enter_context(nc.semaphore("bar"))
        self.ebase = {e: 0 for e in ENGS}
        self.dbase = [0] * self.NDMA
        self.phase = 0
        allsem = list(self.esem.values()) + self.dsem + [self.bar]
        with nc.Block() as block:
            @block.gpsimd
            def _(g):
                for s in allsem:
                    g.sem_clear(s)
        nc.all_engine_barrier()


class Prog:
    def __init__(self, phases, final_norm=True):
        self.nc = nc = bass.Bass("TRN2", target_bir_lowering=False)
        self.phases = phases
        self.final_norm = final_norm
        din = lambda n, s: nc.dram_tensor(n, list(s), F32, kind="ExternalInput").ap()
        self.x = din("x", (SEQ, D))
        self.mem = din("mem", (MEMT, D))
        self.a_w_in = din("a_w_in", (D, A_IN))
        self.a_w_out = din("a_w_out", (D, D))
        self.w_kv = din("w_kv", (D, 1536))
        self.b_w_in = din("b_w_in", (D, D))
        self.b_w_out = din("b_w_out", (D, D))
        self.mem_w_kv = din("mem_w_kv", (2, D, 512))
        self.ffn_w_up = din("ffn_w_up", (2, D, 2 * DFF))
        self.ffn_w_down = din("ffn_w_down", (2, DFF, D))
        self.gT_all = din("gT_all", (128, 5, KC))
        self.final_g = din("final_g", (1, D))
        self.gate_b = din("gate_b", (1, 8))
        self.cwA = din("cwA", (96, 16, 4))
        self.cbA = din("cbA", (96, 16))
        self.head_g = din("head_g", (1, AW))
        self.cwF = din("cwF", (128, 2, NFT, 3))
        self.cbF = din("cbF", (128, 2, NFT))
        self.relbias = din("relbias", (128, 5, 12, 128))
        self.c_ident = din("c_ident", (128, 128))
        self.c_negU = din("c_negU", (128, 128))
        self.c_maskc = din("c_maskc", (128, 128))
        self.xa = nc.dram_tensor("xa", [SEQ, D], F32, kind="Internal").ap()
        self.xb = nc.dram_tensor("xb", [SEQ, D], F32, kind="Internal").ap()
        self.out = nc.dram_tensor("out", [SEQ, D], F32, kind="ExternalOutput").ap()
        self.stats = {}

    def mm(self, S, out, lhsT, rhs, start, stop, reads, writes, skip=False):
        S.op(PE, lambda e: e.matmul(out, lhsT=lhsT, rhs=rhs, start=start, stop=stop,
                                    skip_group_check=skip), reads, writes)

    def tr(self, S, out, in_, ident, reads, writes):
        S.op(PE, lambda e: e.transpose(out=out, in_=in_, identity=ident), reads, writes)

    def act(self, S, out, in_, func, reads, writes, **kw):
        S.op(ACT, lambda e: e.activation(out=out, in_=in_, func=func, **kw), reads, writes)

    def tt(self, S, eng, out, in0, in1, op, reads, writes):
        S.op(eng, lambda e: e.tensor_tensor(out=out, in0=in0, in1=in1, op=op), reads, writes)

    def ts(self, S, eng, out, in0, s1, s2, op0, op1, reads, writes):
        if op1 is None:
            S.op(eng, lambda e: e.tensor_scalar(out=out, in0=in0, scalar1=s1, scalar2=None, op0=op0),
                 reads, writes)
        else:
            S.op(eng, lambda e: e.tensor_scalar(out=out, in0=in0, scalar1=s1, scalar2=s2, op0=op0, op1=op1),
                 reads, writes)

    def stt(self, S, out, in0, scalar, in1, op0, op1, reads, writes):
        S.op(DVE, lambda e: e.scalar_tensor_tensor(out=out, in0=in0, scalar=scalar, in1=in1, op0=op0, op1=op1),
             reads, writes)

    def cp(self, S, eng, out, in_, reads, writes):
        if eng == ACT:
            S.op(ACT, lambda e: e.copy(out=out, in_=in_), reads, writes)
        else:
            S.op(eng, lambda e: e.tensor_copy(out=out, in_=in_), reads, writes)

    def ms(self, S, eng, ap, val, writes):
        S.op(eng, lambda e: e.memset(ap, val), (), writes)

    def ld(self, S, eng, out, in_, writes, key, reads=(), **kw):
        S.dma(eng, lambda e: e.dma_start(out=out, in_=in_, **kw), reads, writes, key=key)

    def load_w(self, S, dst, reg, src2d, c0, c1, kc0=0, kc1=None):
        kcn = src2d.shape[0] // 128
        kc1 = kcn if kc1 is None else kc1
        src = src2d.rearrange("(kc p) n -> p kc n", p=128)
        step = 2048
        for a in range(c0, c1, step):
            b = min(c1, a + step)
            self.ld(S, POOL, dst[:, kc0:kc1, a:b], src[:, kc0:kc1, a:b], [reg], reg)

    def norm_stats(self, S, C, xt_ap, xr):
        i = C["nrm_i"]
        C["nrm_i"] += 1
        k = i % 2
        ss = C["ss"][:, k, 0:1]
        ms_ = C["ss"][:, k, 1:2]
        rstd = C["ss"][:, k, 2:3]
        ssr = C["ss_r"][k]
        xh = C["xh"][:, k, :]
        xhr = C["xh_r"][k]
        self.act(S, xh, xt_ap, AF.Square, [xr], [xhr, ssr], accum_out=ss)
        self.ts(S, DVE, ms_, ss, 1.0 / D, EPS, ALU.mult, ALU.add, [ssr], [ssr])
        self.tt(S, POOL, rstd, ms_, C["neghalf"][:, 0:1], ALU.pow, [ssr, C["const_r"]], [ssr])
        self.act(S, xh, xt_ap, AF.Copy, [xr, ssr], [xhr], scale=rstd)
        return (xh, xhr)

    def norm_tr(self, S, C, hnd, outs):
        xh, xhr = hnd
        pT = C["pT"]
        for kc in range(KC):
            self.tr(S, pT[:, kc * 128:(kc + 1) * 128], xh[:, kc * 128:(kc + 1) * 128], C["identb"][:, :],
                    [xhr, C["const_r"]], [C["pT_r"]])
        pT3 = pT[:, :].rearrange("p (a b) -> p a b", a=KC)
        for gT, dst, dr in outs:
            self.tt(S, DVE, dst, pT3, gT.unsqueeze(2).to_broadcast([128, KC, 128]), ALU.mult,
                    [C["pT_r"], C["const_r"]], [dr])

    def norm_T(self, S, C, xt_ap, xr, outs):
        self.norm_tr(S, C, self.norm_stats(S, C, xt_ap, xr), outs)

    def phase_consts(self, S, st, which_gains):
        nc = self.nc
        C = {"nrm_i": 0}
        self.phase_i = getattr(self, "phase_i", 0) + 1
        pfx = f"p{self.phase_i}_"
        sb = lambda n, s, d: st.enter_context(nc.sbuf_tensor(pfx + n, list(s), d))
        ps = lambda n, s, d: st.enter_context(nc.psum_tensor(pfx + n, list(s), d))
        C["sb"] = sb
        C["ps"] = ps
        C["const_r"] = cr = S.region("const")
        C["identb"] = sb("identb", (128, 128), BF16)
        C["neghalf"] = sb("neghalf", (128, 1), F32)
        C["gT"] = sb("gT", (128, 5, KC), F32)
        self.ld(S, POOL, C["identb"][:, :], self.c_ident[:, :], [cr], cr)
        self.ld(S, SP, C["gT"][:, :, :], self.gT_all[:, :, :], [cr], cr)
        self.ms(S, DVE, C["neghalf"][:, :], -0.5, [cr])
        C["ss"] = sb("ss", (128, 2, 4), F32)
        C["ss_r"] = S.regions(2, "ss")
        C["xh"] = sb("xh", (128, 2, D), BF16)
        C["xh_r"] = S.regions(2, "xh")
        C["pT"] = ps("pT", (128, D), BF16)
        C["pT_r"] = S.region("pT")
        C["xt"] = sb("xt", (128, XSLOTS, D), F32)
        C["xt_r"] = S.regions(XSLOTS, "xt")
        return C

    def load_x(self, S, C, src, gi):
        sl = gi % XSLOTS
        self.ld(S, SP, C["xt"][:, sl, :], src[gi * 128:(gi + 1) * 128, :], [C["xt_r"][sl]], C["xt_r"][sl])

    def phase_ffn(self, l, src, dst, final):
        nc = self.nc
        S = Sched(nc)
        with ExitStack() as st:
            C = self.phase_consts(S, st, None)
            sb, ps = C["sb"], C["ps"]
            gT = C["gT"][:, 1 if l == 0 else 4, :]
            wup = sb("wup", (128, KC, 2 * DFF), BF16)
            wup_r = S.regions(4, "wup")
            wdn = sb("wdn", (128, NFT, D), BF16)
            wdn_r = S.regions(2, "wdn")
            hT = sb("hT", (128, KC, T), BF16)
            hT_r = S.regions(NSUB, "hT")
            aT = sb("aT", (128, NFT, T), BF16)
            aT_r = S.regions(NFT, "aT")
            graw = sb("graw", (128, 2, T + 2), F32)
            graw_r = S.regions(2, "graw")
            acc = sb("acc", (128, 2, T), F32)
            acc_r = S.regions(2, "acc")
            tnh = sb("tnh", (128, 2, T), F32)
            tnh_r = S.regions(2, "tnh")
            halo = sb("halo", (128, NFT, 2), F32)
            halo_r = S.regions(NFT, "halo")
            cw = sb("cw", (128, NFT, 3), F32)
            cb = sb("cb", (128, NFT), F32)
            cr = C["const_r"]
            pb = [ps(f"pb{i}", (128, 512), F32) for i in range(7)]
            pb_r = S.regions(7, "pb")
            if final:
                fg = sb("fg", (128, D), F32)
                self.ld(S, SP, fg[:, :], self.final_g.partition_broadcast(128), [cr], cr)
                fss = sb("fss", (128, 2, 4), F32)
                fss_r = S.regions(2, "fss")
                for r_ in C["xh_r"]:
                    r_.strict = True
            for gi in range(min(XSLOTS, NSUB + 2)):
                self.load_x(S, C, src, gi)
            loaded = min(XSLOTS, NSUB + 2)
            self.ld(S, SP, cw[:, :, :], self.cwF[:, l, :, :], [cr], cr)
            self.ld(S, SP, cb[:, :], self.cbF[:, l, :], [cr], cr)
            self.ts(S, POOL, cw[:, :, :], cw[:, :, :], 0.5, None, ALU.mult, None, [cr], [cr])
            self.ts(S, POOL, cb[:, :], cb[:, :], 0.5, None, ALU.mult, None, [cr], [cr])
            self.ms(S, POOL, halo[:, :, :], 0.0, halo_r)
            wu = self.ffn_w_up[l]
            for c in range(4):
                self.load_w(S, wup, wup_r[c], wu, c * 1408, (c + 1) * 1408)
            wd = self.ffn_w_down[l]
            self.load_w(S, wdn, wdn_r[0], wd, 0, D, 0, 11)
            self.load_w(S, wdn, wdn_r[1], wd, 0, D, 11, 22)

            pbi = 0

            def do_stats(ti, s):
                gi = ti * NSUB + s
                sl = gi % XSLOTS
                return self.norm_stats(S, C, C["xt"][:, sl, :], C["xt_r"][sl])

            def do_tr(hnd, s):
                self.norm_tr(S, C, hnd, [(gT, hT[:, :, s * 128:(s + 1) * 128], hT_r[s])])

            for s in range(NSUB):
                do_tr(do_stats(0, s), s)
            for ti in range(NT):
                for j in range(NFT):
                    pu, pur = pb[pbi % 6], pb_r[pbi % 6]
                    pg, pgr = pb[(pbi + 1) % 6], pb_r[(pbi + 1) % 6]
                    pbi += 2
                    for kc in range(KC):
                        self.mm(S, pg[:, :], wup[:, kc, DFF + j * 128:DFF + (j + 1) * 128], hT[:, kc, :],
                                kc == 0, kc == KC - 1, hT_r + [wup_r[2 + j // 11]], [pgr])
                    for kc in range(KC):
                        self.mm(S, pu[:, :], wup[:, kc, j * 128:(j + 1) * 128], hT[:, kc, :],
                                kc == 0, kc == KC - 1, hT_r + [wup_r[j // 11]], [pur])
                    r = j % 2
                    gr, grr = graw[:, r, :], graw_r[r]
                    ac, acr = acc[:, r, :], acc_r[r]
                    tn, tnr = tnh[:, r, :], tnh_r[r]
                    self.cp(S, POOL, gr[:, 0:2], halo[:, j, :], [halo_r[j]], [grr])
                    self.cp(S, ACT, gr[:, 2:T + 2], pg[:, :], [pgr], [grr])
                    self.cp(S, POOL, halo[:, j, :], gr[:, T:T + 2], [grr], [halo_r[j]])
                    self.act(S, ac, pg[:, :], AF.Identity, [pgr, cr], [acr], scale=cw[:, j, 2:3], bias=cb[:, j:j + 1])
                    self.stt(S, ac, gr[:, 1:T + 1], cw[:, j, 1:2], ac, ALU.mult, ALU.add, [grr, cr, acr], [acr])
                    self.stt(S, ac, gr[:, 0:T], cw[:, j, 0:1], ac, ALU.mult, ALU.add, [grr, cr, acr], [acr])
                    self.act(S, tn, ac, AF.Tanh, [acr], [tnr])
                    self.tt(S, DVE, gr[:, 2:T + 2], ac, pu[:, :], ALU.mult, [acr, pur], [grr])
                    self.stt(S, aT[:, j, :], tn, 1.0, gr[:, 2:T + 2], ALU.add, ALU.mult, [tnr, grr], [aT_r[j]])
                hnds = {}
                for s in range(NSUB):
                    gi = ti * NSUB + s
                    sl = gi % XSLOTS
                    xt = C["xt"][:, sl, :]
                    xr = C["xt_r"][sl]
                    for hf in range(2):
                        po, por = pb[6], pb_r[6]
                        if hf == 1:
                            po, por = pb[pbi % 6], pb_r[pbi % 6]
                            pbi += 1
                        for kc in range(NFT):
                            self.mm(S, po[:, :], aT[:, kc, s * 128:(s + 1) * 128], wdn[:, kc, hf * 512:(hf + 1) * 512],
                                    kc == 0, kc == NFT - 1, [aT_r[kc], wdn_r[kc // 11]], [por])
                        self.tt(S, DVE, xt[:, hf * 512:(hf + 1) * 512], po[:, :], xt[:, hf * 512:(hf + 1) * 512],
                                ALU.add, [por, xr], [xr])
                    if final:
                        k = gi % 2
                        ss = fss[:, k, 0:1]
                        ms_ = fss[:, k, 1:2]
                        rstd = fss[:, k, 2:3]
                        self.act(S, C["xh"][:, k, :], xt, AF.Square, [xr], [C["xh_r"][k], fss_r[k]], accum_out=ss)
                        self.ts(S, DVE, ms_, ss, 1.0 / D, EPS, ALU.mult, ALU.add, [fss_r[k]], [fss_r[k]])
                        self.tt(S, POOL, rstd, ms_, C["neghalf"][:, 0:1], ALU.pow, [fss_r[k], cr], [fss_r[k]])
                        self.stt(S, xt, xt, rstd, fg[:, :], ALU.mult, ALU.mult, [xr, fss_r[k], cr], [xr])
                    self.ld(S, SP, dst[gi * 128:(gi + 1) * 128, :], xt, [], xr, reads=[xr])
                    if loaded < NT * NSUB and loaded % XSLOTS == sl:
                        self.load_x(S, C, src, loaded)
                        loaded += 1
                    if ti + 1 < NT:
                        hnds[s] = do_stats(ti + 1, s)
                        if s >= 1:
                            do_tr(hnds.pop(s - 1), s - 1)
                if ti + 1 < NT:
                    do_tr(hnds.pop(NSUB - 1), NSUB - 1)
            self.stats[f"ffn{l}"] = S.emit(self.G)

    def build(self):
        chain = {"A_mix": self.phase_amix, "A_ffn": lambda s, d, f: self.phase_ffn(0, s, d, False),
                 "B_mix": self.phase_bmix, "B_ffn": lambda s, d, f: self.phase_ffn(1, s, d, f)}
        src = self.x
        scr = [self.xa, self.xb]
        with ExitStack() as gst:
            self.G = SemPool(self.nc, gst)
            for i, ph in enumerate(self.phases):
                last = i == len(self.phases) - 1
                dst = self.out if last else scr[i % 2]
                chain[ph](src, dst, last and self.final_norm)
                src = dst
        return self.nc

    def phase_amix(self, src, dst, final):
        nc = self.nc
        S = Sched(nc)
        with ExitStack() as st:
            C = self.phase_consts(S, st, None)
            sb, ps = C["sb"], C["ps"]
            cr = C["const_r"]
            gT_mix = C["gT"][:, 0, :]
            wain = sb("wain", (128, KC, A_IN), BF16)
            wain_r = S.regions(4, "wain")
            waout = sb("waout", (128, KC, D), BF16)
            waout_r = S.regions(1, "waout")
            hT = sb("hT", (128, KC, T), BF16)
            hT_r = S.regions(NSUB, "hT")
            qkT = sb("qkT", (128, 16, T), BF16)
            qk_r = S.regions(16, "qk")
            qmT = sb("qmT", (128, 2, T), BF16)
            qm_r = S.regions(2, "qm")
            raw = sb("raw", (128, 2, T + 3), F32)
            raw_r = S.regions(2, "raw")
            acc = sb("acc", (128, 2, T), F32)
            acc_r = S.regions(2, "acc")
            tnh = sb("tnh", (128, 2, T), F32)
            tnh_r = S.regions(2, "tnh")
            halo = sb("halo", (128, 16, 3), F32)
            halo_r = S.regions(16, "halo")
            cw = sb("cw", (128, 16, 4), F32)
            cb = sb("cb", (128, 16), F32)
            ktok = sb("ktok", (128, NSUB, AW), BF16)
            ktok_r = S.regions(NSUB, "ktok")
            vw = sb("vw", (128, NSUB, 4, DH + 1), BF16)
            vw_r = S.regions(NSUB, "vw")
            G2 = sb("G2", (128, NSUB, AW), F32)
            G2_r = S.regions(NSUB, "G2")
            hgh = sb("hgh", (128, AW), F32)
            gb_bc = sb("gb_bc", (128, 8), F32)
            gsb = sb("gsb", (128, NSUB, 8), F32)
            gw = sb("gw", (128, 16, 16), F32)
            g_r = S.region("gates")
            EP, SPL, AA, BBL, AMX, MALL, MST, T48 = 0, 1, 2, 3, 5, 6, 7, 8
            WGF = 11
            mcar = sb("mcar", (128, 4), F32)
            am16 = sb("am16", (16, 20), F32)
            identF = sb("identF", (128, 128), F32)
            negU = sb("negU", (128, 128), F32)
            negO = sb("negO", (128, 128), F32)
            ones16 = sb("ones16", (16, 128), F32)
            maskc = sb("maskc", (128, 4, 128), F32)
            Cn = sb("Cn", (128, 4, 2, DH + 1), F32)
            Cn_r = S.regions(4, "Cn")
            Gbf = sb("Gbf", (128, 2, 4, 2, DH + 1), BF16)
            Gbf_r = [S.regions(4, "GbfA"), S.regions(4, "GbfB")]
            sm = sb("sm", (128, 2, 4, 128), BF16)
            sm_r = S.regions(2, "sm")
            hm = sb("hm", (128, 2, 4, DH), F32)
            hm_r = S.regions(2, "hm")
            hj = sb("hj", (128, 4, DH), BF16)
            hj_r = S.regions(4, "hj")
            for r_ in hj_r:
                r_.strict = True
            hst = sb("hst", (128, 2, 16), F32)
            hst_r = S.regions(2, "hst")
            mkT = sb("mkT", (128, 2, MEMT), BF16)
            mvx = sb("mvx", (128, 2, 4, 65), BF16)
            mem_r = S.region("memkv")
            pmT = sb("pmT", (128, 8, T), BF16)
            pmT_r = S.region("pmT")
            mix = sb("mix", (128, 2, D), BF16)
            mix_r = S.regions(2, "mix")
            mixT = sb("mixT", (128, 2, KC, 128), BF16)
            mixT_r = S.regions(2, "mixT")
            rr = sb("rr", (128, 4), F32)
            rr_r = S.region("rr")
            pb = [ps(f"pb{i}", (128, 512), F32) for i in range(7)]
            pb_r = S.regions(7, "pb")

            self.mem_prologue(S, C, 0, qkT[:, 0:8, :], qk_r[0:8], hT, hT_r, mkT, mvx, mem_r, pb[0], pb_r[0])
            self.load_w(S, wain, wain_r[0], self.a_w_in, 0, 1536)
            self.load_w(S, wain, wain_r[1], self.a_w_in, 1536, 2304)
            self.load_w(S, wain, wain_r[2], self.a_w_in, 2304, 3080)
            self.load_w(S, wain, wain_r[3], self.a_w_in, 3080, A_IN)
            self.load_w(S, waout, waout_r[0], self.a_w_out, 0, D)
            self.ld(S, SP, cw[0:96, :, :], self.cwA[:, :, :], [cr], cr)
            self.ld(S, SP, cb[0:96, :], self.cbA[:, :], [cr], cr)
            self.ld(S, SP, hgh[:, :], self.head_g.partition_broadcast(128), [cr], cr)
            self.ld(S, SP, gb_bc[:, :], self.gate_b.partition_broadcast(128), [cr], cr)
            self.ld(S, SP, identF[:, :], self.c_ident[:, :], [cr], cr)
            self.ld(S, SP, negU[:, :], self.c_negU[:, :], [cr], cr)
            for h in range(4):
                self.ld(S, SP, maskc[:, h, :], self.c_maskc[:, :], [cr], cr)
            self.ts(S, POOL, cw[0:96, :, :], cw[0:96, :, :], 0.5, None, ALU.mult, None, [cr], [cr])
            self.ts(S, POOL, cb[0:96, :], cb[0:96, :], 0.5, None, ALU.mult, None, [cr], [cr])
            self.ts(S, POOL, hgh[:, :], hgh[:, :], 0.5, None, ALU.mult, None, [cr], [cr])
            self.ms(S, POOL, negO[:, :], -1.0, [cr])
            self.ms(S, POOL, ones16[:, :], 1.0, [cr])
            self.ms(S, POOL, halo[:, :, :], 0.0, halo_r)
            self.ms(S, POOL, Cn[:, :, :, :], 0.0, Cn_r)
            self.ms(S, POOL, mcar[:, :], 0.0, [g_r])
            self.ms(S, POOL, vw[:, :, :, DH:DH + 1], 1.0, vw_r)
            for gi in range(XSLOTS):
                self.load_x(S, C, src, gi)
            loaded = XSLOTS
            ia = 0
            WK = 14
            ga = lambda i, n=1: gw[:, i:i + n, :].rearrange("p a c -> p (a c)")
            g3 = lambda i: gw[:, i, :].rearrange("p (s h) -> p s h", s=NSUB)
            wv, gv, flv, wkv_ = g3(WGF), g3(WGF + 1), g3(WGF + 2), g3(WK)
            pg, pgr = pb[6], pb_r[6]

            def gate_stages():
                for s in range(NSUB):
                    for kc in range(KC):
                        self.mm(S, pg[:, s * 8:(s + 1) * 8], hT[:, kc, s * 128:(s + 1) * 128], wain[:, kc, 3072:3080],
                                kc == 0, kc == KC - 1, [hT_r[s], wain_r[2]], [pgr])
                self.tt(S, DVE, gsb[:, :, :], pg[:, 0:32].rearrange("p (s g) -> p s g", s=NSUB),
                        gb_bc[:, :].unsqueeze(1).to_broadcast([128, NSUB, 8]), ALU.add, [pgr, cr], [g_r])
                self.act(S, g3(EP), gsb[:, :, 4:8], AF.Exp, [g_r], [g_r], scale=-1.0)
                self.act(S, ga(SPL), ga(EP), AF.Ln, [g_r], [g_r], bias=1.0)
                yield
                self.mm(S, pg[:, 32:48], negU[:, :], ga(SPL), True, True, [g_r, cr], [pgr])
                self.mm(S, pg[:, 48:64], negO[:, :], ga(SPL), True, True, [g_r, cr], [pgr])
                self.cp(S, DVE, ga(BBL, 2), pg[:, 32:64], [pgr], [g_r])
                self.tt(S, DVE, g3(AA), gsb[:, :, 0:4], g3(BBL), ALU.subtract, [g_r], [g_r])
                yield
                self.tr(S, pg[0:16, 64:192], ga(AA), identF[:, :], [g_r, cr], [pgr])
                S.op(DVE, lambda e: e.reduce_max(out=am16[:, 0:1], in_=pg[0:16, 64:192], axis=AX.X), [pgr], [g_r])
                self.ts(S, DVE, am16[:, 4:20], identF[0:16, 0:16], am16[:, 0:1], None, ALU.mult, None, [g_r, cr], [g_r])
                yield
                self.mm(S, pg[:, 192:208], ones16[:, :], am16[:, 4:20], True, True, [g_r, cr], [pgr])
                self.cp(S, DVE, ga(AMX), pg[:, 192:208], [pgr], [g_r])
                yield
                for s in range(NSUB):
                    self.cp(S, DVE, g3(MST)[:, s, :], mcar[:, :], [g_r], [g_r])
                    self.tt(S, DVE, g3(MALL)[:, s, :], mcar[:, :], g3(AMX)[:, s, :], ALU.max, [g_r], [g_r])
                    self.tt(S, DVE, mcar[:, :], g3(BBL + 1)[:, s, :], g3(MALL)[:, s, :], ALU.add, [g_r], [g_r])
                self.tt(S, DVE, ga(T48), ga(AA), ga(MALL), ALU.subtract, [g_r], [g_r])
                self.tt(S, DVE, ga(T48 + 1), ga(MST), ga(MALL), ALU.subtract, [g_r], [g_r])
                self.stt(S, ga(T48 + 2), ga(BBL), -1.0, ga(MALL), ALU.mult, ALU.subtract, [g_r], [g_r])
                self.act(S, ga(WGF, 3), ga(T48, 3), AF.Exp, [g_r], [g_r])
                self.ts(S, DVE, ga(WK), ga(WGF), float(DH ** -0.5), None, ALU.mult, None, [g_r], [g_r])
                yield

            def tail(s_, ti_):
                nonlocal loaded
                gi = ti_ * NSUB + s_
                sl = gi % XSLOTS
                k2 = gi % 2
                self.out_proj(S, C, mix[:, k2, :], mix_r[k2], mixT[:, k2, :, :], mixT_r[k2], waout, waout_r,
                              C["xt"][:, sl, :], C["xt_r"][sl], [pb[0], pb[1]], [pb_r[0], pb_r[1]], dst, gi)
                if loaded < NT * NSUB and loaded % XSLOTS == sl:
                    self.load_x(S, C, src, loaded)
                    loaded += 1

            for ti in range(NT):
                for s in range(NSUB):
                    gi = ti * NSUB + s
                    sl = gi % XSLOTS
                    self.norm_T(S, C, C["xt"][:, sl, :], C["xt_r"][sl],
                                [(gT_mix, hT[:, :, s * 128:(s + 1) * 128], hT_r[s])])
                gs = gate_stages()
                next(gs)
                pend = None
                for i in range(16):
                    b, br = pb[ia % 6], pb_r[ia % 6]
                    ia += 1
                    for kc in range(KC):
                        self.mm(S, b[0:96, :], wain[:, kc, i * 96:(i + 1) * 96], hT[:, kc, :], kc == 0, kc == KC - 1,
                                hT_r + [wain_r[0]], [br])
                    r = i % 2
                    rw, rwr = raw[0:96, r, :], raw_r[r]
                    ac, acr = acc[0:96, r, :], acc_r[r]
                    tn, tnr = tnh[0:96, r, :], tnh_r[r]
                    self.cp(S, POOL, rw[:, 0:3], halo[0:96, i, :], [halo_r[i]], [rwr])
                    self.cp(S, ACT, rw[:, 3:T + 3], b[0:96, :], [br], [rwr])
                    self.cp(S, POOL, halo[0:96, i, :], rw[:, T:T + 3], [rwr], [halo_r[i]])
                    self.act(S, ac, b[0:96, :], AF.Identity, [br, cr], [acr], scale=cw[0:96, i, 3:4], bias=cb[0:96, i:i + 1])
                    for j in (2, 1, 0):
                        self.stt(S, ac, rw[:, j:j + T], cw[0:96, i, j:j + 1], ac, ALU.mult, ALU.add,
                                 [rwr, cr, acr], [acr])
                    if pend is not None:
                        pend()

                    def fin(i=i, ac=ac, acr=acr, tn=tn, tnr=tnr):
                        self.act(S, tn, ac, AF.Tanh, [acr], [tnr])
                        self.stt(S, qkT[0:96, i, :], tn, 1.0, ac, ALU.add, ALU.mult, [tnr, acr], [qk_r[i]])
                    pend = fin
                    if i in (1, 3, 5, 7):
                        next(gs)
                pend()
                for s in range(NSUB):
                    for g in range(2):
                        b, br = pb[ia % 6], pb_r[ia % 6]
                        ia += 1
                        for kc in range(KC):
                            self.mm(S, b[:, 0:384], hT[:, kc, s * 128:(s + 1) * 128],
                                    wain[:, kc, 1536 + g * 384:1536 + (g + 1) * 384], kc == 0, kc == KC - 1,
                                    [hT_r[s], wain_r[1]], [br])
                        self.cp(S, ACT if g == 0 else DVE, vw[:, s, 2 * g:2 * g + 2, 0:DH],
                                b[:, 0:384].rearrange("p (h e) -> p h e", h=2), [br], [vw_r[s]])
                    for g in range(2):
                        b, br = pb[ia % 6], pb_r[ia % 6]
                        ia += 1
                        for kc in range(KC):
                            self.mm(S, b[:, 0:384], hT[:, kc, s * 128:(s + 1) * 128],
                                    wain[:, kc, 2304 + g * 384:2304 + (g + 1) * 384], kc == 0, kc == KC - 1,
                                    [hT_r[s], wain_r[2]], [br])
                        g2 = G2[:, s, g * 384:(g + 1) * 384]
                        self.act(S, g2, b[:, 0:384], AF.Tanh, [br], [G2_r[s]], scale=0.5)
                        self.stt(S, g2, g2, 1.0, hgh[:, g * 384:(g + 1) * 384], ALU.add, ALU.mult, [G2_r[s], cr], [G2_r[s]])
                for j in range(2):
                    b, br = pb[ia % 6], pb_r[ia % 6]
                    ia += 1
                    for kc in range(KC):
                        self.mm(S, b[:, :], wain[:, kc, 3080 + j * 128:3080 + (j + 1) * 128], hT[:, kc, :], kc == 0,
                                kc == KC - 1, hT_r + [wain_r[3]], [br])
                    self.cp(S, ACT, qmT[:, j, :], b[:, :], [br], [qm_r[j]])
                self.mem_scores(S, qmT, qm_r, mkT, mem_r, pmT, pmT_r, [pb[0], pb[1]], [pb_r[0], pb_r[1]])
                for s in range(NSUB):
                    pT = C["pT"]
                    for j in range(8):
                        self.tr(S, pT[:, j * 96:(j + 1) * 96], qkT[0:96, 8 + j, s * 128:(s + 1) * 128],
                                C["identb"][0:96, 0:96], [qk_r[8 + j], cr], [C["pT_r"]])
                    for h in range(4):
                        self.act(S, ktok[:, s, h * DH:(h + 1) * DH], pT[:, h * DH:(h + 1) * DH], AF.Copy,
                                 [C["pT_r"], g_r], [ktok_r[s]], scale=wkv_[:, s, h:h + 1])
                pN = [pb[3], pb[4]]
                pNr = [pb_r[3], pb_r[4]]

                def emit_gbf(s_):
                    gbuf = (ti * NSUB + s_) % 2
                    for h in range(4):
                        self.act(S, Gbf[0:96, gbuf, h, :, :].rearrange("p j e -> p (j e)"),
                                 Cn[0:96, h, :, :].rearrange("p j e -> p (j e)"), AF.Copy, [Cn_r[h], g_r], [Gbf_r[gbuf][h]],
                                 scale=gv[0:96, s_, h:h + 1])

                def stage_A(s):
                    gi = ti * NSUB + s
                    k2 = gi % 2
                    cs = slice(s * 128, (s + 1) * 128)
                    pS, pSr = pb[2], pb_r[2]
                    for h in range(4):
                        for j in range(2):
                            self.mm(S, pS[:, h * 128:(h + 1) * 128], qkT[0:96, 8 + 2 * h + j, cs], qkT[0:96, 2 * h + j, cs],
                                    j == 0, j == 1, [qk_r[8 + 2 * h + j], qk_r[2 * h + j]], [pSr])
                    smv, smr = sm[:, k2, :, :], sm_r[k2]
                    for h in range(4):
                        self.stt(S, smv[:, h, :], pS[:, h * 128:(h + 1) * 128], wv[:, s, h:h + 1], maskc[:, h, :],
                                 ALU.mult, ALU.mult, [pSr, cr, g_r], [smr])
                    if s == 0:
                        emit_gbf(0)
                    for h in range(4):
                        pC, pCr = pb[5 + h % 2], pb_r[5 + h % 2]
                        for j in range(2):
                            self.mm(S, pC[0:96, j * (DH + 1):(j + 1) * (DH + 1)], ktok[:, s, h * DH + j * 96:h * DH + (j + 1) * 96],
                                    vw[:, s, h, :], True, True, [ktok_r[s], vw_r[s]], [pCr])
                        self.stt(S, Cn[0:96, h, :, :].rearrange("p j e -> p (j e)"),
                                 Cn[0:96, h, :, :].rearrange("p j e -> p (j e)"), gv[0:96, s, h:h + 1],
                                 pC[0:96, 0:2 * (DH + 1)], ALU.mult, ALU.add, [Cn_r[h], g_r, pCr], [Cn_r[h]])
                    gbuf = gi % 2
                    for h in range(4):
                        o = pN[h // 2][:, (h % 2) * (DH + 1):(h % 2 + 1) * (DH + 1)]
                        self.mm(S, o, smv[:, h, :], vw[:, s, h, :], True, False, [smr, vw_r[s]], [pNr[h // 2]])
                        for j in range(2):
                            self.mm(S, o, qkT[0:96, 2 * h + j, cs], Gbf[0:96, gbuf, h, j, :], False, j == 1,
                                    [qk_r[2 * h + j], Gbf_r[gbuf][h]], [pNr[h // 2]])
                    if s + 1 < NSUB:
                        emit_gbf(s + 1)

                def stage_B1(s):
                    gi = ti * NSUB + s
                    k2 = gi % 2
                    hs, hsr = hst[:, k2, :], hst_r[k2]
                    hmv, hmr = hm[:, k2, :, :], hm_r[k2]
                    for hp2 in range(2):
                        pv = pN[hp2][:, 0:2 * (DH + 1)].rearrange("p (h e) -> p h e", h=2)
                        self.act(S, hs[:, 2 * hp2:2 * hp2 + 2], pv[:, :, DH], AF.Abs, [pNr[hp2]], [hsr])
                    self.tt(S, DVE, hs[:, 0:4], hs[:, 0:4], flv[:, s, :], ALU.max, [hsr, g_r], [hsr])
                    S.op(DVE, (lambda hs=hs: (lambda e: e.reciprocal(out=hs[:, 4:8], in_=hs[:, 0:4])))(), [hsr], [hsr])
                    for hp2 in range(2):
                        pv = pN[hp2][:, 0:2 * (DH + 1)].rearrange("p (h e) -> p h e", h=2)
                        self.tt(S, DVE, hmv[:, 2 * hp2:2 * hp2 + 2, :], pv[:, :, 0:DH],
                                hs[:, 4 + 2 * hp2:6 + 2 * hp2].unsqueeze(2).to_broadcast([128, 2, DH]), ALU.mult,
                                [pNr[hp2], hsr], [hmr])

                def stage_B2(s):
                    gi = ti * NSUB + s
                    k2 = gi % 2
                    mx, mxr = mix[:, k2, :], mix_r[k2]
                    hs, hsr = hst[:, k2, :], hst_r[k2]
                    hmv, hmr = hm[:, k2, :, :], hm_r[k2]
                    for h in range(4):
                        self.act(S, hj[:, h, :], hmv[:, h, :], AF.Square, [hmr], [hj_r[h], hsr], accum_out=hs[:, 8 + h:9 + h])
                    self.ts(S, DVE, hs[:, 8:12], hs[:, 8:12], 1.0 / DH, EPS, ALU.mult, ALU.add, [hsr], [hsr])
                    self.tt(S, POOL, hs[:, 12:16], hs[:, 8:12], C["neghalf"][:, 0:1].to_broadcast([128, 4]), ALU.pow,
                            [hsr, cr], [hsr])
                    for h in range(4):
                        self.stt(S, mx[:, h * DH:(h + 1) * DH], hmv[:, h, :], hs[:, 12 + h:13 + h],
                                 G2[:, s, h * DH:(h + 1) * DH], ALU.mult, ALU.mult, [hmr, hsr, G2_r[s]], [mxr])
                    self.mem_pv(S, s, pmT, pmT_r, mvx, mem_r, pb[6], pb_r[6], rr, rr_r, mx, mxr)

                for s in range(NSUB):
                    stage_A(s)
                    stage_B1(s)
                    if s >= 1:
                        stage_B2(s - 1)
                    if s >= 2:
                        tail(s - 2, ti)
                stage_B2(NSUB - 1)
                tail(NSUB - 2, ti)
                tail(NSUB - 1, ti)
            self.stats["amix"] = S.emit(self.G)

    def mem_prologue(self, S, C, l, wmem, wmem_rs, memT, memT_rs, mkT, mvx, mem_r, pbank, pbank_r):
        cr = C["const_r"]
        xt, xt_r, xh, xh_r = C["xt"], C["xt_r"], C["xh"], C["xh_r"]
        self.load_w(S, wmem, wmem_rs[0], self.mem_w_kv[l], 0, 512)
        for mt in range(2):
            self.ld(S, SP, xt[:, mt, :], self.mem[mt * 128:(mt + 1) * 128, :], [xt_r[mt]], xt_r[mt])
            self.cp(S, DVE, xh[:, mt, :], xt[:, mt, :], [xt_r[mt]], [xh_r[mt]])
            pT = C["pT"]
            for kc in range(KC):
                self.tr(S, pT[:, kc * 128:(kc + 1) * 128], xh[:, mt, kc * 128:(kc + 1) * 128], C["identb"][:, :],
                        [xh_r[mt], cr], [C["pT_r"]])
            self.cp(S, DVE, memT[:, :, mt * 128:(mt + 1) * 128], pT[:, :].rearrange("p (a b) -> p a b", a=KC),
                    [C["pT_r"]], memT_rs)
        self.ms(S, POOL, mvx[:, :, :, 64:65], 1.0, [mem_r])
        for hp in range(2):
            for kc in range(KC):
                self.mm(S, pbank[:, 0:MEMT], wmem[:, kc, hp * 128:(hp + 1) * 128], memT[:, kc, 0:MEMT],
                        kc == 0, kc == KC - 1, memT_rs + wmem_rs, [pbank_r])
            self.cp(S, ACT, mkT[:, hp, :], pbank[:, 0:MEMT], [pbank_r], [mem_r])
        for mt in range(2):
            for kc in range(KC):
                self.mm(S, pbank[:, 0:256], memT[:, kc, mt * 128:(mt + 1) * 128], wmem[:, kc, 256:512],
                        kc == 0, kc == KC - 1, memT_rs + wmem_rs, [pbank_r])
            self.cp(S, DVE, mvx[:, mt, :, 0:64], pbank[:, 0:256].rearrange("p (h d) -> p h d", h=4),
                    [pbank_r], [mem_r])

    def mem_scores(self, S, qm, qm_rs, mkT, mem_r, pmT, pmT_r, banks, bank_rs):
        i = 0
        dbg = os.environ.get("K_DBG", "")
        for h in range(4):
            hp, hh = h // 2, h % 2
            if "h0" in dbg and hh == 1:
                continue
            for mt in range(2):
                b, br = banks[i % len(banks)], bank_rs[i % len(banks)]
                i += 1
                self.mm(S, b[:, :], mkT[hh * 64:(hh + 1) * 64, hp, mt * 128:(mt + 1) * 128],
                        qm[hh * 64:(hh + 1) * 64, hp, :], True, True, qm_rs + [mem_r], [br])
                if "noact" in dbg:
                    continue
                self.act(S, pmT[:, h * 2 + mt, :], b[:, :], AF.Exp, [br], [pmT_r], scale=0.125)

    def mem_pv(self, S, s, pmT, pmT_r, mvx, mem_r, pom, pom_r, rr, rr_r, mix, mix_r):
        first = True
        for h in range(4):
            for mt in range(2):
                self.mm(S, pom[:, h * 65:(h + 1) * 65], pmT[:, h * 2 + mt, s * 128:(s + 1) * 128], mvx[:, mt, h, :],
                        first, mt == 1, [pmT_r, mem_r], [pom_r], skip=True)
                first = False
        pv = pom[:, 0:260].rearrange("p (h e) -> p h e", h=4)
        S.op(DVE, lambda e: e.reciprocal(out=rr[:, 0:4], in_=pv[:, :, 64]), [pom_r], [rr_r])
        self.tt(S, DVE, mix[:, 768:1024].rearrange("p (h e) -> p h e", h=4), pv[:, :, 0:64],
                rr[:, 0:4].unsqueeze(2).to_broadcast([128, 4, 64]), ALU.mult, [pom_r, rr_r], [mix_r])

    def out_proj(self, S, C, mix, mix_r, mixT, mixT_r, wout, wout_rs, xt, xr, banks, bank_rs, dst, gi):
        cr = C["const_r"]
        pT = C["pT"]
        for kc in range(KC):
            self.tr(S, pT[:, kc * 128:(kc + 1) * 128], mix[:, kc * 128:(kc + 1) * 128], C["identb"][:, :],
                    [mix_r, cr], [C["pT_r"]])
        self.cp(S, ACT, mixT[:, :, :], pT[:, :].rearrange("p (a b) -> p a b", a=KC), [C["pT_r"]], [mixT_r])
        for hf in range(2):
            b, br = banks[hf], bank_rs[hf]
            for kc in range(KC):
                self.mm(S, b[:, :], mixT[:, kc, :], wout[:, kc, hf * 512:(hf + 1) * 512], kc == 0, kc == KC - 1,
                        [mixT_r] + wout_rs, [br])
            self.tt(S, DVE, xt[:, hf * 512:(hf + 1) * 512], b[:, :], xt[:, hf * 512:(hf + 1) * 512], ALU.add,
                    [br, xr], [xr])
        self.ld(S, SP, dst[gi * 128:(gi + 1) * 128, :], xt, [], xr, reads=[xr])

    def phase_bmix(self, src, dst, final):
        nc = self.nc
        S = Sched(nc)
        with ExitStack() as st:
            C = self.phase_consts(S, st, None)
            sb, ps = C["sb"], C["ps"]
            cr = C["const_r"]
            gT_kv = C["gT"][:, 2, :]
            gT_mix = C["gT"][:, 3, :]
            wkv = sb("wkv", (128, KC, 1536), BF16)
            wkv_r = S.regions(2, "wkv")
            wbin = sb("wbin", (128, KC, D), BF16)
            wbin_r = S.regions(1, "wbin")
            wbout = sb("wbout", (128, KC, D), BF16)
            wbout_r = S.regions(1, "wbout")
            biasT = sb("biasT", (128, 5, 12, 128), F32)
            bias_r = S.region("biasT")
            hT = sb("hT", (128, KC, T), BF16)
            hT_r = S.regions(NSUB, "hT")
            hTk = sb("hTk", (128, KC, T), BF16)
            hTk_r = S.regions(NSUB, "hTk")
            KTr = sb("KTr", (128, 2, 6, T), BF16)
            KT_r = [S.regions(6, f"KT{sl}_") for sl in range(2)]
            Vr = sb("Vr", (128, 2 * NSUB, 12, 65), BF16)
            V_r = S.regions(2 * NSUB, "V")
            QT = sb("QT", (128, 8, T), BF16)
            QT_r = S.regions(8, "QT")
            QA = sb("QA", (128, 6, T), BF16)
            QB = sb("QB", (128, 6, T), BF16)
            QAB_r = S.regions(6, "QAB")
            mkT = sb("mkT", (128, 2, MEMT), BF16)
            mvx = sb("mvx", (128, 2, 4, 65), BF16)
            mem_r = S.region("memkv")
            ssb = sb("ssb", (128, 2, 512), F32)
            ssb_r = S.regions(2, "ssb")
            pTs = sb("pTs", (128, 2, 4, 128), BF16)
            pTs_r = S.regions(2, "pTs")
            pmT = sb("pmT", (128, 8, T), BF16)
            pmT_r = S.region("pmT")
            mix = sb("mix", (128, 2, D), BF16)
            mix_r = S.regions(2, "mix")
            mixT = sb("mixT", (128, 2, KC, 128), BF16)
            mixT_r = S.regions(2, "mixT")
            rr = sb("rr", (128, 4, 4), F32)
            rr_r = S.regions(4, "rr")
            pb = [ps(f"pb{i}", (128, 512), F32) for i in range(7)]
            pb_r = S.regions(7, "pb")

            self.mem_prologue(S, C, 1, QT, QT_r, hT, hT_r, mkT, mvx, mem_r, pb[0], pb_r[0])
            self.load_w(S, wkv, wkv_r[0], self.w_kv, 0, 768)
            self.load_w(S, wkv, wkv_r[1], self.w_kv, 768, 1536)
            self.load_w(S, wbin, wbin_r[0], self.b_w_in, 0, D)
            self.load_w(S, wbout, wbout_r[0], self.b_w_out, 0, D)
            for kt in range(5):
                self.ld(S, SP, biasT[:, kt, :, :], self.relbias[:, kt, :, :], [bias_r], bias_r)
            self.ms(S, POOL, biasT[0:64, 0, :, 64:128], NEG, [bias_r])
            self.ms(S, POOL, biasT[64:128, 4, :, 0:64], NEG, [bias_r])
            self.ms(S, POOL, Vr[:, :, :, 64:65], 1.0, V_r)
            self.ms(S, POOL, QA[64:128, :, :], 0.0, QAB_r)
            self.ms(S, POOL, QB[0:64, :, :], 0.0, QAB_r)
            for gi in range(XSLOTS):
                self.load_x(S, C, src, gi)
            loaded = XSLOTS
            ia = 0
            isb = 0
            io = 0
            ipt = 0
            STOP = int(os.environ.get("K_STOP", 99))
            for ti in range(NT if STOP > 0 else 0):
                slot = ti % 2
                for s in range(NSUB):
                    gi = ti * NSUB + s
                    sl = gi % XSLOTS
                    self.norm_T(S, C, C["xt"][:, sl, :], C["xt_r"][sl],
                                [(gT_mix, hT[:, :, s * 128:(s + 1) * 128], hT_r[s]),
                                 (gT_kv, hTk[:, :, s * 128:(s + 1) * 128], hTk_r[s])])
                if STOP <= 1:
                    continue
                for j in range(6):
                    b, br = pb[ia % 2], pb_r[ia % 2]
                    ia += 1
                    for kc in range(KC):
                        self.mm(S, b[:, :], wkv[:, kc, j * 128:(j + 1) * 128], hTk[:, kc, :], kc == 0, kc == KC - 1,
                                hTk_r + [wkv_r[0]], [br])
                    self.cp(S, ACT, KTr[:, slot, j, :], b[:, :], [br], [KT_r[slot][j]])
                for s in range(NSUB):
                    vi = slot * NSUB + s
                    for (c0, c1, h0, h1) in ((768, 1280, 0, 8), (1280, 1536, 8, 12)):
                        b, br = pb[ia % 2], pb_r[ia % 2]
                        ia += 1
                        n = c1 - c0
                        for kc in range(KC):
                            self.mm(S, b[:, 0:n], hTk[:, kc, s * 128:(s + 1) * 128], wkv[:, kc, c0:c1], kc == 0,
                                    kc == KC - 1, [hTk_r[s], wkv_r[1]], [br])
                        self.cp(S, DVE, Vr[:, vi, h0:h1, 0:64], b[:, 0:n].rearrange("p (h d) -> p h d", d=64),
                                [br], [V_r[vi]])
                for j in range(8):
                    b, br = pb[ia % 2], pb_r[ia % 2]
                    ia += 1
                    for kc in range(KC):
                        self.mm(S, b[:, :], wbin[:, kc, j * 128:(j + 1) * 128], hT[:, kc, :], kc == 0, kc == KC - 1,
                                hT_r + wbin_r, [br])
                    if j < 6:
                        self.cp(S, ACT, QA[0:64, j, :], b[0:64, :], [br], [QAB_r[j]])
                        self.cp(S, DVE, QB[64:128, j, :], b[64:128, :], [br], [QAB_r[j]])
                    else:
                        self.cp(S, ACT, QT[:, j, :], b[:, :], [br], [QT_r[j]])
                if STOP <= 2:
                    continue
                self.mem_scores(S, QT[:, 6:8, :], QT_r[6:8], mkT, mem_r, pmT, pmT_r, [pb[0], pb[1]], [pb_r[0], pb_r[1]])
                for s in range(NSUB if STOP > 3 else 0):
                    P = ti * NSUB + s
                    gi = P
                    sl = gi % XSLOTS
                    mx, mxr = mix[:, P % 2, :], mix_r[P % 2]
                    kts = [kt for kt in range(5) if P - 4 + kt >= 0]
                    steps = [(hg, kt) for hg in range(3 if STOP > 4 else 0) for kt in kts]
                    po_of = {}
                    for hg in range(3):
                        po_of[hg] = (pb[4 + io % 2], pb_r[4 + io % 2])
                        io += 1
                    slots = {}

                    def emit_S(step):
                        nonlocal isb
                        hg, kt = step
                        kp = P - 4 + kt
                        sk = (kp // NSUB) % 2
                        subk = kp % NSUB
                        bs, bsr = pb[2 + isb % 2], pb_r[2 + isb % 2]
                        sbuf, sbr = ssb[:, isb % 2, :], ssb_r[isb % 2]
                        pt, ptr = pTs[:, isb % 2, :, :], pTs_r[isb % 2]
                        isb += 1
                        for hl in range(4):
                            h = hg * 4 + hl
                            Qh = QA if h % 2 == 0 else QB
                            self.mm(S, bs[:, hl * 128:(hl + 1) * 128],
                                    KTr[:, sk, h // 2, subk * 128:(subk + 1) * 128],
                                    Qh[:, h // 2, s * 128:(s + 1) * 128], True, True,
                                    [KT_r[sk][h // 2], QAB_r[h // 2]], [bsr])
                        self.stt(S, sbuf, bs[:, :], 0.125,
                                 biasT[:, kt, hg * 4:(hg + 1) * 4, :].rearrange("p h q -> p (h q)"),
                                 ALU.mult, ALU.add, [bsr, bias_r], [sbr])
                        self.act(S, pt.rearrange("p h q -> p (h q)"), sbuf, AF.Exp, [sbr], [ptr])
                        slots[step] = (pt, ptr, sk, subk)

                    def emit_PV(step):
                        hg, kt = step
                        pt, ptr, sk, subk = slots.pop(step)
                        po, por = po_of[hg]
                        for hl in range(4):
                            h = hg * 4 + hl
                            self.mm(S, po[:, hl * 65:(hl + 1) * 65], pt[:, hl, :], Vr[:, sk * NSUB + subk, h, :],
                                    kt == kts[0] and hl == 0, kt == kts[-1], [ptr, V_r[sk * NSUB + subk]], [por],
                                    skip=True)
                        if kt == kts[-1]:
                            pv = po[:, 0:260].rearrange("p (h e) -> p h e", h=4)
                            rq, rqr = rr[:, hg, :], rr_r[hg]
                            S.op(DVE, (lambda pv=pv, rq=rq: (lambda e: e.reciprocal(out=rq, in_=pv[:, :, 64])))(), [por], [rqr])
                            self.tt(S, DVE, mx[:, hg * 256:(hg + 1) * 256].rearrange("p (h e) -> p h e", h=4),
                                    pv[:, :, 0:64], rq.unsqueeze(2).to_broadcast([128, 4, 64]), ALU.mult,
                                    [por, rqr], [mxr])

                    if steps:
                        emit_S(steps[0])
                    for i_, step in enumerate(steps):
                        if i_ + 1 < len(steps):
                            emit_S(steps[i_ + 1])
                        emit_PV(step)
                    if STOP > 5:
                        self.mem_pv(S, s, pmT, pmT_r, mvx, mem_r, pb[6], pb_r[6], rr[:, 3, :], rr_r[3], mx, mxr)
                    if STOP > 6:
                      self.out_proj(S, C, mx, mxr, mixT[:, P % 2, :, :], mixT_r[P % 2], wbout, wbout_r,
                                  C["xt"][:, sl, :], C["xt_r"][sl], [pb[0], pb[1]], [pb_r[0], pb_r[1]], dst, gi)
                    if loaded < NT * NSUB and loaded % XSLOTS == sl:
                        self.load_x(S, C, src, loaded)
                        loaded += 1
            self.stats["bmix"] = S.emit(self.G)


def host_consts():
    ident = np.eye(128, dtype=np.float32)
    s = np.arange(128)[:, None]
    t = np.arange(128)[None, :]
    negU = np.where(s <= t, -1.0, 0.0).astype(np.float32)
    maskc = np.where(s <= t, np.float32(DH ** -0.5), np.float32(0.0)).astype(np.float32)
    return {"c_ident": ident, "c_negU": negU, "c_maskc": maskc}


def host_layout(inp):
    f = lambda a: np.ascontiguousarray(np.asarray(a, dtype=np.float32))
    g = np.stack([f(inp["norm_mix_g"])[0], f(inp["norm_ffn_g"])[0], f(inp["kv_norm_g"]),
                  f(inp["norm_mix_g"])[1], f(inp["norm_ffn_g"])[1]], 0)
    gT_all = np.ascontiguousarray(g.reshape(5, KC, 128).transpose(2, 0, 1))
    cwA = np.ascontiguousarray(f(inp["a_conv_w"])[0].reshape(4, 16, 96).transpose(2, 1, 0))
    cbA = np.ascontiguousarray(f(inp["a_conv_b"])[0].reshape(16, 96).T)
    cwF = np.ascontiguousarray(f(inp["ffn_conv_w"]).reshape(2, 3, NFT, 128).transpose(3, 0, 2, 1))
    cbF = np.ascontiguousarray(f(inp["ffn_conv_b"]).reshape(2, NFT, 128).transpose(2, 0, 1))
    rel = np.arange(768) - 127
    idx = np.clip(rel, -63, 128) + 63
    relext = f(inp["b_rel_bias"])[0][:, idx]
    kj = np.arange(128)[:, None, None]
    kt = np.arange(5)[None, :, None]
    qi = np.arange(128)[None, None, :]
    gidx = qi - kj + (4 - kt) * 128 + 127
    relbias = np.ascontiguousarray(relext[:, gidx].transpose(1, 2, 0, 3))
    shared = {
        "a_w_in": f(inp["a_w_in"])[0], "a_w_out": f(inp["a_w_out"])[0], "w_kv": f(inp["w_kv"]),
        "b_w_in": f(inp["b_w_in"])[0], "b_w_out": f(inp["b_w_out"])[0], "mem_w_kv": f(inp["mem_w_kv"]),
        "ffn_w_up": f(inp["ffn_w_up"]), "ffn_w_down": f(inp["ffn_w_down"]),
        "gT_all": gT_all, "final_g": f(inp["final_g"]).reshape(1, D), "gate_b": f(inp["a_gate_b"]).reshape(1, 8),
        "cwA": cwA, "cbA": cbA, "head_g": f(inp["a_head_g"]).reshape(1, AW), "cwF": cwF, "cbF": cbF,
        "relbias": relbias,
    }
    shared.update(host_consts())
    return shared


_CACHE = {}


def run(inputs, phases=("A_mix", "A_ffn", "B_mix", "B_ffn"), final_norm=True, ncores=8, trace=False):
    key = (tuple(phases), final_norm)
    if key not in _CACHE:
        _CACHE[key] = Prog(phases, final_norm).build()
    nc = _CACHE[key]
    shared = host_layout(inputs)
    x = np.asarray(inputs["x"], dtype=np.float32)
    mem = np.asarray(inputs["mem"], dtype=np.float32)
    in_maps = []
    for c in range(ncores):
        m = dict(shared)
        m["x"] = np.ascontiguousarray(x[c])
        m["mem"] = np.ascontiguousarray(mem[c])
        in_maps.append(m)
    res = run_bass_kernel_spmd(nc, in_maps, core_ids=list(range(ncores)), trace=trace)
    out = np.stack([np.asarray(r["out"]) for r in res.results], 0)
    return out, res


def kernel(**inputs):
    out, _ = run(inputs)
    return out.astype(np.float32)
```

```python
import numpy as np
from contextlib import ExitStack
import concourse.bass as bass
import concourse.mybir as mybir
from concourse.bass_types import AP
from concourse.bass_utils import run_bass_kernel_spmd

F32 = mybir.dt.float32
BF16 = mybir.dt.bfloat16
ALU = mybir.AluOpType
AF = mybir.ActivationFunctionType
AX = mybir.AxisListType

PE, ACT, DVE, POOL, SP = "tensor", "scalar", "vector", "gpsimd", "sync"
ENGS = (PE, ACT, DVE, POOL, SP)

D = 1024
KC = 8
SEQ = 4096
T = 512
import os
NT = int(os.environ.get("K_NT", SEQ // T))
SUB = 128
NSUB = T // SUB
DFF = 2816
NFT = DFF // 128
A_IN = 3336
AW = 768
DH = 192
MEMT = 256
EPS = 1e-6
NEG = -30000.0
XSLOTS = 6


class Region:
    __slots__ = ("name", "writer", "readers", "strict")

    def __init__(self, name):
        self.name = name
        self.writer = None
        self.readers = []
        self.strict = False


class _Op:
    __slots__ = ("eng", "idx", "fn", "waits", "needs_inc", "dma_key", "snap")

    def __init__(self, eng, idx, fn):
        self.eng = eng
        self.idx = idx
        self.fn = fn
        self.waits = []
        self.needs_inc = False
        self.dma_key = None
        self.snap = None


class Sched:
    def __init__(self, nc):
        self.nc = nc
        self.ops = {e: [] for e in ENGS}
        self.clock = {e: {x: -1 for x in ENGS} for e in ENGS}
        self.dclock = {e: {} for e in ENGS}
        self.dma_count = {}
        self.all_regions = []

    def region(self, name=None):
        r = Region(name or f"r{len(self.all_regions)}")
        self.all_regions.append(r)
        return r

    def regions(self, n, name="r"):
        return [self.region(f"{name}{i}") for i in range(n)]

    def _add(self, eng, fn, reads, writes, dma_key=None):
        o = _Op(eng, len(self.ops[eng]), fn)
        o.dma_key = dma_key
        deps = []
        for r in reads:
            if r.writer is not None:
                deps.append((r.writer, True))
        for w in writes:
            if w.writer is not None:
                deps.append((w.writer, w.strict))
            for rd in w.readers:
                deps.append((rd, w.strict))
        clk = self.clock[eng]
        dclk = self.dclock[eng]
        for tok, is_raw in deps:
            if tok[0] == "c":
                _, e2, n = tok
                if e2 == eng and (not is_raw or eng == PE):
                    continue
                if clk[e2] >= n:
                    continue
                o.waits.append(tok)
                self.ops[e2][n].needs_inc = True
                clk[e2] = n
                sn = self.ops[e2][n].snap
                for k, v in sn.items():
                    if k != eng and clk[k] < v:
                        clk[k] = v
            else:
                _, key, val = tok
                if dclk.get(key, 0) >= val:
                    continue
                cur = self.dma_count[key]
                o.waits.append(("d", key, cur))
                dclk[key] = cur
        o.snap = dict(clk)
        self.ops[eng].append(o)
        if dma_key is not None:
            self.dma_count[dma_key] = self.dma_count.get(dma_key, 0) + 16
            tok = ("d", dma_key, self.dma_count[dma_key])
        else:
            tok = ("c", eng, o.idx)
        for r in reads:
            r.readers.append(tok)
        for w in writes:
            w.writer = tok
            w.readers = []
        return o

    def op(self, eng, fn, reads=(), writes=()):
        return self._add(eng, fn, reads, writes)

    def dma(self, eng, fn, reads=(), writes=(), key=None):
        return self._add(eng, fn, reads, writes, dma_key=key.name + "@" + eng)

    def emit(self, G):
        nc = self.nc
        self._add(SP, None, list(self.all_regions), list(self.all_regions))
        keys = list(self.dma_count)
        slot = {}
        nsw = nhw = 0
        for k in keys:
            if k.endswith("@" + POOL):
                slot[k] = nsw
                nsw += 1
            else:
                slot[k] = G.NSW + nhw
                nhw += 1
        assert nsw <= G.NSW and nhw <= G.NDMA - G.NSW, (nsw, nhw)
        dsem = {k: G.dsem[slot[k]] for k in keys}
        dbase = {k: G.dbase[slot[k]] for k in keys}
        esem, ebase = G.esem, dict(G.ebase)
        G.phase += 1
        barv = G.phase
        cnt = {}
        for e in ENGS:
            c = 0
            arr = []
            for o in self.ops[e]:
                if o.needs_inc:
                    c += 1
                arr.append(c)
            cnt[e] = arr
            G.ebase[e] += c
        for k in keys:
            G.dbase[slot[k]] += self.dma_count[k]
        with nc.Block() as block:
            def make(e):
                def body(engh):
                    for o in self.ops[e]:
                        for w in o.waits:
                            if w[0] == "c":
                                engh.wait_ge(esem[w[1]], ebase[w[1]] + cnt[w[1]][w[2]])
                            else:
                                engh.wait_ge(dsem[w[1]], dbase[w[1]] + w[2])
                        if o.fn is None:
                            continue
                        ins = o.fn(engh)
                        if o.dma_key is not None:
                            ins.then_inc(dsem[o.dma_key], 16)
                        elif o.needs_inc:
                            ins.then_inc(esem[e], 1)
                    if e == SP:
                        engh.sem_inc(G.bar, 1)
                    else:
                        engh.wait_ge(G.bar, barv)
                return body

            for e in ENGS:
                getattr(block, e)(make(e))
        return {e: len(self.ops[e]) for e in ENGS}


class SemPool:
    NDMA = 56
    NSW = 16

    def __init__(self, nc, st):
        self.esem = {e: st.enter_context(nc.semaphore(f"s_{e}")) for e in ENGS}
        self.dsem = [st.enter_context(nc.semaphore(f"d_{i}")) for i in range(self.NDMA)]
        self.bar = st.enter_context(nc.semaphore("bar"))
        self.ebase = {e: 0 for e in ENGS}
        self.dbase = [0] * self.NDMA
        self.phase = 0
        allsem = list(self.esem.values()) + self.dsem + [self.bar]
        with nc.Block() as block:
            @block.gpsimd
            def _(g):
                for s in allsem:
                    g.sem_clear(s)
        nc.all_engine_barrier()


class Prog:
    def __init__(self, phases, final_norm=True):
        self.nc = nc = bass.Bass("TRN2", target_bir_lowering=False)
        self.phases = phases
        self.final_norm = final_norm
        din = lambda n, s: nc.dram_tensor(n, list(s), F32, kind="ExternalInput").ap()
        self.x = din("x", (SEQ, D))
        self.mem = din("mem", (MEMT, D))
        self.a_w_in = din("a_w_in", (D, A_IN))
        self.a_w_out = din("a_w_out", (D, D))
        self.w_kv = din("w_kv", (D, 1536))
        self.b_w_in = din("b_w_in", (D, D))
        self.b_w_out = din("b_w_out", (D, D))
        self.mem_w_kv = din("mem_w_kv", (2, D, 512))
        self.ffn_w_up = din("ffn_w_up", (2, D, 2 * DFF))
        self.ffn_w_down = din("ffn_w_down", (2, DFF, D))
        self.gT_all = din("gT_all", (128, 5, KC))
        self.final_g = din("final_g", (1, D))
        self.gate_b = din("gate_b", (1, 8))
        self.cwA = din("cwA", (96, 16, 4))
        self.cbA = din("cbA", (96, 16))
        self.head_g = din("head_g", (1, AW))
        self.cwF = din("cwF", (128, 2, NFT, 3))
        self.cbF = din("cbF", (128, 2, NFT))
        self.relbias = din("relbias", (128, 5, 12, 128))
        self.c_ident = din("c_ident", (128, 128))
        self.c_negU = din("c_negU", (128, 128))
        self.c_maskc = din("c_maskc", (128, 128))
        self.xa = nc.dram_tensor("xa", [SEQ, D], F32, kind="Internal").ap()
        self.xb = nc.dram_tensor("xb", [SEQ, D], F32, kind="Internal").ap()
        self.out = nc.dram_tensor("out", [SEQ, D], F32, kind="ExternalOutput").ap()
        self.stats = {}

    def mm(self, S, out, lhsT, rhs, start, stop, reads, writes, skip=False):
        S.op(PE, lambda e: e.matmul(out, lhsT=lhsT, rhs=rhs, start=start, stop=stop,
                                    skip_group_check=skip), reads, writes)

    def tr(self, S, out, in_, ident, reads, writes):
        S.op(PE, lambda e: e.transpose(out=out, in_=in_, identity=ident), reads, writes)

    def act(self, S, out, in_, func, reads, writes, **kw):
        S.op(ACT, lambda e: e.activation(out=out, in_=in_, func=func, **kw), reads, writes)

    def tt(self, S, eng, out, in0, in1, op, reads, writes):
        S.op(eng, lambda e: e.tensor_tensor(out=out, in0=in0, in1=in1, op=op), reads, writes)

    def ts(self, S, eng, out, in0, s1, s2, op0, op1, reads, writes):
        if op1 is None:
            S.op(eng, lambda e: e.tensor_scalar(out=out, in0=in0, scalar1=s1, scalar2=None, op0=op0),
                 reads, writes)
        else:
            S.op(eng, lambda e: e.tensor_scalar(out=out, in0=in0, scalar1=s1, scalar2=s2, op0=op0, op1=op1),
                 reads, writes)

    def stt(self, S, out, in0, scalar, in1, op0, op1, reads, writes):
        S.op(DVE, lambda e: e.scalar_tensor_tensor(out=out, in0=in0, scalar=scalar, in1=in1, op0=op0, op1=op1),
             reads, writes)

    def cp(self, S, eng, out, in_, reads, writes):
        if eng == ACT:
            S.op(ACT, lambda e: e.copy(out=out, in_=in_), reads, writes)
        else:
            S.op(eng, lambda e: e.tensor_copy(out=out, in_=in_), reads, writes)

    def ms(self, S, eng, ap, val, writes):
        S.op(eng, lambda e: e.memset(ap, val), (), writes)

    def ld(self, S, eng, out, in_, writes, key, reads=(), **kw):
        S.dma(eng, lambda e: e.dma_start(out=out, in_=in_, **kw), reads, writes, key=key)

    def load_w(self, S, dst, reg, src2d, c0, c1, kc0=0, kc1=None):
        kcn = src2d.shape[0] // 128
        kc1 = kcn if kc1 is None else kc1
        src = src2d.rearrange("(kc p) n -> p kc n", p=128)
        step = 2048
        for a in range(c0, c1, step):
            b = min(c1, a + step)
            self.ld(S, POOL, dst[:, kc0:kc1, a:b], src[:, kc0:kc1, a:b], [reg], reg)

    def norm_stats(self, S, C, xt_ap, xr):
        i = C["nrm_i"]
        C["nrm_i"] += 1
        k = i % 2
        ss = C["ss"][:, k, 0:1]
        ms_ = C["ss"][:, k, 1:2]
        rstd = C["ss"][:, k, 2:3]
        ssr = C["ss_r"][k]
        xh = C["xh"][:, k, :]
        xhr = C["xh_r"][k]
        self.act(S, xh, xt_ap, AF.Square, [xr], [xhr, ssr], accum_out=ss)
        self.ts(S, DVE, ms_, ss, 1.0 / D, EPS, ALU.mult, ALU.add, [ssr], [ssr])
        self.tt(S, POOL, rstd, ms_, C["neghalf"][:, 0:1], ALU.pow, [ssr, C["const_r"]], [ssr])
        self.act(S, xh, xt_ap, AF.Copy, [xr, ssr], [xhr], scale=rstd)
        return (xh, xhr)

    def norm_tr(self, S, C, hnd, outs):
        xh, xhr = hnd
        pT = C["pT"]
        for kc in range(KC):
            self.tr(S, pT[:, kc * 128:(kc + 1) * 128], xh[:, kc * 128:(kc + 1) * 128], C["identb"][:, :],
                    [xhr, C["const_r"]], [C["pT_r"]])
        pT3 = pT[:, :].rearrange("p (a b) -> p a b", a=KC)
        for gT, dst, dr in outs:
            self.tt(S, DVE, dst, pT3, gT.unsqueeze(2).to_broadcast([128, KC, 128]), ALU.mult,
                    [C["pT_r"], C["const_r"]], [dr])

    def norm_T(self, S, C, xt_ap, xr, outs):
        self.norm_tr(S, C, self.norm_stats(S, C, xt_ap, xr), outs)

    def phase_consts(self, S, st, which_gains):
        nc = self.nc
        C = {"nrm_i": 0}
        self.phase_i = getattr(self, "phase_i", 0) + 1
        pfx = f"p{self.phase_i}_"
        sb = lambda n, s, d: st.enter_context(nc.sbuf_tensor(pfx + n, list(s), d))
        ps = lambda n, s, d: st.enter_context(nc.psum_tensor(pfx + n, list(s), d))
        C["sb"] = sb
        C["ps"] = ps
        C["const_r"] = cr = S.region("const")
        C["identb"] = sb("identb", (128, 128), BF16)
        C["neghalf"] = sb("neghalf", (128, 1), F32)
        C["gT"] = sb("gT", (128, 5, KC), F32)
        self.ld(S, POOL, C["identb"][:, :], self.c_ident[:, :], [cr], cr)
        self.ld(S, SP, C["gT"][:, :, :], self.gT_all[:, :, :], [cr], cr)
        self.ms(S, DVE, C["neghalf"][:, :], -0.5, [cr])
        C["ss"] = sb("ss", (128, 2, 4), F32)
        C["ss_r"] = S.regions(2, "ss")
        C["xh"] = sb("xh", (128, 2, D), BF16)
        C["xh_r"] = S.regions(2, "xh")
        C["pT"] = ps("pT", (128, D), BF16)
        C["pT_r"] = S.region("pT")
        C["xt"] = sb("xt", (128, XSLOTS, D), F32)
        C["xt_r"] = S.regions(XSLOTS, "xt")
        return C

    def load_x(self, S, C, src, gi):
        sl = gi % XSLOTS
        self.ld(S, SP, C["xt"][:, sl, :], src[gi * 128:(gi + 1) * 128, :], [C["xt_r"][sl]], C["xt_r"][sl])

    def phase_ffn(self, l, src, dst, final):
        nc = self.nc
        S = Sched(nc)
        with ExitStack() as st:
            C = self.phase_consts(S, st, None)
            sb, ps = C["sb"], C["ps"]
            gT = C["gT"][:, 1 if l == 0 else 4, :]
            wup = sb("wup", (128, KC, 2 * DFF), BF16)
            wup_r = S.regions(4, "wup")
            wdn = sb("wdn", (128, NFT, D), BF16)
            wdn_r = S.regions(2, "wdn")
            hT = sb("hT", (128, KC, T), BF16)
            hT_r = S.regions(NSUB, "hT")
            aT = sb("aT", (128, NFT, T), BF16)
            aT_r = S.regions(NFT, "aT")
            graw = sb("graw", (128, 2, T + 2), F32)
            graw_r = S.regions(2, "graw")
            acc = sb("acc", (128, 2, T), F32)
            acc_r = S.regions(2, "acc")
            tnh = sb("tnh", (128, 2, T), F32)
            tnh_r = S.regions(2, "tnh")
            halo = sb("halo", (128, NFT, 2), F32)
            halo_r = S.regions(NFT, "halo")
            cw = sb("cw", (128, NFT, 3), F32)
            cb = sb("cb", (128, NFT), F32)
            cr = C["const_r"]
            pb = [ps(f"pb{i}", (128, 512), F32) for i in range(7)]
            pb_r = S.regions(7, "pb")
            if final:
                fg = sb("fg", (128, D), F32)
                self.ld(S, SP, fg[:, :], self.final_g.partition_broadcast(128), [cr], cr)
                fss = sb("fss", (128, 2, 4), F32)
                fss_r = S.regions(2, "fss")
                for r_ in C["xh_r"]:
                    r_.strict = True
            for gi in range(min(XSLOTS, NSUB + 2)):
                self.load_x(S, C, src, gi)
            loaded = min(XSLOTS, NSUB + 2)
            self.ld(S, SP, cw[:, :, :], self.cwF[:, l, :, :], [cr], cr)
            self.ld(S, SP, cb[:, :], self.cbF[:, l, :], [cr], cr)
            self.ts(S, POOL, cw[:, :, :], cw[:, :, :], 0.5, None, ALU.mult, None, [cr], [cr])
            self.ts(S, POOL, cb[:, :], cb[:, :], 0.5, None, ALU.mult, None, [cr], [cr])
            self.ms(S, POOL, halo[:, :, :], 0.0, halo_r)
            wu = self.ffn_w_up[l]
            for c in range(4):
                self.load_w(S, wup, wup_r[c], wu, c * 1408, (c + 1) * 1408)
            wd = self.ffn_w_down[l]
            self.load_w(S, wdn, wdn_r[0], wd, 0, D, 0, 11)
            self.load_w(S, wdn, wdn_r[1], wd, 0, D, 11, 22)

            pbi = 0

            def do_stats(ti, s):
                gi = ti * NSUB + s
                sl = gi % XSLOTS
                return self.norm_stats(S, C, C["xt"][:, sl, :], C["xt_r"][sl])

            def do_tr(hnd, s):
                self.norm_tr(S, C, hnd, [(gT, hT[:, :, s * 128:(s + 1) * 128], hT_r[s])])

            for s in range(NSUB):
                do_tr(do_stats(0, s), s)
            for ti in range(NT):
                for j in range(NFT):
                    pu, pur = pb[pbi % 6], pb_r[pbi % 6]
                    pg, pgr = pb[(pbi + 1) % 6], pb_r[(pbi + 1) % 6]
                    pbi += 2
                    for kc in range(KC):
                        self.mm(S, pg[:, :], wup[:, kc, DFF + j * 128:DFF + (j + 1) * 128], hT[:, kc, :],
                                kc == 0, kc == KC - 1, hT_r + [wup_r[2 + j // 11]], [pgr])
                    for kc in range(KC):
                        self.mm(S, pu[:, :], wup[:, kc, j * 128:(j + 1) * 128], hT[:, kc, :],
                                kc == 0, kc == KC - 1, hT_r + [wup_r[j // 11]], [pur])
                    r = j % 2
                    gr, grr = graw[:, r, :], graw_r[r]
                    ac, acr = acc[:, r, :], acc_r[r]
                    tn, tnr = tnh[:, r, :], tnh_r[r]
                    self.cp(S, POOL, gr[:, 0:2], halo[:, j, :], [halo_r[j]], [grr])
                    self.cp(S, ACT, gr[:, 2:T + 2], pg[:, :], [pgr], [grr])
                    self.cp(S, POOL, halo[:, j, :], gr[:, T:T + 2], [grr], [halo_r[j]])
                    self.act(S, ac, pg[:, :], AF.Identity, [pgr, cr], [acr], scale=cw[:, j, 2:3], bias=cb[:, j:j + 1])
                    self.stt(S, ac, gr[:, 1:T + 1], cw[:, j, 1:2], ac, ALU.mult, ALU.add, [grr, cr, acr], [acr])
                    self.stt(S, ac, gr[:, 0:T], cw[:, j, 0:1], ac, ALU.mult, ALU.add, [grr, cr, acr], [acr])
                    self.act(S, tn, ac, AF.Tanh, [acr], [tnr])
                    self.tt(S, DVE, gr[:, 2:T + 2], ac, pu[:, :], ALU.mult, [acr, pur], [grr])
                    self.stt(S, aT[:, j, :], tn, 1.0, gr[:, 2:T + 2], ALU.add, ALU.mult, [tnr, grr], [aT_r[j]])
                hnds = {}
                for s in range(NSUB):
                    gi = ti * NSUB + s
                    sl = gi % XSLOTS
                    xt = C["xt"][:, sl, :]
                    xr = C["xt_r"][sl]
                    for hf in range(2):
                        po, por = pb[6], pb_r[6]
                        if hf == 1:
                            po, por = pb[pbi % 6], pb_r[pbi % 6]
                            pbi += 1
                        for kc in range(NFT):
                            self.mm(S, po[:, :], aT[:, kc, s * 128:(s + 1) * 128], wdn[:, kc, hf * 512:(hf + 1) * 512],
                                    kc == 0, kc == NFT - 1, [aT_r[kc], wdn_r[kc // 11]], [por])
                        self.tt(S, DVE, xt[:, hf * 512:(hf + 1) * 512], po[:, :], xt[:, hf * 512:(hf + 1) * 512],
                                ALU.add, [por, xr], [xr])
                    if final:
                        k = gi % 2
                        ss = fss[:, k, 0:1]
                        ms_ = fss[:, k, 1:2]
                        rstd = fss[:, k, 2:3]
                        self.act(S, C["xh"][:, k, :], xt, AF.Square, [xr], [C["xh_r"][k], fss_r[k]], accum_out=ss)
                        self.ts(S, DVE, ms_, ss, 1.0 / D, EPS, ALU.mult, ALU.add, [fss_r[k]], [fss_r[k]])
                        self.tt(S, POOL, rstd, ms_, C["neghalf"][:, 0:1], ALU.pow, [fss_r[k], cr], [fss_r[k]])
                        self.stt(S, xt, xt, rstd, fg[:, :], ALU.mult, ALU.mult, [xr, fss_r[k], cr], [xr])
                    self.ld(S, SP, dst[gi * 128:(gi + 1) * 128, :], xt, [], xr, reads=[xr])
                    if loaded < NT * NSUB and loaded % XSLOTS == sl:
                        self.load_x(S, C, src, loaded)
                        loaded += 1
                    if ti + 1 < NT:
                        hnds[s] = do_stats(ti + 1, s)
                        if s >= 1:
                            do_tr(hnds.pop(s - 1), s - 1)
                if ti + 1 < NT:
                    do_tr(hnds.pop(NSUB - 1), NSUB - 1)
            self.stats[f"ffn{l}"] = S.emit(self.G)

    def build(self):
        chain = {"A_mix": self.phase_amix, "A_ffn": lambda s, d, f: self.phase_ffn(0, s, d, False),
                 "B_mix": self.phase_bmix, "B_ffn": lambda s, d, f: self.phase_ffn(1, s, d, f)}
        src = self.x
        scr = [self.xa, self.xb]
        with ExitStack() as gst:
            self.G = SemPool(self.nc, gst)
            for i, ph in enumerate(self.phases):
                last = i == len(self.phases) - 1
                dst = self.out if last else scr[i % 2]
                chain[ph](src, dst, last and self.final_norm)
                src = dst
        return self.nc

    def phase_amix(self, src, dst, final):
        nc = self.nc
        S = Sched(nc)
        with ExitStack() as st:
            C = self.phase_consts(S, st, None)
            sb, ps = C["sb"], C["ps"]
            cr = C["const_r"]
            gT_mix = C["gT"][:, 0, :]
            wain = sb("wain", (128, KC, A_IN), BF16)
            wain_r = S.regions(4, "wain")
            waout = sb("waout", (128, KC, D), BF16)
            waout_r = S.regions(1, "waout")
            hT = sb("hT", (128, KC, T), BF16)
            hT_r = S.regions(NSUB, "hT")
            qkT = sb("qkT", (128, 16, T), BF16)
            qk_r = S.regions(16, "qk")
            qmT = sb("qmT", (128, 2, T), BF16)
            qm_r = S.regions(2, "qm")
            raw = sb("raw", (128, 2, T + 3), F32)
            raw_r = S.regions(2, "raw")
            acc = sb("acc", (128, 2, T), F32)
            acc_r = S.regions(2, "acc")
            tnh = sb("tnh", (128, 2, T), F32)
            tnh_r = S.regions(2, "tnh")
            halo = sb("halo", (128, 16, 3), F32)
            halo_r = S.regions(16, "halo")
            cw = sb("cw", (128, 16, 4), F32)
            cb = sb("cb", (128, 16), F32)
            ktok = sb("ktok", (128, NSUB, AW), BF16)
            ktok_r = S.regions(NSUB, "ktok")
            vw = sb("vw", (128, NSUB, 4, DH + 1), BF16)
            vw_r = S.regions(NSUB, "vw")
            G2 = sb("G2", (128, NSUB, AW), F32)
            G2_r = S.regions(NSUB, "G2")
            hgh = sb("hgh", (128, AW), F32)
            gb_bc = sb("gb_bc", (128, 8), F32)
            gsb = sb("gsb", (128, NSUB, 8), F32)
            gw = sb("gw", (128, 16, 16), F32)
            g_r = S.region("gates")
            EP, SPL, AA, BBL, AMX, MALL, MST, T48 = 0, 1, 2, 3, 5, 6, 7, 8
            WGF = 11
            mcar = sb("mcar", (128, 4), F32)
            am16 = sb("am16", (16, 20), F32)
            identF = sb("identF", (128, 128), F32)
            negU = sb("negU", (128, 128), F32)
            negO = sb("negO", (128, 128), F32)
            ones16 = sb("ones16", (16, 128), F32)
            maskc = sb("maskc", (128, 4, 128), F32)
            Cn = sb("Cn", (128, 4, 2, DH + 1), F32)
            Cn_r = S.regions(4, "Cn")
            Gbf = sb("Gbf", (128, 2, 4, 2, DH + 1), BF16)
            Gbf_r = [S.regions(4, "GbfA"), S.regions(4, "GbfB")]
            sm = sb("sm", (128, 2, 4, 128), BF16)
            sm_r = S.regions(2, "sm")
            hm = sb("hm", (128, 2, 4, DH), F32)
            hm_r = S.regions(2, "hm")
            hj = sb("hj", (128, 4, DH), BF16)
            hj_r = S.regions(4, "hj")
            for r_ in hj_r:
                r_.strict = True
            hst = sb("hst", (128, 2, 16), F32)
            hst_r = S.regions(2, "hst")
            mkT = sb("mkT", (128, 2, MEMT), BF16)
            mvx = sb("mvx", (128, 2, 4, 65), BF16)
            mem_r = S.region("memkv")
            pmT = sb("pmT", (128, 8, T), BF16)
            pmT_r = S.region("pmT")
            mix = sb("mix", (128, 2, D), BF16)
            mix_r = S.regions(2, "mix")
            mixT = sb("mixT", (128, 2, KC, 128), BF16)
            mixT_r = S.regions(2, "mixT")
            rr = sb("rr", (128, 4), F32)
            rr_r = S.region("rr")
            pb = [ps(f"pb{i}", (128, 512), F32) for i in range(7)]
            pb_r = S.regions(7, "pb")

            self.mem_prologue(S, C, 0, qkT[:, 0:8, :], qk_r[0:8], hT, hT_r, mkT, mvx, mem_r, pb[0], pb_r[0])
            self.load_w(S, wain, wain_r[0], self.a_w_in, 0, 1536)
            self.load_w(S, wain, wain_r[1], self.a_w_in, 1536, 2304)
            self.load_w(S, wain, wain_r[2], self.a_w_in, 2304, 3080)
            self.load_w(S, wain, wain_r[3], self.a_w_in, 3080, A_IN)
            self.load_w(S, waout, waout_r[0], self.a_w_out, 0, D)
            self.ld(S, SP, cw[0:96, :, :], self.cwA[:, :, :], [cr], cr)
            self.ld(S, SP, cb[0:96, :], self.cbA[:, :], [cr], cr)
            self.ld(S, SP, hgh[:, :], self.head_g.partition_broadcast(128), [cr], cr)
            self.ld(S, SP, gb_bc[:, :], self.gate_b.partition_broadcast(128), [cr], cr)
            self.ld(S, SP, identF[:, :], self.c_ident[:, :], [cr], cr)
            self.ld(S, SP, negU[:, :], self.c_negU[:, :], [cr], cr)
            for h in range(4):
                self.ld(S, SP, maskc[:, h, :], self.c_maskc[:, :], [cr], cr)
            self.ts(S, POOL, cw[0:96, :, :], cw[0:96, :, :], 0.5, None, ALU.mult, None, [cr], [cr])
            self.ts(S, POOL, cb[0:96, :], cb[0:96, :], 0.5, None, ALU.mult, None, [cr], [cr])
            self.ts(S, POOL, hgh[:, :], hgh[:, :], 0.5, None, ALU.mult, None, [cr], [cr])
            self.ms(S, POOL, negO[:, :], -1.0, [cr])
            self.ms(S, POOL, ones16[:, :], 1.0, [cr])
            self.ms(S, POOL, halo[:, :, :], 0.0, halo_r)
            self.ms(S, POOL, Cn[:, :, :, :], 0.0, Cn_r)
            self.ms(S, POOL, mcar[:, :], 0.0, [g_r])
            self.ms(S, POOL, vw[:, :, :, DH:DH + 1], 1.0, vw_r)
            for gi in range(XSLOTS):
                self.load_x(S, C, src, gi)
            loaded = XSLOTS
            ia = 0
            WK = 14
            ga = lambda i, n=1: gw[:, i:i + n, :].rearrange("p a c -> p (a c)")
            g3 = lambda i: gw[:, i, :].rearrange("p (s h) -> p s h", s=NSUB)
            wv, gv, flv, wkv_ = g3(WGF), g3(WGF + 1), g3(WGF + 2), g3(WK)
            pg, pgr = pb[6], pb_r[6]

            def gate_stages():
                for s in range(NSUB):
                    for kc in range(KC):
                        self.mm(S, pg[:, s * 8:(s + 1) * 8], hT[:, kc, s * 128:(s + 1) * 128], wain[:, kc, 3072:3080],
                                kc == 0, kc == KC - 1, [hT_r[s], wain_r[2]], [pgr])
                self.tt(S, DVE, gsb[:, :, :], pg[:, 0:32].rearrange("p (s g) -> p s g", s=NSUB),
                        gb_bc[:, :].unsqueeze(1).to_broadcast([128, NSUB, 8]), ALU.add, [pgr, cr], [g_r])
                self.act(S, g3(EP), gsb[:, :, 4:8], AF.Exp, [g_r], [g_r], scale=-1.0)
                self.act(S, ga(SPL), ga(EP), AF.Ln, [g_r], [g_r], bias=1.0)
                yield
                self.mm(S, pg[:, 32:48], negU[:, :], ga(SPL), True, True, [g_r, cr], [pgr])
                self.mm(S, pg[:, 48:64], negO[:, :], ga(SPL), True, True, [g_r, cr], [pgr])
                self.cp(S, DVE, ga(BBL, 2), pg[:, 32:64], [pgr], [g_r])
                self.tt(S, DVE, g3(AA), gsb[:, :, 0:4], g3(BBL), ALU.subtract, [g_r], [g_r])
                yield
                self.tr(S, pg[0:16, 64:192], ga(AA), identF[:, :], [g_r, cr], [pgr])
                S.op(DVE, lambda e: e.reduce_max(out=am16[:, 0:1], in_=pg[0:16, 64:192], axis=AX.X), [pgr], [g_r])
                self.ts(S, DVE, am16[:, 4:20], identF[0:16, 0:16], am16[:, 0:1], None, ALU.mult, None, [g_r, cr], [g_r])
                yield
                self.mm(S, pg[:, 192:208], ones16[:, :], am16[:, 4:20], True, True, [g_r, cr], [pgr])
                self.cp(S, DVE, ga(AMX), pg[:, 192:208], [pgr], [g_r])
                yield
                for s in range(NSUB):
                    self.cp(S, DVE, g3(MST)[:, s, :], mcar[:, :], [g_r], [g_r])
                    self.tt(S, DVE, g3(MALL)[:, s, :], mcar[:, :], g3(AMX)[:, s, :], ALU.max, [g_r], [g_r])
                    self.tt(S, DVE, mcar[:, :], g3(BBL + 1)[:, s, :], g3(MALL)[:, s, :], ALU.add, [g_r], [g_r])
                self.tt(S, DVE, ga(T48), ga(AA), ga(MALL), ALU.subtract, [g_r], [g_r])
                self.tt(S, DVE, ga(T48 + 1), ga(MST), ga(MALL), ALU.subtract, [g_r], [g_r])
                self.stt(S, ga(T48 + 2), ga(BBL), -1.0, ga(MALL), ALU.mult, ALU.subtract, [g_r], [g_r])
                self.act(S, ga(WGF, 3), ga(T48, 3), AF.Exp, [g_r], [g_r])
                self.ts(S, DVE, ga(WK), ga(WGF), float(DH ** -0.5), None, ALU.mult, None, [g_r], [g_r])
                yield

            def tail(s_, ti_):
                nonlocal loaded
                gi = ti_ * NSUB + s_
                sl = gi % XSLOTS
                k2 = gi % 2
                self.out_proj(S, C, mix[:, k2, :], mix_r[k2], mixT[:, k2, :, :], mixT_r[k2], waout, waout_r,
                              C["xt"][:, sl, :], C["xt_r"][sl], [pb[0], pb[1]], [pb_r[0], pb_r[1]], dst, gi)
                if loaded < NT * NSUB and loaded % XSLOTS == sl:
                    self.load_x(S, C, src, loaded)
                    loaded += 1

            for ti in range(NT):
                for s in range(NSUB):
                    gi = ti * NSUB + s
                    sl = gi % XSLOTS
                    self.norm_T(S, C, C["xt"][:, sl, :], C["xt_r"][sl],
                                [(gT_mix, hT[:, :, s * 128:(s + 1) * 128], hT_r[s])])
                gs = gate_stages()
                next(gs)
                pend = None
                for i in range(16):
                    b, br = pb[ia % 6], pb_r[ia % 6]
                    ia += 1
                    for kc in range(KC):
                        self.mm(S, b[0:96, :], wain[:, kc, i * 96:(i + 1) * 96], hT[:, kc, :], kc == 0, kc == KC - 1,
                                hT_r + [wain_r[0]], [br])
                    r = i % 2
                    rw, rwr = raw[0:96, r, :], raw_r[r]
                    ac, acr = acc[0:96, r, :], acc_r[r]
                    tn, tnr = tnh[0:96, r, :], tnh_r[r]
                    self.cp(S, POOL, rw[:, 0:3], halo[0:96, i, :], [halo_r[i]], [rwr])
                    self.cp(S, ACT, rw[:, 3:T + 3], b[0:96, :], [br], [rwr])
                    self.cp(S, POOL, halo[0:96, i, :], rw[:, T:T + 3], [rwr], [halo_r[i]])
                    self.act(S, ac, b[0:96, :], AF.Identity, [br, cr], [acr], scale=cw[0:96, i, 3:4], bias=cb[0:96, i:i + 1])
                    for j in (2, 1, 0):
                        self.stt(S, ac, rw[:, j:j + T], cw[0:96, i, j:j + 1], ac, ALU.mult, ALU.add,
                                 [rwr, cr, acr], [acr])
                    if pend is not None:
                        pend()

                    def fin(i=i, ac=ac, acr=acr, tn=tn, tnr=tnr):
                        self.act(S, tn, ac, AF.Tanh, [acr], [tnr])
                        self.stt(S, qkT[0:96, i, :], tn, 1.0, ac, ALU.add, ALU.mult, [tnr, acr], [qk_r[i]])
                    pend = fin
                    if i in (1, 3, 5, 7):
                        next(gs)
                pend()
                for s in range(NSUB):
                    for g in range(2):
                        b, br = pb[ia % 6], pb_r[ia % 6]
                        ia += 1
                        for kc in range(KC):
                            self.mm(S, b[:, 0:384], hT[:, kc, s * 128:(s + 1) * 128],
                                    wain[:, kc, 1536 + g * 384:1536 + (g + 1) * 384], kc == 0, kc == KC - 1,
                                    [hT_r[s], wain_r[1]], [br])
                        self.cp(S, ACT if g == 0 else DVE, vw[:, s, 2 * g:2 * g + 2, 0:DH],
                                b[:, 0:384].rearrange("p (h e) -> p h e", h=2), [br], [vw_r[s]])
                    for g in range(2):
                        b, br = pb[ia % 6], pb_r[ia % 6]
                        ia += 1
                        for kc in range(KC):
                            self.mm(S, b[:, 0:384], hT[:, kc, s * 128:(s + 1) * 128],
                                    wain[:, kc, 2304 + g * 384:2304 + (g + 1) * 384], kc == 0, kc == KC - 1,
                                    [hT_r[s], wain_r[2]], [br])
                        g2 = G2[:, s, g * 384:(g + 1) * 384]
                        self.act(S, g2, b[:, 0:384], AF.Tanh, [br], [G2_r[s]], scale=0.5)
                        self.stt(S, g2, g2, 1.0, hgh[:, g * 384:(g + 1) * 384], ALU.add, ALU.mult, [G2_r[s], cr], [G2_r[s]])
                for j in range(2):
                    b, br = pb[ia % 6], pb_r[ia % 6]
                    ia += 1
                    for kc in range(KC):
                        self.mm(S, b[:, :], wain[:, kc, 3080 + j * 128:3080 + (j + 1) * 128], hT[:, kc, :], kc == 0,
                                kc == KC - 1, hT_r + [wain_r[3]], [br])
                    self.cp(S, ACT, qmT[:, j, :], b[:, :], [br], [qm_r[j]])
                self.mem_scores(S, qmT, qm_r, mkT, mem_r, pmT, pmT_r, [pb[0], pb[1]], [pb_r[0], pb_r[1]])
                for s in range(NSUB):
                    pT = C["pT"]
                    for j in range(8):
                        self.tr(S, pT[:, j * 96:(j + 1) * 96], qkT[0:96, 8 + j, s * 128:(s + 1) * 128],
                                C["identb"][0:96, 0:96], [qk_r[8 + j], cr], [C["pT_r"]])
                    for h in range(4):
                        self.act(S, ktok[:, s, h * DH:(h + 1) * DH], pT[:, h * DH:(h + 1) * DH], AF.Copy,
                                 [C["pT_r"], g_r], [ktok_r[s]], scale=wkv_[:, s, h:h + 1])
                pN = [pb[3], pb[4]]
                pNr = [pb_r[3], pb_r[4]]

                def emit_gbf(s_):
                    gbuf = (ti * NSUB + s_) % 2
                    for h in range(4):
                        self.act(S, Gbf[0:96, gbuf, h, :, :].rearrange("p j e -> p (j e)"),
                                 Cn[0:96, h, :, :].rearrange("p j e -> p (j e)"), AF.Copy, [Cn_r[h], g_r], [Gbf_r[gbuf][h]],
                                 scale=gv[0:96, s_, h:h + 1])

                def stage_A(s):
                    gi = ti * NSUB + s
                    k2 = gi % 2
                    cs = slice(s * 128, (s + 1) * 128)
                    pS, pSr = pb[2], pb_r[2]
                    for h in range(4):
                        for j in range(2):
                            self.mm(S, pS[:, h * 128:(h + 1) * 128], qkT[0:96, 8 + 2 * h + j, cs], qkT[0:96, 2 * h + j, cs],
                                    j == 0, j == 1, [qk_r[8 + 2 * h + j], qk_r[2 * h + j]], [pSr])
                    smv, smr = sm[:, k2, :, :], sm_r[k2]
                    for h in range(4):
                        self.stt(S, smv[:, h, :], pS[:, h * 128:(h + 1) * 128], wv[:, s, h:h + 1], maskc[:, h, :],
                                 ALU.mult, ALU.mult, [pSr, cr, g_r], [smr])
                    if s == 0:
                        emit_gbf(0)
                    for h in range(4):
                        pC, pCr = pb[5 + h % 2], pb_r[5 + h % 2]
                        for j in range(2):
                            self.mm(S, pC[0:96, j * (DH + 1):(j + 1) * (DH + 1)], ktok[:, s, h * DH + j * 96:h * DH + (j + 1) * 96],
                                    vw[:, s, h, :], True, True, [ktok_r[s], vw_r[s]], [pCr])
                        self.stt(S, Cn[0:96, h, :, :].rearrange("p j e -> p (j e)"),
                                 Cn[0:96, h, :, :].rearrange("p j e -> p (j e)"), gv[0:96, s, h:h + 1],
                                 pC[0:96, 0:2 * (DH + 1)], ALU.mult, ALU.add, [Cn_r[h], g_r, pCr], [Cn_r[h]])
                    gbuf = gi % 2
                    for h in range(4):
                        o = pN[h // 2][:, (h % 2) * (DH + 1):(h % 2 + 1) * (DH + 1)]
                        self.mm(S, o, smv[:, h, :], vw[:, s, h, :], True, False, [smr, vw_r[s]], [pNr[h // 2]])
                        for j in range(2):
                            self.mm(S, o, qkT[0:96, 2 * h + j, cs], Gbf[0:96, gbuf, h, j, :], False, j == 1,
                                    [qk_r[2 * h + j], Gbf_r[gbuf][h]], [pNr[h // 2]])
                    if s + 1 < NSUB:
                        emit_gbf(s + 1)

                def stage_B1(s):
                    gi = ti * NSUB + s
                    k2 = gi % 2
                    hs, hsr = hst[:, k2, :], hst_r[k2]
                    hmv, hmr = hm[:, k2, :, :], hm_r[k2]
                    for hp2 in range(2):
                        pv = pN[hp2][:, 0:2 * (DH + 1)].rearrange("p (h e) -> p h e", h=2)
                        self.act(S, hs[:, 2 * hp2:2 * hp2 + 2], pv[:, :, DH], AF.Abs, [pNr[hp2]], [hsr])
                    self.tt(S, DVE, hs[:, 0:4], hs[:, 0:4], flv[:, s, :], ALU.max, [hsr, g_r], [hsr])
                    S.op(DVE, (lambda hs=hs: (lambda e: e.reciprocal(out=hs[:, 4:8], in_=hs[:, 0:4])))(), [hsr], [hsr])
                    for hp2 in range(2):
                        pv = pN[hp2][:, 0:2 * (DH + 1)].rearrange("p (h e) -> p h e", h=2)
                        self.tt(S, DVE, hmv[:, 2 * hp2:2 * hp2 + 2, :], pv[:, :, 0:DH],
                                hs[:, 4 + 2 * hp2:6 + 2 * hp2].unsqueeze(2).to_broadcast([128, 2, DH]), ALU.mult,
                                [pNr[hp2], hsr], [hmr])

                def stage_B2(s):
                    gi = ti * NSUB + s
                    k2 = gi % 2
                    mx, mxr = mix[:, k2, :], mix_r[k2]
                    hs, hsr = hst[:, k2, :], hst_r[k2]
                    hmv, hmr = hm[:, k2, :, :], hm_r[k2]
                    for h in range(4):
                        self.act(S, hj[:, h, :], hmv[:, h, :], AF.Square, [hmr], [hj_r[h], hsr], accum_out=hs[:, 8 + h:9 + h])
                    self.ts(S, DVE, hs[:, 8:12], hs[:, 8:12], 1.0 / DH, EPS, ALU.mult, ALU.add, [hsr], [hsr])
                    self.tt(S, POOL, hs[:, 12:16], hs[:, 8:12], C["neghalf"][:, 0:1].to_broadcast([128, 4]), ALU.pow,
                            [hsr, cr], [hsr])
                    for h in range(4):
                        self.stt(S, mx[:, h * DH:(h + 1) * DH], hmv[:, h, :], hs[:, 12 + h:13 + h],
                                 G2[:, s, h * DH:(h + 1) * DH], ALU.mult, ALU.mult, [hmr, hsr, G2_r[s]], [mxr])
                    self.mem_pv(S, s, pmT, pmT_r, mvx, mem_r, pb[6], pb_r[6], rr, rr_r, mx, mxr)

                for s in range(NSUB):
                    stage_A(s)
                    stage_B1(s)
                    if s >= 1:
                        stage_B2(s - 1)
                    if s >= 2:
                        tail(s - 2, ti)
                stage_B2(NSUB - 1)
                tail(NSUB - 2, ti)
                tail(NSUB - 1, ti)
            self.stats["amix"] = S.emit(self.G)

    def mem_prologue(self, S, C, l, wmem, wmem_rs, memT, memT_rs, mkT, mvx, mem_r, pbank, pbank_r):
        cr = C["const_r"]
        xt, xt_r, xh, xh_r = C["xt"], C["xt_r"], C["xh"], C["xh_r"]
        self.load_w(S, wmem, wmem_rs[0], self.mem_w_kv[l], 0, 512)
        for mt in range(2):
            self.ld(S, SP, xt[:, mt, :], self.mem[mt * 128:(mt + 1) * 128, :], [xt_r[mt]], xt_r[mt])
            self.cp(S, DVE, xh[:, mt, :], xt[:, mt, :], [xt_r[mt]], [xh_r[mt]])
            pT = C["pT"]
            for kc in range(KC):
                self.tr(S, pT[:, kc * 128:(kc + 1) * 128], xh[:, mt, kc * 128:(kc + 1) * 128], C["identb"][:, :],
                        [xh_r[mt], cr], [C["pT_r"]])
            self.cp(S, DVE, memT[:, :, mt * 128:(mt + 1) * 128], pT[:, :].rearrange("p (a b) -> p a b", a=KC),
                    [C["pT_r"]], memT_rs)
        self.ms(S, POOL, mvx[:, :, :, 64:65], 1.0, [mem_r])
        for hp in range(2):
            for kc in range(KC):
                self.mm(S, pbank[:, 0:MEMT], wmem[:, kc, hp * 128:(hp + 1) * 128], memT[:, kc, 0:MEMT],
                        kc == 0, kc == KC - 1, memT_rs + wmem_rs, [pbank_r])
            self.cp(S, ACT, mkT[:, hp, :], pbank[:, 0:MEMT], [pbank_r], [mem_r])
        for mt in range(2):
            for kc in range(KC):
                self.mm(S, pbank[:, 0:256], memT[:, kc, mt * 128:(mt + 1) * 128], wmem[:, kc, 256:512],
                        kc == 0, kc == KC - 1, memT_rs + wmem_rs, [pbank_r])
            self.cp(S, DVE, mvx[:, mt, :, 0:64], pbank[:, 0:256].rearrange("p (h d) -> p h d", h=4),
                    [pbank_r], [mem_r])

    def mem_scores(self, S, qm, qm_rs, mkT, mem_r, pmT, pmT_r, banks, bank_rs):
        i = 0
        dbg = os.environ.get("K_DBG", "")
        for h in range(4):
            hp, hh = h // 2, h % 2
            if "h0" in dbg and hh == 1:
                continue
            for mt in range(2):
                b, br = banks[i % len(banks)], bank_rs[i % len(banks)]
                i += 1
                self.mm(S, b[:, :], mkT[hh * 64:(hh + 1) * 64, hp, mt * 128:(mt + 1) * 128],
                        qm[hh * 64:(hh + 1) * 64, hp, :], True, True, qm_rs + [mem_r], [br])
                if "noact" in dbg:
                    continue
                self.act(S, pmT[:, h * 2 + mt, :], b[:, :], AF.Exp, [br], [pmT_r], scale=0.125)

    def mem_pv(self, S, s, pmT, pmT_r, mvx, mem_r, pom, pom_r, rr, rr_r, mix, mix_r):
        first = True
        for h in range(4):
            for mt in range(2):
                self.mm(S, pom[:, h * 65:(h + 1) * 65], pmT[:, h * 2 + mt, s * 128:(s + 1) * 128], mvx[:, mt, h, :],
                        first, mt == 1, [pmT_r, mem_r], [pom_r], skip=True)
                first = False
        pv = pom[:, 0:260].rearrange("p (h e) -> p h e", h=4)
        S.op(DVE, lambda e: e.reciprocal(out=rr[:, 0:4], in_=pv[:, :, 64]), [pom_r], [rr_r])
        self.tt(S, DVE, mix[:, 768:1024].rearrange("p (h e) -> p h e", h=4), pv[:, :, 0:64],
                rr[:, 0:4].unsqueeze(2).to_broadcast([128, 4, 64]), ALU.mult, [pom_r, rr_r], [mix_r])

    def out_proj(self, S, C, mix, mix_r, mixT, mixT_r, wout, wout_rs, xt, xr, banks, bank_rs, dst, gi):
        cr = C["const_r"]
        pT = C["pT"]
        for kc in range(KC):
            self.tr(S, pT[:, kc * 128:(kc + 1) * 128], mix[:, kc * 128:(kc + 1) * 128], C["identb"][:, :],
                    [mix_r, cr], [C["pT_r"]])
        self.cp(S, ACT, mixT[:, :, :], pT[:, :].rearrange("p (a b) -> p a b", a=KC), [C["pT_r"]], [mixT_r])
        for hf in range(2):
            b, br = banks[hf], bank_rs[hf]
            for kc in range(KC):
                self.mm(S, b[:, :], mixT[:, kc, :], wout[:, kc, hf * 512:(hf + 1) * 512], kc == 0, kc == KC - 1,
                        [mixT_r] + wout_rs, [br])
            self.tt(S, DVE, xt[:, hf * 512:(hf + 1) * 512], b[:, :], xt[:, hf * 512:(hf + 1) * 512], ALU.add,
                    [br, xr], [xr])
        self.ld(S, SP, dst[gi * 128:(gi + 1) * 128, :], xt, [], xr, reads=[xr])

    def phase_bmix(self, src, dst, final):
        nc = self.nc
        S = Sched(nc)
        with ExitStack() as st:
            C = self.phase_consts(S, st, None)
            sb, ps = C["sb"], C["ps"]
            cr = C["const_r"]
            gT_kv = C["gT"][:, 2, :]
            gT_mix = C["gT"][:, 3, :]
            wkv = sb("wkv", (128, KC, 1536), BF16)
            wkv_r = S.regions(2, "wkv")
            wbin = sb("wbin", (128, KC, D), BF16)
            wbin_r = S.regions(1, "wbin")
            wbout = sb("wbout", (128, KC, D), BF16)
            wbout_r = S.regions(1, "wbout")
            biasT = sb("biasT", (128, 5, 12, 128), F32)
            bias_r = S.region("biasT")
            hT = sb("hT", (128, KC, T), BF16)
            hT_r = S.regions(NSUB, "hT")
            hTk = sb("hTk", (128, KC, T), BF16)
            hTk_r = S.regions(NSUB, "hTk")
            KTr = sb("KTr", (128, 2, 6, T), BF16)
            KT_r = [S.regions(6, f"KT{sl}_") for sl in range(2)]
            Vr = sb("Vr", (128, 2 * NSUB, 12, 65), BF16)
            V_r = S.regions(2 * NSUB, "V")
            QT = sb("QT", (128, 8, T), BF16)
            QT_r = S.regions(8, "QT")
            QA = sb("QA", (128, 6, T), BF16)
            QB = sb("QB", (128, 6, T), BF16)
            QAB_r = S.regions(6, "QAB")
            mkT = sb("mkT", (128, 2, MEMT), BF16)
            mvx = sb("mvx", (128, 2, 4, 65), BF16)
            mem_r = S.region("memkv")
            ssb = sb("ssb", (128, 3, 512), F32)
            ssb_r = S.regions(3, "ssb")
            pTs = sb("pTs", (128, 3, 4, 128), BF16)
            pTs_r = S.regions(3, "pTs")
            pmT = sb("pmT", (128, 8, T), BF16)
            pmT_r = S.region("pmT")
            mix = sb("mix", (128, 2, D), BF16)
            mix_r = S.regions(2, "mix")
            mixT = sb("mixT", (128, 2, KC, 128), BF16)
            mixT_r = S.regions(2, "mixT")
            rr = sb("rr", (128, 4, 4), F32)
            rr_r = S.regions(4, "rr")
            pb = [ps(f"pb{i}", (128, 512), F32) for i in range(7)]
            pb_r = S.regions(7, "pb")

            self.mem_prologue(S, C, 1, QT, QT_r, hT, hT_r, mkT, mvx, mem_r, pb[0], pb_r[0])
            self.load_w(S, wkv, wkv_r[0], self.w_kv, 0, 768)
            self.load_w(S, wkv, wkv_r[1], self.w_kv, 768, 1536)
            self.load_w(S, wbin, wbin_r[0], self.b_w_in, 0, D)
            self.load_w(S, wbout, wbout_r[0], self.b_w_out, 0, D)
            for kt in range(5):
                self.ld(S, SP, biasT[:, kt, :, :], self.relbias[:, kt, :, :], [bias_r], bias_r)
            self.ms(S, POOL, biasT[0:64, 0, :, 64:128], NEG, [bias_r])
            self.ms(S, POOL, biasT[64:128, 4, :, 0:64], NEG, [bias_r])
            self.ms(S, POOL, Vr[:, :, :, 64:65], 1.0, V_r)
            self.ms(S, POOL, QA[64:128, :, :], 0.0, QAB_r)
            self.ms(S, POOL, QB[0:64, :, :], 0.0, QAB_r)
            for gi in range(XSLOTS):
                self.load_x(S, C, src, gi)
            loaded = XSLOTS
            ia = 0
            isb = 0
            io = 0
            SB = [pb[2], pb[3], pb[6]]
            SBr = [pb_r[2], pb_r[3], pb_r[6]]
            STOP = int(os.environ.get("K_STOP", 99))

            def do_stats(ti_, s_):
                gi = ti_ * NSUB + s_
                sl = gi % XSLOTS
                return self.norm_stats(S, C, C["xt"][:, sl, :], C["xt_r"][sl])

            def do_tr(hnd, s_):
                self.norm_tr(S, C, hnd, [(gT_mix, hT[:, :, s_ * 128:(s_ + 1) * 128], hT_r[s_]),
                                         (gT_kv, hTk[:, :, s_ * 128:(s_ + 1) * 128], hTk_r[s_])])

            def tail(P_):
                nonlocal loaded
                sl = P_ % XSLOTS
                self.out_proj(S, C, mix[:, P_ % 2, :], mix_r[P_ % 2], mixT[:, P_ % 2, :, :], mixT_r[P_ % 2], wbout, wbout_r,
                              C["xt"][:, sl, :], C["xt_r"][sl], [pb[0], pb[1]], [pb_r[0], pb_r[1]], dst, P_)
                if loaded < NT * NSUB and loaded % XSLOTS == sl:
                    self.load_x(S, C, src, loaded)
                    loaded += 1

            for s in range(NSUB):
                do_tr(do_stats(0, s), s)
            for ti in range(NT):
                slot = ti % 2
                for j in range(6):
                    b, br = pb[ia % 2], pb_r[ia % 2]
                    ia += 1
                    for kc in range(KC):
                        self.mm(S, b[:, :], wkv[:, kc, j * 128:(j + 1) * 128], hTk[:, kc, :], kc == 0, kc == KC - 1,
                                hTk_r + [wkv_r[0]], [br])
                    self.cp(S, ACT, KTr[:, slot, j, :], b[:, :], [br], [KT_r[slot][j]])
                for s in range(NSUB):
                    vi = slot * NSUB + s
                    for (c0, c1, h0, h1) in ((768, 1280, 0, 8), (1280, 1536, 8, 12)):
                        b, br = pb[ia % 2], pb_r[ia % 2]
                        ia += 1
                        n = c1 - c0
                        for kc in range(KC):
                            self.mm(S, b[:, 0:n], hTk[:, kc, s * 128:(s + 1) * 128], wkv[:, kc, c0:c1], kc == 0,
                                    kc == KC - 1, [hTk_r[s], wkv_r[1]], [br])
                        self.cp(S, DVE, Vr[:, vi, h0:h1, 0:64], b[:, 0:n].rearrange("p (h d) -> p h d", d=64),
                                [br], [V_r[vi]])
                for j in range(8):
                    b, br = pb[ia % 2], pb_r[ia % 2]
                    ia += 1
                    for kc in range(KC):
                        self.mm(S, b[:, :], wbin[:, kc, j * 128:(j + 1) * 128], hT[:, kc, :], kc == 0, kc == KC - 1,
                                hT_r + wbin_r, [br])
                    if j < 6:
                        self.cp(S, ACT, QA[0:64, j, :], b[0:64, :], [br], [QAB_r[j]])
                        self.cp(S, DVE, QB[64:128, j, :], b[64:128, :], [br], [QAB_r[j]])
                    else:
                        self.cp(S, ACT, QT[:, j, :], b[:, :], [br], [QT_r[j]])
                self.mem_scores(S, QT[:, 6:8, :], QT_r[6:8], mkT, mem_r, pmT, pmT_r, [pb[0], pb[1]], [pb_r[0], pb_r[1]])
                hnds = {}
                for s in range(NSUB):
                    P = ti * NSUB + s
                    mx, mxr = mix[:, P % 2, :], mix_r[P % 2]
                    kts = [kt for kt in range(5) if P - 4 + kt >= 0]
                    steps = [(hg, kt) for hg in range(3) for kt in kts]
                    po_of = {}
                    for hg in range(3):
                        po_of[hg] = (pb[4 + io % 2], pb_r[4 + io % 2])
                        io += 1
                    pom, pomr = pb[4 + io % 2], pb_r[4 + io % 2]
                    io += 1
                    slots = {}

                    def emit_S(step):
                        nonlocal isb
                        hg, kt = step
                        kp = P - 4 + kt
                        sk = (kp // NSUB) % 2
                        subk = kp % NSUB
                        q3 = isb % 3
                        bs, bsr = SB[q3], SBr[q3]
                        sbuf, sbr = ssb[:, q3, :], ssb_r[q3]
                        pt, ptr = pTs[:, q3, :, :], pTs_r[q3]
                        isb += 1
                        for hl in range(4):
                            h = hg * 4 + hl
                            Qh = QA if h % 2 == 0 else QB
                            self.mm(S, bs[:, hl * 128:(hl + 1) * 128],
                                    KTr[:, sk, h // 2, subk * 128:(subk + 1) * 128],
                                    Qh[:, h // 2, s * 128:(s + 1) * 128], True, True,
                                    [KT_r[sk][h // 2], QAB_r[h // 2]], [bsr])
                        self.stt(S, sbuf, bs[:, :], 0.125,
                                 biasT[:, kt, hg * 4:(hg + 1) * 4, :].rearrange("p h q -> p (h q)"),
                                 ALU.mult, ALU.add, [bsr, bias_r], [sbr])
                        self.act(S, pt.rearrange("p h q -> p (h q)"), sbuf, AF.Exp, [sbr], [ptr])
                        slots[step] = (pt, ptr, sk, subk)

                    def emit_PV(step):
                        hg, kt = step
                        pt, ptr, sk, subk = slots.pop(step)
                        po, por = po_of[hg]
                        for hl in range(4):
                            h = hg * 4 + hl
                            self.mm(S, po[:, hl * 65:(hl + 1) * 65], pt[:, hl, :], Vr[:, sk * NSUB + subk, h, :],
                                    kt == kts[0] and hl == 0, kt == kts[-1], [ptr, V_r[sk * NSUB + subk]], [por],
                                    skip=True)
                        if kt == kts[-1]:
                            pv = po[:, 0:260].rearrange("p (h e) -> p h e", h=4)
                            rq, rqr = rr[:, hg, :], rr_r[hg]
                            S.op(DVE, (lambda pv=pv, rq=rq: (lambda e: e.reciprocal(out=rq, in_=pv[:, :, 64])))(), [por], [rqr])
                            self.tt(S, DVE, mx[:, hg * 256:(hg + 1) * 256].rearrange("p (h e) -> p h e", h=4),
                                    pv[:, :, 0:64], rq.unsqueeze(2).to_broadcast([128, 4, 64]), ALU.mult,
                                    [por, rqr], [mxr])

                    for i_ in range(min(2, len(steps))):
                        emit_S(steps[i_])
                    for i_, step in enumerate(steps):
                        if i_ + 2 < len(steps):
                            emit_S(steps[i_ + 2])
                        emit_PV(step)
                        if i_ == min(3, len(steps) - 1) and s >= 1:
                            tail(P - 1)
                            if ti + 1 < NT:
                                hnds[s - 1] = do_stats(ti + 1, s - 1)
                                if s >= 2:
                                    do_tr(hnds.pop(s - 2), s - 2)
                    self.mem_pv(S, s, pmT, pmT_r, mvx, mem_r, pom, pomr, rr[:, 3, :], rr_r[3], mx, mxr)
                tail(ti * NSUB + NSUB - 1)
                if ti + 1 < NT:
                    hnds[NSUB - 1] = do_stats(ti + 1, NSUB - 1)
                    do_tr(hnds.pop(NSUB - 2), NSUB - 2)
                    do_tr(hnds.pop(NSUB - 1), NSUB - 1)
            self.stats["bmix"] = S.emit(self.G)


def host_consts():
    ident = np.eye(128, dtype=np.float32)
    s = np.arange(128)[:, None]
    t = np.arange(128)[None, :]
    negU = np.where(s <= t, -1.0, 0.0).astype(np.float32)
    maskc = np.where(s <= t, np.float32(DH ** -0.5), np.float32(0.0)).astype(np.float32)
    return {"c_ident": ident, "c_negU": negU, "c_maskc": maskc}


def host_layout(inp):
    f = lambda a: np.ascontiguousarray(np.asarray(a, dtype=np.float32))
    g = np.stack([f(inp["norm_mix_g"])[0], f(inp["norm_ffn_g"])[0], f(inp["kv_norm_g"]),
                  f(inp["norm_mix_g"])[1], f(inp["norm_ffn_g"])[1]], 0)
    gT_all = np.ascontiguousarray(g.reshape(5, KC, 128).transpose(2, 0, 1))
    cwA = np.ascontiguousarray(f(inp["a_conv_w"])[0].reshape(4, 16, 96).transpose(2, 1, 0))
    cbA = np.ascontiguousarray(f(inp["a_conv_b"])[0].reshape(16, 96).T)
    cwF = np.ascontiguousarray(f(inp["ffn_conv_w"]).reshape(2, 3, NFT, 128).transpose(3, 0, 2, 1))
    cbF = np.ascontiguousarray(f(inp["ffn_conv_b"]).reshape(2, NFT, 128).transpose(2, 0, 1))
    rel = np.arange(768) - 127
    idx = np.clip(rel, -63, 128) + 63
    relext = f(inp["b_rel_bias"])[0][:, idx]
    kj = np.arange(128)[:, None, None]
    kt = np.arange(5)[None, :, None]
    qi = np.arange(128)[None, None, :]
    gidx = qi - kj + (4 - kt) * 128 + 127
    relbias = np.ascontiguousarray(relext[:, gidx].transpose(1, 2, 0, 3))
    shared = {
        "a_w_in": f(inp["a_w_in"])[0], "a_w_out": f(inp["a_w_out"])[0], "w_kv": f(inp["w_kv"]),
        "b_w_in": f(inp["b_w_in"])[0], "b_w_out": f(inp["b_w_out"])[0], "mem_w_kv": f(inp["mem_w_kv"]),
        "ffn_w_up": f(inp["ffn_w_up"]), "ffn_w_down": f(inp["ffn_w_down"]),
        "gT_all": gT_all, "final_g": f(inp["final_g"]).reshape(1, D), "gate_b": f(inp["a_gate_b"]).reshape(1, 8),
        "cwA": cwA, "cbA": cbA, "head_g": f(inp["a_head_g"]).reshape(1, AW), "cwF": cwF, "cbF": cbF,
        "relbias": relbias,
    }
    shared.update(host_consts())
    return shared


_CACHE = {}


def run(inputs, phases=("A_mix", "A_ffn", "B_mix", "B_ffn"), final_norm=True, ncores=8, trace=False):
    key = (tuple(phases), final_norm)
    if key not in _CACHE:
        _CACHE[key] = Prog(phases, final_norm).build()
    nc = _CACHE[key]
    shared = host_layout(inputs)
    x = np.asarray(inputs["x"], dtype=np.float32)
    mem = np.asarray(inputs["mem"], dtype=np.float32)
    in_maps = []
    for c in range(ncores):
        m = dict(shared)
        m["x"] = np.ascontiguousarray(x[c])
        m["mem"] = np.ascontiguousarray(mem[c])
        in_maps.append(m)
    res = run_bass_kernel_spmd(nc, in_maps, core_ids=list(range(ncores)), trace=trace)
    out = np.stack([np.asarray(r["out"]) for r in res.results], 0)
    return out, res


def kernel(**inputs):
    out, _ = run(inputs)
    return out.astype(np.float32)
```

```python
import numpy as np
from contextlib import ExitStack
import concourse.bass as bass
import concourse.mybir as mybir
from concourse.bass_types import AP
from concourse.bass_utils import run_bass_kernel_spmd

F32 = mybir.dt.float32
BF16 = mybir.dt.bfloat16
ALU = mybir.AluOpType
AF = mybir.ActivationFunctionType
AX = mybir.AxisListType

PE, ACT, DVE, POOL, SP = "tensor", "scalar", "vector", "gpsimd", "sync"
ENGS = (PE, ACT, DVE, POOL, SP)

D = 1024
KC = 8
SEQ = 4096
T = 512
import os
NT = int(os.environ.get("K_NT", SEQ // T))
SUB = 128
NSUB = T // SUB
DFF = 2816
NFT = DFF // 128
A_IN = 3336
AW = 768
DH = 192
MEMT = 256
EPS = 1e-6
NEG = -30000.0
XSLOTS = 6


class Region:
    __slots__ = ("name", "writer", "readers", "strict")

    def __init__(self, name):
        self.name = name
        self.writer = None
        self.readers = []
        self.strict = False


class _Op:
    __slots__ = ("eng", "idx", "fn", "waits", "needs_inc", "dma_key", "snap")

    def __init__(self, eng, idx, fn):
        self.eng = eng
        self.idx = idx
        self.fn = fn
        self.waits = []
        self.needs_inc = False
        self.dma_key = None
        self.snap = None


class Sched:
    def __init__(self, nc):
        self.nc = nc
        self.ops = {e: [] for e in ENGS}
        self.clock = {e: {x: -1 for x in ENGS} for e in ENGS}
        self.dclock = {e: {} for e in ENGS}
        self.dma_count = {}
        self.all_regions = []

    def region(self, name=None):
        r = Region(name or f"r{len(self.all_regions)}")
        self.all_regions.append(r)
        return r

    def regions(self, n, name="r"):
        return [self.region(f"{name}{i}") for i in range(n)]

    def _add(self, eng, fn, reads, writes, dma_key=None):
        o = _Op(eng, len(self.ops[eng]), fn)
        o.dma_key = dma_key
        deps = []
        for r in reads:
            if r.writer is not None:
                deps.append((r.writer, True))
        for w in writes:
            if w.writer is not None:
                deps.append((w.writer, w.strict))
            for rd in w.readers:
                deps.append((rd, w.strict))
        clk = self.clock[eng]
        dclk = self.dclock[eng]
        for tok, is_raw in deps:
            if tok[0] == "c":
                _, e2, n = tok
                if e2 == eng and (not is_raw or eng == PE):
                    continue
                if clk[e2] >= n:
                    continue
                o.waits.append(tok)
                self.ops[e2][n].needs_inc = True
                clk[e2] = n
                sn = self.ops[e2][n].snap
                for k, v in sn.items():
                    if k != eng and clk[k] < v:
                        clk[k] = v
            else:
                _, key, val = tok
                if dclk.get(key, 0) >= val:
                    continue
                cur = self.dma_count[key]
                o.waits.append(("d", key, cur))
                dclk[key] = cur
        o.snap = dict(clk)
        self.ops[eng].append(o)
        if dma_key is not None:
            self.dma_count[dma_key] = self.dma_count.get(dma_key, 0) + 16
            tok = ("d", dma_key, self.dma_count[dma_key])
        else:
            tok = ("c", eng, o.idx)
        for r in reads:
            r.readers.append(tok)
        for w in writes:
            w.writer = tok
            w.readers = []
        return o

    def op(self, eng, fn, reads=(), writes=()):
        return self._add(eng, fn, reads, writes)

    def dma(self, eng, fn, reads=(), writes=(), key=None):
        return self._add(eng, fn, reads, writes, dma_key=key.name + "@" + eng)

    def emit(self, G):
        nc = self.nc
        self._add(SP, None, list(self.all_regions), list(self.all_regions))
        keys = list(self.dma_count)
        slot = {}
        nsw = nhw = 0
        for k in keys:
            if k.endswith("@" + POOL):
                slot[k] = nsw
                nsw += 1
            else:
                slot[k] = G.NSW + nhw
                nhw += 1
        assert nsw <= G.NSW and nhw <= G.NDMA - G.NSW, (nsw, nhw)
        dsem = {k: G.dsem[slot[k]] for k in keys}
        dbase = {k: G.dbase[slot[k]] for k in keys}
        esem, ebase = G.esem, dict(G.ebase)
        G.phase += 1
        barv = G.phase
        cnt = {}
        for e in ENGS:
            c = 0
            arr = []
            for o in self.ops[e]:
                if o.needs_inc:
                    c += 1
                arr.append(c)
            cnt[e] = arr
            G.ebase[e] += c
        for k in keys:
            G.dbase[slot[k]] += self.dma_count[k]
        with nc.Block() as block:
            def make(e):
                def body(engh):
                    for o in self.ops[e]:
                        for w in o.waits:
                            if w[0] == "c":
                                engh.wait_ge(esem[w[1]], ebase[w[1]] + cnt[w[1]][w[2]])
                            else:
                                engh.wait_ge(dsem[w[1]], dbase[w[1]] + w[2])
                        if o.fn is None:
                            continue
                        ins = o.fn(engh)
                        if o.dma_key is not None:
                            ins.then_inc(dsem[o.dma_key], 16)
                        elif o.needs_inc:
                            ins.then_inc(esem[e], 1)
                    if e == SP:
                        engh.sem_inc(G.bar, 1)
                    else:
                        engh.wait_ge(G.bar, barv)
                return body

            for e in ENGS:
                getattr(block, e)(make(e))
        return {e: len(self.ops[e]) for e in ENGS}


class SemPool:
    NDMA = 56
    NSW = 16

    def __init__(self, nc, st):
        self.esem = {e: st.enter_context(nc.semaphore(f"s_{e}")) for e in ENGS}
        self.dsem = [st.enter_context(nc.semaphore(f"d_{i}")) for i in range(self.NDMA)]
        self.bar = st.enter_context(nc.semaphore("bar"))
        self.ebase = {e: 0 for e in ENGS}
        self.dbase = [0] * self.NDMA
        self.phase = 0
        allsem = list(self.esem.values()) + self.dsem + [self.bar]
        with nc.Block() as block:
            @block.gpsimd
            def _(g):
                for s in allsem:
                    g.sem_clear(s)
        nc.all_engine_barrier()


class Prog:
    def __init__(self, phases, final_norm=True):
        self.nc = nc = bass.Bass("TRN2", target_bir_lowering=False)
        self.phases = phases
        self.final_norm = final_norm
        din = lambda n, s: nc.dram_tensor(n, list(s), F32, kind="ExternalInput").ap()
        self.x = din("x", (SEQ, D))
        self.mem = din("mem", (MEMT, D))
        self.a_w_in = din("a_w_in", (D, A_IN))
        self.a_w_out = din("a_w_out", (D, D))
        self.w_kv = din("w_kv", (D, 1536))
        self.b_w_in = din("b_w_in", (D, D))
        self.b_w_out = din("b_w_out", (D, D))
        self.mem_w_kv = din("mem_w_kv", (2, D, 512))
        self.ffn_w_up = din("ffn_w_up", (2, D, 2 * DFF))
        self.ffn_w_down = din("ffn_w_down", (2, DFF, D))
        self.gT_all = din("gT_all", (128, 5, KC))
        self.final_g = din("final_g", (1, D))
        self.gate_b = din("gate_b", (1, 8))
        self.cwA = din("cwA", (96, 16, 4))
        self.cbA = din("cbA", (96, 16))
        self.head_g = din("head_g", (1, AW))
        self.cwF = din("cwF", (128, 2, NFT, 3))
        self.cbF = din("cbF", (128, 2, NFT))
        self.relbias = din("relbias", (128, 5, 12, 128))
        self.c_ident = din("c_ident", (128, 128))
        self.c_negU = din("c_negU", (128, 128))
        self.c_maskc = din("c_maskc", (128, 128))
        self.xa = nc.dram_tensor("xa", [SEQ, D], F32, kind="Internal").ap()
        self.xb = nc.dram_tensor("xb", [SEQ, D], F32, kind="Internal").ap()
        self.out = nc.dram_tensor("out", [SEQ, D], F32, kind="ExternalOutput").ap()
        self.stats = {}

    def mm(self, S, out, lhsT, rhs, start, stop, reads, writes, skip=False):
        S.op(PE, lambda e: e.matmul(out, lhsT=lhsT, rhs=rhs, start=start, stop=stop,
                                    skip_group_check=skip), reads, writes)

    def tr(self, S, out, in_, ident, reads, writes):
        S.op(PE, lambda e: e.transpose(out=out, in_=in_, identity=ident), reads, writes)

    def act(self, S, out, in_, func, reads, writes, **kw):
        S.op(ACT, lambda e: e.activation(out=out, in_=in_, func=func, **kw), reads, writes)

    def tt(self, S, eng, out, in0, in1, op, reads, writes):
        S.op(eng, lambda e: e.tensor_tensor(out=out, in0=in0, in1=in1, op=op), reads, writes)

    def ts(self, S, eng, out, in0, s1, s2, op0, op1, reads, writes):
        if op1 is None:
            S.op(eng, lambda e: e.tensor_scalar(out=out, in0=in0, scalar1=s1, scalar2=None, op0=op0),
                 reads, writes)
        else:
            S.op(eng, lambda e: e.tensor_scalar(out=out, in0=in0, scalar1=s1, scalar2=s2, op0=op0, op1=op1),
                 reads, writes)

    def stt(self, S, out, in0, scalar, in1, op0, op1, reads, writes):
        S.op(DVE, lambda e: e.scalar_tensor_tensor(out=out, in0=in0, scalar=scalar, in1=in1, op0=op0, op1=op1),
             reads, writes)

    def cp(self, S, eng, out, in_, reads, writes):
        if eng == ACT:
            S.op(ACT, lambda e: e.copy(out=out, in_=in_), reads, writes)
        else:
            S.op(eng, lambda e: e.tensor_copy(out=out, in_=in_), reads, writes)

    def ms(self, S, eng, ap, val, writes):
        S.op(eng, lambda e: e.memset(ap, val), (), writes)

    def ld(self, S, eng, out, in_, writes, key, reads=(), **kw):
        S.dma(eng, lambda e: e.dma_start(out=out, in_=in_, **kw), reads, writes, key=key)

    def load_w(self, S, dst, reg, src2d, c0, c1, kc0=0, kc1=None):
        kcn = src2d.shape[0] // 128
        kc1 = kcn if kc1 is None else kc1
        src = src2d.rearrange("(kc p) n -> p kc n", p=128)
        step = 2048
        for a in range(c0, c1, step):
            b = min(c1, a + step)
            self.ld(S, POOL, dst[:, kc0:kc1, a:b], src[:, kc0:kc1, a:b], [reg], reg)

    def norm_stats(self, S, C, xt_ap, xr):
        i = C["nrm_i"]
        C["nrm_i"] += 1
        k = i % 2
        ss = C["ss"][:, k, 0:1]
        ms_ = C["ss"][:, k, 1:2]
        rstd = C["ss"][:, k, 2:3]
        ssr = C["ss_r"][k]
        xh = C["xh"][:, k, :]
        xhr = C["xh_r"][k]
        self.act(S, xh, xt_ap, AF.Square, [xr], [xhr, ssr], accum_out=ss)
        self.ts(S, DVE, ms_, ss, 1.0 / D, EPS, ALU.mult, ALU.add, [ssr], [ssr])
        self.tt(S, POOL, rstd, ms_, C["neghalf"][:, 0:1], ALU.pow, [ssr, C["const_r"]], [ssr])
        self.act(S, xh, xt_ap, AF.Copy, [xr, ssr], [xhr], scale=rstd)
        return (xh, xhr)

    def norm_tr(self, S, C, hnd, outs):
        xh, xhr = hnd
        pT = C["pT"]
        for kc in range(KC):
            self.tr(S, pT[:, kc * 128:(kc + 1) * 128], xh[:, kc * 128:(kc + 1) * 128], C["identb"][:, :],
                    [xhr, C["const_r"]], [C["pT_r"]])
        pT3 = pT[:, :].rearrange("p (a b) -> p a b", a=KC)
        for gT, dst, dr in outs:
            self.tt(S, DVE, dst, pT3, gT.unsqueeze(2).to_broadcast([128, KC, 128]), ALU.mult,
                    [C["pT_r"], C["const_r"]], [dr])

    def norm_T(self, S, C, xt_ap, xr, outs):
        self.norm_tr(S, C, self.norm_stats(S, C, xt_ap, xr), outs)

    def phase_consts(self, S, st, which_gains):
        nc = self.nc
        C = {"nrm_i": 0}
        self.phase_i = getattr(self, "phase_i", 0) + 1
        pfx = f"p{self.phase_i}_"
        sb = lambda n, s, d: st.enter_context(nc.sbuf_tensor(pfx + n, list(s), d))
        ps = lambda n, s, d: st.enter_context(nc.psum_tensor(pfx + n, list(s), d))
        C["sb"] = sb
        C["ps"] = ps
        C["const_r"] = cr = S.region("const")
        C["identb"] = sb("identb", (128, 128), BF16)
        C["neghalf"] = sb("neghalf", (128, 1), F32)
        C["gT"] = sb("gT", (128, 5, KC), F32)
        self.ld(S, POOL, C["identb"][:, :], self.c_ident[:, :], [cr], cr)
        self.ld(S, SP, C["gT"][:, :, :], self.gT_all[:, :, :], [cr], cr)
        self.ms(S, DVE, C["neghalf"][:, :], -0.5, [cr])
        C["ss"] = sb("ss", (128, 2, 4), F32)
        C["ss_r"] = S.regions(2, "ss")
        C["xh"] = sb("xh", (128, 2, D), BF16)
        C["xh_r"] = S.regions(2, "xh")
        C["pT"] = ps("pT", (128, D), BF16)
        C["pT_r"] = S.region("pT")
        C["xt"] = sb("xt", (128, XSLOTS, D), F32)
        C["xt_r"] = S.regions(XSLOTS, "xt")
        return C

    def load_x(self, S, C, src, gi):
        sl = gi % XSLOTS
        self.ld(S, SP, C["xt"][:, sl, :], src[gi * 128:(gi + 1) * 128, :], [C["xt_r"][sl]], C["xt_r"][sl])

    def phase_ffn(self, l, src, dst, final):
        nc = self.nc
        S = Sched(nc)
        with ExitStack() as st:
            C = self.phase_consts(S, st, None)
            sb, ps = C["sb"], C["ps"]
            gT = C["gT"][:, 1 if l == 0 else 4, :]
            wup = sb("wup", (128, KC, 2 * DFF), BF16)
            wup_r = S.regions(4, "wup")
            wdn = sb("wdn", (128, NFT, D), BF16)
            wdn_r = S.regions(2, "wdn")
            hT = sb("hT", (128, KC, T), BF16)
            hT_r = S.regions(NSUB, "hT")
            aT = sb("aT", (128, NFT, T), BF16)
            aT_r = S.regions(NFT, "aT")
            graw = sb("graw", (128, 2, T + 2), F32)
            graw_r = S.regions(2, "graw")
            acc = sb("acc", (128, 2, T), F32)
            acc_r = S.regions(2, "acc")
            tnh = sb("tnh", (128, 2, T), F32)
            tnh_r = S.regions(2, "tnh")
            halo = sb("halo", (128, NFT, 2), F32)
            halo_r = S.regions(NFT, "halo")
            cw = sb("cw", (128, NFT, 3), F32)
            cb = sb("cb", (128, NFT), F32)
            cr = C["const_r"]
            pb = [ps(f"pb{i}", (128, 512), F32) for i in range(7)]
            pb_r = S.regions(7, "pb")
            if final:
                fg = sb("fg", (128, D), F32)
                self.ld(S, SP, fg[:, :], self.final_g.partition_broadcast(128), [cr], cr)
                fss = sb("fss", (128, 2, 4), F32)
                fss_r = S.regions(2, "fss")
                for r_ in C["xh_r"]:
                    r_.strict = True
            for gi in range(min(XSLOTS, NSUB + 2)):
                self.load_x(S, C, src, gi)
            loaded = min(XSLOTS, NSUB + 2)
            self.ld(S, SP, cw[:, :, :], self.cwF[:, l, :, :], [cr], cr)
            self.ld(S, SP, cb[:, :], self.cbF[:, l, :], [cr], cr)
            self.ts(S, POOL, cw[:, :, :], cw[:, :, :], 0.5, None, ALU.mult, None, [cr], [cr])
            self.ts(S, POOL, cb[:, :], cb[:, :], 0.5, None, ALU.mult, None, [cr], [cr])
            self.ms(S, POOL, halo[:, :, :], 0.0, halo_r)
            wu = self.ffn_w_up[l]
            for c in range(4):
                self.load_w(S, wup, wup_r[c], wu, c * 1408, (c + 1) * 1408)
            wd = self.ffn_w_down[l]
            self.load_w(S, wdn, wdn_r[0], wd, 0, D, 0, 11)
            self.load_w(S, wdn, wdn_r[1], wd, 0, D, 11, 22)

            pbi = 0

            def do_stats(ti, s):
                gi = ti * NSUB + s
                sl = gi % XSLOTS
                return self.norm_stats(S, C, C["xt"][:, sl, :], C["xt_r"][sl])

            def do_tr(hnd, s):
                self.norm_tr(S, C, hnd, [(gT, hT[:, :, s * 128:(s + 1) * 128], hT_r[s])])

            for s in range(NSUB):
                do_tr(do_stats(0, s), s)
            for ti in range(NT):
                for j in range(NFT):
                    pu, pur = pb[pbi % 6], pb_r[pbi % 6]
                    pg, pgr = pb[(pbi + 1) % 6], pb_r[(pbi + 1) % 6]
                    pbi += 2
                    for kc in range(KC):
                        self.mm(S, pg[:, :], wup[:, kc, DFF + j * 128:DFF + (j + 1) * 128], hT[:, kc, :],
                                kc == 0, kc == KC - 1, hT_r + [wup_r[2 + j // 11]], [pgr])
                    for kc in range(KC):
                        self.mm(S, pu[:, :], wup[:, kc, j * 128:(j + 1) * 128], hT[:, kc, :],
                                kc == 0, kc == KC - 1, hT_r + [wup_r[j // 11]], [pur])
                    r = j % 2
                    gr, grr = graw[:, r, :], graw_r[r]
                    ac, acr = acc[:, r, :], acc_r[r]
                    tn, tnr = tnh[:, r, :], tnh_r[r]
                    self.cp(S, POOL, gr[:, 0:2], halo[:, j, :], [halo_r[j]], [grr])
                    self.cp(S, ACT, gr[:, 2:T + 2], pg[:, :], [pgr], [grr])
                    self.cp(S, POOL, halo[:, j, :], gr[:, T:T + 2], [grr], [halo_r[j]])
                    self.act(S, ac, pg[:, :], AF.Identity, [pgr, cr], [acr], scale=cw[:, j, 2:3], bias=cb[:, j:j + 1])
                    self.stt(S, ac, gr[:, 1:T + 1], cw[:, j, 1:2], ac, ALU.mult, ALU.add, [grr, cr, acr], [acr])
                    self.stt(S, ac, gr[:, 0:T], cw[:, j, 0:1], ac, ALU.mult, ALU.add, [grr, cr, acr], [acr])
                    self.act(S, tn, ac, AF.Tanh, [acr], [tnr])
                    self.tt(S, DVE, gr[:, 2:T + 2], ac, pu[:, :], ALU.mult, [acr, pur], [grr])
                    self.stt(S, aT[:, j, :], tn, 1.0, gr[:, 2:T + 2], ALU.add, ALU.mult, [tnr, grr], [aT_r[j]])
                hnds = {}
                for s in range(NSUB):
                    gi = ti * NSUB + s
                    sl = gi % XSLOTS
                    xt = C["xt"][:, sl, :]
                    xr = C["xt_r"][sl]
                    for hf in range(2):
                        po, por = pb[6], pb_r[6]
                        if hf == 1:
                            po, por = pb[pbi % 6], pb_r[pbi % 6]
                            pbi += 1
                        for kc in range(NFT):
                            self.mm(S, po[:, :], aT[:, kc, s * 128:(s + 1) * 128], wdn[:, kc, hf * 512:(hf + 1) * 512],
                                    kc == 0, kc == NFT - 1, [aT_r[kc], wdn_r[kc // 11]], [por])
                        self.tt(S, DVE, xt[:, hf * 512:(hf + 1) * 512], po[:, :], xt[:, hf * 512:(hf + 1) * 512],
                                ALU.add, [por, xr], [xr])
                    if final:
                        k = gi % 2
                        ss = fss[:, k, 0:1]
                        ms_ = fss[:, k, 1:2]
                        rstd = fss[:, k, 2:3]
                        self.act(S, C["xh"][:, k, :], xt, AF.Square, [xr], [C["xh_r"][k], fss_r[k]], accum_out=ss)
                        self.ts(S, DVE, ms_, ss, 1.0 / D, EPS, ALU.mult, ALU.add, [fss_r[k]], [fss_r[k]])
                        self.tt(S, POOL, rstd, ms_, C["neghalf"][:, 0:1], ALU.pow, [fss_r[k], cr], [fss_r[k]])
                        self.stt(S, xt, xt, rstd, fg[:, :], ALU.mult, ALU.mult, [xr, fss_r[k], cr], [xr])
                    self.ld(S, SP, dst[gi * 128:(gi + 1) * 128, :], xt, [], xr, reads=[xr])
                    if loaded < NT * NSUB and loaded % XSLOTS == sl:
                        self.load_x(S, C, src, loaded)
                        loaded += 1
                    if ti + 1 < NT:
                        hnds[s] = do_stats(ti + 1, s)
                        if s >= 1:
                            do_tr(hnds.pop(s - 1), s - 1)
                if ti + 1 < NT:
                    do_tr(hnds.pop(NSUB - 1), NSUB - 1)
            self.stats[f"ffn{l}"] = S.emit(self.G)

    def build(self):
        chain = {"A_mix": self.phase_amix, "A_ffn": lambda s, d, f: self.phase_ffn(0, s, d, False),
                 "B_mix": self.phase_bmix, "B_ffn": lambda s, d, f: self.phase_ffn(1, s, d, f)}
        src = self.x
        scr = [self.xa, self.xb]
        with ExitStack() as gst:
            self.G = SemPool(self.nc, gst)
            for i, ph in enumerate(self.phases):
                last = i == len(self.phases) - 1
                dst = self.out if last else scr[i % 2]
                chain[ph](src, dst, last and self.final_norm)
                src = dst
        return self.nc

    def phase_amix(self, src, dst, final):
        nc = self.nc
        S = Sched(nc)
        with ExitStack() as st:
            C = self.phase_consts(S, st, None)
            sb, ps = C["sb"], C["ps"]
            cr = C["const_r"]
            gT_mix = C["gT"][:, 0, :]
            wain = sb("wain", (128, KC, A_IN), BF16)
            wain_r = S.regions(4, "wain")
            waout = sb("waout", (128, KC, D), BF16)
            waout_r = S.regions(1, "waout")
            hT = sb("hT", (128, KC, T), BF16)
            hT_r = S.regions(NSUB, "hT")
            qkT = sb("qkT", (128, 16, T), BF16)
            qk_r = S.regions(16, "qk")
            qmT = sb("qmT", (128, 2, T), BF16)
            qm_r = S.regions(2, "qm")
            raw = sb("raw", (128, 2, T + 3), F32)
            raw_r = S.regions(2, "raw")
            acc = sb("acc", (128, 2, T), F32)
            acc_r = S.regions(2, "acc")
            tnh = sb("tnh", (128, 2, T), F32)
            tnh_r = S.regions(2, "tnh")
            halo = sb("halo", (128, 16, 3), F32)
            halo_r = S.regions(16, "halo")
            cw = sb("cw", (128, 16, 4), F32)
            cb = sb("cb", (128, 16), F32)
            ktok = sb("ktok", (128, NSUB, AW), BF16)
            ktok_r = S.regions(NSUB, "ktok")
            vw = sb("vw", (128, NSUB, 4, DH + 1), BF16)
            vw_r = S.regions(NSUB, "vw")
            G2 = sb("G2", (128, NSUB, AW), F32)
            G2_r = S.regions(NSUB, "G2")
            hgh = sb("hgh", (128, AW), F32)
            gb_bc = sb("gb_bc", (128, 8), F32)
            gsb = sb("gsb", (128, NSUB, 8), F32)
            gw = sb("gw", (128, 16, 16), F32)
            g_r = S.region("gates")
            EP, SPL, AA, BBL, AMX, MALL, MST, T48 = 0, 1, 2, 3, 5, 6, 7, 8
            WGF = 11
            mcar = sb("mcar", (128, 4), F32)
            am16 = sb("am16", (16, 20), F32)
            identF = sb("identF", (128, 128), F32)
            negU = sb("negU", (128, 128), F32)
            negO = sb("negO", (128, 128), F32)
            ones16 = sb("ones16", (16, 128), F32)
            maskc = sb("maskc", (128, 4, 128), F32)
            Cn = sb("Cn", (128, 4, 2, DH + 1), F32)
            Cn_r = S.regions(4, "Cn")
            Gbf = sb("Gbf", (128, 2, 4, 2, DH + 1), BF16)
            Gbf_r = [S.regions(4, "GbfA"), S.regions(4, "GbfB")]
            sm = sb("sm", (128, 2, 4, 128), BF16)
            sm_r = S.regions(2, "sm")
            hm = sb("hm", (128, 2, 4, DH), F32)
            hm_r = S.regions(2, "hm")
            hj = sb("hj", (128, 4, DH), BF16)
            hj_r = S.regions(4, "hj")
            for r_ in hj_r:
                r_.strict = True
            hst = sb("hst", (128, 2, 16), F32)
            hst_r = S.regions(2, "hst")
            mkT = sb("mkT", (128, 2, MEMT), BF16)
            mvx = sb("mvx", (128, 2, 4, 65), BF16)
            mem_r = S.region("memkv")
            pmT = sb("pmT", (128, 8, T), BF16)
            pmT_r = S.region("pmT")
            mix = sb("mix", (128, 2, D), BF16)
            mix_r = S.regions(2, "mix")
            mixT = sb("mixT", (128, 2, KC, 128), BF16)
            mixT_r = S.regions(2, "mixT")
            rr = sb("rr", (128, 4), F32)
            rr_r = S.region("rr")
            pb = [ps(f"pb{i}", (128, 512), F32) for i in range(7)]
            pb_r = S.regions(7, "pb")

            self.mem_prologue(S, C, 0, qkT[:, 0:8, :], qk_r[0:8], hT, hT_r, mkT, mvx, mem_r, pb[0], pb_r[0])
            self.load_w(S, wain, wain_r[0], self.a_w_in, 0, 1536)
            self.load_w(S, wain, wain_r[1], self.a_w_in, 1536, 2304)
            self.load_w(S, wain, wain_r[2], self.a_w_in, 2304, 3080)
            self.load_w(S, wain, wain_r[3], self.a_w_in, 3080, A_IN)
            self.load_w(S, waout, waout_r[0], self.a_w_out, 0, D)
            self.ld(S, SP, cw[0:96, :, :], self.cwA[:, :, :], [cr], cr)
            self.ld(S, SP, cb[0:96, :], self.cbA[:, :], [cr], cr)
            self.ld(S, SP, hgh[:, :], self.head_g.partition_broadcast(128), [cr], cr)
            self.ld(S, SP, gb_bc[:, :], self.gate_b.partition_broadcast(128), [cr], cr)
            self.ld(S, SP, identF[:, :], self.c_ident[:, :], [cr], cr)
            self.ld(S, SP, negU[:, :], self.c_negU[:, :], [cr], cr)
            for h in range(4):
                self.ld(S, SP, maskc[:, h, :], self.c_maskc[:, :], [cr], cr)
            self.ts(S, POOL, cw[0:96, :, :], cw[0:96, :, :], 0.5, None, ALU.mult, None, [cr], [cr])
            self.ts(S, POOL, cb[0:96, :], cb[0:96, :], 0.5, None, ALU.mult, None, [cr], [cr])
            self.ts(S, POOL, hgh[:, :], hgh[:, :], 0.5, None, ALU.mult, None, [cr], [cr])
            self.ms(S, POOL, negO[:, :], -1.0, [cr])
            self.ms(S, POOL, ones16[:, :], 1.0, [cr])
            self.ms(S, POOL, halo[:, :, :], 0.0, halo_r)
            self.ms(S, POOL, Cn[:, :, :, :], 0.0, Cn_r)
            self.ms(S, POOL, mcar[:, :], 0.0, [g_r])
            self.ms(S, POOL, vw[:, :, :, DH:DH + 1], 1.0, vw_r)
            for gi in range(XSLOTS):
                self.load_x(S, C, src, gi)
            loaded = XSLOTS
            ia = 0
            WK = 14
            ga = lambda i, n=1: gw[:, i:i + n, :].rearrange("p a c -> p (a c)")
            g3 = lambda i: gw[:, i, :].rearrange("p (s h) -> p s h", s=NSUB)
            wv, gv, flv, wkv_ = g3(WGF), g3(WGF + 1), g3(WGF + 2), g3(WK)
            pg, pgr = pb[6], pb_r[6]

            def gate_stages():
                for s in range(NSUB):
                    for kc in range(KC):
                        self.mm(S, pg[:, s * 8:(s + 1) * 8], hT[:, kc, s * 128:(s + 1) * 128], wain[:, kc, 3072:3080],
                                kc == 0, kc == KC - 1, [hT_r[s], wain_r[2]], [pgr])
                self.tt(S, DVE, gsb[:, :, :], pg[:, 0:32].rearrange("p (s g) -> p s g", s=NSUB),
                        gb_bc[:, :].unsqueeze(1).to_broadcast([128, NSUB, 8]), ALU.add, [pgr, cr], [g_r])
                self.act(S, g3(EP), gsb[:, :, 4:8], AF.Exp, [g_r], [g_r], scale=-1.0)
                self.act(S, ga(SPL), ga(EP), AF.Ln, [g_r], [g_r], bias=1.0)
                yield
                self.mm(S, pg[:, 32:48], negU[:, :], ga(SPL), True, True, [g_r, cr], [pgr])
                self.mm(S, pg[:, 48:64], negO[:, :], ga(SPL), True, True, [g_r, cr], [pgr])
                self.cp(S, DVE, ga(BBL, 2), pg[:, 32:64], [pgr], [g_r])
                self.tt(S, DVE, g3(AA), gsb[:, :, 0:4], g3(BBL), ALU.subtract, [g_r], [g_r])
                yield
                self.tr(S, pg[0:16, 64:192], ga(AA), identF[:, :], [g_r, cr], [pgr])
                S.op(DVE, lambda e: e.reduce_max(out=am16[:, 0:1], in_=pg[0:16, 64:192], axis=AX.X), [pgr], [g_r])
                self.ts(S, DVE, am16[:, 4:20], identF[0:16, 0:16], am16[:, 0:1], None, ALU.mult, None, [g_r, cr], [g_r])
                yield
                self.mm(S, pg[:, 192:208], ones16[:, :], am16[:, 4:20], True, True, [g_r, cr], [pgr])
                self.cp(S, DVE, ga(AMX), pg[:, 192:208], [pgr], [g_r])
                yield
                for s in range(NSUB):
                    self.cp(S, DVE, g3(MST)[:, s, :], mcar[:, :], [g_r], [g_r])
                    self.tt(S, DVE, g3(MALL)[:, s, :], mcar[:, :], g3(AMX)[:, s, :], ALU.max, [g_r], [g_r])
                    self.tt(S, DVE, mcar[:, :], g3(BBL + 1)[:, s, :], g3(MALL)[:, s, :], ALU.add, [g_r], [g_r])
                self.tt(S, DVE, ga(T48), ga(AA), ga(MALL), ALU.subtract, [g_r], [g_r])
                self.tt(S, DVE, ga(T48 + 1), ga(MST), ga(MALL), ALU.subtract, [g_r], [g_r])
                self.stt(S, ga(T48 + 2), ga(BBL), -1.0, ga(MALL), ALU.mult, ALU.subtract, [g_r], [g_r])
                self.act(S, ga(WGF, 3), ga(T48, 3), AF.Exp, [g_r], [g_r])
                self.ts(S, DVE, ga(WK), ga(WGF), float(DH ** -0.5), None, ALU.mult, None, [g_r], [g_r])
                yield

            def tail(s_, ti_):
                nonlocal loaded
                gi = ti_ * NSUB + s_
                sl = gi % XSLOTS
                k2 = gi % 2
                self.out_proj(S, C, mix[:, k2, :], mix_r[k2], mixT[:, k2, :, :], mixT_r[k2], waout, waout_r,
                              C["xt"][:, sl, :], C["xt_r"][sl], [pb[0], pb[1]], [pb_r[0], pb_r[1]], dst, gi)
                if loaded < NT * NSUB and loaded % XSLOTS == sl:
                    self.load_x(S, C, src, loaded)
                    loaded += 1

            for ti in range(NT):
                NORM_IL = False
                for s in range(NSUB if (ti == 0 or not NORM_IL) else 0):
                    gi = ti * NSUB + s
                    sl = gi % XSLOTS
                    self.norm_T(S, C, C["xt"][:, sl, :], C["xt_r"][sl],
                                [(gT_mix, hT[:, :, s * 128:(s + 1) * 128], hT_r[s])])
                gs = gate_stages()
                next(gs)
                pend = None
                for i in range(16):
                    b, br = pb[ia % 6], pb_r[ia % 6]
                    ia += 1
                    for kc in range(KC):
                        self.mm(S, b[0:96, :], wain[:, kc, i * 96:(i + 1) * 96], hT[:, kc, :], kc == 0, kc == KC - 1,
                                hT_r + [wain_r[0]], [br])
                    r = i % 2
                    rw, rwr = raw[0:96, r, :], raw_r[r]
                    ac, acr = acc[0:96, r, :], acc_r[r]
                    tn, tnr = tnh[0:96, r, :], tnh_r[r]
                    self.cp(S, POOL, rw[:, 0:3], halo[0:96, i, :], [halo_r[i]], [rwr])
                    self.cp(S, ACT, rw[:, 3:T + 3], b[0:96, :], [br], [rwr])
                    self.cp(S, POOL, halo[0:96, i, :], rw[:, T:T + 3], [rwr], [halo_r[i]])
                    self.act(S, ac, b[0:96, :], AF.Identity, [br, cr], [acr], scale=cw[0:96, i, 3:4], bias=cb[0:96, i:i + 1])
                    for j in (2, 1, 0):
                        self.stt(S, ac, rw[:, j:j + T], cw[0:96, i, j:j + 1], ac, ALU.mult, ALU.add,
                                 [rwr, cr, acr], [acr])
                    if pend is not None:
                        pend()

                    def fin(i=i, ac=ac, acr=acr, tn=tn, tnr=tnr):
                        self.act(S, tn, ac, AF.Tanh, [acr], [tnr])
                        self.stt(S, qkT[0:96, i, :], tn, 1.0, ac, ALU.add, ALU.mult, [tnr, acr], [qk_r[i]])
                    pend = fin
                    if i in (1, 3, 5, 7):
                        next(gs)
                pend()
                for s in range(NSUB):
                    for g in range(2):
                        b, br = pb[ia % 6], pb_r[ia % 6]
                        ia += 1
                        for kc in range(KC):
                            self.mm(S, b[:, 0:384], hT[:, kc, s * 128:(s + 1) * 128],
                                    wain[:, kc, 1536 + g * 384:1536 + (g + 1) * 384], kc == 0, kc == KC - 1,
                                    [hT_r[s], wain_r[1]], [br])
                        self.cp(S, ACT if g == 0 else DVE, vw[:, s, 2 * g:2 * g + 2, 0:DH],
                                b[:, 0:384].rearrange("p (h e) -> p h e", h=2), [br], [vw_r[s]])
                    for g in range(2):
                        b, br = pb[ia % 6], pb_r[ia % 6]
                        ia += 1
                        for kc in range(KC):
                            self.mm(S, b[:, 0:384], hT[:, kc, s * 128:(s + 1) * 128],
                                    wain[:, kc, 2304 + g * 384:2304 + (g + 1) * 384], kc == 0, kc == KC - 1,
                                    [hT_r[s], wain_r[2]], [br])
                        g2 = G2[:, s, g * 384:(g + 1) * 384]
                        self.act(S, g2, b[:, 0:384], AF.Tanh, [br], [G2_r[s]], scale=0.5)
                        self.stt(S, g2, g2, 1.0, hgh[:, g * 384:(g + 1) * 384], ALU.add, ALU.mult, [G2_r[s], cr], [G2_r[s]])
                for j in range(2):
                    b, br = pb[ia % 6], pb_r[ia % 6]
                    ia += 1
                    for kc in range(KC):
                        self.mm(S, b[:, :], wain[:, kc, 3080 + j * 128:3080 + (j + 1) * 128], hT[:, kc, :], kc == 0,
                                kc == KC - 1, hT_r + [wain_r[3]], [br])
                    self.cp(S, ACT, qmT[:, j, :], b[:, :], [br], [qm_r[j]])
                self.mem_scores(S, qmT, qm_r, mkT, mem_r, pmT, pmT_r, [pb[0], pb[1]], [pb_r[0], pb_r[1]])
                def emit_ktok(s):
                    pT = C["pT"]
                    for j in range(8):
                        self.tr(S, pT[:, j * 96:(j + 1) * 96], qkT[0:96, 8 + j, s * 128:(s + 1) * 128],
                                C["identb"][0:96, 0:96], [qk_r[8 + j], cr], [C["pT_r"]])
                    for h in range(4):
                        self.act(S, ktok[:, s, h * DH:(h + 1) * DH], pT[:, h * DH:(h + 1) * DH], AF.Copy,
                                 [C["pT_r"], g_r], [ktok_r[s]], scale=wkv_[:, s, h:h + 1])

                emit_ktok(0)
                pN = [pb[3], pb[4]]
                pNr = [pb_r[3], pb_r[4]]

                def emit_gbf(s_):
                    gbuf = (ti * NSUB + s_) % 2
                    for h in range(4):
                        self.act(S, Gbf[0:96, gbuf, h, :, :].rearrange("p j e -> p (j e)"),
                                 Cn[0:96, h, :, :].rearrange("p j e -> p (j e)"), AF.Copy, [Cn_r[h], g_r], [Gbf_r[gbuf][h]],
                                 scale=gv[0:96, s_, h:h + 1])

                def stage_A(s):
                    gi = ti * NSUB + s
                    k2 = gi % 2
                    cs = slice(s * 128, (s + 1) * 128)
                    pS, pSr = pb[2], pb_r[2]
                    for h in range(4):
                        for j in range(2):
                            self.mm(S, pS[:, h * 128:(h + 1) * 128], qkT[0:96, 8 + 2 * h + j, cs], qkT[0:96, 2 * h + j, cs],
                                    j == 0, j == 1, [qk_r[8 + 2 * h + j], qk_r[2 * h + j]], [pSr])
                    smv, smr = sm[:, k2, :, :], sm_r[k2]
                    for h in range(4):
                        self.stt(S, smv[:, h, :], pS[:, h * 128:(h + 1) * 128], wv[:, s, h:h + 1], maskc[:, h, :],
                                 ALU.mult, ALU.mult, [pSr, cr, g_r], [smr])
                    if s == 0:
                        emit_gbf(0)
                    for h in range(4):
                        pC, pCr = pb[5 + h % 2], pb_r[5 + h % 2]
                        for j in range(2):
                            self.mm(S, pC[0:96, j * (DH + 1):(j + 1) * (DH + 1)], ktok[:, s, h * DH + j * 96:h * DH + (j + 1) * 96],
                                    vw[:, s, h, :], True, True, [ktok_r[s], vw_r[s]], [pCr])
                        self.stt(S, Cn[0:96, h, :, :].rearrange("p j e -> p (j e)"),
                                 Cn[0:96, h, :, :].rearrange("p j e -> p (j e)"), gv[0:96, s, h:h + 1],
                                 pC[0:96, 0:2 * (DH + 1)], ALU.mult, ALU.add, [Cn_r[h], g_r, pCr], [Cn_r[h]])
                    gbuf = gi % 2
                    for h in range(4):
                        o = pN[h // 2][:, (h % 2) * (DH + 1):(h % 2 + 1) * (DH + 1)]
                        self.mm(S, o, smv[:, h, :], vw[:, s, h, :], True, False, [smr, vw_r[s]], [pNr[h // 2]])
                        for j in range(2):
                            self.mm(S, o, qkT[0:96, 2 * h + j, cs], Gbf[0:96, gbuf, h, j, :], False, j == 1,
                                    [qk_r[2 * h + j], Gbf_r[gbuf][h]], [pNr[h // 2]])
                    if s + 1 < NSUB:
                        emit_gbf(s + 1)

                def stage_B1(s):
                    gi = ti * NSUB + s
                    k2 = gi % 2
                    hs, hsr = hst[:, k2, :], hst_r[k2]
                    hmv, hmr = hm[:, k2, :, :], hm_r[k2]
                    for hp2 in range(2):
                        pv = pN[hp2][:, 0:2 * (DH + 1)].rearrange("p (h e) -> p h e", h=2)
                        self.act(S, hs[:, 2 * hp2:2 * hp2 + 2], pv[:, :, DH], AF.Abs, [pNr[hp2]], [hsr])
                    self.tt(S, DVE, hs[:, 0:4], hs[:, 0:4], flv[:, s, :], ALU.max, [hsr, g_r], [hsr])
                    S.op(DVE, (lambda hs=hs: (lambda e: e.reciprocal(out=hs[:, 4:8], in_=hs[:, 0:4])))(), [hsr], [hsr])
                    for hp2 in range(2):
                        pv = pN[hp2][:, 0:2 * (DH + 1)].rearrange("p (h e) -> p h e", h=2)
                        self.tt(S, DVE, hmv[:, 2 * hp2:2 * hp2 + 2, :], pv[:, :, 0:DH],
                                hs[:, 4 + 2 * hp2:6 + 2 * hp2].unsqueeze(2).to_broadcast([128, 2, DH]), ALU.mult,
                                [pNr[hp2], hsr], [hmr])

                def stage_B2(s):
                    gi = ti * NSUB + s
                    k2 = gi % 2
                    mx, mxr = mix[:, k2, :], mix_r[k2]
                    hs, hsr = hst[:, k2, :], hst_r[k2]
                    hmv, hmr = hm[:, k2, :, :], hm_r[k2]
                    for h in range(4):
                        self.act(S, hj[:, h, :], hmv[:, h, :], AF.Square, [hmr], [hj_r[h], hsr], accum_out=hs[:, 8 + h:9 + h])
                    self.ts(S, DVE, hs[:, 8:12], hs[:, 8:12], 1.0 / DH, EPS, ALU.mult, ALU.add, [hsr], [hsr])
                    self.tt(S, POOL, hs[:, 12:16], hs[:, 8:12], C["neghalf"][:, 0:1].to_broadcast([128, 4]), ALU.pow,
                            [hsr, cr], [hsr])
                    for h in range(4):
                        self.stt(S, mx[:, h * DH:(h + 1) * DH], hmv[:, h, :], hs[:, 12 + h:13 + h],
                                 G2[:, s, h * DH:(h + 1) * DH], ALU.mult, ALU.mult, [hmr, hsr, G2_r[s]], [mxr])
                    self.mem_pv(S, s, pmT, pmT_r, mvx, mem_r, pb[6], pb_r[6], rr, rr_r, mx, mxr)

                hnds = {}
                for s in range(NSUB):
                    stage_A(s)
                    stage_B1(s)
                    if s + 1 < NSUB:
                        emit_ktok(s + 1)
                    if s >= 1:
                        stage_B2(s - 1)
                    if s >= 2:
                        tail(s - 2, ti)
                    if ti + 1 < NT and NORM_IL:
                        hnds[s] = self.norm_stats(S, C, C["xt"][:, ((ti + 1) * NSUB + s) % XSLOTS, :],
                                                  C["xt_r"][((ti + 1) * NSUB + s) % XSLOTS])
                        if s >= 1:
                            self.norm_tr(S, C, hnds.pop(s - 1),
                                         [(gT_mix, hT[:, :, (s - 1) * 128:s * 128], hT_r[s - 1])])
                stage_B2(NSUB - 1)
                tail(NSUB - 2, ti)
                tail(NSUB - 1, ti)
                if ti + 1 < NT and NORM_IL:
                    self.norm_tr(S, C, hnds.pop(NSUB - 1), [(gT_mix, hT[:, :, (NSUB - 1) * 128:NSUB * 128], hT_r[NSUB - 1])])
            self.stats["amix"] = S.emit(self.G)

    def mem_prologue(self, S, C, l, wmem, wmem_rs, memT, memT_rs, mkT, mvx, mem_r, pbank, pbank_r):
        cr = C["const_r"]
        xt, xt_r, xh, xh_r = C["xt"], C["xt_r"], C["xh"], C["xh_r"]
        self.load_w(S, wmem, wmem_rs[0], self.mem_w_kv[l], 0, 512)
        for mt in range(2):
            self.ld(S, SP, xt[:, mt, :], self.mem[mt * 128:(mt + 1) * 128, :], [xt_r[mt]], xt_r[mt])
            self.cp(S, DVE, xh[:, mt, :], xt[:, mt, :], [xt_r[mt]], [xh_r[mt]])
            pT = C["pT"]
            for kc in range(KC):
                self.tr(S, pT[:, kc * 128:(kc + 1) * 128], xh[:, mt, kc * 128:(kc + 1) * 128], C["identb"][:, :],
                        [xh_r[mt], cr], [C["pT_r"]])
            self.cp(S, DVE, memT[:, :, mt * 128:(mt + 1) * 128], pT[:, :].rearrange("p (a b) -> p a b", a=KC),
                    [C["pT_r"]], memT_rs)
        self.ms(S, POOL, mvx[:, :, :, 64:65], 1.0, [mem_r])
        for hp in range(2):
            for kc in range(KC):
                self.mm(S, pbank[:, 0:MEMT], wmem[:, kc, hp * 128:(hp + 1) * 128], memT[:, kc, 0:MEMT],
                        kc == 0, kc == KC - 1, memT_rs + wmem_rs, [pbank_r])
            self.cp(S, ACT, mkT[:, hp, :], pbank[:, 0:MEMT], [pbank_r], [mem_r])
        for mt in range(2):
            for kc in range(KC):
                self.mm(S, pbank[:, 0:256], memT[:, kc, mt * 128:(mt + 1) * 128], wmem[:, kc, 256:512],
                        kc == 0, kc == KC - 1, memT_rs + wmem_rs, [pbank_r])
            self.cp(S, DVE, mvx[:, mt, :, 0:64], pbank[:, 0:256].rearrange("p (h d) -> p h d", h=4),
                    [pbank_r], [mem_r])

    def mem_scores(self, S, qm, qm_rs, mkT, mem_r, pmT, pmT_r, banks, bank_rs):
        i = 0
        dbg = os.environ.get("K_DBG", "")
        for h in range(4):
            hp, hh = h // 2, h % 2
            if "h0" in dbg and hh == 1:
                continue
            for mt in range(2):
                b, br = banks[i % len(banks)], bank_rs[i % len(banks)]
                i += 1
                self.mm(S, b[:, :], mkT[hh * 64:(hh + 1) * 64, hp, mt * 128:(mt + 1) * 128],
                        qm[hh * 64:(hh + 1) * 64, hp, :], True, True, qm_rs + [mem_r], [br])
                if "noact" in dbg:
                    continue
                self.act(S, pmT[:, h * 2 + mt, :], b[:, :], AF.Exp, [br], [pmT_r], scale=0.125)

    def mem_pv(self, S, s, pmT, pmT_r, mvx, mem_r, pom, pom_r, rr, rr_r, mix, mix_r):
        first = True
        for h in range(4):
            for mt in range(2):
                self.mm(S, pom[:, h * 65:(h + 1) * 65], pmT[:, h * 2 + mt, s * 128:(s + 1) * 128], mvx[:, mt, h, :],
                        first, mt == 1, [pmT_r, mem_r], [pom_r], skip=True)
                first = False
        pv = pom[:, 0:260].rearrange("p (h e) -> p h e", h=4)
        S.op(DVE, lambda e: e.reciprocal(out=rr[:, 0:4], in_=pv[:, :, 64]), [pom_r], [rr_r])
        self.tt(S, DVE, mix[:, 768:1024].rearrange("p (h e) -> p h e", h=4), pv[:, :, 0:64],
                rr[:, 0:4].unsqueeze(2).to_broadcast([128, 4, 64]), ALU.mult, [pom_r, rr_r], [mix_r])

    def out_proj(self, S, C, mix, mix_r, mixT, mixT_r, wout, wout_rs, xt, xr, banks, bank_rs, dst, gi):
        cr = C["const_r"]
        pT = C["pT"]
        for kc in range(KC):
            self.tr(S, pT[:, kc * 128:(kc + 1) * 128], mix[:, kc * 128:(kc + 1) * 128], C["identb"][:, :],
                    [mix_r, cr], [C["pT_r"]])
        self.cp(S, ACT, mixT[:, :, :], pT[:, :].rearrange("p (a b) -> p a b", a=KC), [C["pT_r"]], [mixT_r])
        for hf in range(2):
            b, br = banks[hf], bank_rs[hf]
            for kc in range(KC):
                self.mm(S, b[:, :], mixT[:, kc, :], wout[:, kc, hf * 512:(hf + 1) * 512], kc == 0, kc == KC - 1,
                        [mixT_r] + wout_rs, [br])
            self.tt(S, DVE, xt[:, hf * 512:(hf + 1) * 512], b[:, :], xt[:, hf * 512:(hf + 1) * 512], ALU.add,
                    [br, xr], [xr])
        self.ld(S, SP, dst[gi * 128:(gi + 1) * 128, :], xt, [], xr, reads=[xr])

    def phase_bmix(self, src, dst, final):
        nc = self.nc
        S = Sched(nc)
        with ExitStack() as st:
            C = self.phase_consts(S, st, None)
            sb, ps = C["sb"], C["ps"]
            cr = C["const_r"]
            gT_kv = C["gT"][:, 2, :]
            gT_mix = C["gT"][:, 3, :]
            wkv = sb("wkv", (128, KC, 1536), BF16)
            wkv_r = S.regions(2, "wkv")
            wbin = sb("wbin", (128, KC, D), BF16)
            wbin_r = S.regions(1, "wbin")
            wbout = sb("wbout", (128, KC, D), BF16)
            wbout_r = S.regions(1, "wbout")
            biasT = sb("biasT", (128, 5, 12, 128), F32)
            bias_r = S.region("biasT")
            hT = sb("hT", (128, KC, T), BF16)
            hT_r = S.regions(NSUB, "hT")
            hTk = sb("hTk", (128, KC, T), BF16)
            hTk_r = S.regions(NSUB, "hTk")
            KTr = sb("KTr", (128, 2, 6, T), BF16)
            KT_r = [S.regions(6, f"KT{sl}_") for sl in range(2)]
            Vr = sb("Vr", (128, 2 * NSUB, 12, 65), BF16)
            V_r = S.regions(2 * NSUB, "V")
            QT = sb("QT", (128, 8, T), BF16)
            QT_r = S.regions(8, "QT")
            QA = sb("QA", (128, 6, T), BF16)
            QB = sb("QB", (128, 6, T), BF16)
            QAB_r = S.regions(6, "QAB")
            mkT = sb("mkT", (128, 2, MEMT), BF16)
            mvx = sb("mvx", (128, 2, 4, 65), BF16)
            mem_r = S.region("memkv")
            ssb = sb("ssb", (128, 3, 512), F32)
            ssb_r = S.regions(3, "ssb")
            pTs = sb("pTs", (128, 3, 4, 128), BF16)
            pTs_r = S.regions(3, "pTs")
            pmT = sb("pmT", (128, 8, T), BF16)
            pmT_r = S.region("pmT")
            mix = sb("mix", (128, 2, D), BF16)
            mix_r = S.regions(2, "mix")
            mixT = sb("mixT", (128, 2, KC, 128), BF16)
            mixT_r = S.regions(2, "mixT")
            rr = sb("rr", (128, 4, 4), F32)
            rr_r = S.regions(4, "rr")
            pb = [ps(f"pb{i}", (128, 512), F32) for i in range(7)]
            pb_r = S.regions(7, "pb")

            self.mem_prologue(S, C, 1, QT, QT_r, hT, hT_r, mkT, mvx, mem_r, pb[0], pb_r[0])
            self.load_w(S, wkv, wkv_r[0], self.w_kv, 0, 768)
            self.load_w(S, wkv, wkv_r[1], self.w_kv, 768, 1536)
            self.load_w(S, wbin, wbin_r[0], self.b_w_in, 0, D)
            self.load_w(S, wbout, wbout_r[0], self.b_w_out, 0, D)
            for kt in range(5):
                self.ld(S, SP, biasT[:, kt, :, :], self.relbias[:, kt, :, :], [bias_r], bias_r)
            self.ms(S, POOL, biasT[0:64, 0, :, 64:128], NEG, [bias_r])
            self.ms(S, POOL, biasT[64:128, 4, :, 0:64], NEG, [bias_r])
            self.ms(S, POOL, Vr[:, :, :, 64:65], 1.0, V_r)
            self.ms(S, POOL, QA[64:128, :, :], 0.0, QAB_r)
            self.ms(S, POOL, QB[0:64, :, :], 0.0, QAB_r)
            for gi in range(XSLOTS):
                self.load_x(S, C, src, gi)
            loaded = XSLOTS
            ia = 0
            isb = 0
            io = 0
            SB = [pb[2], pb[3], pb[6]]
            SBr = [pb_r[2], pb_r[3], pb_r[6]]
            STOP = int(os.environ.get("K_STOP", 99))

            def do_stats(ti_, s_):
                gi = ti_ * NSUB + s_
                sl = gi % XSLOTS
                return self.norm_stats(S, C, C["xt"][:, sl, :], C["xt_r"][sl])

            def do_tr(hnd, s_):
                self.norm_tr(S, C, hnd, [(gT_mix, hT[:, :, s_ * 128:(s_ + 1) * 128], hT_r[s_]),
                                         (gT_kv, hTk[:, :, s_ * 128:(s_ + 1) * 128], hTk_r[s_])])

            def tail(P_):
                nonlocal loaded
                sl = P_ % XSLOTS
                self.out_proj(S, C, mix[:, P_ % 2, :], mix_r[P_ % 2], mixT[:, P_ % 2, :, :], mixT_r[P_ % 2], wbout, wbout_r,
                              C["xt"][:, sl, :], C["xt_r"][sl], [pb[0], pb[1]], [pb_r[0], pb_r[1]], dst, P_)
                if loaded < NT * NSUB and loaded % XSLOTS == sl:
                    self.load_x(S, C, src, loaded)
                    loaded += 1

            for s in range(NSUB):
                do_tr(do_stats(0, s), s)
            for ti in range(NT):
                slot = ti % 2
                for j in range(6):
                    b, br = pb[ia % 2], pb_r[ia % 2]
                    ia += 1
                    for kc in range(KC):
                        self.mm(S, b[:, :], wkv[:, kc, j * 128:(j + 1) * 128], hTk[:, kc, :], kc == 0, kc == KC - 1,
                                hTk_r + [wkv_r[0]], [br])
                    self.cp(S, ACT, KTr[:, slot, j, :], b[:, :], [br], [KT_r[slot][j]])
                for s in range(NSUB):
                    vi = slot * NSUB + s
                    for (c0, c1, h0, h1) in ((768, 1280, 0, 8), (1280, 1536, 8, 12)):
                        b, br = pb[ia % 2], pb_r[ia % 2]
                        ia += 1
                        n = c1 - c0
                        for kc in range(KC):
                            self.mm(S, b[:, 0:n], hTk[:, kc, s * 128:(s + 1) * 128], wkv[:, kc, c0:c1], kc == 0,
                                    kc == KC - 1, [hTk_r[s], wkv_r[1]], [br])
                        self.cp(S, DVE, Vr[:, vi, h0:h1, 0:64], b[:, 0:n].rearrange("p (h d) -> p h d", d=64),
                                [br], [V_r[vi]])
                for j in range(8):
                    b, br = pb[ia % 2], pb_r[ia % 2]
                    ia += 1
                    for kc in range(KC):
                        self.mm(S, b[:, :], wbin[:, kc, j * 128:(j + 1) * 128], hT[:, kc, :], kc == 0, kc == KC - 1,
                                hT_r + wbin_r, [br])
                    if j < 6:
                        self.cp(S, ACT, QA[0:64, j, :], b[0:64, :], [br], [QAB_r[j]])
                        self.cp(S, DVE, QB[64:128, j, :], b[64:128, :], [br], [QAB_r[j]])
                    else:
                        self.cp(S, ACT, QT[:, j, :], b[:, :], [br], [QT_r[j]])
                self.mem_scores(S, QT[:, 6:8, :], QT_r[6:8], mkT, mem_r, pmT, pmT_r, [pb[0], pb[1]], [pb_r[0], pb_r[1]])
                hnds = {}
                for s in range(NSUB):
                    P = ti * NSUB + s
                    mx, mxr = mix[:, P % 2, :], mix_r[P % 2]
                    kts = [kt for kt in range(5) if P - 4 + kt >= 0]
                    steps = [(hg, kt) for hg in range(3) for kt in kts]
                    po_of = {}
                    for hg in range(3):
                        po_of[hg] = (pb[4 + io % 2], pb_r[4 + io % 2])
                        io += 1
                    pom, pomr = pb[4 + io % 2], pb_r[4 + io % 2]
                    io += 1
                    slots = {}

                    def emit_S(step):
                        nonlocal isb
                        hg, kt = step
                        kp = P - 4 + kt
                        sk = (kp // NSUB) % 2
                        subk = kp % NSUB
                        q3 = isb % 3
                        bs, bsr = SB[q3], SBr[q3]
                        sbuf, sbr = ssb[:, q3, :], ssb_r[q3]
                        pt, ptr = pTs[:, q3, :, :], pTs_r[q3]
                        isb += 1
                        for hl in range(4):
                            h = hg * 4 + hl
                            Qh = QA if h % 2 == 0 else QB
                            self.mm(S, bs[:, hl * 128:(hl + 1) * 128],
                                    KTr[:, sk, h // 2, subk * 128:(subk + 1) * 128],
                                    Qh[:, h // 2, s * 128:(s + 1) * 128], True, True,
                                    [KT_r[sk][h // 2], QAB_r[h // 2]], [bsr])
                        self.stt(S, sbuf, bs[:, :], 0.125,
                                 biasT[:, kt, hg * 4:(hg + 1) * 4, :].rearrange("p h q -> p (h q)"),
                                 ALU.mult, ALU.add, [bsr, bias_r], [sbr])
                        self.act(S, pt.rearrange("p h q -> p (h q)"), sbuf, AF.Exp, [sbr], [ptr])
                        slots[step] = (pt, ptr, sk, subk)

                    def emit_PV(step):
                        hg, kt = step
                        pt, ptr, sk, subk = slots.pop(step)
                        po, por = po_of[hg]
                        for hl in range(4):
                            h = hg * 4 + hl
                            self.mm(S, po[:, hl * 65:(hl + 1) * 65], pt[:, hl, :], Vr[:, sk * NSUB + subk, h, :],
                                    kt == kts[0] and hl == 0, kt == kts[-1], [ptr, V_r[sk * NSUB + subk]], [por],
                                    skip=True)
                        if kt == kts[-1]:
                            pv = po[:, 0:260].rearrange("p (h e) -> p h e", h=4)
                            rq, rqr = rr[:, hg, :], rr_r[hg]
                            S.op(DVE, (lambda pv=pv, rq=rq: (lambda e: e.reciprocal(out=rq, in_=pv[:, :, 64])))(), [por], [rqr])
                            self.tt(S, DVE, mx[:, hg * 256:(hg + 1) * 256].rearrange("p (h e) -> p h e", h=4),
                                    pv[:, :, 0:64], rq.unsqueeze(2).to_broadcast([128, 4, 64]), ALU.mult,
                                    [por, rqr], [mxr])

                    for i_ in range(min(2, len(steps))):
                        emit_S(steps[i_])
                    for i_, step in enumerate(steps):
                        if i_ + 2 < len(steps):
                            emit_S(steps[i_ + 2])
                        emit_PV(step)
                        if i_ == min(3, len(steps) - 1) and s >= 1:
                            tail(P - 1)
                            if ti + 1 < NT:
                                hnds[s - 1] = do_stats(ti + 1, s - 1)
                                if s >= 2:
                                    do_tr(hnds.pop(s - 2), s - 2)
                    self.mem_pv(S, s, pmT, pmT_r, mvx, mem_r, pom, pomr, rr[:, 3, :], rr_r[3], mx, mxr)
                tail(ti * NSUB + NSUB - 1)
                if ti + 1 < NT:
                    hnds[NSUB - 1] = do_stats(ti + 1, NSUB - 1)
                    do_tr(hnds.pop(NSUB - 2), NSUB - 2)
                    do_tr(hnds.pop(NSUB - 1), NSUB - 1)
            self.stats["bmix"] = S.emit(self.G)


def host_consts():
    ident = np.eye(128, dtype=np.float32)
    s = np.arange(128)[:, None]
    t = np.arange(128)[None, :]
    negU = np.where(s <= t, -1.0, 0.0).astype(np.float32)
    maskc = np.where(s <= t, np.float32(DH ** -0.5), np.float32(0.0)).astype(np.float32)
    return {"c_ident": ident, "c_negU": negU, "c_maskc": maskc}


def host_layout(inp):
    f = lambda a: np.ascontiguousarray(np.asarray(a, dtype=np.float32))
    g = np.stack([f(inp["norm_mix_g"])[0], f(inp["norm_ffn_g"])[0], f(inp["kv_norm_g"]),
                  f(inp["norm_mix_g"])[1], f(inp["norm_ffn_g"])[1]], 0)
    gT_all = np.ascontiguousarray(g.reshape(5, KC, 128).transpose(2, 0, 1))
    cwA = np.ascontiguousarray(f(inp["a_conv_w"])[0].reshape(4, 16, 96).transpose(2, 1, 0))
    cbA = np.ascontiguousarray(f(inp["a_conv_b"])[0].reshape(16, 96).T)
    cwF = np.ascontiguousarray(f(inp["ffn_conv_w"]).reshape(2, 3, NFT, 128).transpose(3, 0, 2, 1))
    cbF = np.ascontiguousarray(f(inp["ffn_conv_b"]).reshape(2, NFT, 128).transpose(2, 0, 1))
    rel = np.arange(768) - 127
    idx = np.clip(rel, -63, 128) + 63
    relext = f(inp["b_rel_bias"])[0][:, idx]
    kj = np.arange(128)[:, None, None]
    kt = np.arange(5)[None, :, None]
    qi = np.arange(128)[None, None, :]
    gidx = qi - kj + (4 - kt) * 128 + 127
    relbias = np.ascontiguousarray(relext[:, gidx].transpose(1, 2, 0, 3))
    shared = {
        "a_w_in": f(inp["a_w_in"])[0], "a_w_out": f(inp["a_w_out"])[0], "w_kv": f(inp["w_kv"]),
        "b_w_in": f(inp["b_w_in"])[0], "b_w_out": f(inp["b_w_out"])[0], "mem_w_kv": f(inp["mem_w_kv"]),
        "ffn_w_up": f(inp["ffn_w_up"]), "ffn_w_down": f(inp["ffn_w_down"]),
        "gT_all": gT_all, "final_g": f(inp["final_g"]).reshape(1, D), "gate_b": f(inp["a_gate_b"]).reshape(1, 8),
        "cwA": cwA, "cbA": cbA, "head_g": f(inp["a_head_g"]).reshape(1, AW), "cwF": cwF, "cbF": cbF,
        "relbias": relbias,
    }
    shared.update(host_consts())
    return shared


_CACHE = {}


def run(inputs, phases=("A_mix", "A_ffn", "B_mix", "B_ffn"), final_norm=True, ncores=8, trace=False):
    key = (tuple(phases), final_norm)
    if key not in _CACHE:
        _CACHE[key] = Prog(phases, final_norm).build()
    nc = _CACHE[key]
    shared = host_layout(inputs)
    x = np.asarray(inputs["x"], dtype=np.float32)
    mem = np.asarray(inputs["mem"], dtype=np.float32)
    in_maps = []
    for c in range(ncores):
        m = dict(shared)
        m["x"] = np.ascontiguousarray(x[c])
        m["mem"] = np.ascontiguousarray(mem[c])
        in_maps.append(m)
    res = run_bass_kernel_spmd(nc, in_maps, core_ids=list(range(ncores)), trace=trace)
    out = np.stack([np.asarray(r["out"]) for r in res.results], 0)
    return out, res


def kernel(**inputs):
    out, _ = run(inputs)
    return out.astype(np.float32)
```

```python
import numpy as np
from contextlib import ExitStack
import concourse.bass as bass
import concourse.mybir as mybir
from concourse.bass_types import AP
from concourse.bass_utils import run_bass_kernel_spmd

F32 = mybir.dt.float32
BF16 = mybir.dt.bfloat16
ALU = mybir.AluOpType
AF = mybir.ActivationFunctionType
AX = mybir.AxisListType

PE, ACT, DVE, POOL, SP = "tensor", "scalar", "vector", "gpsimd", "sync"
ENGS = (PE, ACT, DVE, POOL, SP)

D = 1024
KC = 8
SEQ = 4096
T = 512
import os
NT = int(os.environ.get("K_NT", SEQ // T))
SUB = 128
NSUB = T // SUB
DFF = 2816
NFT = DFF // 128
A_IN = 3336
AW = 768
DH = 192
MEMT = 256
EPS = 1e-6
NEG = -30000.0
XSLOTS = 6


class Region:
    __slots__ = ("name", "writer", "readers", "strict")

    def __init__(self, name):
        self.name = name
        self.writer = None
        self.readers = []
        self.strict = False


class _Op:
    __slots__ = ("eng", "idx", "fn", "waits", "needs_inc", "dma_key", "snap")

    def __init__(self, eng, idx, fn):
        self.eng = eng
        self.idx = idx
        self.fn = fn
        self.waits = []
        self.needs_inc = False
        self.dma_key = None
        self.snap = None


class Sched:
    def __init__(self, nc):
        self.nc = nc
        self.ops = {e: [] for e in ENGS}
        self.clock = {e: {x: -1 for x in ENGS} for e in ENGS}
        self.dclock = {e: {} for e in ENGS}
        self.dma_count = {}
        self.all_regions = []

    def region(self, name=None):
        r = Region(name or f"r{len(self.all_regions)}")
        self.all_regions.append(r)
        return r

    def regions(self, n, name="r"):
        return [self.region(f"{name}{i}") for i in range(n)]

    def _add(self, eng, fn, reads, writes, dma_key=None):
        o = _Op(eng, len(self.ops[eng]), fn)
        o.dma_key = dma_key
        deps = []
        for r in reads:
            if r.writer is not None:
                deps.append((r.writer, True))
        for w in writes:
            if w.writer is not None:
                deps.append((w.writer, w.strict))
            for rd in w.readers:
                deps.append((rd, w.strict))
        clk = self.clock[eng]
        dclk = self.dclock[eng]
        for tok, is_raw in deps:
            if tok[0] == "c":
                _, e2, n = tok
                if e2 == eng and (not is_raw or eng == PE):
                    continue
                if clk[e2] >= n:
                    continue
                o.waits.append(tok)
                self.ops[e2][n].needs_inc = True
                clk[e2] = n
                sn = self.ops[e2][n].snap
                for k, v in sn.items():
                    if k != eng and clk[k] < v:
                        clk[k] = v
            else:
                _, key, val = tok
                if dclk.get(key, 0) >= val:
                    continue
                cur = self.dma_count[key]
                o.waits.append(("d", key, cur))
                dclk[key] = cur
        o.snap = dict(clk)
        self.ops[eng].append(o)
        if dma_key is not None:
            self.dma_count[dma_key] = self.dma_count.get(dma_key, 0) + 16
            tok = ("d", dma_key, self.dma_count[dma_key])
        else:
            tok = ("c", eng, o.idx)
        for r in reads:
            r.readers.append(tok)
        for w in writes:
            w.writer = tok
            w.readers = []
        return o

    def op(self, eng, fn, reads=(), writes=()):
        return self._add(eng, fn, reads, writes)

    def dma(self, eng, fn, reads=(), writes=(), key=None):
        return self._add(eng, fn, reads, writes, dma_key=key.name + "@" + eng)

    def emit(self, G):
        nc = self.nc
        self._add(SP, None, list(self.all_regions), list(self.all_regions))
        keys = list(self.dma_count)
        slot = {}
        nsw = nhw = 0
        for k in keys:
            if k.endswith("@" + POOL):
                slot[k] = nsw
                nsw += 1
            else:
                slot[k] = G.NSW + nhw
                nhw += 1
        assert nsw <= G.NSW and nhw <= G.NDMA - G.NSW, (nsw, nhw)
        dsem = {k: G.dsem[slot[k]] for k in keys}
        dbase = {k: G.dbase[slot[k]] for k in keys}
        esem, ebase = G.esem, dict(G.ebase)
        G.phase += 1
        barv = G.phase
        cnt = {}
        for e in ENGS:
            c = 0
            arr = []
            for o in self.ops[e]:
                if o.needs_inc:
                    c += 1
                arr.append(c)
            cnt[e] = arr
            G.ebase[e] += c
        for k in keys:
            G.dbase[slot[k]] += self.dma_count[k]
        with nc.Block() as block:
            def make(e):
                def body(engh):
                    for o in self.ops[e]:
                        for w in o.waits:
                            if w[0] == "c":
                                engh.wait_ge(esem[w[1]], ebase[w[1]] + cnt[w[1]][w[2]])
                            else:
                                engh.wait_ge(dsem[w[1]], dbase[w[1]] + w[2])
                        if o.fn is None:
                            continue
                        ins = o.fn(engh)
                        if o.dma_key is not None:
                            ins.then_inc(dsem[o.dma_key], 16)
                        elif o.needs_inc:
                            ins.then_inc(esem[e], 1)
                    if e == SP:
                        engh.sem_inc(G.bar, 1)
                    else:
                        engh.wait_ge(G.bar, barv)
                return body

            for e in ENGS:
                getattr(block, e)(make(e))
        return {e: len(self.ops[e]) for e in ENGS}


class SemPool:
    NDMA = 56
    NSW = 16

    def __init__(self, nc, st):
        self.esem = {e: st.enter_context(nc.semaphore(f"s_{e}")) for e in ENGS}
        self.dsem = [st.enter_context(nc.semaphore(f"d_{i}")) for i in range(self.NDMA)]
        self.bar = st.enter_context(nc.semaphore("bar"))
        self.ebase = {e: 0 for e in ENGS}
        self.dbase = [0] * self.NDMA
        self.phase = 0
        allsem = list(self.esem.values()) + self.dsem + [self.bar]
        with nc.Block() as block:
            @block.gpsimd
            def _(g):
                for s in allsem:
                    g.sem_clear(s)
        nc.all_engine_barrier()


class Prog:
    def __init__(self, phases, final_norm=True):
        self.nc = nc = bass.Bass("TRN2", target_bir_lowering=False)
        self.phases = phases
        self.final_norm = final_norm
        din = lambda n, s: nc.dram_tensor(n, list(s), F32, kind="ExternalInput").ap()
        self.x = din("x", (SEQ, D))
        self.mem = din("mem", (MEMT, D))
        self.a_w_in = din("a_w_in", (D, A_IN))
        self.a_w_out = din("a_w_out", (D, D))
        self.w_kv = din("w_kv", (D, 1536))
        self.b_w_in = din("b_w_in", (D, D))
        self.b_w_out = din("b_w_out", (D, D))
        self.mem_w_kv = din("mem_w_kv", (2, D, 512))
        self.ffn_w_up = din("ffn_w_up", (2, D, 2 * DFF))
        self.ffn_w_down = din("ffn_w_down", (2, DFF, D))
        self.gT_all = din("gT_all", (128, 5, KC))
        self.final_g = din("final_g", (1, D))
        self.gate_b = din("gate_b", (1, 8))
        self.cwA = din("cwA", (96, 16, 4))
        self.cbA = din("cbA", (96, 16))
        self.head_g = din("head_g", (1, AW))
        self.cwF = din("cwF", (128, 2, NFT, 3))
        self.cbF = din("cbF", (128, 2, NFT))
        self.relbias = din("relbias", (128, 5, 12, 128))
        self.c_ident = din("c_ident", (128, 128))
        self.c_negU = din("c_negU", (128, 128))
        self.c_maskc = din("c_maskc", (128, 128))
        self.xa = nc.dram_tensor("xa", [SEQ, D], F32, kind="Internal").ap()
        self.xb = nc.dram_tensor("xb", [SEQ, D], F32, kind="Internal").ap()
        self.out = nc.dram_tensor("out", [SEQ, D], F32, kind="ExternalOutput").ap()
        self.stats = {}

    def mm(self, S, out, lhsT, rhs, start, stop, reads, writes, skip=False):
        S.op(PE, lambda e: e.matmul(out, lhsT=lhsT, rhs=rhs, start=start, stop=stop,
                                    skip_group_check=skip), reads, writes)

    def tr(self, S, out, in_, ident, reads, writes):
        S.op(PE, lambda e: e.transpose(out=out, in_=in_, identity=ident), reads, writes)

    def act(self, S, out, in_, func, reads, writes, **kw):
        S.op(ACT, lambda e: e.activation(out=out, in_=in_, func=func, **kw), reads, writes)

    def tt(self, S, eng, out, in0, in1, op, reads, writes):
        S.op(eng, lambda e: e.tensor_tensor(out=out, in0=in0, in1=in1, op=op), reads, writes)

    def ts(self, S, eng, out, in0, s1, s2, op0, op1, reads, writes):
        if op1 is None:
            S.op(eng, lambda e: e.tensor_scalar(out=out, in0=in0, scalar1=s1, scalar2=None, op0=op0),
                 reads, writes)
        else:
            S.op(eng, lambda e: e.tensor_scalar(out=out, in0=in0, scalar1=s1, scalar2=s2, op0=op0, op1=op1),
                 reads, writes)

    def stt(self, S, out, in0, scalar, in1, op0, op1, reads, writes):
        S.op(DVE, lambda e: e.scalar_tensor_tensor(out=out, in0=in0, scalar=scalar, in1=in1, op0=op0, op1=op1),
             reads, writes)

    def cp(self, S, eng, out, in_, reads, writes):
        if eng == ACT:
            S.op(ACT, lambda e: e.copy(out=out, in_=in_), reads, writes)
        else:
            S.op(eng, lambda e: e.tensor_copy(out=out, in_=in_), reads, writes)

    def ms(self, S, eng, ap, val, writes):
        S.op(eng, lambda e: e.memset(ap, val), (), writes)

    def ld(self, S, eng, out, in_, writes, key, reads=(), **kw):
        S.dma(eng, lambda e: e.dma_start(out=out, in_=in_, **kw), reads, writes, key=key)

    def load_w(self, S, dst, reg, src2d, c0, c1, kc0=0, kc1=None):
        kcn = src2d.shape[0] // 128
        kc1 = kcn if kc1 is None else kc1
        src = src2d.rearrange("(kc p) n -> p kc n", p=128)
        step = 2048
        for a in range(c0, c1, step):
            b = min(c1, a + step)
            self.ld(S, POOL, dst[:, kc0:kc1, a:b], src[:, kc0:kc1, a:b], [reg], reg)

    def norm_stats(self, S, C, xt_ap, xr):
        i = C["nrm_i"]
        C["nrm_i"] += 1
        k = i % 2
        ss = C["ss"][:, k, 0:1]
        ms_ = C["ss"][:, k, 1:2]
        rstd = C["ss"][:, k, 2:3]
        ssr = C["ss_r"][k]
        xh = C["xh"][:, k, :]
        xhr = C["xh_r"][k]
        self.act(S, xh, xt_ap, AF.Square, [xr], [xhr, ssr], accum_out=ss)
        self.ts(S, DVE, ms_, ss, 1.0 / D, EPS, ALU.mult, ALU.add, [ssr], [ssr])
        self.tt(S, POOL, rstd, ms_, C["neghalf"][:, 0:1], ALU.pow, [ssr, C["const_r"]], [ssr])
        self.act(S, xh, xt_ap, AF.Copy, [xr, ssr], [xhr], scale=rstd)
        return (xh, xhr)

    def norm_tr(self, S, C, hnd, outs):
        xh, xhr = hnd
        pT = C["pT"]
        for kc in range(KC):
            self.tr(S, pT[:, kc * 128:(kc + 1) * 128], xh[:, kc * 128:(kc + 1) * 128], C["identb"][:, :],
                    [xhr, C["const_r"]], [C["pT_r"]])
        pT3 = pT[:, :].rearrange("p (a b) -> p a b", a=KC)
        for gT, dst, dr in outs:
            self.tt(S, DVE, dst, pT3, gT.unsqueeze(2).to_broadcast([128, KC, 128]), ALU.mult,
                    [C["pT_r"], C["const_r"]], [dr])

    def norm_T(self, S, C, xt_ap, xr, outs):
        self.norm_tr(S, C, self.norm_stats(S, C, xt_ap, xr), outs)

    def phase_consts(self, S, st, which_gains):
        nc = self.nc
        C = {"nrm_i": 0}
        self.phase_i = getattr(self, "phase_i", 0) + 1
        pfx = f"p{self.phase_i}_"
        sb = lambda n, s, d: st.enter_context(nc.sbuf_tensor(pfx + n, list(s), d))
        ps = lambda n, s, d: st.enter_context(nc.psum_tensor(pfx + n, list(s), d))
        C["sb"] = sb
        C["ps"] = ps
        C["const_r"] = cr = S.region("const")
        C["identb"] = sb("identb", (128, 128), BF16)
        C["neghalf"] = sb("neghalf", (128, 1), F32)
        C["gT"] = sb("gT", (128, 5, KC), F32)
        self.ld(S, POOL, C["identb"][:, :], self.c_ident[:, :], [cr], cr)
        self.ld(S, SP, C["gT"][:, :, :], self.gT_all[:, :, :], [cr], cr)
        self.ms(S, DVE, C["neghalf"][:, :], -0.5, [cr])
        C["ss"] = sb("ss", (128, 2, 4), F32)
        C["ss_r"] = S.regions(2, "ss")
        C["xh"] = sb("xh", (128, 2, D), BF16)
        C["xh_r"] = S.regions(2, "xh")
        C["pT"] = ps("pT", (128, D), BF16)
        C["pT_r"] = S.region("pT")
        C["xt"] = sb("xt", (128, XSLOTS, D), F32)
        C["xt_r"] = S.regions(XSLOTS, "xt")
        return C

    def load_x(self, S, C, src, gi):
        sl = gi % XSLOTS
        self.ld(S, SP, C["xt"][:, sl, :], src[gi * 128:(gi + 1) * 128, :], [C["xt_r"][sl]], C["xt_r"][sl])

    def phase_ffn(self, l, src, dst, final):
        nc = self.nc
        S = Sched(nc)
        with ExitStack() as st:
            C = self.phase_consts(S, st, None)
            sb, ps = C["sb"], C["ps"]
            gT = C["gT"][:, 1 if l == 0 else 4, :]
            wup = sb("wup", (128, KC, 2 * DFF), BF16)
            wup_r = S.regions(4, "wup")
            wdn = sb("wdn", (128, NFT, D), BF16)
            wdn_r = S.regions(2, "wdn")
            hT = sb("hT", (128, KC, T), BF16)
            hT_r = S.regions(NSUB, "hT")
            aT = sb("aT", (128, NFT, T), BF16)
            aT_r = S.regions(NFT, "aT")
            graw = sb("graw", (128, 2, T + 2), F32)
            graw_r = S.regions(2, "graw")
            acc = sb("acc", (128, 2, T), F32)
            acc_r = S.regions(2, "acc")
            tnh = sb("tnh", (128, 2, T), F32)
            tnh_r = S.regions(2, "tnh")
            halo = sb("halo", (128, NFT, 2), F32)
            halo_r = S.regions(NFT, "halo")
            cw = sb("cw", (128, NFT, 3), F32)
            cb = sb("cb", (128, NFT), F32)
            cr = C["const_r"]
            pb = [ps(f"pb{i}", (128, 512), F32) for i in range(7)]
            pb_r = S.regions(7, "pb")
            if final:
                fg = sb("fg", (128, D), F32)
                self.ld(S, SP, fg[:, :], self.final_g.partition_broadcast(128), [cr], cr)
                fss = sb("fss", (128, 2, 4), F32)
                fss_r = S.regions(2, "fss")
                for r_ in C["xh_r"]:
                    r_.strict = True
            for gi in range(min(XSLOTS, NSUB + 2)):
                self.load_x(S, C, src, gi)
            loaded = min(XSLOTS, NSUB + 2)
            self.ld(S, SP, cw[:, :, :], self.cwF[:, l, :, :], [cr], cr)
            self.ld(S, SP, cb[:, :], self.cbF[:, l, :], [cr], cr)
            self.ts(S, POOL, cw[:, :, :], cw[:, :, :], 0.5, None, ALU.mult, None, [cr], [cr])
            self.ts(S, POOL, cb[:, :], cb[:, :], 0.5, None, ALU.mult, None, [cr], [cr])
            self.ms(S, POOL, halo[:, :, :], 0.0, halo_r)
            wu = self.ffn_w_up[l]
            for c in range(4):
                self.load_w(S, wup, wup_r[c], wu, c * 1408, (c + 1) * 1408)
            wd = self.ffn_w_down[l]
            self.load_w(S, wdn, wdn_r[0], wd, 0, D, 0, 11)
            self.load_w(S, wdn, wdn_r[1], wd, 0, D, 11, 22)

            pbi = 0

            def do_stats(ti, s):
                gi = ti * NSUB + s
                sl = gi % XSLOTS
                return self.norm_stats(S, C, C["xt"][:, sl, :], C["xt_r"][sl])

            def do_tr(hnd, s):
                self.norm_tr(S, C, hnd, [(gT, hT[:, :, s * 128:(s + 1) * 128], hT_r[s])])

            for s in range(NSUB):
                do_tr(do_stats(0, s), s)
            for ti in range(NT):
                for j in range(NFT):
                    pu, pur = pb[pbi % 6], pb_r[pbi % 6]
                    pg, pgr = pb[(pbi + 1) % 6], pb_r[(pbi + 1) % 6]
                    pbi += 2
                    for kc in range(KC):
                        self.mm(S, pg[:, :], wup[:, kc, DFF + j * 128:DFF + (j + 1) * 128], hT[:, kc, :],
                                kc == 0, kc == KC - 1, hT_r + [wup_r[2 + j // 11]], [pgr])
                    for kc in range(KC):
                        self.mm(S, pu[:, :], wup[:, kc, j * 128:(j + 1) * 128], hT[:, kc, :],
                                kc == 0, kc == KC - 1, hT_r + [wup_r[j // 11]], [pur])
                    r = j % 2
                    gr, grr = graw[:, r, :], graw_r[r]
                    ac, acr = acc[:, r, :], acc_r[r]
                    tn, tnr = tnh[:, r, :], tnh_r[r]
                    self.cp(S, POOL, gr[:, 0:2], halo[:, j, :], [halo_r[j]], [grr])
                    self.cp(S, ACT, gr[:, 2:T + 2], pg[:, :], [pgr], [grr])
                    self.cp(S, POOL, halo[:, j, :], gr[:, T:T + 2], [grr], [halo_r[j]])
                    self.act(S, ac, pg[:, :], AF.Identity, [pgr, cr], [acr], scale=cw[:, j, 2:3], bias=cb[:, j:j + 1])
                    self.stt(S, ac, gr[:, 1:T + 1], cw[:, j, 1:2], ac, ALU.mult, ALU.add, [grr, cr, acr], [acr])
                    self.stt(S, ac, gr[:, 0:T], cw[:, j, 0:1], ac, ALU.mult, ALU.add, [grr, cr, acr], [acr])
                    self.act(S, tn, ac, AF.Tanh, [acr], [tnr])
                    self.tt(S, DVE, gr[:, 2:T + 2], ac, pu[:, :], ALU.mult, [acr, pur], [grr])
                    self.stt(S, aT[:, j, :], tn, 1.0, gr[:, 2:T + 2], ALU.add, ALU.mult, [tnr, grr], [aT_r[j]])
                hnds = {}
                for s in range(NSUB):
                    gi = ti * NSUB + s
                    sl = gi % XSLOTS
                    xt = C["xt"][:, sl, :]
                    xr = C["xt_r"][sl]
                    for hf in range(2):
                        po, por = pb[6], pb_r[6]
                        if hf == 1:
                            po, por = pb[pbi % 6], pb_r[pbi % 6]
                            pbi += 1
                        for kc in range(NFT):
                            self.mm(S, po[:, :], aT[:, kc, s * 128:(s + 1) * 128], wdn[:, kc, hf * 512:(hf + 1) * 512],
                                    kc == 0, kc == NFT - 1, [aT_r[kc], wdn_r[kc // 11]], [por])
                        self.tt(S, DVE, xt[:, hf * 512:(hf + 1) * 512], po[:, :], xt[:, hf * 512:(hf + 1) * 512],
                                ALU.add, [por, xr], [xr])
                    if final:
                        k = gi % 2
                        ss = fss[:, k, 0:1]
                        ms_ = fss[:, k, 1:2]
                        rstd = fss[:, k, 2:3]
                        self.act(S, C["xh"][:, k, :], xt, AF.Square, [xr], [C["xh_r"][k], fss_r[k]], accum_out=ss)
                        self.ts(S, DVE, ms_, ss, 1.0 / D, EPS, ALU.mult, ALU.add, [fss_r[k]], [fss_r[k]])
                        self.tt(S, POOL, rstd, ms_, C["neghalf"][:, 0:1], ALU.pow, [fss_r[k], cr], [fss_r[k]])
                        self.stt(S, xt, xt, rstd, fg[:, :], ALU.mult, ALU.mult, [xr, fss_r[k], cr], [xr])
                    self.ld(S, SP, dst[gi * 128:(gi + 1) * 128, :], xt, [], xr, reads=[xr])
                    if loaded < NT * NSUB and loaded % XSLOTS == sl:
                        self.load_x(S, C, src, loaded)
                        loaded += 1
                    if ti + 1 < NT:
                        hnds[s] = do_stats(ti + 1, s)
                        if s >= 1:
                            do_tr(hnds.pop(s - 1), s - 1)
                if ti + 1 < NT:
                    do_tr(hnds.pop(NSUB - 1), NSUB - 1)
            self.stats[f"ffn{l}"] = S.emit(self.G)

    def build(self):
        chain = {"A_mix": self.phase_amix, "A_ffn": lambda s, d, f: self.phase_ffn(0, s, d, False),
                 "B_mix": self.phase_bmix, "B_ffn": lambda s, d, f: self.phase_ffn(1, s, d, f)}
        src = self.x
        scr = [self.xa, self.xb]
        with ExitStack() as gst:
            self.G = SemPool(self.nc, gst)
            for i, ph in enumerate(self.phases):
                last = i == len(self.phases) - 1
                dst = self.out if last else scr[i % 2]
                chain[ph](src, dst, last and self.final_norm)
                src = dst
        return self.nc

    def phase_amix(self, src, dst, final):
        nc = self.nc
        S = Sched(nc)
        with ExitStack() as st:
            C = self.phase_consts(S, st, None)
            sb, ps = C["sb"], C["ps"]
            cr = C["const_r"]
            gT_mix = C["gT"][:, 0, :]
            wain = sb("wain", (128, KC, A_IN), BF16)
            wain_r = S.regions(4, "wain")
            waout = sb("waout", (128, KC, D), BF16)
            waout_r = S.regions(1, "waout")
            hT = sb("hT", (128, KC, T), BF16)
            hT_r = S.regions(NSUB, "hT")
            qkT = sb("qkT", (128, 16, T), BF16)
            qk_r = S.regions(16, "qk")
            qmT = sb("qmT", (128, 2, T), BF16)
            qm_r = S.regions(2, "qm")
            raw = sb("raw", (128, 2, T + 3), F32)
            raw_r = S.regions(2, "raw")
            acc = sb("acc", (128, 2, T), F32)
            acc_r = S.regions(2, "acc")
            tnh = sb("tnh", (128, 2, T), F32)
            tnh_r = S.regions(2, "tnh")
            halo = sb("halo", (128, 16, 3), F32)
            halo_r = S.regions(16, "halo")
            cw = sb("cw", (128, 16, 4), F32)
            cb = sb("cb", (128, 16), F32)
            ktok = sb("ktok", (128, NSUB, AW), BF16)
            ktok_r = S.regions(NSUB, "ktok")
            vw = sb("vw", (128, NSUB, 4, DH + 1), BF16)
            vw_r = S.regions(NSUB, "vw")
            G2 = sb("G2", (128, NSUB, AW), F32)
            G2_r = S.regions(NSUB, "G2")
            hgh = sb("hgh", (128, AW), F32)
            gb_bc = sb("gb_bc", (128, 8), F32)
            gsb = sb("gsb", (128, NSUB, 8), F32)
            gw = sb("gw", (128, 16, 16), F32)
            g_r = S.region("gates")
            EP, SPL, AA, BBL, AMX, MALL, MST, T48 = 0, 1, 2, 3, 5, 6, 7, 8
            WGF = 11
            mcar = sb("mcar", (128, 4), F32)
            am16 = sb("am16", (16, 20), F32)
            identF = sb("identF", (128, 128), F32)
            negU = sb("negU", (128, 128), F32)
            negO = sb("negO", (128, 128), F32)
            ones16 = sb("ones16", (16, 128), F32)
            maskc = sb("maskc", (128, 4, 128), F32)
            Cn = sb("Cn", (128, 4, 2, DH + 1), F32)
            Cn_r = S.regions(4, "Cn")
            Gbf = sb("Gbf", (128, 2, 4, 2, DH + 1), BF16)
            Gbf_r = [S.regions(4, "GbfA"), S.regions(4, "GbfB")]
            sm = sb("sm", (128, 2, 4, 128), BF16)
            sm_r = S.regions(2, "sm")
            hm = sb("hm", (128, 2, 4, DH), F32)
            hm_r = S.regions(2, "hm")
            hj = sb("hj", (128, 4, DH), BF16)
            hj_r = S.regions(4, "hj")
            for r_ in hj_r:
                r_.strict = True
            hst = sb("hst", (128, 2, 16), F32)
            hst_r = S.regions(2, "hst")
            mkT = sb("mkT", (128, 2, MEMT), BF16)
            mvx = sb("mvx", (128, 2, 4, 65), BF16)
            mem_r = S.region("memkv")
            pmT = sb("pmT", (128, 8, T), BF16)
            pmT_r = S.region("pmT")
            mix = sb("mix", (128, 2, D), BF16)
            mix_r = S.regions(2, "mix")
            mixT = sb("mixT", (128, 2, KC, 128), BF16)
            mixT_r = S.regions(2, "mixT")
            rr = sb("rr", (128, 4), F32)
            rr_r = S.region("rr")
            pb = [ps(f"pb{i}", (128, 512), F32) for i in range(7)]
            pb_r = S.regions(7, "pb")

            self.mem_prologue(S, C, 0, qkT[:, 0:8, :], qk_r[0:8], hT, hT_r, mkT, mvx, mem_r, pb[0], pb_r[0])
            self.load_w(S, wain, wain_r[0], self.a_w_in, 0, 1536)
            self.load_w(S, wain, wain_r[1], self.a_w_in, 1536, 2304)
            self.load_w(S, wain, wain_r[2], self.a_w_in, 2304, 3080)
            self.load_w(S, wain, wain_r[3], self.a_w_in, 3080, A_IN)
            self.load_w(S, waout, waout_r[0], self.a_w_out, 0, D)
            self.ld(S, SP, cw[0:96, :, :], self.cwA[:, :, :], [cr], cr)
            self.ld(S, SP, cb[0:96, :], self.cbA[:, :], [cr], cr)
            self.ld(S, SP, hgh[:, :], self.head_g.partition_broadcast(128), [cr], cr)
            self.ld(S, SP, gb_bc[:, :], self.gate_b.partition_broadcast(128), [cr], cr)
            self.ld(S, SP, identF[:, :], self.c_ident[:, :], [cr], cr)
            self.ld(S, SP, negU[:, :], self.c_negU[:, :], [cr], cr)
            for h in range(4):
                self.ld(S, SP, maskc[:, h, :], self.c_maskc[:, :], [cr], cr)
            self.ts(S, POOL, cw[0:96, :, :], cw[0:96, :, :], 0.5, None, ALU.mult, None, [cr], [cr])
            self.ts(S, POOL, cb[0:96, :], cb[0:96, :], 0.5, None, ALU.mult, None, [cr], [cr])
            self.ts(S, POOL, hgh[:, :], hgh[:, :], 0.5, None, ALU.mult, None, [cr], [cr])
            self.ms(S, POOL, negO[:, :], -1.0, [cr])
            self.ms(S, POOL, ones16[:, :], 1.0, [cr])
            self.ms(S, POOL, halo[:, :, :], 0.0, halo_r)
            self.ms(S, POOL, Cn[:, :, :, :], 0.0, Cn_r)
            self.ms(S, POOL, mcar[:, :], 0.0, [g_r])
            self.ms(S, POOL, vw[:, :, :, DH:DH + 1], 1.0, vw_r)
            for gi in range(XSLOTS):
                self.load_x(S, C, src, gi)
            loaded = XSLOTS
            ia = 0
            WK = 14
            ga = lambda i, n=1: gw[:, i:i + n, :].rearrange("p a c -> p (a c)")
            g3 = lambda i: gw[:, i, :].rearrange("p (s h) -> p s h", s=NSUB)
            wv, gv, flv, wkv_ = g3(WGF), g3(WGF + 1), g3(WGF + 2), g3(WK)
            pg, pgr = pb[6], pb_r[6]

            def gate_stages():
                for s in range(NSUB):
                    for kc in range(KC):
                        self.mm(S, pg[:, s * 8:(s + 1) * 8], hT[:, kc, s * 128:(s + 1) * 128], wain[:, kc, 3072:3080],
                                kc == 0, kc == KC - 1, [hT_r[s], wain_r[2]], [pgr])
                self.tt(S, DVE, gsb[:, :, :], pg[:, 0:32].rearrange("p (s g) -> p s g", s=NSUB),
                        gb_bc[:, :].unsqueeze(1).to_broadcast([128, NSUB, 8]), ALU.add, [pgr, cr], [g_r])
                self.act(S, g3(EP), gsb[:, :, 4:8], AF.Exp, [g_r], [g_r], scale=-1.0)
                self.act(S, ga(SPL), ga(EP), AF.Ln, [g_r], [g_r], bias=1.0)
                yield
                self.mm(S, pg[:, 32:48], negU[:, :], ga(SPL), True, True, [g_r, cr], [pgr])
                self.mm(S, pg[:, 48:64], negO[:, :], ga(SPL), True, True, [g_r, cr], [pgr])
                self.cp(S, DVE, ga(BBL, 2), pg[:, 32:64], [pgr], [g_r])
                self.tt(S, DVE, g3(AA), gsb[:, :, 0:4], g3(BBL), ALU.subtract, [g_r], [g_r])
                yield
                self.tr(S, pg[0:16, 64:192], ga(AA), identF[:, :], [g_r, cr], [pgr])
                S.op(DVE, lambda e: e.reduce_max(out=am16[:, 0:1], in_=pg[0:16, 64:192], axis=AX.X), [pgr], [g_r])
                self.ts(S, DVE, am16[:, 4:20], identF[0:16, 0:16], am16[:, 0:1], None, ALU.mult, None, [g_r, cr], [g_r])
                yield
                self.mm(S, pg[:, 192:208], ones16[:, :], am16[:, 4:20], True, True, [g_r, cr], [pgr])
                self.cp(S, DVE, ga(AMX), pg[:, 192:208], [pgr], [g_r])
                yield
                for s in range(NSUB):
                    self.cp(S, DVE, g3(MST)[:, s, :], mcar[:, :], [g_r], [g_r])
                    self.tt(S, DVE, g3(MALL)[:, s, :], mcar[:, :], g3(AMX)[:, s, :], ALU.max, [g_r], [g_r])
                    self.tt(S, DVE, mcar[:, :], g3(BBL + 1)[:, s, :], g3(MALL)[:, s, :], ALU.add, [g_r], [g_r])
                self.tt(S, DVE, ga(T48), ga(AA), ga(MALL), ALU.subtract, [g_r], [g_r])
                self.tt(S, DVE, ga(T48 + 1), ga(MST), ga(MALL), ALU.subtract, [g_r], [g_r])
                self.stt(S, ga(T48 + 2), ga(BBL), -1.0, ga(MALL), ALU.mult, ALU.subtract, [g_r], [g_r])
                self.act(S, ga(WGF, 3), ga(T48, 3), AF.Exp, [g_r], [g_r])
                self.ts(S, DVE, ga(WK), ga(WGF), float(DH ** -0.5), None, ALU.mult, None, [g_r], [g_r])
                yield

            def tail_a(s_, ti_):
                gi = ti_ * NSUB + s_
                k2 = gi % 2
                self.out_proj_a(S, C, mix[:, k2, :], mix_r[k2], mixT[:, k2, :, :], mixT_r[k2], waout, waout_r,
                                [pb[0], pb[1]], [pb_r[0], pb_r[1]])

            def tail_b(s_, ti_):
                nonlocal loaded
                gi = ti_ * NSUB + s_
                sl = gi % XSLOTS
                self.out_proj_b(S, C["xt"][:, sl, :], C["xt_r"][sl], [pb[0], pb[1]], [pb_r[0], pb_r[1]], dst, gi)
                if loaded < NT * NSUB and loaded % XSLOTS == sl:
                    self.load_x(S, C, src, loaded)
                    loaded += 1

            def tail(s_, ti_):
                tail_a(s_, ti_)
                tail_b(s_, ti_)

            for ti in range(NT):
                NORM_IL = False
                for s in range(NSUB if (ti == 0 or not NORM_IL) else 0):
                    gi = ti * NSUB + s
                    sl = gi % XSLOTS
                    self.norm_T(S, C, C["xt"][:, sl, :], C["xt_r"][sl],
                                [(gT_mix, hT[:, :, s * 128:(s + 1) * 128], hT_r[s])])
                gs = gate_stages()
                next(gs)
                def side_groups():
                    nonlocal ia
                    for s in range(NSUB):
                        for g in range(2):
                            b, br = pb[ia % 6], pb_r[ia % 6]
                            ia += 1
                            for kc in range(KC):
                                self.mm(S, b[:, 0:384], hT[:, kc, s * 128:(s + 1) * 128],
                                        wain[:, kc, 1536 + g * 384:1536 + (g + 1) * 384], kc == 0, kc == KC - 1,
                                        [hT_r[s], wain_r[1]], [br])
                            self.cp(S, ACT if g == 0 else DVE, vw[:, s, 2 * g:2 * g + 2, 0:DH],
                                    b[:, 0:384].rearrange("p (h e) -> p h e", h=2), [br], [vw_r[s]])
                            yield
                        for g in range(2):
                            b, br = pb[ia % 6], pb_r[ia % 6]
                            ia += 1
                            for kc in range(KC):
                                self.mm(S, b[:, 0:384], hT[:, kc, s * 128:(s + 1) * 128],
                                        wain[:, kc, 2304 + g * 384:2304 + (g + 1) * 384], kc == 0, kc == KC - 1,
                                        [hT_r[s], wain_r[2]], [br])
                            g2 = G2[:, s, g * 384:(g + 1) * 384]
                            self.act(S, g2, b[:, 0:384], AF.Tanh, [br], [G2_r[s]], scale=0.5)
                            self.stt(S, g2, g2, 1.0, hgh[:, g * 384:(g + 1) * 384], ALU.add, ALU.mult, [G2_r[s], cr], [G2_r[s]])
                            yield
                    for j in range(2):
                        b, br = pb[ia % 6], pb_r[ia % 6]
                        ia += 1
                        for kc in range(KC):
                            self.mm(S, b[:, :], wain[:, kc, 3080 + j * 128:3080 + (j + 1) * 128], hT[:, kc, :], kc == 0,
                                    kc == KC - 1, hT_r + [wain_r[3]], [br])
                        self.cp(S, ACT, qmT[:, j, :], b[:, :], [br], [qm_r[j]])
                        yield
                sg = side_groups()
                pend = None
                for i in range(16):
                    b, br = pb[ia % 6], pb_r[ia % 6]
                    ia += 1
                    for kc in range(KC):
                        self.mm(S, b[0:96, :], wain[:, kc, i * 96:(i + 1) * 96], hT[:, kc, :], kc == 0, kc == KC - 1,
                                hT_r + [wain_r[0]], [br])
                    r = i % 2
                    rw, rwr = raw[0:96, r, :], raw_r[r]
                    ac, acr = acc[0:96, r, :], acc_r[r]
                    tn, tnr = tnh[0:96, r, :], tnh_r[r]
                    self.cp(S, POOL, rw[:, 0:3], halo[0:96, i, :], [halo_r[i]], [rwr])
                    self.cp(S, ACT, rw[:, 3:T + 3], b[0:96, :], [br], [rwr])
                    self.cp(S, POOL, halo[0:96, i, :], rw[:, T:T + 3], [rwr], [halo_r[i]])
                    self.act(S, ac, b[0:96, :], AF.Identity, [br, cr], [acr], scale=cw[0:96, i, 3:4], bias=cb[0:96, i:i + 1])
                    for j in (2, 1, 0):
                        self.stt(S, ac, rw[:, j:j + T], cw[0:96, i, j:j + 1], ac, ALU.mult, ALU.add,
                                 [rwr, cr, acr], [acr])
                    if pend is not None:
                        pend()

                    def fin(i=i, ac=ac, acr=acr, tn=tn, tnr=tnr):
                        self.act(S, tn, ac, AF.Tanh, [acr], [tnr])
                        self.stt(S, qkT[0:96, i, :], tn, 1.0, ac, ALU.add, ALU.mult, [tnr, acr], [qk_r[i]])
                    pend = fin
                    if i in (1, 3, 5, 7):
                        next(gs)
                    if i >= 2:
                        for _ in range(2 if i < 15 else 99):
                            if next(sg, "done") == "done":
                                break
                pend()
                for _ in sg:
                    pass
                self.mem_scores(S, qmT, qm_r, mkT, mem_r, pmT, pmT_r, [pb[0], pb[1]], [pb_r[0], pb_r[1]])
                def emit_ktok(s):
                    pT = C["pT"]
                    for j in range(8):
                        self.tr(S, pT[:, j * 96:(j + 1) * 96], qkT[0:96, 8 + j, s * 128:(s + 1) * 128],
                                C["identb"][0:96, 0:96], [qk_r[8 + j], cr], [C["pT_r"]])
                    for h in range(4):
                        self.act(S, ktok[:, s, h * DH:(h + 1) * DH], pT[:, h * DH:(h + 1) * DH], AF.Copy,
                                 [C["pT_r"], g_r], [ktok_r[s]], scale=wkv_[:, s, h:h + 1])

                emit_ktok(0)
                pN = [pb[3], pb[4]]
                pNr = [pb_r[3], pb_r[4]]

                def emit_gbf(s_):
                    gbuf = (ti * NSUB + s_) % 2
                    for h in range(4):
                        self.act(S, Gbf[0:96, gbuf, h, :, :].rearrange("p j e -> p (j e)"),
                                 Cn[0:96, h, :, :].rearrange("p j e -> p (j e)"), AF.Copy, [Cn_r[h], g_r], [Gbf_r[gbuf][h]],
                                 scale=gv[0:96, s_, h:h + 1])

                def stage_A(s):
                    gi = ti * NSUB + s
                    k2 = gi % 2
                    cs = slice(s * 128, (s + 1) * 128)
                    pS, pSr = pb[2], pb_r[2]
                    for h in range(4):
                        for j in range(2):
                            self.mm(S, pS[:, h * 128:(h + 1) * 128], qkT[0:96, 8 + 2 * h + j, cs], qkT[0:96, 2 * h + j, cs],
                                    j == 0, j == 1, [qk_r[8 + 2 * h + j], qk_r[2 * h + j]], [pSr])
                    smv, smr = sm[:, k2, :, :], sm_r[k2]
                    for h in range(4):
                        self.stt(S, smv[:, h, :], pS[:, h * 128:(h + 1) * 128], wv[:, s, h:h + 1], maskc[:, h, :],
                                 ALU.mult, ALU.mult, [pSr, cr, g_r], [smr])
                    if s == 0:
                        emit_gbf(0)
                    for h in range(4):
                        pC, pCr = pb[5 + h % 2], pb_r[5 + h % 2]
                        for j in range(2):
                            self.mm(S, pC[0:96, j * (DH + 1):(j + 1) * (DH + 1)], ktok[:, s, h * DH + j * 96:h * DH + (j + 1) * 96],
                                    vw[:, s, h, :], True, True, [ktok_r[s], vw_r[s]], [pCr])
                        self.stt(S, Cn[0:96, h, :, :].rearrange("p j e -> p (j e)"),
                                 Cn[0:96, h, :, :].rearrange("p j e -> p (j e)"), gv[0:96, s, h:h + 1],
                                 pC[0:96, 0:2 * (DH + 1)], ALU.mult, ALU.add, [Cn_r[h], g_r, pCr], [Cn_r[h]])
                    gbuf = gi % 2
                    for h in range(4):
                        o = pN[h // 2][:, (h % 2) * (DH + 1):(h % 2 + 1) * (DH + 1)]
                        self.mm(S, o, smv[:, h, :], vw[:, s, h, :], True, False, [smr, vw_r[s]], [pNr[h // 2]])
                        for j in range(2):
                            self.mm(S, o, qkT[0:96, 2 * h + j, cs], Gbf[0:96, gbuf, h, j, :], False, j == 1,
                                    [qk_r[2 * h + j], Gbf_r[gbuf][h]], [pNr[h // 2]])
                    if s + 1 < NSUB:
                        emit_gbf(s + 1)

                def stage_B1(s):
                    gi = ti * NSUB + s
                    k2 = gi % 2
                    hs, hsr = hst[:, k2, :], hst_r[k2]
                    hmv, hmr = hm[:, k2, :, :], hm_r[k2]
                    for hp2 in range(2):
                        pv = pN[hp2][:, 0:2 * (DH + 1)].rearrange("p (h e) -> p h e", h=2)
                        self.act(S, hs[:, 2 * hp2:2 * hp2 + 2], pv[:, :, DH], AF.Abs, [pNr[hp2]], [hsr])
                    self.tt(S, DVE, hs[:, 0:4], hs[:, 0:4], flv[:, s, :], ALU.max, [hsr, g_r], [hsr])
                    S.op(DVE, (lambda hs=hs: (lambda e: e.reciprocal(out=hs[:, 4:8], in_=hs[:, 0:4])))(), [hsr], [hsr])
                    for hp2 in range(2):
                        pv = pN[hp2][:, 0:2 * (DH + 1)].rearrange("p (h e) -> p h e", h=2)
                        self.tt(S, DVE, hmv[:, 2 * hp2:2 * hp2 + 2, :], pv[:, :, 0:DH],
                                hs[:, 4 + 2 * hp2:6 + 2 * hp2].unsqueeze(2).to_broadcast([128, 2, DH]), ALU.mult,
                                [pNr[hp2], hsr], [hmr])

                def stage_B2(s):
                    gi = ti * NSUB + s
                    k2 = gi % 2
                    mx, mxr = mix[:, k2, :], mix_r[k2]
                    hs, hsr = hst[:, k2, :], hst_r[k2]
                    hmv, hmr = hm[:, k2, :, :], hm_r[k2]
                    for h in range(4):
                        self.act(S, hj[:, h, :], hmv[:, h, :], AF.Square, [hmr], [hj_r[h], hsr], accum_out=hs[:, 8 + h:9 + h])
                    self.ts(S, DVE, hs[:, 8:12], hs[:, 8:12], 1.0 / DH, EPS, ALU.mult, ALU.add, [hsr], [hsr])
                    self.tt(S, POOL, hs[:, 12:16], hs[:, 8:12], C["neghalf"][:, 0:1].to_broadcast([128, 4]), ALU.pow,
                            [hsr, cr], [hsr])
                    for h in range(4):
                        self.stt(S, mx[:, h * DH:(h + 1) * DH], hmv[:, h, :], hs[:, 12 + h:13 + h],
                                 G2[:, s, h * DH:(h + 1) * DH], ALU.mult, ALU.mult, [hmr, hsr, G2_r[s]], [mxr])
                    self.mem_pv(S, s, pmT, pmT_r, mvx, mem_r, pb[6], pb_r[6], rr, rr_r, mx, mxr)

                hnds = {}
                for s in range(NSUB):
                    stage_A(s)
                    stage_B1(s)
                    if s + 1 < NSUB:
                        emit_ktok(s + 1)
                    if s >= 1:
                        stage_B2(s - 1)
                    if s >= 2:
                        tail(s - 2, ti)
                    if ti + 1 < NT and NORM_IL:
                        hnds[s] = self.norm_stats(S, C, C["xt"][:, ((ti + 1) * NSUB + s) % XSLOTS, :],
                                                  C["xt_r"][((ti + 1) * NSUB + s) % XSLOTS])
                        if s >= 1:
                            self.norm_tr(S, C, hnds.pop(s - 1),
                                         [(gT_mix, hT[:, :, (s - 1) * 128:s * 128], hT_r[s - 1])])
                stage_B2(NSUB - 1)
                tail(NSUB - 2, ti)
                tail(NSUB - 1, ti)
                if ti + 1 < NT and NORM_IL:
                    self.norm_tr(S, C, hnds.pop(NSUB - 1), [(gT_mix, hT[:, :, (NSUB - 1) * 128:NSUB * 128], hT_r[NSUB - 1])])
            self.stats["amix"] = S.emit(self.G)

    def mem_prologue(self, S, C, l, wmem, wmem_rs, memT, memT_rs, mkT, mvx, mem_r, pbank, pbank_r):
        cr = C["const_r"]
        xt, xt_r, xh, xh_r = C["xt"], C["xt_r"], C["xh"], C["xh_r"]
        self.load_w(S, wmem, wmem_rs[0], self.mem_w_kv[l], 0, 512)
        for mt in range(2):
            self.ld(S, SP, xt[:, mt, :], self.mem[mt * 128:(mt + 1) * 128, :], [xt_r[mt]], xt_r[mt])
            self.cp(S, DVE, xh[:, mt, :], xt[:, mt, :], [xt_r[mt]], [xh_r[mt]])
            pT = C["pT"]
            for kc in range(KC):
                self.tr(S, pT[:, kc * 128:(kc + 1) * 128], xh[:, mt, kc * 128:(kc + 1) * 128], C["identb"][:, :],
                        [xh_r[mt], cr], [C["pT_r"]])
            self.cp(S, DVE, memT[:, :, mt * 128:(mt + 1) * 128], pT[:, :].rearrange("p (a b) -> p a b", a=KC),
                    [C["pT_r"]], memT_rs)
        self.ms(S, POOL, mvx[:, :, :, 64:65], 1.0, [mem_r])
        for hp in range(2):
            for kc in range(KC):
                self.mm(S, pbank[:, 0:MEMT], wmem[:, kc, hp * 128:(hp + 1) * 128], memT[:, kc, 0:MEMT],
                        kc == 0, kc == KC - 1, memT_rs + wmem_rs, [pbank_r])
            self.cp(S, ACT, mkT[:, hp, :], pbank[:, 0:MEMT], [pbank_r], [mem_r])
        for mt in range(2):
            for kc in range(KC):
                self.mm(S, pbank[:, 0:256], memT[:, kc, mt * 128:(mt + 1) * 128], wmem[:, kc, 256:512],
                        kc == 0, kc == KC - 1, memT_rs + wmem_rs, [pbank_r])
            self.cp(S, DVE, mvx[:, mt, :, 0:64], pbank[:, 0:256].rearrange("p (h d) -> p h d", h=4),
                    [pbank_r], [mem_r])

    def mem_scores(self, S, qm, qm_rs, mkT, mem_r, pmT, pmT_r, banks, bank_rs):
        i = 0
        dbg = os.environ.get("K_DBG", "")
        for h in range(4):
            hp, hh = h // 2, h % 2
            if "h0" in dbg and hh == 1:
                continue
            for mt in range(2):
                b, br = banks[i % len(banks)], bank_rs[i % len(banks)]
                i += 1
                self.mm(S, b[:, :], mkT[hh * 64:(hh + 1) * 64, hp, mt * 128:(mt + 1) * 128],
                        qm[hh * 64:(hh + 1) * 64, hp, :], True, True, qm_rs + [mem_r], [br])
                if "noact" in dbg:
                    continue
                self.act(S, pmT[:, h * 2 + mt, :], b[:, :], AF.Exp, [br], [pmT_r], scale=0.125)

    def mem_pv(self, S, s, pmT, pmT_r, mvx, mem_r, pom, pom_r, rr, rr_r, mix, mix_r):
        first = True
        for h in range(4):
            for mt in range(2):
                self.mm(S, pom[:, h * 65:(h + 1) * 65], pmT[:, h * 2 + mt, s * 128:(s + 1) * 128], mvx[:, mt, h, :],
                        first, mt == 1, [pmT_r, mem_r], [pom_r], skip=True)
                first = False
        pv = pom[:, 0:260].rearrange("p (h e) -> p h e", h=4)
        S.op(DVE, lambda e: e.reciprocal(out=rr[:, 0:4], in_=pv[:, :, 64]), [pom_r], [rr_r])
        self.tt(S, DVE, mix[:, 768:1024].rearrange("p (h e) -> p h e", h=4), pv[:, :, 0:64],
                rr[:, 0:4].unsqueeze(2).to_broadcast([128, 4, 64]), ALU.mult, [pom_r, rr_r], [mix_r])

    def out_proj_a(self, S, C, mix, mix_r, mixT, mixT_r, wout, wout_rs, banks, bank_rs):
        cr = C["const_r"]
        pT = C["pT"]
        for kc in range(KC):
            self.tr(S, pT[:, kc * 128:(kc + 1) * 128], mix[:, kc * 128:(kc + 1) * 128], C["identb"][:, :],
                    [mix_r, cr], [C["pT_r"]])
        self.cp(S, ACT, mixT[:, :, :], pT[:, :].rearrange("p (a b) -> p a b", a=KC), [C["pT_r"]], [mixT_r])
        for hf in range(2):
            b, br = banks[hf], bank_rs[hf]
            for kc in range(KC):
                self.mm(S, b[:, :], mixT[:, kc, :], wout[:, kc, hf * 512:(hf + 1) * 512], kc == 0, kc == KC - 1,
                        [mixT_r] + wout_rs, [br])

    def out_proj_b(self, S, xt, xr, banks, bank_rs, dst, gi):
        for hf in range(2):
            b, br = banks[hf], bank_rs[hf]
            self.tt(S, DVE, xt[:, hf * 512:(hf + 1) * 512], b[:, :], xt[:, hf * 512:(hf + 1) * 512], ALU.add,
                    [br, xr], [xr])
        self.ld(S, SP, dst[gi * 128:(gi + 1) * 128, :], xt, [], xr, reads=[xr])

    def out_proj(self, S, C, mix, mix_r, mixT, mixT_r, wout, wout_rs, xt, xr, banks, bank_rs, dst, gi):
        self.out_proj_a(S, C, mix, mix_r, mixT, mixT_r, wout, wout_rs, banks, bank_rs)
        self.out_proj_b(S, xt, xr, banks, bank_rs, dst, gi)

    def phase_bmix(self, src, dst, final):
        nc = self.nc
        S = Sched(nc)
        with ExitStack() as st:
            C = self.phase_consts(S, st, None)
            sb, ps = C["sb"], C["ps"]
            cr = C["const_r"]
            gT_kv = C["gT"][:, 2, :]
            gT_mix = C["gT"][:, 3, :]
            wkv = sb("wkv", (128, KC, 1536), BF16)
            wkv_r = S.regions(2, "wkv")
            wbin = sb("wbin", (128, KC, D), BF16)
            wbin_r = S.regions(1, "wbin")
            wbout = sb("wbout", (128, KC, D), BF16)
            wbout_r = S.regions(1, "wbout")
            biasT = sb("biasT", (128, 5, 12, 128), F32)
            bias_r = S.region("biasT")
            hT = sb("hT", (128, KC, T), BF16)
            hT_r = S.regions(NSUB, "hT")
            hTk = sb("hTk", (128, KC, T), BF16)
            hTk_r = S.regions(NSUB, "hTk")
            KTr = sb("KTr", (128, 2, 6, T), BF16)
            KT_r = [S.regions(6, f"KT{sl}_") for sl in range(2)]
            Vr = sb("Vr", (128, 2 * NSUB, 12, 65), BF16)
            V_r = S.regions(2 * NSUB, "V")
            QT = sb("QT", (128, 8, T), BF16)
            QT_r = S.regions(8, "QT")
            QA = sb("QA", (128, 6, T), BF16)
            QB = sb("QB", (128, 6, T), BF16)
            QAB_r = S.regions(6, "QAB")
            mkT = sb("mkT", (128, 2, MEMT), BF16)
            mvx = sb("mvx", (128, 2, 4, 65), BF16)
            mem_r = S.region("memkv")
            ssb = sb("ssb", (128, 3, 512), F32)
            ssb_r = S.regions(3, "ssb")
            pTs = sb("pTs", (128, 3, 4, 128), BF16)
            pTs_r = S.regions(3, "pTs")
            pmT = sb("pmT", (128, 8, T), BF16)
            pmT_r = S.region("pmT")
            mix = sb("mix", (128, 2, D), BF16)
            mix_r = S.regions(2, "mix")
            mixT = sb("mixT", (128, 2, KC, 128), BF16)
            mixT_r = S.regions(2, "mixT")
            rr = sb("rr", (128, 4, 4), F32)
            rr_r = S.regions(4, "rr")
            pb = [ps(f"pb{i}", (128, 512), F32) for i in range(7)]
            pb_r = S.regions(7, "pb")

            self.mem_prologue(S, C, 1, QT, QT_r, hT, hT_r, mkT, mvx, mem_r, pb[0], pb_r[0])
            self.load_w(S, wkv, wkv_r[0], self.w_kv, 0, 768)
            self.load_w(S, wkv, wkv_r[1], self.w_kv, 768, 1536)
            self.load_w(S, wbin, wbin_r[0], self.b_w_in, 0, D)
            self.load_w(S, wbout, wbout_r[0], self.b_w_out, 0, D)
            for kt in range(5):
                self.ld(S, SP, biasT[:, kt, :, :], self.relbias[:, kt, :, :], [bias_r], bias_r)
            self.ms(S, POOL, biasT[0:64, 0, :, 64:128], NEG, [bias_r])
            self.ms(S, POOL, biasT[64:128, 4, :, 0:64], NEG, [bias_r])
            self.ms(S, POOL, Vr[:, :, :, 64:65], 1.0, V_r)
            self.ms(S, POOL, QA[64:128, :, :], 0.0, QAB_r)
            self.ms(S, POOL, QB[0:64, :, :], 0.0, QAB_r)
            for gi in range(XSLOTS):
                self.load_x(S, C, src, gi)
            loaded = XSLOTS
            ia = 0
            isb = 0
            io = 0
            SB = [pb[2], pb[3], pb[6]]
            SBr = [pb_r[2], pb_r[3], pb_r[6]]
            STOP = int(os.environ.get("K_STOP", 99))

            def do_stats(ti_, s_):
                gi = ti_ * NSUB + s_
                sl = gi % XSLOTS
                return self.norm_stats(S, C, C["xt"][:, sl, :], C["xt_r"][sl])

            def do_tr(hnd, s_):
                self.norm_tr(S, C, hnd, [(gT_mix, hT[:, :, s_ * 128:(s_ + 1) * 128], hT_r[s_]),
                                         (gT_kv, hTk[:, :, s_ * 128:(s_ + 1) * 128], hTk_r[s_])])

            def tail_a(P_):
                self.out_proj_a(S, C, mix[:, P_ % 2, :], mix_r[P_ % 2], mixT[:, P_ % 2, :, :], mixT_r[P_ % 2], wbout, wbout_r,
                                [pb[0], pb[1]], [pb_r[0], pb_r[1]])

            def tail_b(P_):
                nonlocal loaded
                sl = P_ % XSLOTS
                self.out_proj_b(S, C["xt"][:, sl, :], C["xt_r"][sl], [pb[0], pb[1]], [pb_r[0], pb_r[1]], dst, P_)
                if loaded < NT * NSUB and loaded % XSLOTS == sl:
                    self.load_x(S, C, src, loaded)
                    loaded += 1

            def tail(P_):
                tail_a(P_)
                tail_b(P_)

            for s in range(NSUB):
                do_tr(do_stats(0, s), s)
            for ti in range(NT):
                slot = ti % 2
                for j in range(6):
                    b, br = pb[ia % 2], pb_r[ia % 2]
                    ia += 1
                    for kc in range(KC):
                        self.mm(S, b[:, :], wkv[:, kc, j * 128:(j + 1) * 128], hTk[:, kc, :], kc == 0, kc == KC - 1,
                                hTk_r + [wkv_r[0]], [br])
                    self.cp(S, ACT, KTr[:, slot, j, :], b[:, :], [br], [KT_r[slot][j]])
                for s in range(NSUB):
                    vi = slot * NSUB + s
                    for (c0, c1, h0, h1) in ((768, 1280, 0, 8), (1280, 1536, 8, 12)):
                        b, br = pb[ia % 2], pb_r[ia % 2]
                        ia += 1
                        n = c1 - c0
                        for kc in range(KC):
                            self.mm(S, b[:, 0:n], hTk[:, kc, s * 128:(s + 1) * 128], wkv[:, kc, c0:c1], kc == 0,
                                    kc == KC - 1, [hTk_r[s], wkv_r[1]], [br])
                        self.cp(S, DVE, Vr[:, vi, h0:h1, 0:64], b[:, 0:n].rearrange("p (h d) -> p h d", d=64),
                                [br], [V_r[vi]])
                for j in range(8):
                    b, br = pb[ia % 2], pb_r[ia % 2]
                    ia += 1
                    for kc in range(KC):
                        self.mm(S, b[:, :], wbin[:, kc, j * 128:(j + 1) * 128], hT[:, kc, :], kc == 0, kc == KC - 1,
                                hT_r + wbin_r, [br])
                    if j < 6:
                        self.cp(S, ACT, QA[0:64, j, :], b[0:64, :], [br], [QAB_r[j]])
                        self.cp(S, DVE, QB[64:128, j, :], b[64:128, :], [br], [QAB_r[j]])
                    else:
                        self.cp(S, ACT, QT[:, j, :], b[:, :], [br], [QT_r[j]])
                self.mem_scores(S, QT[:, 6:8, :], QT_r[6:8], mkT, mem_r, pmT, pmT_r, [pb[0], pb[1]], [pb_r[0], pb_r[1]])
                hnds = {}
                for s in range(NSUB):
                    P = ti * NSUB + s
                    mx, mxr = mix[:, P % 2, :], mix_r[P % 2]
                    kts = [kt for kt in range(5) if P - 4 + kt >= 0]
                    steps = [(hg, kt) for hg in range(3) for kt in kts]
                    po_of = {}
                    for hg in range(3):
                        po_of[hg] = (pb[4 + io % 2], pb_r[4 + io % 2])
                        io += 1
                    pom, pomr = pb[4 + io % 2], pb_r[4 + io % 2]
                    io += 1
                    slots = {}

                    def emit_S(step):
                        nonlocal isb
                        hg, kt = step
                        kp = P - 4 + kt
                        sk = (kp // NSUB) % 2
                        subk = kp % NSUB
                        q3 = isb % 3
                        bs, bsr = SB[q3], SBr[q3]
                        sbuf, sbr = ssb[:, q3, :], ssb_r[q3]
                        pt, ptr = pTs[:, q3, :, :], pTs_r[q3]
                        isb += 1
                        for hl in range(4):
                            h = hg * 4 + hl
                            Qh = QA if h % 2 == 0 else QB
                            self.mm(S, bs[:, hl * 128:(hl + 1) * 128],
                                    KTr[:, sk, h // 2, subk * 128:(subk + 1) * 128],
                                    Qh[:, h // 2, s * 128:(s + 1) * 128], True, True,
                                    [KT_r[sk][h // 2], QAB_r[h // 2]], [bsr])
                        self.stt(S, sbuf, bs[:, :], 0.125,
                                 biasT[:, kt, hg * 4:(hg + 1) * 4, :].rearrange("p h q -> p (h q)"),
                                 ALU.mult, ALU.add, [bsr, bias_r], [sbr])
                        self.act(S, pt.rearrange("p h q -> p (h q)"), sbuf, AF.Exp, [sbr], [ptr])
                        slots[step] = (pt, ptr, sk, subk)

                    def emit_PV(step):
                        hg, kt = step
                        pt, ptr, sk, subk = slots.pop(step)
                        po, por = po_of[hg]
                        for hl in range(4):
                            h = hg * 4 + hl
                            self.mm(S, po[:, hl * 65:(hl + 1) * 65], pt[:, hl, :], Vr[:, sk * NSUB + subk, h, :],
                                    kt == kts[0] and hl == 0, kt == kts[-1], [ptr, V_r[sk * NSUB + subk]], [por],
                                    skip=True)
                        if kt == kts[-1]:
                            pv = po[:, 0:260].rearrange("p (h e) -> p h e", h=4)
                            rq, rqr = rr[:, hg, :], rr_r[hg]
                            S.op(DVE, (lambda pv=pv, rq=rq: (lambda e: e.reciprocal(out=rq, in_=pv[:, :, 64])))(), [por], [rqr])
                            self.tt(S, DVE, mx[:, hg * 256:(hg + 1) * 256].rearrange("p (h e) -> p h e", h=4),
                                    pv[:, :, 0:64], rq.unsqueeze(2).to_broadcast([128, 4, 64]), ALU.mult,
                                    [por, rqr], [mxr])

                    for i_ in range(min(2, len(steps))):
                        emit_S(steps[i_])
                    for i_, step in enumerate(steps):
                        if i_ + 2 < len(steps):
                            emit_S(steps[i_ + 2])
                        emit_PV(step)
                        if i_ == min(3, len(steps) - 1) and s >= 1:
                            tail(P - 1)
                            if ti + 1 < NT:
                                hnds[s - 1] = do_stats(ti + 1, s - 1)
                                if s >= 2:
                                    do_tr(hnds.pop(s - 2), s - 2)
                    self.mem_pv(S, s, pmT, pmT_r, mvx, mem_r, pom, pomr, rr[:, 3, :], rr_r[3], mx, mxr)
                tail(ti * NSUB + NSUB - 1)
                if ti + 1 < NT:
                    hnds[NSUB - 1] = do_stats(ti + 1, NSUB - 1)
                    do_tr(hnds.pop(NSUB - 2), NSUB - 2)
                    do_tr(hnds.pop(NSUB - 1), NSUB - 1)
            self.stats["bmix"] = S.emit(self.G)


def host_consts():
    ident = np.eye(128, dtype=np.float32)
    s = np.arange(128)[:, None]
    t = np.arange(128)[None, :]
    negU = np.where(s <= t, -1.0, 0.0).astype(np.float32)
    maskc = np.where(s <= t, np.float32(DH ** -0.5), np.float32(0.0)).astype(np.float32)
    return {"c_ident": ident, "c_negU": negU, "c_maskc": maskc}


def host_layout(inp):
    f = lambda a: np.ascontiguousarray(np.asarray(a, dtype=np.float32))
    g = np.stack([f(inp["norm_mix_g"])[0], f(inp["norm_ffn_g"])[0], f(inp["kv_norm_g"]),
                  f(inp["norm_mix_g"])[1], f(inp["norm_ffn_g"])[1]], 0)
    gT_all = np.ascontiguousarray(g.reshape(5, KC, 128).transpose(2, 0, 1))
    cwA = np.ascontiguousarray(f(inp["a_conv_w"])[0].reshape(4, 16, 96).transpose(2, 1, 0))
    cbA = np.ascontiguousarray(f(inp["a_conv_b"])[0].reshape(16, 96).T)
    cwF = np.ascontiguousarray(f(inp["ffn_conv_w"]).reshape(2, 3, NFT, 128).transpose(3, 0, 2, 1))
    cbF = np.ascontiguousarray(f(inp["ffn_conv_b"]).reshape(2, NFT, 128).transpose(2, 0, 1))
    rel = np.arange(768) - 127
    idx = np.clip(rel, -63, 128) + 63
    relext = f(inp["b_rel_bias"])[0][:, idx]
    kj = np.arange(128)[:, None, None]
    kt = np.arange(5)[None, :, None]
    qi = np.arange(128)[None, None, :]
    gidx = qi - kj + (4 - kt) * 128 + 127
    relbias = np.ascontiguousarray(relext[:, gidx].transpose(1, 2, 0, 3))
    shared = {
        "a_w_in": f(inp["a_w_in"])[0], "a_w_out": f(inp["a_w_out"])[0], "w_kv": f(inp["w_kv"]),
        "b_w_in": f(inp["b_w_in"])[0], "b_w_out": f(inp["b_w_out"])[0], "mem_w_kv": f(inp["mem_w_kv"]),
        "ffn_w_up": f(inp["ffn_w_up"]), "ffn_w_down": f(inp["ffn_w_down"]),
        "gT_all": gT_all, "final_g": f(inp["final_g"]).reshape(1, D), "gate_b": f(inp["a_gate_b"]).reshape(1, 8),
        "cwA": cwA, "cbA": cbA, "head_g": f(inp["a_head_g"]).reshape(1, AW), "cwF": cwF, "cbF": cbF,
        "relbias": relbias,
    }
    shared.update(host_consts())
    return shared


_CACHE = {}


def run(inputs, phases=("A_mix", "A_ffn", "B_mix", "B_ffn"), final_norm=True, ncores=8, trace=False):
    key = (tuple(phases), final_norm)
    if key not in _CACHE:
        _CACHE[key] = Prog(phases, final_norm).build()
    nc = _CACHE[key]
    shared = host_layout(inputs)
    x = np.asarray(inputs["x"], dtype=np.float32)
    mem = np.asarray(inputs["mem"], dtype=np.float32)
    in_maps = []
    for c in range(ncores):
        m = dict(shared)
        m["x"] = np.ascontiguousarray(x[c])
        m["mem"] = np.ascontiguousarray(mem[c])
        in_maps.append(m)
    res = run_bass_kernel_spmd(nc, in_maps, core_ids=list(range(ncores)), trace=trace)
    out = np.stack([np.asarray(r["out"]) for r in res.results], 0)
    return out, res


def kernel(**inputs):
    out, _ = run(inputs)
    return out.astype(np.float32)
```

```python
import numpy as np
from contextlib import ExitStack
import concourse.bass as bass
import concourse.mybir as mybir
from concourse.bass_types import AP
from concourse.bass_utils import run_bass_kernel_spmd

F32 = mybir.dt.float32
BF16 = mybir.dt.bfloat16
ALU = mybir.AluOpType
AF = mybir.ActivationFunctionType
AX = mybir.AxisListType

PE, ACT, DVE, POOL, SP = "tensor", "scalar", "vector", "gpsimd", "sync"
ENGS = (PE, ACT, DVE, POOL, SP)

D = 1024
KC = 8
SEQ = 4096
T = 512
import os
NT = int(os.environ.get("K_NT", SEQ // T))
SUB = 128
NSUB = T // SUB
DFF = 2816
NFT = DFF // 128
A_IN = 3336
AW = 768
DH = 192
MEMT = 256
EPS = 1e-6
NEG = -30000.0
XSLOTS = 6


class Region:
    __slots__ = ("name", "writer", "readers", "strict")

    def __init__(self, name):
        self.name = name
        self.writer = None
        self.readers = []
        self.strict = False


class _Op:
    __slots__ = ("eng", "idx", "fn", "waits", "needs_inc", "dma_key", "snap")

    def __init__(self, eng, idx, fn):
        self.eng = eng
        self.idx = idx
        self.fn = fn
        self.waits = []
        self.needs_inc = False
        self.dma_key = None
        self.snap = None


class Sched:
    def __init__(self, nc):
        self.nc = nc
        self.ops = {e: [] for e in ENGS}
        self.clock = {e: {x: -1 for x in ENGS} for e in ENGS}
        self.dclock = {e: {} for e in ENGS}
        self.dma_count = {}
        self.all_regions = []

    def region(self, name=None):
        r = Region(name or f"r{len(self.all_regions)}")
        self.all_regions.append(r)
        return r

    def regions(self, n, name="r"):
        return [self.region(f"{name}{i}") for i in range(n)]

    def _add(self, eng, fn, reads, writes, dma_key=None):
        o = _Op(eng, len(self.ops[eng]), fn)
        o.dma_key = dma_key
        deps = []
        for r in reads:
            if r.writer is not None:
                deps.append((r.writer, True))
        for w in writes:
            if w.writer is not None:
                deps.append((w.writer, w.strict))
            for rd in w.readers:
                deps.append((rd, w.strict))
        clk = self.clock[eng]
        dclk = self.dclock[eng]
        for tok, is_raw in deps:
            if tok[0] == "c":
                _, e2, n = tok
                if e2 == eng and (not is_raw or eng == PE):
                    continue
                if clk[e2] >= n:
                    continue
                o.waits.append(tok)
                self.ops[e2][n].needs_inc = True
                clk[e2] = n
                sn = self.ops[e2][n].snap
                for k, v in sn.items():
                    if k != eng and clk[k] < v:
                        clk[k] = v
            else:
                _, key, val = tok
                if dclk.get(key, 0) >= val:
                    continue
                cur = self.dma_count[key]
                o.waits.append(("d", key, cur))
                dclk[key] = cur
        o.snap = dict(clk)
        self.ops[eng].append(o)
        if dma_key is not None:
            self.dma_count[dma_key] = self.dma_count.get(dma_key, 0) + 16
            tok = ("d", dma_key, self.dma_count[dma_key])
        else:
            tok = ("c", eng, o.idx)
        for r in reads:
            r.readers.append(tok)
        for w in writes:
            w.writer = tok
            w.readers = []
        return o

    def op(self, eng, fn, reads=(), writes=()):
        return self._add(eng, fn, reads, writes)

    def dma(self, eng, fn, reads=(), writes=(), key=None):
        return self._add(eng, fn, reads, writes, dma_key=key.name + "@" + eng)

    def emit(self, G):
        nc = self.nc
        self._add(SP, None, list(self.all_regions), list(self.all_regions))
        keys = list(self.dma_count)
        slot = {}
        nsw = nhw = 0
        for k in keys:
            if k.endswith("@" + POOL):
                slot[k] = nsw
                nsw += 1
            else:
                slot[k] = G.NSW + nhw
                nhw += 1
        assert nsw <= G.NSW and nhw <= G.NDMA - G.NSW, (nsw, nhw)
        dsem = {k: G.dsem[slot[k]] for k in keys}
        dbase = {k: G.dbase[slot[k]] for k in keys}
        esem, ebase = G.esem, dict(G.ebase)
        G.phase += 1
        barv = G.phase
        cnt = {}
        for e in ENGS:
            c = 0
            arr = []
            for o in self.ops[e]:
                if o.needs_inc:
                    c += 1
                arr.append(c)
            cnt[e] = arr
            G.ebase[e] += c
        for k in keys:
            G.dbase[slot[k]] += self.dma_count[k]
        with nc.Block() as block:
            def make(e):
                def body(engh):
                    for o in self.ops[e]:
                        for w in o.waits:
                            if w[0] == "c":
                                engh.wait_ge(esem[w[1]], ebase[w[1]] + cnt[w[1]][w[2]])
                            else:
                                engh.wait_ge(dsem[w[1]], dbase[w[1]] + w[2])
                        if o.fn is None:
                            continue
                        ins = o.fn(engh)
                        if o.dma_key is not None:
                            ins.then_inc(dsem[o.dma_key], 16)
                        elif o.needs_inc:
                            ins.then_inc(esem[e], 1)
                    if e == SP:
                        engh.sem_inc(G.bar, 1)
                    else:
                        engh.wait_ge(G.bar, barv)
                return body

            for e in ENGS:
                getattr(block, e)(make(e))
        return {e: len(self.ops[e]) for e in ENGS}


class SemPool:
    NDMA = 56
    NSW = 16

    def __init__(self, nc, st):
        self.esem = {e: st.enter_context(nc.semaphore(f"s_{e}")) for e in ENGS}
        self.dsem = [st.enter_context(nc.semaphore(f"d_{i}")) for i in range(self.NDMA)]
        self.bar = st.enter_context(nc.semaphore("bar"))
        self.ebase = {e: 0 for e in ENGS}
        self.dbase = [0] * self.NDMA
        self.phase = 0
        allsem = list(self.esem.values()) + self.dsem + [self.bar]
        with nc.Block() as block:
            @block.gpsimd
            def _(g):
                for s in allsem:
                    g.sem_clear(s)
        nc.all_engine_barrier()


class Prog:
    def __init__(self, phases, final_norm=True):
        self.nc = nc = bass.Bass("TRN2", target_bir_lowering=False)
        self.phases = phases
        self.final_norm = final_norm
        din = lambda n, s: nc.dram_tensor(n, list(s), F32, kind="ExternalInput").ap()
        self.x = din("x", (SEQ, D))
        self.mem = din("mem", (MEMT, D))
        self.a_w_in = din("a_w_in", (D, A_IN))
        self.a_w_out = din("a_w_out", (D, D))
        self.w_kv = din("w_kv", (D, 1536))
        self.b_w_in = din("b_w_in", (D, D))
        self.b_w_out = din("b_w_out", (D, D))
        self.mem_w_kv = din("mem_w_kv", (2, D, 512))
        self.ffn_w_up = din("ffn_w_up", (2, D, 2 * DFF))
        self.ffn_w_down = din("ffn_w_down", (2, DFF, D))
        self.gT_all = din("gT_all", (128, 5, KC))
        self.final_g = din("final_g", (1, D))
        self.gate_b = din("gate_b", (1, 8))
        self.cwA = din("cwA", (96, 16, 4))
        self.cbA = din("cbA", (96, 16))
        self.head_g = din("head_g", (1, AW))
        self.cwF = din("cwF", (128, 2, NFT, 3))
        self.cbF = din("cbF", (128, 2, NFT))
        self.relbias = din("relbias", (128, 5, 12, 128))
        self.c_ident = din("c_ident", (128, 128))
        self.c_negU = din("c_negU", (128, 128))
        self.c_maskc = din("c_maskc", (128, 128))
        self.xa = nc.dram_tensor("xa", [SEQ, D], F32, kind="Internal").ap()
        self.xb = nc.dram_tensor("xb", [SEQ, D], F32, kind="Internal").ap()
        self.out = nc.dram_tensor("out", [SEQ, D], F32, kind="ExternalOutput").ap()
        self.stats = {}

    def mm(self, S, out, lhsT, rhs, start, stop, reads, writes, skip=False):
        S.op(PE, lambda e: e.matmul(out, lhsT=lhsT, rhs=rhs, start=start, stop=stop,
                                    skip_group_check=skip), reads, writes)

    def tr(self, S, out, in_, ident, reads, writes):
        S.op(PE, lambda e: e.transpose(out=out, in_=in_, identity=ident), reads, writes)

    def act(self, S, out, in_, func, reads, writes, **kw):
        S.op(ACT, lambda e: e.activation(out=out, in_=in_, func=func, **kw), reads, writes)

    def tt(self, S, eng, out, in0, in1, op, reads, writes):
        S.op(eng, lambda e: e.tensor_tensor(out=out, in0=in0, in1=in1, op=op), reads, writes)

    def ts(self, S, eng, out, in0, s1, s2, op0, op1, reads, writes):
        if op1 is None:
            S.op(eng, lambda e: e.tensor_scalar(out=out, in0=in0, scalar1=s1, scalar2=None, op0=op0),
                 reads, writes)
        else:
            S.op(eng, lambda e: e.tensor_scalar(out=out, in0=in0, scalar1=s1, scalar2=s2, op0=op0, op1=op1),
                 reads, writes)

    def stt(self, S, out, in0, scalar, in1, op0, op1, reads, writes):
        S.op(DVE, lambda e: e.scalar_tensor_tensor(out=out, in0=in0, scalar=scalar, in1=in1, op0=op0, op1=op1),
             reads, writes)

    def cp(self, S, eng, out, in_, reads, writes):
        if eng == ACT:
            S.op(ACT, lambda e: e.copy(out=out, in_=in_), reads, writes)
        else:
            S.op(eng, lambda e: e.tensor_copy(out=out, in_=in_), reads, writes)

    def ms(self, S, eng, ap, val, writes):
        S.op(eng, lambda e: e.memset(ap, val), (), writes)

    def ld(self, S, eng, out, in_, writes, key, reads=(), **kw):
        S.dma(eng, lambda e: e.dma_start(out=out, in_=in_, **kw), reads, writes, key=key)

    def load_w(self, S, dst, reg, src2d, c0, c1, kc0=0, kc1=None):
        kcn = src2d.shape[0] // 128
        kc1 = kcn if kc1 is None else kc1
        src = src2d.rearrange("(kc p) n -> p kc n", p=128)
        step = 2048
        for a in range(c0, c1, step):
            b = min(c1, a + step)
            self.ld(S, POOL, dst[:, kc0:kc1, a:b], src[:, kc0:kc1, a:b], [reg], reg)

    def norm_stats(self, S, C, xt_ap, xr):
        i = C["nrm_i"]
        C["nrm_i"] += 1
        k = i % 2
        ss = C["ss"][:, k, 0:1]
        ms_ = C["ss"][:, k, 1:2]
        rstd = C["ss"][:, k, 2:3]
        ssr = C["ss_r"][k]
        xh = C["xh"][:, k, :]
        xhr = C["xh_r"][k]
        self.act(S, xh, xt_ap, AF.Square, [xr], [xhr, ssr], accum_out=ss)
        self.ts(S, DVE, ms_, ss, 1.0 / D, EPS, ALU.mult, ALU.add, [ssr], [ssr])
        self.tt(S, POOL, rstd, ms_, C["neghalf"][:, 0:1], ALU.pow, [ssr, C["const_r"]], [ssr])
        self.act(S, xh, xt_ap, AF.Copy, [xr, ssr], [xhr], scale=rstd)
        return (xh, xhr)

    def norm_tr(self, S, C, hnd, outs):
        xh, xhr = hnd
        pT = C["pT"]
        for kc in range(KC):
            self.tr(S, pT[:, kc * 128:(kc + 1) * 128], xh[:, kc * 128:(kc + 1) * 128], C["identb"][:, :],
                    [xhr, C["const_r"]], [C["pT_r"]])
        pT3 = pT[:, :].rearrange("p (a b) -> p a b", a=KC)
        for gT, dst, dr in outs:
            self.tt(S, DVE, dst, pT3, gT.unsqueeze(2).to_broadcast([128, KC, 128]), ALU.mult,
                    [C["pT_r"], C["const_r"]], [dr])

    def norm_T(self, S, C, xt_ap, xr, outs):
        self.norm_tr(S, C, self.norm_stats(S, C, xt_ap, xr), outs)

    def phase_consts(self, S, st, which_gains):
        nc = self.nc
        C = {"nrm_i": 0}
        self.phase_i = getattr(self, "phase_i", 0) + 1
        pfx = f"p{self.phase_i}_"
        sb = lambda n, s, d: st.enter_context(nc.sbuf_tensor(pfx + n, list(s), d))
        ps = lambda n, s, d: st.enter_context(nc.psum_tensor(pfx + n, list(s), d))
        C["sb"] = sb
        C["ps"] = ps
        C["const_r"] = cr = S.region("const")
        C["identb"] = sb("identb", (128, 128), BF16)
        C["neghalf"] = sb("neghalf", (128, 1), F32)
        C["gT"] = sb("gT", (128, 5, KC), F32)
        self.ld(S, POOL, C["identb"][:, :], self.c_ident[:, :], [cr], cr)
        self.ld(S, SP, C["gT"][:, :, :], self.gT_all[:, :, :], [cr], cr)
        self.ms(S, DVE, C["neghalf"][:, :], -0.5, [cr])
        C["ss"] = sb("ss", (128, 2, 4), F32)
        C["ss_r"] = S.regions(2, "ss")
        C["xh"] = sb("xh", (128, 2, D), BF16)
        C["xh_r"] = S.regions(2, "xh")
        C["pT"] = ps("pT", (128, D), BF16)
        C["pT_r"] = S.region("pT")
        C["xt"] = sb("xt", (128, XSLOTS, D), F32)
        C["xt_r"] = S.regions(XSLOTS, "xt")
        return C

    def load_x(self, S, C, src, gi):
        sl = gi % XSLOTS
        self.ld(S, SP, C["xt"][:, sl, :], src[gi * 128:(gi + 1) * 128, :], [C["xt_r"][sl]], C["xt_r"][sl])

    def phase_ffn(self, l, src, dst, final):
        nc = self.nc
        S = Sched(nc)
        with ExitStack() as st:
            C = self.phase_consts(S, st, None)
            sb, ps = C["sb"], C["ps"]
            gT = C["gT"][:, 1 if l == 0 else 4, :]
            wup = sb("wup", (128, KC, 2 * DFF), BF16)
            wup_r = S.regions(4, "wup")
            wdn = sb("wdn", (128, NFT, D), BF16)
            wdn_r = S.regions(2, "wdn")
            hT = sb("hT", (128, KC, T), BF16)
            hT_r = S.regions(NSUB, "hT")
            aT = sb("aT", (128, NFT, T), BF16)
            aT_r = S.regions(NFT, "aT")
            graw = sb("graw", (128, 2, T + 2), F32)
            graw_r = S.regions(2, "graw")
            acc = sb("acc", (128, 2, T), F32)
            acc_r = S.regions(2, "acc")
            tnh = sb("tnh", (128, 2, T), F32)
            tnh_r = S.regions(2, "tnh")
            halo = sb("halo", (128, NFT, 2), F32)
            halo_r = S.regions(NFT, "halo")
            cw = sb("cw", (128, NFT, 3), F32)
            cb = sb("cb", (128, NFT), F32)
            cr = C["const_r"]
            pb = [ps(f"pb{i}", (128, 512), F32) for i in range(7)]
            pb_r = S.regions(7, "pb")
            if final:
                fg = sb("fg", (128, D), F32)
                self.ld(S, SP, fg[:, :], self.final_g.partition_broadcast(128), [cr], cr)
                fss = sb("fss", (128, 2, 4), F32)
                fss_r = S.regions(2, "fss")
                for r_ in C["xh_r"]:
                    r_.strict = True
            for gi in range(min(XSLOTS, NSUB + 2)):
                self.load_x(S, C, src, gi)
            loaded = min(XSLOTS, NSUB + 2)
            self.ld(S, SP, cw[:, :, :], self.cwF[:, l, :, :], [cr], cr)
            self.ld(S, SP, cb[:, :], self.cbF[:, l, :], [cr], cr)
            self.ts(S, POOL, cw[:, :, :], cw[:, :, :], 0.5, None, ALU.mult, None, [cr], [cr])
            self.ts(S, POOL, cb[:, :], cb[:, :], 0.5, None, ALU.mult, None, [cr], [cr])
            self.ms(S, POOL, halo[:, :, :], 0.0, halo_r)
            wu = self.ffn_w_up[l]
            for c in (2, 0, 3, 1):
                self.load_w(S, wup, wup_r[c], wu, c * 1408, (c + 1) * 1408)
            wd = self.ffn_w_down[l]
            self.load_w(S, wdn, wdn_r[0], wd, 0, D, 0, 11)
            self.load_w(S, wdn, wdn_r[1], wd, 0, D, 11, 22)

            pbi = 0

            def do_stats(ti, s):
                gi = ti * NSUB + s
                sl = gi % XSLOTS
                return self.norm_stats(S, C, C["xt"][:, sl, :], C["xt_r"][sl])

            def do_tr(hnd, s):
                self.norm_tr(S, C, hnd, [(gT, hT[:, :, s * 128:(s + 1) * 128], hT_r[s])])

            for s in range(NSUB):
                do_tr(do_stats(0, s), s)
            for ti in range(NT):
                for j in range(NFT):
                    pu, pur = pb[pbi % 6], pb_r[pbi % 6]
                    pg, pgr = pb[(pbi + 1) % 6], pb_r[(pbi + 1) % 6]
                    pbi += 2
                    for kc in range(KC):
                        self.mm(S, pg[:, :], wup[:, kc, DFF + j * 128:DFF + (j + 1) * 128], hT[:, kc, :],
                                kc == 0, kc == KC - 1, hT_r + [wup_r[2 + j // 11]], [pgr])
                    for kc in range(KC):
                        self.mm(S, pu[:, :], wup[:, kc, j * 128:(j + 1) * 128], hT[:, kc, :],
                                kc == 0, kc == KC - 1, hT_r + [wup_r[j // 11]], [pur])
                    r = j % 2
                    gr, grr = graw[:, r, :], graw_r[r]
                    ac, acr = acc[:, r, :], acc_r[r]
                    tn, tnr = tnh[:, r, :], tnh_r[r]
                    self.cp(S, POOL, gr[:, 0:2], halo[:, j, :], [halo_r[j]], [grr])
                    self.cp(S, ACT, gr[:, 2:T + 2], pg[:, :], [pgr], [grr])
                    self.cp(S, POOL, halo[:, j, :], gr[:, T:T + 2], [grr], [halo_r[j]])
                    self.act(S, ac, pg[:, :], AF.Identity, [pgr, cr], [acr], scale=cw[:, j, 2:3], bias=cb[:, j:j + 1])
                    self.stt(S, ac, gr[:, 1:T + 1], cw[:, j, 1:2], ac, ALU.mult, ALU.add, [grr, cr, acr], [acr])
                    self.stt(S, ac, gr[:, 0:T], cw[:, j, 0:1], ac, ALU.mult, ALU.add, [grr, cr, acr], [acr])
                    self.act(S, tn, ac, AF.Tanh, [acr], [tnr])
                    self.tt(S, DVE, gr[:, 2:T + 2], ac, pu[:, :], ALU.mult, [acr, pur], [grr])
                    self.stt(S, aT[:, j, :], tn, 1.0, gr[:, 2:T + 2], ALU.add, ALU.mult, [tnr, grr], [aT_r[j]])
                hnds = {}
                for s in range(NSUB):
                    gi = ti * NSUB + s
                    sl = gi % XSLOTS
                    xt = C["xt"][:, sl, :]
                    xr = C["xt_r"][sl]
                    for hf in range(2):
                        po, por = pb[6], pb_r[6]
                        if hf == 1:
                            po, por = pb[pbi % 6], pb_r[pbi % 6]
                            pbi += 1
                        for kc in range(NFT):
                            self.mm(S, po[:, :], aT[:, kc, s * 128:(s + 1) * 128], wdn[:, kc, hf * 512:(hf + 1) * 512],
                                    kc == 0, kc == NFT - 1, [aT_r[kc], wdn_r[kc // 11]], [por])
                        self.tt(S, DVE, xt[:, hf * 512:(hf + 1) * 512], po[:, :], xt[:, hf * 512:(hf + 1) * 512],
                                ALU.add, [por, xr], [xr])
                    if final:
                        k = gi % 2
                        ss = fss[:, k, 0:1]
                        ms_ = fss[:, k, 1:2]
                        rstd = fss[:, k, 2:3]
                        self.act(S, C["xh"][:, k, :], xt, AF.Square, [xr], [C["xh_r"][k], fss_r[k]], accum_out=ss)
                        self.ts(S, DVE, ms_, ss, 1.0 / D, EPS, ALU.mult, ALU.add, [fss_r[k]], [fss_r[k]])
                        self.tt(S, POOL, rstd, ms_, C["neghalf"][:, 0:1], ALU.pow, [fss_r[k], cr], [fss_r[k]])
                        self.stt(S, xt, xt, rstd, fg[:, :], ALU.mult, ALU.mult, [xr, fss_r[k], cr], [xr])
                    self.ld(S, SP, dst[gi * 128:(gi + 1) * 128, :], xt, [], xr, reads=[xr])
                    if loaded < NT * NSUB and loaded % XSLOTS == sl:
                        self.load_x(S, C, src, loaded)
                        loaded += 1
                    if ti + 1 < NT:
                        hnds[s] = do_stats(ti + 1, s)
                        if s >= 1:
                            do_tr(hnds.pop(s - 1), s - 1)
                if ti + 1 < NT:
                    do_tr(hnds.pop(NSUB - 1), NSUB - 1)
            self.stats[f"ffn{l}"] = S.emit(self.G)

    def build(self):
        chain = {"A_mix": self.phase_amix, "A_ffn": lambda s, d, f: self.phase_ffn(0, s, d, False),
                 "B_mix": self.phase_bmix, "B_ffn": lambda s, d, f: self.phase_ffn(1, s, d, f)}
        src = self.x
        scr = [self.xa, self.xb]
        with ExitStack() as gst:
            self.G = SemPool(self.nc, gst)
            for i, ph in enumerate(self.phases):
                last = i == len(self.phases) - 1
                dst = self.out if last else scr[i % 2]
                chain[ph](src, dst, last and self.final_norm)
                src = dst
        return self.nc

    def phase_amix(self, src, dst, final):
        nc = self.nc
        S = Sched(nc)
        with ExitStack() as st:
            C = self.phase_consts(S, st, None)
            sb, ps = C["sb"], C["ps"]
            cr = C["const_r"]
            gT_mix = C["gT"][:, 0, :]
            wain = sb("wain", (128, KC, A_IN), BF16)
            wain_r = S.regions(4, "wain")
            waout = sb("waout", (128, KC, D), BF16)
            waout_r = S.regions(1, "waout")
            hT = sb("hT", (128, KC, T), BF16)
            hT_r = S.regions(NSUB, "hT")
            qkT = sb("qkT", (128, 16, T), BF16)
            qk_r = S.regions(16, "qk")
            qmT = sb("qmT", (128, 2, T), BF16)
            qm_r = S.regions(2, "qm")
            raw = sb("raw", (128, 2, T + 3), F32)
            raw_r = S.regions(2, "raw")
            acc = sb("acc", (128, 2, T), F32)
            acc_r = S.regions(2, "acc")
            tnh = sb("tnh", (128, 2, T), F32)
            tnh_r = S.regions(2, "tnh")
            halo = sb("halo", (128, 16, 3), F32)
            halo_r = S.regions(16, "halo")
            cw = sb("cw", (128, 16, 4), F32)
            cb = sb("cb", (128, 16), F32)
            ktok = sb("ktok", (128, NSUB, AW), BF16)
            ktok_r = S.regions(NSUB, "ktok")
            vw = sb("vw", (128, NSUB, 4, DH + 1), BF16)
            vw_r = S.regions(NSUB, "vw")
            G2 = sb("G2", (128, NSUB, AW), F32)
            G2_r = S.regions(NSUB, "G2")
            hgh = sb("hgh", (128, AW), F32)
            gb_bc = sb("gb_bc", (128, 8), F32)
            gsb = sb("gsb", (128, NSUB, 8), F32)
            gw = sb("gw", (128, 16, 16), F32)
            g_r = S.region("gates")
            EP, SPL, AA, BBL, AMX, MALL, MST, T48 = 0, 1, 2, 3, 5, 6, 7, 8
            WGF = 11
            mcar = sb("mcar", (128, 4), F32)
            am16 = sb("am16", (16, 20), F32)
            identF = sb("identF", (128, 128), F32)
            negU = sb("negU", (128, 128), F32)
            negO = sb("negO", (128, 128), F32)
            ones16 = sb("ones16", (16, 128), F32)
            maskc = sb("maskc", (128, 4, 128), F32)
            Cn = sb("Cn", (128, 4, 2, DH + 1), F32)
            Cn_r = S.regions(4, "Cn")
            Gbf = sb("Gbf", (128, 2, 4, 2, DH + 1), BF16)
            Gbf_r = [S.regions(4, "GbfA"), S.regions(4, "GbfB")]
            sm = sb("sm", (128, 2, 4, 128), BF16)
            sm_r = S.regions(2, "sm")
            hm = sb("hm", (128, 2, 4, DH), F32)
            hm_r = S.regions(2, "hm")
            hj = sb("hj", (128, 4, DH), BF16)
            hj_r = S.regions(4, "hj")
            for r_ in hj_r:
                r_.strict = True
            hst = sb("hst", (128, 2, 16), F32)
            hst_r = S.regions(2, "hst")
            mkT = sb("mkT", (128, 2, MEMT), BF16)
            mvx = sb("mvx", (128, 2, 4, 65), BF16)
            mem_r = S.region("memkv")
            pmT = sb("pmT", (128, 8, T), BF16)
            pmT_r = S.region("pmT")
            mix = sb("mix", (128, 2, D), BF16)
            mix_r = S.regions(2, "mix")
            mixT = sb("mixT", (128, 2, KC, 128), BF16)
            mixT_r = S.regions(2, "mixT")
            rr = sb("rr", (128, 4), F32)
            rr_r = S.region("rr")
            pb = [ps(f"pb{i}", (128, 512), F32) for i in range(7)]
            pb_r = S.regions(7, "pb")

            self.mem_prologue(S, C, 0, qkT[:, 0:8, :], qk_r[0:8], hT, hT_r, mkT, mvx, mem_r, pb[0], pb_r[0])
            self.load_w(S, wain, wain_r[0], self.a_w_in, 0, 1536)
            self.load_w(S, wain, wain_r[1], self.a_w_in, 1536, 2304)
            self.load_w(S, wain, wain_r[2], self.a_w_in, 2304, 3080)
            self.load_w(S, wain, wain_r[3], self.a_w_in, 3080, A_IN)
            self.load_w(S, waout, waout_r[0], self.a_w_out, 0, D)
            self.ld(S, SP, cw[0:96, :, :], self.cwA[:, :, :], [cr], cr)
            self.ld(S, SP, cb[0:96, :], self.cbA[:, :], [cr], cr)
            self.ld(S, SP, hgh[:, :], self.head_g.partition_broadcast(128), [cr], cr)
            self.ld(S, SP, gb_bc[:, :], self.gate_b.partition_broadcast(128), [cr], cr)
            self.ld(S, SP, identF[:, :], self.c_ident[:, :], [cr], cr)
            self.ld(S, SP, negU[:, :], self.c_negU[:, :], [cr], cr)
            for h in range(4):
                self.ld(S, SP, maskc[:, h, :], self.c_maskc[:, :], [cr], cr)
            self.ts(S, POOL, cw[0:96, :, :], cw[0:96, :, :], 0.5, None, ALU.mult, None, [cr], [cr])
            self.ts(S, POOL, cb[0:96, :], cb[0:96, :], 0.5, None, ALU.mult, None, [cr], [cr])
            self.ts(S, POOL, hgh[:, :], hgh[:, :], 0.5, None, ALU.mult, None, [cr], [cr])
            self.ms(S, POOL, negO[:, :], -1.0, [cr])
            self.ms(S, POOL, ones16[:, :], 1.0, [cr])
            self.ms(S, POOL, halo[:, :, :], 0.0, halo_r)
            self.ms(S, POOL, Cn[:, :, :, :], 0.0, Cn_r)
            self.ms(S, POOL, mcar[:, :], 0.0, [g_r])
            self.ms(S, POOL, vw[:, :, :, DH:DH + 1], 1.0, vw_r)
            for gi in range(XSLOTS):
                self.load_x(S, C, src, gi)
            loaded = XSLOTS
            ia = 0
            WK = 14
            ga = lambda i, n=1: gw[:, i:i + n, :].rearrange("p a c -> p (a c)")
            g3 = lambda i: gw[:, i, :].rearrange("p (s h) -> p s h", s=NSUB)
            wv, gv, flv, wkv_ = g3(WGF), g3(WGF + 1), g3(WGF + 2), g3(WK)
            pg, pgr = pb[6], pb_r[6]

            def gate_stages():
                for s in range(NSUB):
                    for kc in range(KC):
                        self.mm(S, pg[:, s * 8:(s + 1) * 8], hT[:, kc, s * 128:(s + 1) * 128], wain[:, kc, 3072:3080],
                                kc == 0, kc == KC - 1, [hT_r[s], wain_r[2]], [pgr])
                self.tt(S, DVE, gsb[:, :, :], pg[:, 0:32].rearrange("p (s g) -> p s g", s=NSUB),
                        gb_bc[:, :].unsqueeze(1).to_broadcast([128, NSUB, 8]), ALU.add, [pgr, cr], [g_r])
                self.act(S, g3(EP), gsb[:, :, 4:8], AF.Exp, [g_r], [g_r], scale=-1.0)
                self.act(S, ga(SPL), ga(EP), AF.Ln, [g_r], [g_r], bias=1.0)
                yield
                self.mm(S, pg[:, 32:48], negU[:, :], ga(SPL), True, True, [g_r, cr], [pgr])
                self.mm(S, pg[:, 48:64], negO[:, :], ga(SPL), True, True, [g_r, cr], [pgr])
                self.cp(S, DVE, ga(BBL, 2), pg[:, 32:64], [pgr], [g_r])
                self.tt(S, DVE, g3(AA), gsb[:, :, 0:4], g3(BBL), ALU.subtract, [g_r], [g_r])
                yield
                self.tr(S, pg[0:16, 64:192], ga(AA), identF[:, :], [g_r, cr], [pgr])
                S.op(DVE, lambda e: e.reduce_max(out=am16[:, 0:1], in_=pg[0:16, 64:192], axis=AX.X), [pgr], [g_r])
                self.ts(S, DVE, am16[:, 4:20], identF[0:16, 0:16], am16[:, 0:1], None, ALU.mult, None, [g_r, cr], [g_r])
                yield
                self.mm(S, pg[:, 192:208], ones16[:, :], am16[:, 4:20], True, True, [g_r, cr], [pgr])
                self.cp(S, DVE, ga(AMX), pg[:, 192:208], [pgr], [g_r])
                yield
                for s in range(NSUB):
                    self.cp(S, DVE, g3(MST)[:, s, :], mcar[:, :], [g_r], [g_r])
                    self.tt(S, DVE, g3(MALL)[:, s, :], mcar[:, :], g3(AMX)[:, s, :], ALU.max, [g_r], [g_r])
                    self.tt(S, DVE, mcar[:, :], g3(BBL + 1)[:, s, :], g3(MALL)[:, s, :], ALU.add, [g_r], [g_r])
                self.tt(S, DVE, ga(T48), ga(AA), ga(MALL), ALU.subtract, [g_r], [g_r])
                self.tt(S, DVE, ga(T48 + 1), ga(MST), ga(MALL), ALU.subtract, [g_r], [g_r])
                self.stt(S, ga(T48 + 2), ga(BBL), -1.0, ga(MALL), ALU.mult, ALU.subtract, [g_r], [g_r])
                self.act(S, ga(WGF, 3), ga(T48, 3), AF.Exp, [g_r], [g_r])
                self.ts(S, DVE, ga(WK), ga(WGF), float(DH ** -0.5), None, ALU.mult, None, [g_r], [g_r])
                yield

            def tail_a(s_, ti_):
                gi = ti_ * NSUB + s_
                k2 = gi % 2
                self.out_proj_a(S, C, mix[:, k2, :], mix_r[k2], mixT[:, k2, :, :], mixT_r[k2], waout, waout_r,
                                [pb[0], pb[1]], [pb_r[0], pb_r[1]])

            def tail_b(s_, ti_):
                nonlocal loaded
                gi = ti_ * NSUB + s_
                sl = gi % XSLOTS
                self.out_proj_b(S, C["xt"][:, sl, :], C["xt_r"][sl], [pb[0], pb[1]], [pb_r[0], pb_r[1]], dst, gi)
                if loaded < NT * NSUB and loaded % XSLOTS == sl:
                    self.load_x(S, C, src, loaded)
                    loaded += 1

            def tail(s_, ti_):
                tail_a(s_, ti_)
                tail_b(s_, ti_)

            for ti in range(NT):
                NORM_IL = False
                for s in range(NSUB if (ti == 0 or not NORM_IL) else 0):
                    gi = ti * NSUB + s
                    sl = gi % XSLOTS
                    self.norm_T(S, C, C["xt"][:, sl, :], C["xt_r"][sl],
                                [(gT_mix, hT[:, :, s * 128:(s + 1) * 128], hT_r[s])])
                gs = gate_stages()
                next(gs)
                def side_groups():
                    nonlocal ia
                    for s in range(NSUB):
                        for g in range(2):
                            b, br = pb[ia % 6], pb_r[ia % 6]
                            ia += 1
                            for kc in range(KC):
                                self.mm(S, b[:, 0:384], hT[:, kc, s * 128:(s + 1) * 128],
                                        wain[:, kc, 1536 + g * 384:1536 + (g + 1) * 384], kc == 0, kc == KC - 1,
                                        [hT_r[s], wain_r[1]], [br])
                            self.cp(S, ACT if g == 0 else DVE, vw[:, s, 2 * g:2 * g + 2, 0:DH],
                                    b[:, 0:384].rearrange("p (h e) -> p h e", h=2), [br], [vw_r[s]])
                            yield
                        for g in range(2):
                            b, br = pb[ia % 6], pb_r[ia % 6]
                            ia += 1
                            for kc in range(KC):
                                self.mm(S, b[:, 0:384], hT[:, kc, s * 128:(s + 1) * 128],
                                        wain[:, kc, 2304 + g * 384:2304 + (g + 1) * 384], kc == 0, kc == KC - 1,
                                        [hT_r[s], wain_r[2]], [br])
                            g2 = G2[:, s, g * 384:(g + 1) * 384]
                            self.act(S, g2, b[:, 0:384], AF.Tanh, [br], [G2_r[s]], scale=0.5)
                            self.stt(S, g2, g2, 1.0, hgh[:, g * 384:(g + 1) * 384], ALU.add, ALU.mult, [G2_r[s], cr], [G2_r[s]])
                            yield
                    for j in range(2):
                        b, br = pb[ia % 6], pb_r[ia % 6]
                        ia += 1
                        for kc in range(KC):
                            self.mm(S, b[:, :], wain[:, kc, 3080 + j * 128:3080 + (j + 1) * 128], hT[:, kc, :], kc == 0,
                                    kc == KC - 1, hT_r + [wain_r[3]], [br])
                        self.cp(S, ACT, qmT[:, j, :], b[:, :], [br], [qm_r[j]])
                        yield
                sg = side_groups()
                pend = None
                for i in range(16):
                    b, br = pb[ia % 6], pb_r[ia % 6]
                    ia += 1
                    for kc in range(KC):
                        self.mm(S, b[0:96, :], wain[:, kc, i * 96:(i + 1) * 96], hT[:, kc, :], kc == 0, kc == KC - 1,
                                hT_r + [wain_r[0]], [br])
                    r = i % 2
                    rw, rwr = raw[0:96, r, :], raw_r[r]
                    ac, acr = acc[0:96, r, :], acc_r[r]
                    tn, tnr = tnh[0:96, r, :], tnh_r[r]
                    self.cp(S, POOL, rw[:, 0:3], halo[0:96, i, :], [halo_r[i]], [rwr])
                    self.cp(S, ACT, rw[:, 3:T + 3], b[0:96, :], [br], [rwr])
                    self.cp(S, POOL, halo[0:96, i, :], rw[:, T:T + 3], [rwr], [halo_r[i]])
                    self.act(S, ac, b[0:96, :], AF.Identity, [br, cr], [acr], scale=cw[0:96, i, 3:4], bias=cb[0:96, i:i + 1])
                    for j in (2, 1, 0):
                        self.stt(S, ac, rw[:, j:j + T], cw[0:96, i, j:j + 1], ac, ALU.mult, ALU.add,
                                 [rwr, cr, acr], [acr])
                    if pend is not None:
                        pend()

                    def fin(i=i, ac=ac, acr=acr, tn=tn, tnr=tnr):
                        self.act(S, tn, ac, AF.Tanh, [acr], [tnr])
                        self.stt(S, qkT[0:96, i, :], tn, 1.0, ac, ALU.add, ALU.mult, [tnr, acr], [qk_r[i]])
                    pend = fin
                    if i in (1, 3, 5, 7):
                        next(gs)
                    if i >= 2:
                        for _ in range(2 if i < 15 else 99):
                            if next(sg, "done") == "done":
                                break
                pend()
                for _ in sg:
                    pass
                self.mem_scores(S, qmT, qm_r, mkT, mem_r, pmT, pmT_r, [pb[0], pb[1]], [pb_r[0], pb_r[1]])
                def emit_ktok(s):
                    pT = C["pT"]
                    for j in range(8):
                        self.tr(S, pT[:, j * 96:(j + 1) * 96], qkT[0:96, 8 + j, s * 128:(s + 1) * 128],
                                C["identb"][0:96, 0:96], [qk_r[8 + j], cr], [C["pT_r"]])
                    for h in range(4):
                        self.act(S, ktok[:, s, h * DH:(h + 1) * DH], pT[:, h * DH:(h + 1) * DH], AF.Copy,
                                 [C["pT_r"], g_r], [ktok_r[s]], scale=wkv_[:, s, h:h + 1])

                emit_ktok(0)
                pN = [pb[3], pb[4]]
                pNr = [pb_r[3], pb_r[4]]

                def emit_gbf(s_):
                    gbuf = (ti * NSUB + s_) % 2
                    for h in range(4):
                        self.act(S, Gbf[0:96, gbuf, h, :, :].rearrange("p j e -> p (j e)"),
                                 Cn[0:96, h, :, :].rearrange("p j e -> p (j e)"), AF.Copy, [Cn_r[h], g_r], [Gbf_r[gbuf][h]],
                                 scale=gv[0:96, s_, h:h + 1])

                def stage_A(s):
                    gi = ti * NSUB + s
                    k2 = gi % 2
                    cs = slice(s * 128, (s + 1) * 128)
                    pS, pSr = pb[2], pb_r[2]
                    for h in range(4):
                        for j in range(2):
                            self.mm(S, pS[:, h * 128:(h + 1) * 128], qkT[0:96, 8 + 2 * h + j, cs], qkT[0:96, 2 * h + j, cs],
                                    j == 0, j == 1, [qk_r[8 + 2 * h + j], qk_r[2 * h + j]], [pSr])
                    smv, smr = sm[:, k2, :, :], sm_r[k2]
                    for h in range(4):
                        self.stt(S, smv[:, h, :], pS[:, h * 128:(h + 1) * 128], wv[:, s, h:h + 1], maskc[:, h, :],
                                 ALU.mult, ALU.mult, [pSr, cr, g_r], [smr])
                    if s == 0:
                        emit_gbf(0)
                    for h in range(4):
                        pC, pCr = pb[5 + h % 2], pb_r[5 + h % 2]
                        for j in range(2):
                            self.mm(S, pC[0:96, j * (DH + 1):(j + 1) * (DH + 1)], ktok[:, s, h * DH + j * 96:h * DH + (j + 1) * 96],
                                    vw[:, s, h, :], True, True, [ktok_r[s], vw_r[s]], [pCr])
                        self.stt(S, Cn[0:96, h, :, :].rearrange("p j e -> p (j e)"),
                                 Cn[0:96, h, :, :].rearrange("p j e -> p (j e)"), gv[0:96, s, h:h + 1],
                                 pC[0:96, 0:2 * (DH + 1)], ALU.mult, ALU.add, [Cn_r[h], g_r, pCr], [Cn_r[h]])
                    gbuf = gi % 2
                    for h in range(4):
                        o = pN[h // 2][:, (h % 2) * (DH + 1):(h % 2 + 1) * (DH + 1)]
                        self.mm(S, o, smv[:, h, :], vw[:, s, h, :], True, False, [smr, vw_r[s]], [pNr[h // 2]])
                        for j in range(2):
                            self.mm(S, o, qkT[0:96, 2 * h + j, cs], Gbf[0:96, gbuf, h, j, :], False, j == 1,
                                    [qk_r[2 * h + j], Gbf_r[gbuf][h]], [pNr[h // 2]])
                    if s + 1 < NSUB:
                        emit_gbf(s + 1)

                def stage_B1(s):
                    gi = ti * NSUB + s
                    k2 = gi % 2
                    hs, hsr = hst[:, k2, :], hst_r[k2]
                    hmv, hmr = hm[:, k2, :, :], hm_r[k2]
                    for hp2 in range(2):
                        pv = pN[hp2][:, 0:2 * (DH + 1)].rearrange("p (h e) -> p h e", h=2)
                        self.act(S, hs[:, 2 * hp2:2 * hp2 + 2], pv[:, :, DH], AF.Abs, [pNr[hp2]], [hsr])
                    self.tt(S, DVE, hs[:, 0:4], hs[:, 0:4], flv[:, s, :], ALU.max, [hsr, g_r], [hsr])
                    S.op(DVE, (lambda hs=hs: (lambda e: e.reciprocal(out=hs[:, 4:8], in_=hs[:, 0:4])))(), [hsr], [hsr])
                    for hp2 in range(2):
                        pv = pN[hp2][:, 0:2 * (DH + 1)].rearrange("p (h e) -> p h e", h=2)
                        self.tt(S, DVE, hmv[:, 2 * hp2:2 * hp2 + 2, :], pv[:, :, 0:DH],
                                hs[:, 4 + 2 * hp2:6 + 2 * hp2].unsqueeze(2).to_broadcast([128, 2, DH]), ALU.mult,
                                [pNr[hp2], hsr], [hmr])

                def stage_B2(s):
                    gi = ti * NSUB + s
                    k2 = gi % 2
                    mx, mxr = mix[:, k2, :], mix_r[k2]
                    hs, hsr = hst[:, k2, :], hst_r[k2]
                    hmv, hmr = hm[:, k2, :, :], hm_r[k2]
                    for h in range(4):
                        self.act(S, hj[:, h, :], hmv[:, h, :], AF.Square, [hmr], [hj_r[h], hsr], accum_out=hs[:, 8 + h:9 + h])
                    self.ts(S, DVE, hs[:, 8:12], hs[:, 8:12], 1.0 / DH, EPS, ALU.mult, ALU.add, [hsr], [hsr])
                    self.tt(S, POOL, hs[:, 12:16], hs[:, 8:12], C["neghalf"][:, 0:1].to_broadcast([128, 4]), ALU.pow,
                            [hsr, cr], [hsr])
                    for h in range(4):
                        self.stt(S, mx[:, h * DH:(h + 1) * DH], hmv[:, h, :], hs[:, 12 + h:13 + h],
                                 G2[:, s, h * DH:(h + 1) * DH], ALU.mult, ALU.mult, [hmr, hsr, G2_r[s]], [mxr])
                    self.mem_pv(S, s, pmT, pmT_r, mvx, mem_r, pb[6], pb_r[6], rr, rr_r, mx, mxr)

                hnds = {}
                for s in range(NSUB):
                    stage_A(s)
                    stage_B1(s)
                    if s + 1 < NSUB:
                        emit_ktok(s + 1)
                    if s >= 1:
                        stage_B2(s - 1)
                    if s >= 2:
                        tail(s - 2, ti)
                    if ti + 1 < NT and NORM_IL:
                        hnds[s] = self.norm_stats(S, C, C["xt"][:, ((ti + 1) * NSUB + s) % XSLOTS, :],
                                                  C["xt_r"][((ti + 1) * NSUB + s) % XSLOTS])
                        if s >= 1:
                            self.norm_tr(S, C, hnds.pop(s - 1),
                                         [(gT_mix, hT[:, :, (s - 1) * 128:s * 128], hT_r[s - 1])])
                stage_B2(NSUB - 1)
                tail(NSUB - 2, ti)
                tail(NSUB - 1, ti)
                if ti + 1 < NT and NORM_IL:
                    self.norm_tr(S, C, hnds.pop(NSUB - 1), [(gT_mix, hT[:, :, (NSUB - 1) * 128:NSUB * 128], hT_r[NSUB - 1])])
            self.stats["amix"] = S.emit(self.G)

    def mem_prologue(self, S, C, l, wmem, wmem_rs, memT, memT_rs, mkT, mvx, mem_r, pbank, pbank_r):
        cr = C["const_r"]
        xt, xt_r, xh, xh_r = C["xt"], C["xt_r"], C["xh"], C["xh_r"]
        self.load_w(S, wmem, wmem_rs[0], self.mem_w_kv[l], 0, 512)
        for mt in range(2):
            self.ld(S, SP, xt[:, mt, :], self.mem[mt * 128:(mt + 1) * 128, :], [xt_r[mt]], xt_r[mt])
            self.cp(S, DVE, xh[:, mt, :], xt[:, mt, :], [xt_r[mt]], [xh_r[mt]])
            pT = C["pT"]
            for kc in range(KC):
                self.tr(S, pT[:, kc * 128:(kc + 1) * 128], xh[:, mt, kc * 128:(kc + 1) * 128], C["identb"][:, :],
                        [xh_r[mt], cr], [C["pT_r"]])
            self.cp(S, DVE, memT[:, :, mt * 128:(mt + 1) * 128], pT[:, :].rearrange("p (a b) -> p a b", a=KC),
                    [C["pT_r"]], memT_rs)
        self.ms(S, POOL, mvx[:, :, :, 64:65], 1.0, [mem_r])
        for hp in range(2):
            for kc in range(KC):
                self.mm(S, pbank[:, 0:MEMT], wmem[:, kc, hp * 128:(hp + 1) * 128], memT[:, kc, 0:MEMT],
                        kc == 0, kc == KC - 1, memT_rs + wmem_rs, [pbank_r])
            self.cp(S, ACT, mkT[:, hp, :], pbank[:, 0:MEMT], [pbank_r], [mem_r])
        for mt in range(2):
            for kc in range(KC):
                self.mm(S, pbank[:, 0:256], memT[:, kc, mt * 128:(mt + 1) * 128], wmem[:, kc, 256:512],
                        kc == 0, kc == KC - 1, memT_rs + wmem_rs, [pbank_r])
            self.cp(S, DVE, mvx[:, mt, :, 0:64], pbank[:, 0:256].rearrange("p (h d) -> p h d", h=4),
                    [pbank_r], [mem_r])

    def mem_scores(self, S, qm, qm_rs, mkT, mem_r, pmT, pmT_r, banks, bank_rs):
        i = 0
        dbg = os.environ.get("K_DBG", "")
        for h in range(4):
            hp, hh = h // 2, h % 2
            if "h0" in dbg and hh == 1:
                continue
            for mt in range(2):
                b, br = banks[i % len(banks)], bank_rs[i % len(banks)]
                i += 1
                self.mm(S, b[:, :], mkT[hh * 64:(hh + 1) * 64, hp, mt * 128:(mt + 1) * 128],
                        qm[hh * 64:(hh + 1) * 64, hp, :], True, True, qm_rs + [mem_r], [br])
                if "noact" in dbg:
                    continue
                self.act(S, pmT[:, h * 2 + mt, :], b[:, :], AF.Exp, [br], [pmT_r], scale=0.125)

    def mem_pv(self, S, s, pmT, pmT_r, mvx, mem_r, pom, pom_r, rr, rr_r, mix, mix_r):
        first = True
        for h in range(4):
            for mt in range(2):
                self.mm(S, pom[:, h * 65:(h + 1) * 65], pmT[:, h * 2 + mt, s * 128:(s + 1) * 128], mvx[:, mt, h, :],
                        first, mt == 1, [pmT_r, mem_r], [pom_r], skip=True)
                first = False
        pv = pom[:, 0:260].rearrange("p (h e) -> p h e", h=4)
        S.op(DVE, lambda e: e.reciprocal(out=rr[:, 0:4], in_=pv[:, :, 64]), [pom_r], [rr_r])
        self.tt(S, DVE, mix[:, 768:1024].rearrange("p (h e) -> p h e", h=4), pv[:, :, 0:64],
                rr[:, 0:4].unsqueeze(2).to_broadcast([128, 4, 64]), ALU.mult, [pom_r, rr_r], [mix_r])

    def out_proj_a(self, S, C, mix, mix_r, mixT, mixT_r, wout, wout_rs, banks, bank_rs):
        cr = C["const_r"]
        pT = C["pT"]
        for kc in range(KC):
            self.tr(S, pT[:, kc * 128:(kc + 1) * 128], mix[:, kc * 128:(kc + 1) * 128], C["identb"][:, :],
                    [mix_r, cr], [C["pT_r"]])
        self.cp(S, ACT, mixT[:, :, :], pT[:, :].rearrange("p (a b) -> p a b", a=KC), [C["pT_r"]], [mixT_r])
        for hf in range(2):
            b, br = banks[hf], bank_rs[hf]
            for kc in range(KC):
                self.mm(S, b[:, :], mixT[:, kc, :], wout[:, kc, hf * 512:(hf + 1) * 512], kc == 0, kc == KC - 1,
                        [mixT_r] + wout_rs, [br])

    def out_proj_b(self, S, xt, xr, banks, bank_rs, dst, gi):
        for hf in range(2):
            b, br = banks[hf], bank_rs[hf]
            self.tt(S, DVE, xt[:, hf * 512:(hf + 1) * 512], b[:, :], xt[:, hf * 512:(hf + 1) * 512], ALU.add,
                    [br, xr], [xr])
        self.ld(S, SP, dst[gi * 128:(gi + 1) * 128, :], xt, [], xr, reads=[xr])

    def out_proj(self, S, C, mix, mix_r, mixT, mixT_r, wout, wout_rs, xt, xr, banks, bank_rs, dst, gi):
        self.out_proj_a(S, C, mix, mix_r, mixT, mixT_r, wout, wout_rs, banks, bank_rs)
        self.out_proj_b(S, xt, xr, banks, bank_rs, dst, gi)

    def phase_bmix(self, src, dst, final):
        nc = self.nc
        S = Sched(nc)
        with ExitStack() as st:
            C = self.phase_consts(S, st, None)
            sb, ps = C["sb"], C["ps"]
            cr = C["const_r"]
            gT_kv = C["gT"][:, 2, :]
            gT_mix = C["gT"][:, 3, :]
            wkv = sb("wkv", (128, KC, 1536), BF16)
            wkv_r = S.regions(2, "wkv")
            wbin = sb("wbin", (128, KC, D), BF16)
            wbin_r = S.regions(1, "wbin")
            wbout = sb("wbout", (128, KC, D), BF16)
            wbout_r = S.regions(1, "wbout")
            biasT = sb("biasT", (128, 5, 12, 128), F32)
            bias_r = S.region("biasT")
            hT = sb("hT", (128, KC, T), BF16)
            hT_r = S.regions(NSUB, "hT")
            hTk = sb("hTk", (128, KC, T), BF16)
            hTk_r = S.regions(NSUB, "hTk")
            KTr = sb("KTr", (128, 2, 6, T), BF16)
            KT_r = [S.regions(6, f"KT{sl}_") for sl in range(2)]
            Vr = sb("Vr", (128, 2 * NSUB, 12, 65), BF16)
            V_r = S.regions(2 * NSUB, "V")
            QT = sb("QT", (128, 8, T), BF16)
            QT_r = S.regions(8, "QT")
            QA = sb("QA", (128, 6, T), BF16)
            QB = sb("QB", (128, 6, T), BF16)
            QAB_r = S.regions(6, "QAB")
            mkT = sb("mkT", (128, 2, MEMT), BF16)
            mvx = sb("mvx", (128, 2, 4, 65), BF16)
            mem_r = S.region("memkv")
            ssb = sb("ssb", (128, 3, 512), F32)
            ssb_r = S.regions(3, "ssb")
            pTs = sb("pTs", (128, 3, 4, 128), BF16)
            pTs_r = S.regions(3, "pTs")
            pmT = sb("pmT", (128, 8, T), BF16)
            pmT_r = S.region("pmT")
            mix = sb("mix", (128, 2, D), BF16)
            mix_r = S.regions(2, "mix")
            mixT = sb("mixT", (128, 2, KC, 128), BF16)
            mixT_r = S.regions(2, "mixT")
            rr = sb("rr", (128, 4, 4), F32)
            rr_r = S.regions(4, "rr")
            pb = [ps(f"pb{i}", (128, 512), F32) for i in range(7)]
            pb_r = S.regions(7, "pb")

            self.mem_prologue(S, C, 1, QT, QT_r, hT, hT_r, mkT, mvx, mem_r, pb[0], pb_r[0])
            self.load_w(S, wkv, wkv_r[0], self.w_kv, 0, 768)
            self.load_w(S, wkv, wkv_r[1], self.w_kv, 768, 1536)
            self.load_w(S, wbin, wbin_r[0], self.b_w_in, 0, D)
            self.load_w(S, wbout, wbout_r[0], self.b_w_out, 0, D)
            for kt in range(5):
                self.ld(S, SP, biasT[:, kt, :, :], self.relbias[:, kt, :, :], [bias_r], bias_r)
            self.ms(S, POOL, biasT[0:64, 0, :, 64:128], NEG, [bias_r])
            self.ms(S, POOL, biasT[64:128, 4, :, 0:64], NEG, [bias_r])
            self.ms(S, POOL, Vr[:, :, :, 64:65], 1.0, V_r)
            self.ms(S, POOL, QA[64:128, :, :], 0.0, QAB_r)
            self.ms(S, POOL, QB[0:64, :, :], 0.0, QAB_r)
            for gi in range(XSLOTS):
                self.load_x(S, C, src, gi)
            loaded = XSLOTS
            ia = 0
            isb = 0
            io = 0
            SB = [pb[2], pb[3], pb[6]]
            SBr = [pb_r[2], pb_r[3], pb_r[6]]
            STOP = int(os.environ.get("K_STOP", 99))

            def do_stats(ti_, s_):
                gi = ti_ * NSUB + s_
                sl = gi % XSLOTS
                return self.norm_stats(S, C, C["xt"][:, sl, :], C["xt_r"][sl])

            def do_tr(hnd, s_):
                self.norm_tr(S, C, hnd, [(gT_mix, hT[:, :, s_ * 128:(s_ + 1) * 128], hT_r[s_]),
                                         (gT_kv, hTk[:, :, s_ * 128:(s_ + 1) * 128], hTk_r[s_])])

            def tail_a(P_):
                self.out_proj_a(S, C, mix[:, P_ % 2, :], mix_r[P_ % 2], mixT[:, P_ % 2, :, :], mixT_r[P_ % 2], wbout, wbout_r,
                                [pb[0], pb[1]], [pb_r[0], pb_r[1]])

            def tail_b(P_):
                nonlocal loaded
                sl = P_ % XSLOTS
                self.out_proj_b(S, C["xt"][:, sl, :], C["xt_r"][sl], [pb[0], pb[1]], [pb_r[0], pb_r[1]], dst, P_)
                if loaded < NT * NSUB and loaded % XSLOTS == sl:
                    self.load_x(S, C, src, loaded)
                    loaded += 1

            def tail(P_):
                tail_a(P_)
                tail_b(P_)

            for s in range(NSUB):
                do_tr(do_stats(0, s), s)
            pending_tail = None
            for ti in range(NT):
                slot = ti % 2
                for j in range(6):
                    b, br = pb[ia % 2], pb_r[ia % 2]
                    ia += 1
                    for kc in range(KC):
                        self.mm(S, b[:, :], wkv[:, kc, j * 128:(j + 1) * 128], hTk[:, kc, :], kc == 0, kc == KC - 1,
                                hTk_r + [wkv_r[0]], [br])
                    self.cp(S, ACT, KTr[:, slot, j, :], b[:, :], [br], [KT_r[slot][j]])
                    if j == 2 and pending_tail is not None:
                        tail(pending_tail)
                        pending_tail = None
                for s in range(NSUB):
                    vi = slot * NSUB + s
                    for (c0, c1, h0, h1) in ((768, 1280, 0, 8), (1280, 1536, 8, 12)):
                        b, br = pb[ia % 2], pb_r[ia % 2]
                        ia += 1
                        n = c1 - c0
                        for kc in range(KC):
                            self.mm(S, b[:, 0:n], hTk[:, kc, s * 128:(s + 1) * 128], wkv[:, kc, c0:c1], kc == 0,
                                    kc == KC - 1, [hTk_r[s], wkv_r[1]], [br])
                        self.cp(S, DVE, Vr[:, vi, h0:h1, 0:64], b[:, 0:n].rearrange("p (h d) -> p h d", d=64),
                                [br], [V_r[vi]])
                for j in range(8):
                    b, br = pb[ia % 2], pb_r[ia % 2]
                    ia += 1
                    for kc in range(KC):
                        self.mm(S, b[:, :], wbin[:, kc, j * 128:(j + 1) * 128], hT[:, kc, :], kc == 0, kc == KC - 1,
                                hT_r + wbin_r, [br])
                    if j < 6:
                        self.cp(S, ACT, QA[0:64, j, :], b[0:64, :], [br], [QAB_r[j]])
                        self.cp(S, DVE, QB[64:128, j, :], b[64:128, :], [br], [QAB_r[j]])
                    else:
                        self.cp(S, ACT, QT[:, j, :], b[:, :], [br], [QT_r[j]])
                self.mem_scores(S, QT[:, 6:8, :], QT_r[6:8], mkT, mem_r, pmT, pmT_r, [pb[0], pb[1]], [pb_r[0], pb_r[1]])
                hnds = {}
                for s in range(NSUB):
                    P = ti * NSUB + s
                    mx, mxr = mix[:, P % 2, :], mix_r[P % 2]
                    kts = [kt for kt in range(5) if P - 4 + kt >= 0]
                    steps = [(hg, kt) for hg in range(3) for kt in kts]
                    po_of = {}
                    for hg in range(3):
                        po_of[hg] = (pb[4 + io % 2], pb_r[4 + io % 2])
                        io += 1
                    pom, pomr = pb[4 + io % 2], pb_r[4 + io % 2]
                    io += 1
                    slots = {}

                    def emit_S(step):
                        nonlocal isb
                        hg, kt = step
                        kp = P - 4 + kt
                        sk = (kp // NSUB) % 2
                        subk = kp % NSUB
                        q3 = isb % 3
                        bs, bsr = SB[q3], SBr[q3]
                        sbuf, sbr = ssb[:, q3, :], ssb_r[q3]
                        pt, ptr = pTs[:, q3, :, :], pTs_r[q3]
                        isb += 1
                        for hl in range(4):
                            h = hg * 4 + hl
                            Qh = QA if h % 2 == 0 else QB
                            self.mm(S, bs[:, hl * 128:(hl + 1) * 128],
                                    KTr[:, sk, h // 2, subk * 128:(subk + 1) * 128],
                                    Qh[:, h // 2, s * 128:(s + 1) * 128], True, True,
                                    [KT_r[sk][h // 2], QAB_r[h // 2]], [bsr])
                        self.stt(S, sbuf, bs[:, :], 0.125,
                                 biasT[:, kt, hg * 4:(hg + 1) * 4, :].rearrange("p h q -> p (h q)"),
                                 ALU.mult, ALU.add, [bsr, bias_r], [sbr])
                        self.act(S, pt.rearrange("p h q -> p (h q)"), sbuf, AF.Exp, [sbr], [ptr])
                        slots[step] = (pt, ptr, sk, subk)

                    def emit_PV(step):
                        hg, kt = step
                        pt, ptr, sk, subk = slots.pop(step)
                        po, por = po_of[hg]
                        for hl in range(4):
                            h = hg * 4 + hl
                            self.mm(S, po[:, hl * 65:(hl + 1) * 65], pt[:, hl, :], Vr[:, sk * NSUB + subk, h, :],
                                    kt == kts[0] and hl == 0, kt == kts[-1], [ptr, V_r[sk * NSUB + subk]], [por],
                                    skip=True)
                        if kt == kts[-1]:
                            pv = po[:, 0:260].rearrange("p (h e) -> p h e", h=4)
                            rq, rqr = rr[:, hg, :], rr_r[hg]
                            S.op(DVE, (lambda pv=pv, rq=rq: (lambda e: e.reciprocal(out=rq, in_=pv[:, :, 64])))(), [por], [rqr])
                            self.tt(S, DVE, mx[:, hg * 256:(hg + 1) * 256].rearrange("p (h e) -> p h e", h=4),
                                    pv[:, :, 0:64], rq.unsqueeze(2).to_broadcast([128, 4, 64]), ALU.mult,
                                    [por, rqr], [mxr])

                    for i_ in range(min(2, len(steps))):
                        emit_S(steps[i_])
                    for i_, step in enumerate(steps):
                        if i_ + 2 < len(steps):
                            emit_S(steps[i_ + 2])
                        emit_PV(step)
                        if i_ == min(3, len(steps) - 1) and s >= 1:
                            tail(P - 1)
                            if ti + 1 < NT:
                                hnds[s - 1] = do_stats(ti + 1, s - 1)
                                if s >= 2:
                                    do_tr(hnds.pop(s - 2), s - 2)
                                if s == NSUB - 1:
                                    hnds[NSUB - 1] = do_stats(ti + 1, NSUB - 1)
                    self.mem_pv(S, s, pmT, pmT_r, mvx, mem_r, pom, pomr, rr[:, 3, :], rr_r[3], mx, mxr)
                if ti + 1 < NT:
                    do_tr(hnds.pop(NSUB - 2), NSUB - 2)
                    do_tr(hnds.pop(NSUB - 1), NSUB - 1)
                    pending_tail = ti * NSUB + NSUB - 1
                else:
                    tail(ti * NSUB + NSUB - 1)
            self.stats["bmix"] = S.emit(self.G)


def host_consts():
    ident = np.eye(128, dtype=np.float32)
    s = np.arange(128)[:, None]
    t = np.arange(128)[None, :]
    negU = np.where(s <= t, -1.0, 0.0).astype(np.float32)
    maskc = np.where(s <= t, np.float32(DH ** -0.5), np.float32(0.0)).astype(np.float32)
    return {"c_ident": ident, "c_negU": negU, "c_maskc": maskc}


def host_layout(inp):
    f = lambda a: np.ascontiguousarray(np.asarray(a, dtype=np.float32))
    g = np.stack([f(inp["norm_mix_g"])[0], f(inp["norm_ffn_g"])[0], f(inp["kv_norm_g"]),
                  f(inp["norm_mix_g"])[1], f(inp["norm_ffn_g"])[1]], 0)
    gT_all = np.ascontiguousarray(g.reshape(5, KC, 128).transpose(2, 0, 1))
    cwA = np.ascontiguousarray(f(inp["a_conv_w"])[0].reshape(4, 16, 96).transpose(2, 1, 0))
    cbA = np.ascontiguousarray(f(inp["a_conv_b"])[0].reshape(16, 96).T)
    cwF = np.ascontiguousarray(f(inp["ffn_conv_w"]).reshape(2, 3, NFT, 128).transpose(3, 0, 2, 1))
    cbF = np.ascontiguousarray(f(inp["ffn_conv_b"]).reshape(2, NFT, 128).transpose(2, 0, 1))
    rel = np.arange(768) - 127
    idx = np.clip(rel, -63, 128) + 63
    relext = f(inp["b_rel_bias"])[0][:, idx]
    kj = np.arange(128)[:, None, None]
    kt = np.arange(5)[None, :, None]
    qi = np.arange(128)[None, None, :]
    gidx = qi - kj + (4 - kt) * 128 + 127
    relbias = np.ascontiguousarray(relext[:, gidx].transpose(1, 2, 0, 3))
    shared = {
        "a_w_in": f(inp["a_w_in"])[0], "a_w_out": f(inp["a_w_out"])[0], "w_kv": f(inp["w_kv"]),
        "b_w_in": f(inp["b_w_in"])[0], "b_w_out": f(inp["b_w_out"])[0], "mem_w_kv": f(inp["mem_w_kv"]),
        "ffn_w_up": f(inp["ffn_w_up"]), "ffn_w_down": f(inp["ffn_w_down"]),
        "gT_all": gT_all, "final_g": f(inp["final_g"]).reshape(1, D), "gate_b": f(inp["a_gate_b"]).reshape(1, 8),
        "cwA": cwA, "cbA": cbA, "head_g": f(inp["a_head_g"]).reshape(1, AW), "cwF": cwF, "cbF": cbF,
        "relbias": relbias,
    }
    shared.update(host_consts())
    return shared


_CACHE = {}


def run(inputs, phases=("A_mix", "A_ffn", "B_mix", "B_ffn"), final_norm=True, ncores=8, trace=False):
    key = (tuple(phases), final_norm)
    if key not in _CACHE:
        _CACHE[key] = Prog(phases, final_norm).build()
    nc = _CACHE[key]
    shared = host_layout(inputs)
    x = np.asarray(inputs["x"], dtype=np.float32)
    mem = np.asarray(inputs["mem"], dtype=np.float32)
    in_maps = []
    for c in range(ncores):
        m = dict(shared)
        m["x"] = np.ascontiguousarray(x[c])
        m["mem"] = np.ascontiguousarray(mem[c])
        in_maps.append(m)
    res = run_bass_kernel_spmd(nc, in_maps, core_ids=list(range(ncores)), trace=trace)
    out = np.stack([np.asarray(r["out"]) for r in res.results], 0)
    return out, res


def kernel(**inputs):
    out, _ = run(inputs)
    return out.astype(np.float32)
```

```python
import numpy as np
from contextlib import ExitStack
import concourse.bass as bass
import concourse.mybir as mybir
from concourse.bass_types import AP
from concourse.bass_utils import run_bass_kernel_spmd

F32 = mybir.dt.float32
BF16 = mybir.dt.bfloat16
ALU = mybir.AluOpType
AF = mybir.ActivationFunctionType
AX = mybir.AxisListType

PE, ACT, DVE, POOL, SP = "tensor", "scalar", "vector", "gpsimd", "sync"
ENGS = (PE, ACT, DVE, POOL, SP)

D = 1024
KC = 8
SEQ = 4096
T = 512
import os
NT = int(os.environ.get("K_NT", SEQ // T))
SUB = 128
NSUB = T // SUB
DFF = 2816
NFT = DFF // 128
A_IN = 3336
AW = 768
DH = 192
MEMT = 256
EPS = 1e-6
NEG = -30000.0
XSLOTS = 6


class Region:
    __slots__ = ("name", "writer", "readers", "strict")

    def __init__(self, name):
        self.name = name
        self.writer = None
        self.readers = []
        self.strict = False


class _Op:
    __slots__ = ("eng", "idx", "fn", "waits", "needs_inc", "dma_key", "snap")

    def __init__(self, eng, idx, fn):
        self.eng = eng
        self.idx = idx
        self.fn = fn
        self.waits = []
        self.needs_inc = False
        self.dma_key = None
        self.snap = None


class Sched:
    def __init__(self, nc):
        self.nc = nc
        self.ops = {e: [] for e in ENGS}
        self.clock = {e: {x: -1 for x in ENGS} for e in ENGS}
        self.dclock = {e: {} for e in ENGS}
        self.dma_count = {}
        self.all_regions = []

    def region(self, name=None):
        r = Region(name or f"r{len(self.all_regions)}")
        self.all_regions.append(r)
        return r

    def regions(self, n, name="r"):
        return [self.region(f"{name}{i}") for i in range(n)]

    def _add(self, eng, fn, reads, writes, dma_key=None):
        o = _Op(eng, len(self.ops[eng]), fn)
        o.dma_key = dma_key
        deps = []
        for r in reads:
            if r.writer is not None:
                deps.append((r.writer, True))
        for w in writes:
            if w.writer is not None:
                deps.append((w.writer, w.strict))
            for rd in w.readers:
                deps.append((rd, w.strict))
        clk = self.clock[eng]
        dclk = self.dclock[eng]
        for tok, is_raw in deps:
            if tok[0] == "c":
                _, e2, n = tok
                if e2 == eng and (not is_raw or eng == PE):
                    continue
                if clk[e2] >= n:
                    continue
                o.waits.append(tok)
                self.ops[e2][n].needs_inc = True
                clk[e2] = n
                sn = self.ops[e2][n].snap
                for k, v in sn.items():
                    if k != eng and clk[k] < v:
                        clk[k] = v
            else:
                _, key, val = tok
                if dclk.get(key, 0) >= val:
                    continue
                cur = self.dma_count[key]
                o.waits.append(("d", key, cur))
                dclk[key] = cur
        o.snap = dict(clk)
        self.ops[eng].append(o)
        if dma_key is not None:
            self.dma_count[dma_key] = self.dma_count.get(dma_key, 0) + 16
            tok = ("d", dma_key, self.dma_count[dma_key])
        else:
            tok = ("c", eng, o.idx)
        for r in reads:
            r.readers.append(tok)
        for w in writes:
            w.writer = tok
            w.readers = []
        return o

    def op(self, eng, fn, reads=(), writes=()):
        return self._add(eng, fn, reads, writes)

    def dma(self, eng, fn, reads=(), writes=(), key=None):
        return self._add(eng, fn, reads, writes, dma_key=key.name + "@" + eng)

    def emit(self, G):
        nc = self.nc
        self._add(SP, None, list(self.all_regions), list(self.all_regions))
        keys = list(self.dma_count)
        slot = {}
        nsw = nhw = 0
        for k in keys:
            if k.endswith("@" + POOL):
                slot[k] = nsw
                nsw += 1
            else:
                slot[k] = G.NSW + nhw
                nhw += 1
        assert nsw <= G.NSW and nhw <= G.NDMA - G.NSW, (nsw, nhw)
        dsem = {k: G.dsem[slot[k]] for k in keys}
        dbase = {k: G.dbase[slot[k]] for k in keys}
        esem, ebase = G.esem, dict(G.ebase)
        G.phase += 1
        barv = G.phase
        cnt = {}
        for e in ENGS:
            c = 0
            arr = []
            for o in self.ops[e]:
                if o.needs_inc:
                    c += 1
                arr.append(c)
            cnt[e] = arr
            G.ebase[e] += c
        for k in keys:
            G.dbase[slot[k]] += self.dma_count[k]
        with nc.Block() as block:
            def make(e):
                def body(engh):
                    for o in self.ops[e]:
                        for w in o.waits:
                            if w[0] == "c":
                                engh.wait_ge(esem[w[1]], ebase[w[1]] + cnt[w[1]][w[2]])
                            else:
                                engh.wait_ge(dsem[w[1]], dbase[w[1]] + w[2])
                        if o.fn is None:
                            continue
                        ins = o.fn(engh)
                        if o.dma_key is not None:
                            ins.then_inc(dsem[o.dma_key], 16)
                        elif o.needs_inc:
                            ins.then_inc(esem[e], 1)
                    if e == SP:
                        engh.sem_inc(G.bar, 1)
                    else:
                        engh.wait_ge(G.bar, barv)
                return body

            for e in ENGS:
                getattr(block, e)(make(e))
        return {e: len(self.ops[e]) for e in ENGS}


class SemPool:
    NDMA = 56
    NSW = 16

    def __init__(self, nc, st):
        self.esem = {e: st.enter_context(nc.semaphore(f"s_{e}")) for e in ENGS}
        self.dsem = [st.enter_context(nc.semaphore(f"d_{i}")) for i in range(self.NDMA)]
        self.bar = st.enter_context(nc.semaphore("bar"))
        self.ebase = {e: 0 for e in ENGS}
        self.dbase = [0] * self.NDMA
        self.phase = 0
        allsem = list(self.esem.values()) + self.dsem + [self.bar]
        with nc.Block() as block:
            @block.gpsimd
            def _(g):
                for s in allsem:
                    g.sem_clear(s)
        nc.all_engine_barrier()


class Prog:
    def __init__(self, phases, final_norm=True):
        self.nc = nc = bass.Bass("TRN2", target_bir_lowering=False)
        self.phases = phases
        self.final_norm = final_norm
        din = lambda n, s: nc.dram_tensor(n, list(s), F32, kind="ExternalInput").ap()
        self.x = din("x", (SEQ, D))
        self.mem = din("mem", (MEMT, D))
        self.a_w_in = din("a_w_in", (D, A_IN))
        self.a_w_out = din("a_w_out", (D, D))
        self.w_kv = din("w_kv", (D, 1536))
        self.b_w_in = din("b_w_in", (D, D))
        self.b_w_out = din("b_w_out", (D, D))
        self.mem_w_kv = din("mem_w_kv", (2, D, 512))
        self.ffn_w_up = din("ffn_w_up", (2, D, 2 * DFF))
        self.ffn_w_down = din("ffn_w_down", (2, DFF, D))
        self.gT_all = din("gT_all", (128, 5, KC))
        self.final_g = din("final_g", (1, D))
        self.gate_b = din("gate_b", (1, 8))
        self.cwA = din("cwA", (96, 16, 4))
        self.cbA = din("cbA", (96, 16))
        self.head_g = din("head_g", (1, AW))
        self.cwF = din("cwF", (128, 2, NFT, 3))
        self.cbF = din("cbF", (128, 2, NFT))
        self.relbias = din("relbias", (128, 5, 12, 128))
        self.c_ident = din("c_ident", (128, 128))
        self.c_negU = din("c_negU", (128, 128))
        self.c_maskc = din("c_maskc", (128, 128))
        self.xa = nc.dram_tensor("xa", [SEQ, D], F32, kind="Internal").ap()
        self.xb = nc.dram_tensor("xb", [SEQ, D], F32, kind="Internal").ap()
        self.out = nc.dram_tensor("out", [SEQ, D], F32, kind="ExternalOutput").ap()
        self.stats = {}

    def mm(self, S, out, lhsT, rhs, start, stop, reads, writes, skip=False):
        S.op(PE, lambda e: e.matmul(out, lhsT=lhsT, rhs=rhs, start=start, stop=stop,
                                    skip_group_check=skip), reads, writes)

    def tr(self, S, out, in_, ident, reads, writes):
        S.op(PE, lambda e: e.transpose(out=out, in_=in_, identity=ident), reads, writes)

    def act(self, S, out, in_, func, reads, writes, **kw):
        S.op(ACT, lambda e: e.activation(out=out, in_=in_, func=func, **kw), reads, writes)

    def tt(self, S, eng, out, in0, in1, op, reads, writes):
        S.op(eng, lambda e: e.tensor_tensor(out=out, in0=in0, in1=in1, op=op), reads, writes)

    def ts(self, S, eng, out, in0, s1, s2, op0, op1, reads, writes):
        if op1 is None:
            S.op(eng, lambda e: e.tensor_scalar(out=out, in0=in0, scalar1=s1, scalar2=None, op0=op0),
                 reads, writes)
        else:
            S.op(eng, lambda e: e.tensor_scalar(out=out, in0=in0, scalar1=s1, scalar2=s2, op0=op0, op1=op1),
                 reads, writes)

    def stt(self, S, out, in0, scalar, in1, op0, op1, reads, writes):
        S.op(DVE, lambda e: e.scalar_tensor_tensor(out=out, in0=in0, scalar=scalar, in1=in1, op0=op0, op1=op1),
             reads, writes)

    def cp(self, S, eng, out, in_, reads, writes):
        if eng == ACT:
            S.op(ACT, lambda e: e.copy(out=out, in_=in_), reads, writes)
        else:
            S.op(eng, lambda e: e.tensor_copy(out=out, in_=in_), reads, writes)

    def ms(self, S, eng, ap, val, writes):
        S.op(eng, lambda e: e.memset(ap, val), (), writes)

    def ld(self, S, eng, out, in_, writes, key, reads=(), **kw):
        S.dma(eng, lambda e: e.dma_start(out=out, in_=in_, **kw), reads, writes, key=key)

    def load_w(self, S, dst, reg, src2d, c0, c1, kc0=0, kc1=None):
        kcn = src2d.shape[0] // 128
        kc1 = kcn if kc1 is None else kc1
        src = src2d.rearrange("(kc p) n -> p kc n", p=128)
        step = 2048
        for a in range(c0, c1, step):
            b = min(c1, a + step)
            self.ld(S, POOL, dst[:, kc0:kc1, a:b], src[:, kc0:kc1, a:b], [reg], reg)

    def norm_stats(self, S, C, xt_ap, xr):
        i = C["nrm_i"]
        C["nrm_i"] += 1
        k = i % 2
        ss = C["ss"][:, k, 0:1]
        ms_ = C["ss"][:, k, 1:2]
        rstd = C["ss"][:, k, 2:3]
        ssr = C["ss_r"][k]
        xh = C["xh"][:, k, :]
        xhr = C["xh_r"][k]
        self.act(S, xh, xt_ap, AF.Square, [xr], [xhr, ssr], accum_out=ss)
        self.ts(S, DVE, ms_, ss, 1.0 / D, EPS, ALU.mult, ALU.add, [ssr], [ssr])
        self.tt(S, POOL, rstd, ms_, C["neghalf"][:, 0:1], ALU.pow, [ssr, C["const_r"]], [ssr])
        self.act(S, xh, xt_ap, AF.Copy, [xr, ssr], [xhr], scale=rstd)
        return (xh, xhr)

    def norm_tr(self, S, C, hnd, outs):
        xh, xhr = hnd
        pT = C["pT"]
        for kc in range(KC):
            self.tr(S, pT[:, kc * 128:(kc + 1) * 128], xh[:, kc * 128:(kc + 1) * 128], C["identb"][:, :],
                    [xhr, C["const_r"]], [C["pT_r"]])
        pT3 = pT[:, :].rearrange("p (a b) -> p a b", a=KC)
        for gT, dst, dr in outs:
            self.tt(S, DVE, dst, pT3, gT.unsqueeze(2).to_broadcast([128, KC, 128]), ALU.mult,
                    [C["pT_r"], C["const_r"]], [dr])

    def norm_T(self, S, C, xt_ap, xr, outs):
        self.norm_tr(S, C, self.norm_stats(S, C, xt_ap, xr), outs)

    def phase_consts(self, S, st, which_gains):
        nc = self.nc
        C = {"nrm_i": 0}
        self.phase_i = getattr(self, "phase_i", 0) + 1
        pfx = f"p{self.phase_i}_"
        sb = lambda n, s, d: st.enter_context(nc.sbuf_tensor(pfx + n, list(s), d))
        ps = lambda n, s, d: st.enter_context(nc.psum_tensor(pfx + n, list(s), d))
        C["sb"] = sb
        C["ps"] = ps
        C["const_r"] = cr = S.region("const")
        C["identb"] = sb("identb", (128, 128), BF16)
        C["neghalf"] = sb("neghalf", (128, 1), F32)
        C["gT"] = sb("gT", (128, 5, KC), F32)
        self.ld(S, POOL, C["identb"][:, :], self.c_ident[:, :], [cr], cr)
        self.ld(S, SP, C["gT"][:, :, :], self.gT_all[:, :, :], [cr], cr)
        self.ms(S, DVE, C["neghalf"][:, :], -0.5, [cr])
        C["ss"] = sb("ss", (128, 2, 4), F32)
        C["ss_r"] = S.regions(2, "ss")
        C["xh"] = sb("xh", (128, 2, D), BF16)
        C["xh_r"] = S.regions(2, "xh")
        C["pT"] = ps("pT", (128, D), BF16)
        C["pT_r"] = S.region("pT")
        C["xt"] = sb("xt", (128, XSLOTS, D), F32)
        C["xt_r"] = S.regions(XSLOTS, "xt")
        return C

    def load_x(self, S, C, src, gi):
        sl = gi % XSLOTS
        self.ld(S, SP, C["xt"][:, sl, :], src[gi * 128:(gi + 1) * 128, :], [C["xt_r"][sl]], C["xt_r"][sl])

    def phase_ffn(self, l, src, dst, final):
        nc = self.nc
        S = Sched(nc)
        with ExitStack() as st:
            C = self.phase_consts(S, st, None)
            sb, ps = C["sb"], C["ps"]
            gT = C["gT"][:, 1 if l == 0 else 4, :]
            wup = sb("wup", (128, KC, 2 * DFF), BF16)
            wup_r = S.regions(4, "wup")
            wdn = sb("wdn", (128, NFT, D), BF16)
            wdn_r = S.regions(2, "wdn")
            hT = sb("hT", (128, KC, T), BF16)
            hT_r = S.regions(NSUB, "hT")
            aT = sb("aT", (128, NFT, T), BF16)
            aT_r = S.regions(NFT, "aT")
            graw = sb("graw", (128, 2, T + 2), F32)
            graw_r = S.regions(2, "graw")
            acc = sb("acc", (128, 2, T), F32)
            acc_r = S.regions(2, "acc")
            tnh = sb("tnh", (128, 2, T), F32)
            tnh_r = S.regions(2, "tnh")
            halo = sb("halo", (128, NFT, 2), F32)
            halo_r = S.regions(NFT, "halo")
            cw = sb("cw", (128, NFT, 3), F32)
            cb = sb("cb", (128, NFT), F32)
            cr = C["const_r"]
            pb = [ps(f"pb{i}", (128, 512), F32) for i in range(7)]
            pb_r = S.regions(7, "pb")
            if final:
                fg = sb("fg", (128, D), F32)
                self.ld(S, SP, fg[:, :], self.final_g.partition_broadcast(128), [cr], cr)
                fss = sb("fss", (128, 2, 4), F32)
                fss_r = S.regions(2, "fss")
                for r_ in C["xh_r"]:
                    r_.strict = True
            for gi in range(min(XSLOTS, NSUB + 2)):
                self.load_x(S, C, src, gi)
            loaded = min(XSLOTS, NSUB + 2)
            self.ld(S, SP, cw[:, :, :], self.cwF[:, l, :, :], [cr], cr)
            self.ld(S, SP, cb[:, :], self.cbF[:, l, :], [cr], cr)
            self.ts(S, POOL, cw[:, :, :], cw[:, :, :], 0.5, None, ALU.mult, None, [cr], [cr])
            self.ts(S, POOL, cb[:, :], cb[:, :], 0.5, None, ALU.mult, None, [cr], [cr])
            self.ms(S, POOL, halo[:, :, :], 0.0, halo_r)
            wu = self.ffn_w_up[l]
            for c in (2, 0, 3, 1):
                self.load_w(S, wup, wup_r[c], wu, c * 1408, (c + 1) * 1408)
            wd = self.ffn_w_down[l]
            self.load_w(S, wdn, wdn_r[0], wd, 0, D, 0, 11)
            self.load_w(S, wdn, wdn_r[1], wd, 0, D, 11, 22)

            pbi = 0

            def do_stats(ti, s):
                gi = ti * NSUB + s
                sl = gi % XSLOTS
                return self.norm_stats(S, C, C["xt"][:, sl, :], C["xt_r"][sl])

            def do_tr(hnd, s):
                self.norm_tr(S, C, hnd, [(gT, hT[:, :, s * 128:(s + 1) * 128], hT_r[s])])

            for s in range(NSUB):
                do_tr(do_stats(0, s), s)
            for ti in range(NT):
                for j in range(NFT):
                    pu, pur = pb[pbi % 6], pb_r[pbi % 6]
                    pg, pgr = pb[(pbi + 1) % 6], pb_r[(pbi + 1) % 6]
                    pbi += 2
                    for kc in range(KC):
                        self.mm(S, pg[:, :], wup[:, kc, DFF + j * 128:DFF + (j + 1) * 128], hT[:, kc, :],
                                kc == 0, kc == KC - 1, hT_r + [wup_r[2 + j // 11]], [pgr])
                    for kc in range(KC):
                        self.mm(S, pu[:, :], wup[:, kc, j * 128:(j + 1) * 128], hT[:, kc, :],
                                kc == 0, kc == KC - 1, hT_r + [wup_r[j // 11]], [pur])
                    r = j % 2
                    gr, grr = graw[:, r, :], graw_r[r]
                    ac, acr = acc[:, r, :], acc_r[r]
                    tn, tnr = tnh[:, r, :], tnh_r[r]
                    self.cp(S, POOL, gr[:, 0:2], halo[:, j, :], [halo_r[j]], [grr])
                    self.cp(S, ACT, gr[:, 2:T + 2], pg[:, :], [pgr], [grr])
                    self.cp(S, POOL, halo[:, j, :], gr[:, T:T + 2], [grr], [halo_r[j]])
                    self.act(S, ac, pg[:, :], AF.Identity, [pgr, cr], [acr], scale=cw[:, j, 2:3], bias=cb[:, j:j + 1])
                    self.stt(S, ac, gr[:, 1:T + 1], cw[:, j, 1:2], ac, ALU.mult, ALU.add, [grr, cr, acr], [acr])
                    self.stt(S, ac, gr[:, 0:T], cw[:, j, 0:1], ac, ALU.mult, ALU.add, [grr, cr, acr], [acr])
                    self.act(S, tn, ac, AF.Tanh, [acr], [tnr])
                    self.tt(S, DVE, gr[:, 2:T + 2], ac, pu[:, :], ALU.mult, [acr, pur], [grr])
                    self.stt(S, aT[:, j, :], tn, 1.0, gr[:, 2:T + 2], ALU.add, ALU.mult, [tnr, grr], [aT_r[j]])
                hnds = {}
                for s in range(NSUB):
                    gi = ti * NSUB + s
                    sl = gi % XSLOTS
                    xt = C["xt"][:, sl, :]
                    xr = C["xt_r"][sl]
                    for hf in range(2):
                        po, por = pb[6], pb_r[6]
                        if hf == 1:
                            po, por = pb[pbi % 6], pb_r[pbi % 6]
                            pbi += 1
                        for kc in range(NFT):
                            self.mm(S, po[:, :], aT[:, kc, s * 128:(s + 1) * 128], wdn[:, kc, hf * 512:(hf + 1) * 512],
                                    kc == 0, kc == NFT - 1, [aT_r[kc], wdn_r[kc // 11]], [por])
                        self.tt(S, DVE, xt[:, hf * 512:(hf + 1) * 512], po[:, :], xt[:, hf * 512:(hf + 1) * 512],
                                ALU.add, [por, xr], [xr])
                    if final:
                        k = gi % 2
                        ss = fss[:, k, 0:1]
                        ms_ = fss[:, k, 1:2]
                        rstd = fss[:, k, 2:3]
                        self.act(S, C["xh"][:, k, :], xt, AF.Square, [xr], [C["xh_r"][k], fss_r[k]], accum_out=ss)
                        self.ts(S, DVE, ms_, ss, 1.0 / D, EPS, ALU.mult, ALU.add, [fss_r[k]], [fss_r[k]])
                        self.tt(S, POOL, rstd, ms_, C["neghalf"][:, 0:1], ALU.pow, [fss_r[k], cr], [fss_r[k]])
                        self.stt(S, xt, xt, rstd, fg[:, :], ALU.mult, ALU.mult, [xr, fss_r[k], cr], [xr])
                    self.ld(S, SP, dst[gi * 128:(gi + 1) * 128, :], xt, [], xr, reads=[xr])
                    if loaded < NT * NSUB and loaded % XSLOTS == sl:
                        self.load_x(S, C, src, loaded)
                        loaded += 1
                    if ti + 1 < NT:
                        hnds[s] = do_stats(ti + 1, s)
                        if s >= 1:
                            do_tr(hnds.pop(s - 1), s - 1)
                if ti + 1 < NT:
                    do_tr(hnds.pop(NSUB - 1), NSUB - 1)
            self.stats[f"ffn{l}"] = S.emit(self.G)

    def build(self):
        chain = {"A_mix": self.phase_amix, "A_ffn": lambda s, d, f: self.phase_ffn(0, s, d, False),
                 "B_mix": self.phase_bmix, "B_ffn": lambda s, d, f: self.phase_ffn(1, s, d, f)}
        src = self.x
        scr = [self.xa, self.xb]
        with ExitStack() as gst:
            self.G = SemPool(self.nc, gst)
            for i, ph in enumerate(self.phases):
                last = i == len(self.phases) - 1
                dst = self.out if last else scr[i % 2]
                chain[ph](src, dst, last and self.final_norm)
                src = dst
        return self.nc

    def phase_amix(self, src, dst, final):
        nc = self.nc
        S = Sched(nc)
        with ExitStack() as st:
            C = self.phase_consts(S, st, None)
            sb, ps = C["sb"], C["ps"]
            cr = C["const_r"]
            gT_mix = C["gT"][:, 0, :]
            wain = sb("wain", (128, KC, A_IN), BF16)
            wain_r = S.regions(4, "wain")
            waout = sb("waout", (128, KC, D), BF16)
            waout_r = S.regions(1, "waout")
            hT = sb("hT", (128, KC, T), BF16)
            hT_r = S.regions(NSUB, "hT")
            qkT = sb("qkT", (128, 16, T), BF16)
            qk_r = S.regions(16, "qk")
            qmT = sb("qmT", (128, 2, T), BF16)
            qm_r = S.regions(2, "qm")
            raw = sb("raw", (128, 2, T + 3), F32)
            raw_r = S.regions(2, "raw")
            rawh_r = S.regions(2, "rawh")
            acc = sb("acc", (128, 3, T), F32)
            acc_r = S.regions(3, "acc")
            tnh = sb("tnh", (128, 2, T), F32)
            tnh_r = S.regions(2, "tnh")
            halo = sb("halo", (128, 16, 3), F32)
            halo_r = S.regions(16, "halo")
            cw = sb("cw", (128, 16, 4), F32)
            cb = sb("cb", (128, 16), F32)
            ktok = sb("ktok", (128, NSUB, AW), BF16)
            ktok_r = S.regions(NSUB, "ktok")
            vw = sb("vw", (128, NSUB, 4, DH + 1), BF16)
            vw_r = S.regions(NSUB, "vw")
            G2 = sb("G2", (128, NSUB, AW), F32)
            G2_r = S.regions(NSUB, "G2")
            hgh = sb("hgh", (128, AW), F32)
            gb_bc = sb("gb_bc", (128, 8), F32)
            gsb = sb("gsb", (128, NSUB, 8), F32)
            gw = sb("gw", (128, 16, 16), F32)
            g_r = S.region("gates")
            EP, SPL, AA, BBL, AMX, MALL, MST, T48 = 0, 1, 2, 3, 5, 6, 7, 8
            WGF = 11
            mcar = sb("mcar", (128, 4), F32)
            am16 = sb("am16", (16, 20), F32)
            identF = sb("identF", (128, 128), F32)
            negU = sb("negU", (128, 128), F32)
            negO = sb("negO", (128, 128), F32)
            ones16 = sb("ones16", (16, 128), F32)
            maskc = sb("maskc", (128, 1, 128), F32)
            Cn = sb("Cn", (128, 4, 2, DH + 1), F32)
            Cn_r = S.regions(4, "Cn")
            Gbf = sb("Gbf", (128, 2, 4, 2, DH + 1), BF16)
            Gbf_r = [S.regions(4, "GbfA"), S.regions(4, "GbfB")]
            sm = sb("sm", (128, 2, 4, 128), BF16)
            sm_r = S.regions(2, "sm")
            hm = sb("hm", (128, 2, 4, DH), F32)
            hm_r = S.regions(2, "hm")
            hj = sb("hj", (128, 4, DH), BF16)
            hj_r = S.regions(4, "hj")
            for r_ in hj_r:
                r_.strict = True
            hst = sb("hst", (128, 2, 16), F32)
            hst_r = S.regions(2, "hst")
            mkT = sb("mkT", (128, 2, MEMT), BF16)
            mvx = sb("mvx", (128, 2, 4, 65), BF16)
            mem_r = S.region("memkv")
            pmT = sb("pmT", (128, 8, T), BF16)
            pmT_r = S.region("pmT")
            mix = sb("mix", (128, 2, D), BF16)
            mix_r = S.regions(2, "mix")
            mixT = sb("mixT", (128, 2, KC, 128), BF16)
            mixT_r = S.regions(2, "mixT")
            rr = sb("rr", (128, 4), F32)
            rr_r = S.region("rr")
            pb = [ps(f"pb{i}", (128, 512), F32) for i in range(7)]
            pb_r = S.regions(7, "pb")

            self.mem_prologue(S, C, 0, qkT[:, 0:8, :], qk_r[0:8], hT, hT_r, mkT, mvx, mem_r, pb[0], pb_r[0])
            self.load_w(S, wain, wain_r[0], self.a_w_in, 0, 1536)
            self.load_w(S, wain, wain_r[1], self.a_w_in, 1536, 2304)
            self.load_w(S, wain, wain_r[2], self.a_w_in, 2304, 3080)
            self.load_w(S, wain, wain_r[3], self.a_w_in, 3080, A_IN)
            self.load_w(S, waout, waout_r[0], self.a_w_out, 0, D)
            self.ld(S, SP, cw[0:96, :, :], self.cwA[:, :, :], [cr], cr)
            self.ld(S, SP, cb[0:96, :], self.cbA[:, :], [cr], cr)
            self.ld(S, SP, hgh[:, :], self.head_g.partition_broadcast(128), [cr], cr)
            self.ld(S, SP, gb_bc[:, :], self.gate_b.partition_broadcast(128), [cr], cr)
            self.ld(S, SP, identF[:, :], self.c_ident[:, :], [cr], cr)
            self.ld(S, SP, negU[:, :], self.c_negU[:, :], [cr], cr)
            self.ld(S, SP, maskc[:, 0, :], self.c_maskc[:, :], [cr], cr)
            self.ts(S, POOL, cw[0:96, :, :], cw[0:96, :, :], 0.5, None, ALU.mult, None, [cr], [cr])
            self.ts(S, POOL, cb[0:96, :], cb[0:96, :], 0.5, None, ALU.mult, None, [cr], [cr])
            self.ts(S, POOL, hgh[:, :], hgh[:, :], 0.5, None, ALU.mult, None, [cr], [cr])
            self.ms(S, POOL, negO[:, :], -1.0, [cr])
            self.ms(S, POOL, ones16[:, :], 1.0, [cr])
            self.ms(S, POOL, halo[:, :, :], 0.0, halo_r)
            self.ms(S, POOL, Cn[:, :, :, :], 0.0, Cn_r)
            self.ms(S, POOL, mcar[:, :], 0.0, [g_r])
            self.ms(S, POOL, vw[:, :, :, DH:DH + 1], 1.0, vw_r)
            for gi in range(XSLOTS):
                self.load_x(S, C, src, gi)
            loaded = XSLOTS
            ia = 0
            WK = 14
            ga = lambda i, n=1: gw[:, i:i + n, :].rearrange("p a c -> p (a c)")
            g3 = lambda i: gw[:, i, :].rearrange("p (s h) -> p s h", s=NSUB)
            wv, gv, flv, wkv_ = g3(WGF), g3(WGF + 1), g3(WGF + 2), g3(WK)
            pg, pgr = pb[6], pb_r[6]

            def gate_stages():
                for s in range(NSUB):
                    for kc in range(KC):
                        self.mm(S, pg[:, s * 8:(s + 1) * 8], hT[:, kc, s * 128:(s + 1) * 128], wain[:, kc, 3072:3080],
                                kc == 0, kc == KC - 1, [hT_r[s], wain_r[2]], [pgr])
                self.tt(S, DVE, gsb[:, :, :], pg[:, 0:32].rearrange("p (s g) -> p s g", s=NSUB),
                        gb_bc[:, :].unsqueeze(1).to_broadcast([128, NSUB, 8]), ALU.add, [pgr, cr], [g_r])
                self.act(S, g3(EP), gsb[:, :, 4:8], AF.Exp, [g_r], [g_r], scale=-1.0)
                self.act(S, ga(SPL), ga(EP), AF.Ln, [g_r], [g_r], bias=1.0)
                yield
                self.mm(S, pg[:, 32:48], negU[:, :], ga(SPL), True, True, [g_r, cr], [pgr])
                self.mm(S, pg[:, 48:64], negO[:, :], ga(SPL), True, True, [g_r, cr], [pgr])
                self.cp(S, DVE, ga(BBL, 2), pg[:, 32:64], [pgr], [g_r])
                self.tt(S, DVE, g3(AA), gsb[:, :, 0:4], g3(BBL), ALU.subtract, [g_r], [g_r])
                yield
                self.tr(S, pg[0:16, 64:192], ga(AA), identF[:, :], [g_r, cr], [pgr])
                S.op(DVE, lambda e: e.reduce_max(out=am16[:, 0:1], in_=pg[0:16, 64:192], axis=AX.X), [pgr], [g_r])
                self.ts(S, DVE, am16[:, 4:20], identF[0:16, 0:16], am16[:, 0:1], None, ALU.mult, None, [g_r, cr], [g_r])
                yield
                self.mm(S, pg[:, 192:208], ones16[:, :], am16[:, 4:20], True, True, [g_r, cr], [pgr])
                self.cp(S, DVE, ga(AMX), pg[:, 192:208], [pgr], [g_r])
                yield
                for s in range(NSUB):
                    self.cp(S, DVE, g3(MST)[:, s, :], mcar[:, :], [g_r], [g_r])
                    self.tt(S, DVE, g3(MALL)[:, s, :], mcar[:, :], g3(AMX)[:, s, :], ALU.max, [g_r], [g_r])
                    self.tt(S, DVE, mcar[:, :], g3(BBL + 1)[:, s, :], g3(MALL)[:, s, :], ALU.add, [g_r], [g_r])
                self.tt(S, DVE, ga(T48), ga(AA), ga(MALL), ALU.subtract, [g_r], [g_r])
                self.tt(S, DVE, ga(T48 + 1), ga(MST), ga(MALL), ALU.subtract, [g_r], [g_r])
                self.stt(S, ga(T48 + 2), ga(BBL), -1.0, ga(MALL), ALU.mult, ALU.subtract, [g_r], [g_r])
                self.act(S, ga(WGF, 3), ga(T48, 3), AF.Exp, [g_r], [g_r])
                self.ts(S, DVE, ga(WK), ga(WGF), float(DH ** -0.5), None, ALU.mult, None, [g_r], [g_r])
                yield

            def tail_a(s_, ti_):
                gi = ti_ * NSUB + s_
                k2 = gi % 2
                self.out_proj_a(S, C, mix[:, k2, :], mix_r[k2], mixT[:, k2, :, :], mixT_r[k2], waout, waout_r,
                                [pb[0], pb[1]], [pb_r[0], pb_r[1]])

            def tail_b(s_, ti_):
                nonlocal loaded
                gi = ti_ * NSUB + s_
                sl = gi % XSLOTS
                self.out_proj_b(S, C["xt"][:, sl, :], C["xt_r"][sl], [pb[0], pb[1]], [pb_r[0], pb_r[1]], dst, gi)
                if loaded < NT * NSUB and loaded % XSLOTS == sl:
                    self.load_x(S, C, src, loaded)
                    loaded += 1

            def tail(s_, ti_):
                tail_a(s_, ti_)
                tail_b(s_, ti_)

            for ti in range(NT):
                NORM_IL = False
                for s in range(NSUB if (ti == 0 or not NORM_IL) else 0):
                    gi = ti * NSUB + s
                    sl = gi % XSLOTS
                    self.norm_T(S, C, C["xt"][:, sl, :], C["xt_r"][sl],
                                [(gT_mix, hT[:, :, s * 128:(s + 1) * 128], hT_r[s])])
                gs = gate_stages()
                next(gs)
                def side_groups():
                    nonlocal ia
                    for s in range(NSUB):
                        for g in range(2):
                            b, br = pb[ia % 6], pb_r[ia % 6]
                            ia += 1
                            for kc in range(KC):
                                self.mm(S, b[:, 0:384], hT[:, kc, s * 128:(s + 1) * 128],
                                        wain[:, kc, 1536 + g * 384:1536 + (g + 1) * 384], kc == 0, kc == KC - 1,
                                        [hT_r[s], wain_r[1]], [br])
                            self.cp(S, ACT if g == 0 else DVE, vw[:, s, 2 * g:2 * g + 2, 0:DH],
                                    b[:, 0:384].rearrange("p (h e) -> p h e", h=2), [br], [vw_r[s]])
                            yield
                        for g in range(2):
                            b, br = pb[ia % 6], pb_r[ia % 6]
                            ia += 1
                            for kc in range(KC):
                                self.mm(S, b[:, 0:384], hT[:, kc, s * 128:(s + 1) * 128],
                                        wain[:, kc, 2304 + g * 384:2304 + (g + 1) * 384], kc == 0, kc == KC - 1,
                                        [hT_r[s], wain_r[2]], [br])
                            g2 = G2[:, s, g * 384:(g + 1) * 384]
                            self.act(S, g2, b[:, 0:384], AF.Tanh, [br], [G2_r[s]], scale=0.5)
                            self.stt(S, g2, g2, 1.0, hgh[:, g * 384:(g + 1) * 384], ALU.add, ALU.mult, [G2_r[s], cr], [G2_r[s]])
                            yield
                    for j in range(2):
                        b, br = pb[ia % 6], pb_r[ia % 6]
                        ia += 1
                        for kc in range(KC):
                            self.mm(S, b[:, :], wain[:, kc, 3080 + j * 128:3080 + (j + 1) * 128], hT[:, kc, :], kc == 0,
                                    kc == KC - 1, hT_r + [wain_r[3]], [br])
                        self.cp(S, ACT, qmT[:, j, :], b[:, :], [br], [qm_r[j]])
                        yield
                sg = side_groups()
                pend = None
                for i in range(16):
                    b, br = pb[ia % 6], pb_r[ia % 6]
                    ia += 1
                    for kc in range(KC):
                        self.mm(S, b[0:96, :], wain[:, kc, i * 96:(i + 1) * 96], hT[:, kc, :], kc == 0, kc == KC - 1,
                                hT_r + [wain_r[0]], [br])
                    r = i % 2
                    rw, rwr = raw[0:96, r, :], raw_r[r]
                    r3 = i % 3
                    ac, acr = acc[0:96, r3, :], acc_r[r3]
                    tn, tnr = tnh[0:96, r, :], tnh_r[r]
                    self.cp(S, POOL, rw[:, 0:3], halo[0:96, i, :], [halo_r[i]], [rwr])
                    self.cp(S, ACT, rw[:, 3:T + 3], b[0:96, :], [br], [rwr])
                    self.cp(S, POOL, halo[0:96, i, :], rw[:, T:T + 3], [rwr], [halo_r[i]])
                    self.act(S, ac, b[0:96, :], AF.Identity, [br, cr], [acr], scale=cw[0:96, i, 3:4], bias=cb[0:96, i:i + 1])
                    for j in (2, 1, 0):
                        self.stt(S, ac, rw[:, j:j + T], cw[0:96, i, j:j + 1], ac, ALU.mult, ALU.add,
                                 [rwr, cr, acr], [acr])
                    if pend is not None:
                        pend()

                    def fin(i=i, ac=ac, acr=acr, tn=tn, tnr=tnr):
                        self.act(S, tn, ac, AF.Tanh, [acr], [tnr])
                        self.stt(S, qkT[0:96, i, :], tn, 1.0, ac, ALU.add, ALU.mult, [tnr, acr], [qk_r[i]])
                    pend = fin
                    if i in (1, 3, 5, 7):
                        next(gs)
                    if i >= 2:
                        for _ in range(2 if i < 15 else 99):
                            if next(sg, "done") == "done":
                                break
                pend()
                for _ in sg:
                    pass
                self.mem_scores(S, qmT, qm_r, mkT, mem_r, pmT, pmT_r, [pb[0], pb[1]], [pb_r[0], pb_r[1]])
                def emit_ktok(s):
                    pT = C["pT"]
                    for j in range(8):
                        self.tr(S, pT[:, j * 96:(j + 1) * 96], qkT[0:96, 8 + j, s * 128:(s + 1) * 128],
                                C["identb"][0:96, 0:96], [qk_r[8 + j], cr], [C["pT_r"]])
                    for h in range(4):
                        self.act(S, ktok[:, s, h * DH:(h + 1) * DH], pT[:, h * DH:(h + 1) * DH], AF.Copy,
                                 [C["pT_r"], g_r], [ktok_r[s]], scale=wkv_[:, s, h:h + 1])

                emit_ktok(0)
                pN = [pb[3], pb[4]]
                pNr = [pb_r[3], pb_r[4]]

                def emit_gbf(s_):
                    gbuf = (ti * NSUB + s_) % 2
                    for h in range(4):
                        self.act(S, Gbf[0:96, gbuf, h, :, :].rearrange("p j e -> p (j e)"),
                                 Cn[0:96, h, :, :].rearrange("p j e -> p (j e)"), AF.Copy, [Cn_r[h], g_r], [Gbf_r[gbuf][h]],
                                 scale=gv[0:96, s_, h:h + 1])

                def stage_A(s):
                    gi = ti * NSUB + s
                    k2 = gi % 2
                    cs = slice(s * 128, (s + 1) * 128)
                    pS, pSr = pb[2], pb_r[2]
                    for h in range(4):
                        for j in range(2):
                            self.mm(S, pS[:, h * 128:(h + 1) * 128], qkT[0:96, 8 + 2 * h + j, cs], qkT[0:96, 2 * h + j, cs],
                                    j == 0, j == 1, [qk_r[8 + 2 * h + j], qk_r[2 * h + j]], [pSr])
                    smv, smr = sm[:, k2, :, :], sm_r[k2]
                    for h in range(4):
                        self.stt(S, smv[:, h, :], pS[:, h * 128:(h + 1) * 128], wv[:, s, h:h + 1], maskc[:, 0, :],
                                 ALU.mult, ALU.mult, [pSr, cr, g_r], [smr])
                    if s == 0:
                        emit_gbf(0)
                    for h in range(4):
                        pC, pCr = pb[5 + h % 2], pb_r[5 + h % 2]
                        for j in range(2):
                            self.mm(S, pC[0:96, j * (DH + 1):(j + 1) * (DH + 1)], ktok[:, s, h * DH + j * 96:h * DH + (j + 1) * 96],
                                    vw[:, s, h, :], True, True, [ktok_r[s], vw_r[s]], [pCr])
                        self.stt(S, Cn[0:96, h, :, :].rearrange("p j e -> p (j e)"),
                                 Cn[0:96, h, :, :].rearrange("p j e -> p (j e)"), gv[0:96, s, h:h + 1],
                                 pC[0:96, 0:2 * (DH + 1)], ALU.mult, ALU.add, [Cn_r[h], g_r, pCr], [Cn_r[h]])
                    gbuf = gi % 2
                    for h in range(4):
                        o = pN[h // 2][:, (h % 2) * (DH + 1):(h % 2 + 1) * (DH + 1)]
                        self.mm(S, o, smv[:, h, :], vw[:, s, h, :], True, False, [smr, vw_r[s]], [pNr[h // 2]])
                        for j in range(2):
                            self.mm(S, o, qkT[0:96, 2 * h + j, cs], Gbf[0:96, gbuf, h, j, :], False, j == 1,
                                    [qk_r[2 * h + j], Gbf_r[gbuf][h]], [pNr[h // 2]])
                    if s + 1 < NSUB:
                        emit_gbf(s + 1)

                def stage_B1(s):
                    gi = ti * NSUB + s
                    k2 = gi % 2
                    hs, hsr = hst[:, k2, :], hst_r[k2]
                    hmv, hmr = hm[:, k2, :, :], hm_r[k2]
                    for hp2 in range(2):
                        pv = pN[hp2][:, 0:2 * (DH + 1)].rearrange("p (h e) -> p h e", h=2)
                        self.act(S, hs[:, 2 * hp2:2 * hp2 + 2], pv[:, :, DH], AF.Abs, [pNr[hp2]], [hsr])
                    self.tt(S, DVE, hs[:, 0:4], hs[:, 0:4], flv[:, s, :], ALU.max, [hsr, g_r], [hsr])
                    S.op(DVE, (lambda hs=hs: (lambda e: e.reciprocal(out=hs[:, 4:8], in_=hs[:, 0:4])))(), [hsr], [hsr])
                    for hp2 in range(2):
                        pv = pN[hp2][:, 0:2 * (DH + 1)].rearrange("p (h e) -> p h e", h=2)
                        self.tt(S, DVE, hmv[:, 2 * hp2:2 * hp2 + 2, :], pv[:, :, 0:DH],
                                hs[:, 4 + 2 * hp2:6 + 2 * hp2].unsqueeze(2).to_broadcast([128, 2, DH]), ALU.mult,
                                [pNr[hp2], hsr], [hmr])

                def stage_B2(s):
                    gi = ti * NSUB + s
                    k2 = gi % 2
                    mx, mxr = mix[:, k2, :], mix_r[k2]
                    hs, hsr = hst[:, k2, :], hst_r[k2]
                    hmv, hmr = hm[:, k2, :, :], hm_r[k2]
                    for h in range(4):
                        self.act(S, hj[:, h, :], hmv[:, h, :], AF.Square, [hmr], [hj_r[h], hsr], accum_out=hs[:, 8 + h:9 + h])
                    self.ts(S, DVE, hs[:, 8:12], hs[:, 8:12], 1.0 / DH, EPS, ALU.mult, ALU.add, [hsr], [hsr])
                    self.tt(S, POOL, hs[:, 12:16], hs[:, 8:12], C["neghalf"][:, 0:1].to_broadcast([128, 4]), ALU.pow,
                            [hsr, cr], [hsr])
                    for h in range(4):
                        self.stt(S, mx[:, h * DH:(h + 1) * DH], hmv[:, h, :], hs[:, 12 + h:13 + h],
                                 G2[:, s, h * DH:(h + 1) * DH], ALU.mult, ALU.mult, [hmr, hsr, G2_r[s]], [mxr])
                    self.mem_pv(S, s, pmT, pmT_r, mvx, mem_r, pb[6], pb_r[6], rr, rr_r, mx, mxr)

                hnds = {}
                for s in range(NSUB):
                    stage_A(s)
                    stage_B1(s)
                    if s + 1 < NSUB:
                        emit_ktok(s + 1)
                    if s >= 1:
                        stage_B2(s - 1)
                    if s >= 2:
                        tail(s - 2, ti)
                    if ti + 1 < NT and NORM_IL:
                        hnds[s] = self.norm_stats(S, C, C["xt"][:, ((ti + 1) * NSUB + s) % XSLOTS, :],
                                                  C["xt_r"][((ti + 1) * NSUB + s) % XSLOTS])
                        if s >= 1:
                            self.norm_tr(S, C, hnds.pop(s - 1),
                                         [(gT_mix, hT[:, :, (s - 1) * 128:s * 128], hT_r[s - 1])])
                stage_B2(NSUB - 1)
                tail(NSUB - 2, ti)
                tail(NSUB - 1, ti)
                if ti + 1 < NT and NORM_IL:
                    self.norm_tr(S, C, hnds.pop(NSUB - 1), [(gT_mix, hT[:, :, (NSUB - 1) * 128:NSUB * 128], hT_r[NSUB - 1])])
            self.stats["amix"] = S.emit(self.G)

    def mem_prologue(self, S, C, l, wmem, wmem_rs, memT, memT_rs, mkT, mvx, mem_r, pbank, pbank_r):
        cr = C["const_r"]
        xt, xt_r, xh, xh_r = C["xt"], C["xt_r"], C["xh"], C["xh_r"]
        self.load_w(S, wmem, wmem_rs[0], self.mem_w_kv[l], 0, 512)
        for mt in range(2):
            self.ld(S, SP, xt[:, mt, :], self.mem[mt * 128:(mt + 1) * 128, :], [xt_r[mt]], xt_r[mt])
            self.cp(S, DVE, xh[:, mt, :], xt[:, mt, :], [xt_r[mt]], [xh_r[mt]])
            pT = C["pT"]
            for kc in range(KC):
                self.tr(S, pT[:, kc * 128:(kc + 1) * 128], xh[:, mt, kc * 128:(kc + 1) * 128], C["identb"][:, :],
                        [xh_r[mt], cr], [C["pT_r"]])
            self.cp(S, DVE, memT[:, :, mt * 128:(mt + 1) * 128], pT[:, :].rearrange("p (a b) -> p a b", a=KC),
                    [C["pT_r"]], memT_rs)
        self.ms(S, POOL, mvx[:, :, :, 64:65], 1.0, [mem_r])
        for hp in range(2):
            for kc in range(KC):
                self.mm(S, pbank[:, 0:MEMT], wmem[:, kc, hp * 128:(hp + 1) * 128], memT[:, kc, 0:MEMT],
                        kc == 0, kc == KC - 1, memT_rs + wmem_rs, [pbank_r])
            self.cp(S, ACT, mkT[:, hp, :], pbank[:, 0:MEMT], [pbank_r], [mem_r])
        for mt in range(2):
            for kc in range(KC):
                self.mm(S, pbank[:, 0:256], memT[:, kc, mt * 128:(mt + 1) * 128], wmem[:, kc, 256:512],
                        kc == 0, kc == KC - 1, memT_rs + wmem_rs, [pbank_r])
            self.cp(S, DVE, mvx[:, mt, :, 0:64], pbank[:, 0:256].rearrange("p (h d) -> p h d", h=4),
                    [pbank_r], [mem_r])

    def mem_scores(self, S, qm, qm_rs, mkT, mem_r, pmT, pmT_r, banks, bank_rs):
        i = 0
        dbg = os.environ.get("K_DBG", "")
        for h in range(4):
            hp, hh = h // 2, h % 2
            if "h0" in dbg and hh == 1:
                continue
            for mt in range(2):
                b, br = banks[i % len(banks)], bank_rs[i % len(banks)]
                i += 1
                self.mm(S, b[:, :], mkT[hh * 64:(hh + 1) * 64, hp, mt * 128:(mt + 1) * 128],
                        qm[hh * 64:(hh + 1) * 64, hp, :], True, True, qm_rs + [mem_r], [br])
                if "noact" in dbg:
                    continue
                self.act(S, pmT[:, h * 2 + mt, :], b[:, :], AF.Exp, [br], [pmT_r], scale=0.125)

    def mem_pv(self, S, s, pmT, pmT_r, mvx, mem_r, pom, pom_r, rr, rr_r, mix, mix_r):
        first = True
        for h in range(4):
            for mt in range(2):
                self.mm(S, pom[:, h * 65:(h + 1) * 65], pmT[:, h * 2 + mt, s * 128:(s + 1) * 128], mvx[:, mt, h, :],
                        first, mt == 1, [pmT_r, mem_r], [pom_r], skip=True)
                first = False
        pv = pom[:, 0:260].rearrange("p (h e) -> p h e", h=4)
        S.op(DVE, lambda e: e.reciprocal(out=rr[:, 0:4], in_=pv[:, :, 64]), [pom_r], [rr_r])
        self.tt(S, DVE, mix[:, 768:1024].rearrange("p (h e) -> p h e", h=4), pv[:, :, 0:64],
                rr[:, 0:4].unsqueeze(2).to_broadcast([128, 4, 64]), ALU.mult, [pom_r, rr_r], [mix_r])

    def out_proj_a(self, S, C, mix, mix_r, mixT, mixT_r, wout, wout_rs, banks, bank_rs):
        cr = C["const_r"]
        pT = C["pT"]
        for kc in range(KC):
            self.tr(S, pT[:, kc * 128:(kc + 1) * 128], mix[:, kc * 128:(kc + 1) * 128], C["identb"][:, :],
                    [mix_r, cr], [C["pT_r"]])
        self.cp(S, ACT, mixT[:, :, :], pT[:, :].rearrange("p (a b) -> p a b", a=KC), [C["pT_r"]], [mixT_r])
        for hf in range(2):
            b, br = banks[hf], bank_rs[hf]
            for kc in range(KC):
                self.mm(S, b[:, :], mixT[:, kc, :], wout[:, kc, hf * 512:(hf + 1) * 512], kc == 0, kc == KC - 1,
                        [mixT_r] + wout_rs, [br])

    def out_proj_b(self, S, xt, xr, banks, bank_rs, dst, gi):
        for hf in range(2):
            b, br = banks[hf], bank_rs[hf]
            self.tt(S, DVE, xt[:, hf * 512:(hf + 1) * 512], b[:, :], xt[:, hf * 512:(hf + 1) * 512], ALU.add,
                    [br, xr], [xr])
        self.ld(S, SP, dst[gi * 128:(gi + 1) * 128, :], xt, [], xr, reads=[xr])

    def out_proj(self, S, C, mix, mix_r, mixT, mixT_r, wout, wout_rs, xt, xr, banks, bank_rs, dst, gi):
        self.out_proj_a(S, C, mix, mix_r, mixT, mixT_r, wout, wout_rs, banks, bank_rs)
        self.out_proj_b(S, xt, xr, banks, bank_rs, dst, gi)

    def phase_bmix(self, src, dst, final):
        nc = self.nc
        S = Sched(nc)
        with ExitStack() as st:
            C = self.phase_consts(S, st, None)
            sb, ps = C["sb"], C["ps"]
            cr = C["const_r"]
            gT_kv = C["gT"][:, 2, :]
            gT_mix = C["gT"][:, 3, :]
            wkv = sb("wkv", (128, KC, 1536), BF16)
            wkv_r = S.regions(2, "wkv")
            wbin = sb("wbin", (128, KC, D), BF16)
            wbin_r = S.regions(1, "wbin")
            wbout = sb("wbout", (128, KC, D), BF16)
            wbout_r = S.regions(1, "wbout")
            biasT = sb("biasT", (128, 5, 12, 128), F32)
            bias_r = S.region("biasT")
            hT = sb("hT", (128, KC, T), BF16)
            hT_r = S.regions(NSUB, "hT")
            hTk = sb("hTk", (128, KC, T), BF16)
            hTk_r = S.regions(NSUB, "hTk")
            KTr = sb("KTr", (128, 2, 6, T), BF16)
            KT_r = [S.regions(6, f"KT{sl}_") for sl in range(2)]
            Vr = sb("Vr", (128, 2 * NSUB, 12, 65), BF16)
            V_r = S.regions(2 * NSUB, "V")
            QT = sb("QT", (128, 8, T), BF16)
            QT_r = S.regions(8, "QT")
            QA = sb("QA", (128, 6, T), BF16)
            QB = sb("QB", (128, 6, T), BF16)
            QAB_r = S.regions(6, "QAB")
            mkT = sb("mkT", (128, 2, MEMT), BF16)
            mvx = sb("mvx", (128, 2, 4, 65), BF16)
            mem_r = S.region("memkv")
            ssb = sb("ssb", (128, 3, 512), F32)
            ssb_r = S.regions(3, "ssb")
            pTs = sb("pTs", (128, 3, 4, 128), BF16)
            pTs_r = S.regions(3, "pTs")
            pmT = sb("pmT", (128, 8, T), BF16)
            pmT_r = S.region("pmT")
            mix = sb("mix", (128, 2, D), BF16)
            mix_r = S.regions(2, "mix")
            mixT = sb("mixT", (128, 2, KC, 128), BF16)
            mixT_r = S.regions(2, "mixT")
            rr = sb("rr", (128, 4, 4), F32)
            rr_r = S.regions(4, "rr")
            pb = [ps(f"pb{i}", (128, 512), F32) for i in range(7)]
            pb_r = S.regions(7, "pb")

            self.mem_prologue(S, C, 1, QT, QT_r, hT, hT_r, mkT, mvx, mem_r, pb[0], pb_r[0])
            self.load_w(S, wkv, wkv_r[0], self.w_kv, 0, 768)
            self.load_w(S, wkv, wkv_r[1], self.w_kv, 768, 1536)
            self.load_w(S, wbin, wbin_r[0], self.b_w_in, 0, D)
            self.load_w(S, wbout, wbout_r[0], self.b_w_out, 0, D)
            for kt in range(5):
                self.ld(S, SP, biasT[:, kt, :, :], self.relbias[:, kt, :, :], [bias_r], bias_r)
            self.ms(S, POOL, biasT[0:64, 0, :, 64:128], NEG, [bias_r])
            self.ms(S, POOL, biasT[64:128, 4, :, 0:64], NEG, [bias_r])
            self.ms(S, POOL, Vr[:, :, :, 64:65], 1.0, V_r)
            self.ms(S, POOL, QA[64:128, :, :], 0.0, QAB_r)
            self.ms(S, POOL, QB[0:64, :, :], 0.0, QAB_r)
            for gi in range(XSLOTS):
                self.load_x(S, C, src, gi)
            loaded = XSLOTS
            ia = 0
            isb = 0
            io = 0
            SB = [pb[2], pb[3], pb[6]]
            SBr = [pb_r[2], pb_r[3], pb_r[6]]
            STOP = int(os.environ.get("K_STOP", 99))

            def do_stats(ti_, s_):
                gi = ti_ * NSUB + s_
                sl = gi % XSLOTS
                return self.norm_stats(S, C, C["xt"][:, sl, :], C["xt_r"][sl])

            def do_tr(hnd, s_):
                self.norm_tr(S, C, hnd, [(gT_mix, hT[:, :, s_ * 128:(s_ + 1) * 128], hT_r[s_]),
                                         (gT_kv, hTk[:, :, s_ * 128:(s_ + 1) * 128], hTk_r[s_])])

            def tail_a(P_):
                self.out_proj_a(S, C, mix[:, P_ % 2, :], mix_r[P_ % 2], mixT[:, P_ % 2, :, :], mixT_r[P_ % 2], wbout, wbout_r,
                                [pb[0], pb[1]], [pb_r[0], pb_r[1]])

            def tail_b(P_):
                nonlocal loaded
                sl = P_ % XSLOTS
                self.out_proj_b(S, C["xt"][:, sl, :], C["xt_r"][sl], [pb[0], pb[1]], [pb_r[0], pb_r[1]], dst, P_)
                if loaded < NT * NSUB and loaded % XSLOTS == sl:
                    self.load_x(S, C, src, loaded)
                    loaded += 1

            def tail(P_):
                tail_a(P_)
                tail_b(P_)

            for s in range(NSUB):
                do_tr(do_stats(0, s), s)
            pending_tail = None
            for ti in range(NT):
                slot = ti % 2
                for j in range(6):
                    b, br = pb[ia % 2], pb_r[ia % 2]
                    ia += 1
                    for kc in range(KC):
                        self.mm(S, b[:, :], wkv[:, kc, j * 128:(j + 1) * 128], hTk[:, kc, :], kc == 0, kc == KC - 1,
                                hTk_r + [wkv_r[0]], [br])
                    self.cp(S, ACT, KTr[:, slot, j, :], b[:, :], [br], [KT_r[slot][j]])
                    if j == 2 and pending_tail is not None:
                        tail(pending_tail)
                        pending_tail = None
                for s in range(NSUB):
                    vi = slot * NSUB + s
                    for (c0, c1, h0, h1) in ((768, 1280, 0, 8), (1280, 1536, 8, 12)):
                        b, br = pb[ia % 2], pb_r[ia % 2]
                        ia += 1
                        n = c1 - c0
                        for kc in range(KC):
                            self.mm(S, b[:, 0:n], hTk[:, kc, s * 128:(s + 1) * 128], wkv[:, kc, c0:c1], kc == 0,
                                    kc == KC - 1, [hTk_r[s], wkv_r[1]], [br])
                        self.cp(S, DVE, Vr[:, vi, h0:h1, 0:64], b[:, 0:n].rearrange("p (h d) -> p h d", d=64),
                                [br], [V_r[vi]])
                for j in range(8):
                    b, br = pb[ia % 2], pb_r[ia % 2]
                    ia += 1
                    for kc in range(KC):
                        self.mm(S, b[:, :], wbin[:, kc, j * 128:(j + 1) * 128], hT[:, kc, :], kc == 0, kc == KC - 1,
                                hT_r + wbin_r, [br])
                    if j < 6:
                        self.cp(S, ACT, QA[0:64, j, :], b[0:64, :], [br], [QAB_r[j]])
                        self.cp(S, DVE, QB[64:128, j, :], b[64:128, :], [br], [QAB_r[j]])
                    else:
                        self.cp(S, ACT, QT[:, j, :], b[:, :], [br], [QT_r[j]])
                self.mem_scores(S, QT[:, 6:8, :], QT_r[6:8], mkT, mem_r, pmT, pmT_r, [pb[0], pb[1]], [pb_r[0], pb_r[1]])
                hnds = {}
                for s in range(NSUB):
                    P = ti * NSUB + s
                    mx, mxr = mix[:, P % 2, :], mix_r[P % 2]
                    kts = [kt for kt in range(5) if P - 4 + kt >= 0]
                    steps = [(hg, kt) for hg in range(3) for kt in kts]
                    po_of = {}
                    for hg in range(3):
                        po_of[hg] = (pb[4 + io % 2], pb_r[4 + io % 2])
                        io += 1
                    pom, pomr = pb[4 + io % 2], pb_r[4 + io % 2]
                    io += 1
                    slots = {}

                    def emit_S(step):
                        nonlocal isb
                        hg, kt = step
                        kp = P - 4 + kt
                        sk = (kp // NSUB) % 2
                        subk = kp % NSUB
                        q3 = isb % 3
                        bs, bsr = SB[q3], SBr[q3]
                        sbuf, sbr = ssb[:, q3, :], ssb_r[q3]
                        pt, ptr = pTs[:, q3, :, :], pTs_r[q3]
                        isb += 1
                        for hl in range(4):
                            h = hg * 4 + hl
                            Qh = QA if h % 2 == 0 else QB
                            self.mm(S, bs[:, hl * 128:(hl + 1) * 128],
                                    KTr[:, sk, h // 2, subk * 128:(subk + 1) * 128],
                                    Qh[:, h // 2, s * 128:(s + 1) * 128], True, True,
                                    [KT_r[sk][h // 2], QAB_r[h // 2]], [bsr])
                        self.stt(S, sbuf, bs[:, :], 0.125,
                                 biasT[:, kt, hg * 4:(hg + 1) * 4, :].rearrange("p h q -> p (h q)"),
                                 ALU.mult, ALU.add, [bsr, bias_r], [sbr])
                        self.act(S, pt.rearrange("p h q -> p (h q)"), sbuf, AF.Exp, [sbr], [ptr])
                        slots[step] = (pt, ptr, sk, subk)

                    def emit_PV(step):
                        hg, kt = step
                        pt, ptr, sk, subk = slots.pop(step)
                        po, por = po_of[hg]
                        for hl in range(4):
                            h = hg * 4 + hl
                            self.mm(S, po[:, hl * 65:(hl + 1) * 65], pt[:, hl, :], Vr[:, sk * NSUB + subk, h, :],
                                    kt == kts[0] and hl == 0, kt == kts[-1], [ptr, V_r[sk * NSUB + subk]], [por],
                                    skip=True)
                        if kt == kts[-1]:
                            pv = po[:, 0:260].rearrange("p (h e) -> p h e", h=4)
                            rq, rqr = rr[:, hg, :], rr_r[hg]
                            S.op(DVE, (lambda pv=pv, rq=rq: (lambda e: e.reciprocal(out=rq, in_=pv[:, :, 64])))(), [por], [rqr])
                            self.tt(S, DVE, mx[:, hg * 256:(hg + 1) * 256].rearrange("p (h e) -> p h e", h=4),
                                    pv[:, :, 0:64], rq.unsqueeze(2).to_broadcast([128, 4, 64]), ALU.mult,
                                    [por, rqr], [mxr])

                    for i_ in range(min(2, len(steps))):
                        emit_S(steps[i_])
                    for i_, step in enumerate(steps):
                        if i_ + 2 < len(steps):
                            emit_S(steps[i_ + 2])
                        emit_PV(step)
                        if i_ == min(3, len(steps) - 1) and s >= 1:
                            tail(P - 1)
                            if ti + 1 < NT:
                                hnds[s - 1] = do_stats(ti + 1, s - 1)
                                if s >= 2:
                                    do_tr(hnds.pop(s - 2), s - 2)
                                if s == NSUB - 1:
                                    hnds[NSUB - 1] = do_stats(ti + 1, NSUB - 1)
                    self.mem_pv(S, s, pmT, pmT_r, mvx, mem_r, pom, pomr, rr[:, 3, :], rr_r[3], mx, mxr)
                if ti + 1 < NT:
                    do_tr(hnds.pop(NSUB - 2), NSUB - 2)
                    do_tr(hnds.pop(NSUB - 1), NSUB - 1)
                    pending_tail = ti * NSUB + NSUB - 1
                else:
                    tail(ti * NSUB + NSUB - 1)
            self.stats["bmix"] = S.emit(self.G)


def host_consts():
    ident = np.eye(128, dtype=np.float32)
    s = np.arange(128)[:, None]
    t = np.arange(128)[None, :]
    negU = np.where(s <= t, -1.0, 0.0).astype(np.float32)
    maskc = np.where(s <= t, np.float32(DH ** -0.5), np.float32(0.0)).astype(np.float32)
    return {"c_ident": ident, "c_negU": negU, "c_maskc": maskc}


def host_layout(inp):
    f = lambda a: np.ascontiguousarray(np.asarray(a, dtype=np.float32))
    g = np.stack([f(inp["norm_mix_g"])[0], f(inp["norm_ffn_g"])[0], f(inp["kv_norm_g"]),
                  f(inp["norm_mix_g"])[1], f(inp["norm_ffn_g"])[1]], 0)
    gT_all = np.ascontiguousarray(g.reshape(5, KC, 128).transpose(2, 0, 1))
    cwA = np.ascontiguousarray(f(inp["a_conv_w"])[0].reshape(4, 16, 96).transpose(2, 1, 0))
    cbA = np.ascontiguousarray(f(inp["a_conv_b"])[0].reshape(16, 96).T)
    cwF = np.ascontiguousarray(f(inp["ffn_conv_w"]).reshape(2, 3, NFT, 128).transpose(3, 0, 2, 1))
    cbF = np.ascontiguousarray(f(inp["ffn_conv_b"]).reshape(2, NFT, 128).transpose(2, 0, 1))
    rel = np.arange(768) - 127
    idx = np.clip(rel, -63, 128) + 63
    relext = f(inp["b_rel_bias"])[0][:, idx]
    kj = np.arange(128)[:, None, None]
    kt = np.arange(5)[None, :, None]
    qi = np.arange(128)[None, None, :]
    gidx = qi - kj + (4 - kt) * 128 + 127
    relbias = np.ascontiguousarray(relext[:, gidx].transpose(1, 2, 0, 3))
    shared = {
        "a_w_in": f(inp["a_w_in"])[0], "a_w_out": f(inp["a_w_out"])[0], "w_kv": f(inp["w_kv"]),
        "b_w_in": f(inp["b_w_in"])[0], "b_w_out": f(inp["b_w_out"])[0], "mem_w_kv": f(inp["mem_w_kv"]),
        "ffn_w_up": f(inp["ffn_w_up"]), "ffn_w_down": f(inp["ffn_w_down"]),
        "gT_all": gT_all, "final_g": f(inp["final_g"]).reshape(1, D), "gate_b": f(inp["a_gate_b"]).reshape(1, 8),
        "cwA": cwA, "cbA": cbA, "head_g": f(inp["a_head_g"]).reshape(1, AW), "cwF": cwF, "cbF": cbF,
        "relbias": relbias,
    }
    shared.update(host_consts())
    return shared


_CACHE = {}


def run(inputs, phases=("A_mix", "A_ffn", "B_mix", "B_ffn"), final_norm=True, ncores=8, trace=False):
    key = (tuple(phases), final_norm)
    if key not in _CACHE:
        _CACHE[key] = Prog(phases, final_norm).build()
    nc = _CACHE[key]
    shared = host_layout(inputs)
    x = np.asarray(inputs["x"], dtype=np.float32)
    mem = np.asarray(inputs["mem"], dtype=np.float32)
    in_maps = []
    for c in range(ncores):
        m = dict(shared)
        m["x"] = np.ascontiguousarray(x[c])
        m["mem"] = np.ascontiguousarray(mem[c])
        in_maps.append(m)
    res = run_bass_kernel_spmd(nc, in_maps, core_ids=list(range(ncores)), trace=trace)
    out = np.stack([np.asarray(r["out"]) for r in res.results], 0)
    return out, res


def kernel(**inputs):
    out, _ = run(inputs)
    return out.astype(np.float32)
```
